# Optimizing a Trainium2 kernel written in Bass

```python
import math
import jax, jax.numpy as jnp
from jax import lax
import numpy as np

D_MODEL = 1024
BATCH = 4
SEQ = 4096
DEPTH = 1

MEM_LEN = 256
EPS = 1e-6
NEG_INF = -1e30

SSD_HEADS = 20
SSD_HEAD_DIM = 64
SSD_INNER = SSD_HEADS * SSD_HEAD_DIM
SSD_GROUPS = 4
SSD_STATE = 128
SSD_BC = SSD_GROUPS * SSD_STATE
SSD_CONV = 4
SSD_CHUNK = 128
CONV_CH = SSD_INNER + 2 * SSD_BC

ATT_HEADS = 12
ATT_HEAD_DIM = 64
ATT_INNER = ATT_HEADS * ATT_HEAD_DIM
DILATION_PAIRS = ((128, 1), (512, 4), (2048, 16))
ATT_BLOCK = 128

MIX_WIDTH = SSD_INNER + ATT_INNER
OFF_Z = SSD_INNER
OFF_XBC = OFF_Z + CONV_CH
OFF_DT = OFF_XBC + SSD_HEADS
OFF_Q = OFF_DT + ATT_INNER
OFF_K = OFF_Q + ATT_INNER
IN_COLS = OFF_K + ATT_INNER

MEM_HEADS = 4
MEM_HEAD_DIM = 128
MEM_INNER = MEM_HEADS * MEM_HEAD_DIM

PEER_KEYS = 128
PEER_EXPERTS = PEER_KEYS * PEER_KEYS
PEER_HEADS = 8
PEER_QDIM = 256
PEER_TOPK = 16
PEER_BLOCK = 64

kernel_name = "hybrid_ssd_dilated_swa_memxattn_peer"


def rms_norm(x, g):
    xf = x.astype(jnp.float32)
    y = xf * lax.rsqrt(jnp.mean(xf * xf, axis=-1, keepdims=True) + EPS)
    return (y * g.astype(jnp.float32)).astype(x.dtype)


def alibi_slopes(n_heads):
    return 2.0 ** (-8.0 * jnp.arange(1, n_heads + 1, dtype=jnp.float32) / n_heads)


def causal_depthwise_conv(u, w, b):
    k = w.shape[0]
    y = lax.conv_general_dilated(
        u, w[:, None, :].astype(u.dtype), window_strides=(1,), padding=[(k - 1, 0)],
        dimension_numbers=('NWC', 'WIO', 'NWC'), feature_group_count=u.shape[-1])
    return y + b.astype(u.dtype)


def segsum(a):
    q = a.shape[-1]
    cs = jnp.cumsum(a, axis=-1)
    d = cs[..., :, None] - cs[..., None, :]
    return jnp.where(jnp.tril(jnp.ones((q, q), dtype=bool)), d, -jnp.inf)


def ssd_chunked(x, dt, a, bm, cm):
    b, s, h, p = x.shape
    g, n = bm.shape[2], bm.shape[3]
    r = h // g
    q = SSD_CHUNK
    c = s // q
    xdt = (x * dt[..., None]).reshape(b, c, q, g, r, p)
    bc = bm.reshape(b, c, q, g, n)
    cc = cm.reshape(b, c, q, g, n)
    adt = (dt * a).reshape(b, c, q, g, r).transpose(0, 3, 4, 1, 2)
    a_cs = jnp.cumsum(adt, axis=-1)
    lmat = jnp.exp(segsum(adt))
    cb = jnp.einsum('bclgn,bcsgn->bgcls', cc, bc)
    y_diag = jnp.einsum('bgcls,bgrcls,bcsgrp->bclgrp', cb, lmat, xdt)
    decay_states = jnp.exp(a_cs[..., -1:] - a_cs)
    states = jnp.einsum('bcsgn,bgrcs,bcsgrp->cbgrpn', bc, decay_states, xdt)
    chunk_decay = jnp.exp(a_cs[..., -1]).transpose(3, 0, 1, 2)

    def step(hstate, inp):
        st, dec = inp
        return hstate * dec[..., None, None] + st, hstate

    h0 = jnp.zeros((b, g, r, p, n), dtype=jnp.float32)
    _, prev = lax.scan(step, h0, (states, chunk_decay))
    y_off = jnp.einsum('bclgn,cbgrpn,bgrcl->bclgrp', cc, prev, jnp.exp(a_cs))
    return (y_diag + y_off).reshape(b, s, h, p)


def banded_window_attention(q, k, v, w_units, dist_unit, slopes):
    n, L, h, dh = q.shape
    blk = ATT_BLOCK
    nb = -(-L // blk)
    pad = ((0, 0), (0, nb * blk - L), (0, 0), (0, 0))

    def blocks(t):
        return jnp.pad(t, pad).reshape(n, nb, blk, h, dh)

    def with_prev(t):
        prev = jnp.concatenate([jnp.zeros_like(t[:, :1]), t[:, :-1]], axis=1)
        return jnp.concatenate([prev, t], axis=2)

    qb = blocks(q)
    kb = with_prev(blocks(k))
    vb = with_prev(blocks(v))
    s = jnp.einsum('nbqhd,nbkhd->nbhqk', qb, kb).astype(jnp.float32) * (dh ** -0.5)
    dist = (jnp.arange(blk)[:, None] + blk) - jnp.arange(2 * blk)[None, :]
    kpos = jnp.arange(nb)[:, None, None] * blk + jnp.arange(2 * blk)[None, None, :] - blk
    valid = (dist >= 0) & (dist <= w_units) & (kpos >= 0)
    bias = -slopes[:, None, None] * (dist * dist_unit).astype(jnp.float32)
    s = jnp.where(valid[None, :, None], s + bias[None, None], NEG_INF)
    lse = jax.nn.logsumexp(s, axis=-1)
    p = jnp.exp(s - lse[..., None])
    o = jnp.einsum('nbhqk,nbkhd->nbqhd', p, vb.astype(jnp.float32))
    o = o.reshape(n, nb * blk, h, dh)[:, :L]
    lse = lse.transpose(0, 1, 3, 2).reshape(n, nb * blk, h)[:, :L]
    return o, lse


def dilated_attention(q, k, v):
    b, s, h, dh = q.shape
    slopes = alibi_slopes(h)
    outs, lses = [], []
    for window, dil in DILATION_PAIRS:
        sub = s // dil

        def to_sub(t):
            return t.reshape(b, sub, dil, h, dh).transpose(0, 2, 1, 3, 4).reshape(b * dil, sub, h, dh)

        o, l = banded_window_attention(to_sub(q), to_sub(k), to_sub(v), window // dil, dil, slopes)
        outs.append(o.reshape(b, dil, sub, h, dh).transpose(0, 2, 1, 3, 4).reshape(b, s, h, dh))
        lses.append(l.reshape(b, dil, sub, h).transpose(0, 2, 1, 3).reshape(b, s, h))
    wts = jax.nn.softmax(jnp.stack(lses, axis=0), axis=0)
    return jnp.einsum('gbsh,gbshd->bshd', wts, jnp.stack(outs, axis=0))


def parallel_mixer(h, w_in, conv_w, conv_b, dt_bias, a_log, d_skip, ssd_norm_g,
                   attn_q_norm_g, attn_k_norm_g, w_out):
    b, s, _ = h.shape
    proj = h @ w_in
    z = proj[..., :OFF_Z]
    xbc = proj[..., OFF_Z:OFF_XBC]
    dt_raw = proj[..., OFF_XBC:OFF_DT]
    q = proj[..., OFF_DT:OFF_Q]
    k = proj[..., OFF_Q:OFF_K]
    v = proj[..., OFF_K:]

    xbc = jax.nn.silu(causal_depthwise_conv(xbc, conv_w, conv_b))
    xs = xbc[..., :SSD_INNER].reshape(b, s, SSD_HEADS, SSD_HEAD_DIM).astype(jnp.float32)
    bm = xbc[..., SSD_INNER:SSD_INNER + SSD_BC].reshape(b, s, SSD_GROUPS, SSD_STATE).astype(jnp.float32)
    cm = xbc[..., SSD_INNER + SSD_BC:].reshape(b, s, SSD_GROUPS, SSD_STATE).astype(jnp.float32)
    dt = jax.nn.softplus(dt_raw.astype(jnp.float32) + dt_bias.astype(jnp.float32))
    a = -jnp.exp(a_log.astype(jnp.float32))
    y = ssd_chunked(xs, dt, a, bm, cm) + d_skip.astype(jnp.float32)[:, None] * xs
    y = y.reshape(b, s, SSD_INNER) * jax.nn.silu(z.astype(jnp.float32))
    y = rms_norm(y.reshape(b, s, SSD_GROUPS, SSD_INNER // SSD_GROUPS),
                 ssd_norm_g.reshape(SSD_GROUPS, SSD_INNER // SSD_GROUPS)).reshape(b, s, SSD_INNER)

    q = rms_norm(q.reshape(b, s, ATT_HEADS, ATT_HEAD_DIM), attn_q_norm_g)
    k = rms_norm(k.reshape(b, s, ATT_HEADS, ATT_HEAD_DIM), attn_k_norm_g)
    v = v.reshape(b, s, ATT_HEADS, ATT_HEAD_DIM)
    o = dilated_attention(q, k, v).reshape(b, s, ATT_INNER)

    mixed = jnp.concatenate([y.astype(h.dtype), o.astype(h.dtype)], axis=-1)
    return mixed @ w_out


def memory_cross_attention(h, mem_n, w_q, w_kv, q_norm_g, k_norm_g, w_o):
    b, s, _ = h.shape
    m = mem_n.shape[1]
    q = rms_norm((h @ w_q).reshape(b, s, MEM_HEADS, MEM_HEAD_DIM), q_norm_g)
    kv = mem_n @ w_kv
    k = rms_norm(kv[..., :MEM_INNER].reshape(b, m, MEM_HEADS, MEM_HEAD_DIM), k_norm_g)
    v = kv[..., MEM_INNER:].reshape(b, m, MEM_HEADS, MEM_HEAD_DIM)
    sc = jnp.einsum('bshd,bmhd->bhsm', q, k).astype(jnp.float32) * (MEM_HEAD_DIM ** -0.5)
    p = jax.nn.softmax(sc, axis=-1)
    o = jnp.einsum('bhsm,bmhd->bshd', p, v.astype(jnp.float32)).astype(h.dtype)
    return o.reshape(b, s, MEM_INNER) @ w_o


def peer_ffn(h, w_query, sub_keys1, sub_keys2, u_table, v_table):
    b, s, d = h.shape
    half = PEER_QDIM // 2
    qr = (h @ w_query).reshape(b, s, PEER_HEADS, PEER_QDIM).astype(jnp.float32)
    s1 = jnp.einsum('bshd,kd->bshk', qr[..., :half], sub_keys1.astype(jnp.float32))
    s2 = jnp.einsum('bshd,kd->bshk', qr[..., half:], sub_keys2.astype(jnp.float32))
    v1, i1 = lax.top_k(s1, PEER_TOPK)
    v2, i2 = lax.top_k(s2, PEER_TOPK)
    cand = (v1[..., :, None] + v2[..., None, :]).reshape(b, s, PEER_HEADS, PEER_TOPK * PEER_TOPK)
    cand_idx = (i1[..., :, None] * PEER_KEYS + i2[..., None, :]).reshape(b, s, PEER_HEADS, PEER_TOPK * PEER_TOPK)
    top_s, top_pos = lax.top_k(cand, PEER_TOPK)
    expert_idx = jnp.take_along_axis(cand_idx, top_pos, axis=-1)
    gate = jax.nn.softmax(top_s, axis=-1)

    nblk = s // PEER_BLOCK

    def to_blocks(t):
        return t.reshape(b, nblk, PEER_BLOCK, *t.shape[2:]).swapaxes(0, 1)

    def block_fn(args):
        hb, idx, g = args
        u = u_table[idx]
        act = jax.nn.gelu(jnp.einsum('btd,bthkd->bthk', hb, u).astype(jnp.float32), approximate=False)
        return jnp.einsum('bthk,bthkd->btd', (g * act).astype(hb.dtype), v_table[idx])

    out = lax.map(block_fn, (to_blocks(h), to_blocks(expert_idx), to_blocks(gate)))
    return out.swapaxes(0, 1).reshape(b, s, d)


def setup_inputs(seed: int = 0) -> dict:
    key = jax.random.key(seed)
    ks = iter(jax.random.split(key, 40))
    f32 = jnp.float32

    def normal(shape, scale):
        return jax.random.normal(next(ks), shape, f32) * scale

    def gain(shape):
        return 1.0 + normal(shape, 0.02)

    L = DEPTH
    dt0 = jnp.exp(jax.random.uniform(next(ks), (L, SSD_HEADS), f32, math.log(1e-3), math.log(1e-1)))
    dt_bias = dt0 + jnp.log(-jnp.expm1(-dt0))
    a_log = jnp.log(jax.random.uniform(next(ks), (L, SSD_HEADS), f32, 1.0, 16.0))
    return {
        "x": normal((BATCH, SEQ, D_MODEL), 1.0),
        "mem": normal((BATCH, MEM_LEN, D_MODEL), 1.0),
        "mix_norm_g": gain((L, D_MODEL)),
        "w_in": normal((L, D_MODEL, IN_COLS), D_MODEL ** -0.5),
        "conv_w": normal((L, SSD_CONV, CONV_CH), SSD_CONV ** -0.5),
        "conv_b": normal((L, CONV_CH), 0.02),
        "dt_bias": dt_bias,
        "a_log": a_log,
        "d_skip": 1.0 + normal((L, SSD_HEADS), 0.1),
        "ssd_norm_g": gain((L, SSD_INNER)),
        "attn_q_norm_g": gain((L, ATT_HEAD_DIM)),
        "attn_k_norm_g": gain((L, ATT_HEAD_DIM)),
        "w_out": normal((L, MIX_WIDTH, D_MODEL), MIX_WIDTH ** -0.5),
        "xattn_norm_g": gain((L, D_MODEL)),
        "mem_norm_g": gain((L, D_MODEL)),
        "mem_w_q": normal((L, D_MODEL, MEM_INNER), D_MODEL ** -0.5),
        "mem_w_kv": normal((L, D_MODEL, 2 * MEM_INNER), D_MODEL ** -0.5),
        "mem_q_norm_g": gain((L, MEM_HEAD_DIM)),
        "mem_k_norm_g": gain((L, MEM_HEAD_DIM)),
        "mem_w_o": normal((L, MEM_INNER, D_MODEL), MEM_INNER ** -0.5),
        "ffn_norm_g": gain((L, D_MODEL)),
        "peer_w_query": normal((L, D_MODEL, PEER_HEADS * PEER_QDIM), D_MODEL ** -0.5),
        "peer_sub_keys1": normal((L, PEER_KEYS, PEER_QDIM // 2), (PEER_QDIM // 2) ** -0.5),
        "peer_sub_keys2": normal((L, PEER_KEYS, PEER_QDIM // 2), (PEER_QDIM // 2) ** -0.5),
        "peer_u": normal((L, PEER_EXPERTS, D_MODEL), D_MODEL ** -0.5),
        "peer_v": normal((L, PEER_EXPERTS, D_MODEL), (PEER_HEADS * PEER_TOPK) ** -0.5),
    }


def reference(x, mem, mix_norm_g, w_in, conv_w, conv_b, dt_bias, a_log, d_skip, ssd_norm_g,
              attn_q_norm_g, attn_k_norm_g, w_out, xattn_norm_g, mem_norm_g, mem_w_q, mem_w_kv,
              mem_q_norm_g, mem_k_norm_g, mem_w_o, ffn_norm_g, peer_w_query, peer_sub_keys1,
              peer_sub_keys2, peer_u, peer_v):
    for l in range(DEPTH):
        h = rms_norm(x, mix_norm_g[l])
        x = x + parallel_mixer(h, w_in[l], conv_w[l], conv_b[l], dt_bias[l], a_log[l], d_skip[l],
                               ssd_norm_g[l], attn_q_norm_g[l], attn_k_norm_g[l], w_out[l])
        h = rms_norm(x, xattn_norm_g[l])
        m = rms_norm(mem, mem_norm_g[l])
        x = x + memory_cross_attention(h, m, mem_w_q[l], mem_w_kv[l], mem_q_norm_g[l],
                                       mem_k_norm_g[l], mem_w_o[l])
        h = rms_norm(x, ffn_norm_g[l])
        x = x + peer_ffn(h, peer_w_query[l], peer_sub_keys1[l], peer_sub_keys2[l], peer_u[l], peer_v[l])
    return x
```

```python
import os
import math
import numpy as np
import concourse.bass as bass
import concourse.mybir as mybir
from concourse.bass_utils import run_bass_kernel_spmd

F32 = mybir.dt.float32
BF16 = mybir.dt.bfloat16
I32 = mybir.dt.int32
U32 = mybir.dt.uint32
AF = mybir.ActivationFunctionType
ALU = mybir.AluOpType
AX = mybir.AxisListType

D = 1024
NB = 4
SEQ = 4096
HALF = 2048
WIN = 4096
EPS = 1e-6
OFF_Z = 1280
OFF_XBC = OFF_Z + 2304
OFF_DT = OFF_XBC + 20
OFF_Q = OFF_DT + 768
OFF_K = OFF_Q + 768
IN_COLS = OFF_K + 768
NCONST = 128 * 4 + 16


class Res:
    __slots__ = ("w", "r")

    def __init__(self):
        self.w = None
        self.r = {}


class Sched:
    COMPUTE = ("pe", "act", "dve", "pool")
    NDMA = 12

    def __init__(self, nc):
        self.nc = nc
        self.streams = {k: [] for k in ("pe", "act", "dve", "pool", "sp")}
        self.esem = {k: nc.alloc_semaphore("es_" + k) for k in self.COMPUTE}
        self.ecnt = {k: 0 for k in self.COMPUTE}
        self.dsem = {q: [nc.alloc_semaphore("ds_%s%d" % (q, i)) for i in range(self.NDMA)]
                     for q in ("sp", "pool")}
        self.dcnt = {q: [0] * self.NDMA for q in ("sp", "pool")}
        self.dnext = {q: 0 for q in ("sp", "pool")}
        self.known = {k: {} for k in self.streams}
        self.nwaits = 0
        self.nops = 0

    def _sem_of(self, pid):
        if pid[0] == "e":
            return self.esem[pid[1]], pid[2], ("e", pid[1])
        return self.dsem[pid[1]][pid[2]], pid[3], ("d", pid[1], pid[2])

    def _wait(self, eng, pid):
        sem, val, key = self._sem_of(pid)
        if self.known[eng].get(key, 0) >= val:
            return
        self.known[eng][key] = val
        self.nwaits += 1
        self.streams[eng].append(lambda e, sem=sem, val=val: e.wait_ge(sem, val))

    def _deps(self, eng, reads, writes):
        deps = []
        for r in reads:
            if r.w is not None:
                deps.append(r.w)
        for w in writes:
            if w.w is not None:
                deps.append(w.w)
            deps.extend(w.r.values())
        for pid in deps:
            if pid[0] == "e" and pid[1] == eng and eng == "pe":
                continue
            self._wait(eng, pid)

    def _commit(self, pid, reads, writes):
        key = pid[:2] if pid[0] == "e" else pid[:3]
        for r in reads:
            r.r[key] = pid
        for w in writes:
            w.w = pid
            w.r = {}

    def op(self, eng, meth, reads=(), writes=(), **kw):
        fn = lambda e, meth=meth, kw=kw: getattr(e, meth)(**kw)
        self._deps(eng, reads, writes)
        self.ecnt[eng] += 1
        idx = self.ecnt[eng]
        sem = self.esem[eng]
        self.streams[eng].append(lambda e, fn=fn, sem=sem: fn(e).then_inc(sem, 1))
        self.nops += 1
        self._commit(("e", eng, idx), reads, writes)

    def dma(self, q, reads=(), writes=(), meth="dma_start", **kw):
        fn = lambda e, meth=meth, kw=kw: getattr(e, meth)(**kw)
        i = self.dnext[q]
        self.dnext[q] = (i + 1) % self.NDMA
        prev = self.dcnt[q][i]
        if prev > 0:
            self._wait(q, ("d", q, i, prev))
        self._deps(q, reads, writes)
        self.dcnt[q][i] = prev + 16
        sem = self.dsem[q][i]
        self.streams[q].append(lambda e, fn=fn, sem=sem: fn(e).then_inc(sem, 16))
        self.nops += 1
        self._commit(("d", q, i, prev + 16), reads, writes)

    def barrier(self):
        for eng in self.streams:
            for k in self.COMPUTE:
                if k != eng and self.ecnt[k] > 0:
                    self._wait(eng, ("e", k, self.ecnt[k]))
            for q in self.dsem:
                for i in range(self.NDMA):
                    if self.dcnt[q][i] > 0:
                        self._wait(eng, ("d", q, i, self.dcnt[q][i]))

    def emit(self):
        nc = self.nc
        with nc.Block() as block:
            @block.tensor
            def _(e):
                for it in self.streams["pe"]:
                    it(e)

            @block.scalar
            def _(e):
                for it in self.streams["act"]:
                    it(e)

            @block.vector
            def _(e):
                for it in self.streams["dve"]:
                    it(e)

            @block.gpsimd
            def _(e):
                for it in self.streams["pool"]:
                    it(e)

            @block.sync
            def _(e):
                for it in self.streams["sp"]:
                    it(e)


class Arena:
    def __init__(self, nc):
        self.nc = nc
        self.off = (nc.sbuf_base + 31) // 32 * 32
        self.top = nc.sbuf_top
        self.n = 0

    def alloc(self, shape, dt=F32, name="t"):
        sz = {F32: 4, BF16: 2, I32: 4, U32: 4}[dt]
        nb = int(np.prod(shape[1:])) * sz
        off = self.off
        self.off += (nb + 31) // 32 * 32
        assert self.off <= self.top, "SBUF overflow at %s: %d > %d" % (name, self.off, self.top)
        self.n += 1
        return self.nc.alloc_sbuf_tensor_at("%s_%d" % (name, self.n), list(shape), dt, offset=off)

    def mark(self):
        return self.off

    def release(self, m):
        self.off = m


def build(stop_after=None):
    nc = bass.Bass("TRN2", target_bir_lowering=False)
    S = Sched(nc)
    A = Arena(nc)

    def din(name, shape, dt=F32):
        return nc.dram_tensor(name, list(shape), dt, kind="ExternalInput").ap()

    xw_d = din("xw", [WIN, D])
    flag_d = din("flag", [128, 1])
    mem_d = din("mem", [256, D])
    cst_d = din("cst", [128, NCONST])
    mtab_d = din("mtab", [128, 17 * 128])
    w_in_d = din("w_in", [D, IN_COLS])
    convw_d = din("convw", [128, 72])
    convb_d = din("convb", [128, 18])
    vec_d = {}
    for nm, n in [("mix_norm_g", 1024), ("dt_bias", 20), ("a_log", 20), ("d_skip", 20), ("ssd_norm_g", 1280),
                  ("attn_q_norm_g", 64), ("attn_k_norm_g", 64), ("xattn_norm_g", 1024), ("mem_norm_g", 1024),
                  ("mem_q_norm_g", 128), ("mem_k_norm_g", 128), ("ffn_norm_g", 1024)]:
        vec_d[nm] = din(nm, [1, n])
    w_out_d = din("w_out", [2048, D])
    mem_w_q_d = din("mem_w_q", [D, 512])
    mem_w_kv_d = din("mem_w_kv", [D, 1024])
    mem_w_o_d = din("mem_w_o", [512, D])
    peer_wq_d = din("peer_w_query", [D, 2048])
    keys1T_d = din("keys1T", [128, 128])
    keys2T_d = din("keys2T", [128, 128])
    peer_u_d = din("peer_u", [16384, D])
    peer_v_d = din("peer_v", [16384, D])
    out_d = nc.dram_tensor("out", [HALF, D], F32, kind="ExternalOutput").ap()
    yT_scr = nc.dram_tensor("yT_scr", [128, 10, HALF], BF16).ap()
    oT_scr = nc.dram_tensor("oT_scr", [128, 6, HALF], BF16).ap()

    PS = [nc.alloc_psum_tensor("ps%d" % i, [128, 512], F32) for i in range(7)]
    RPS = [Res() for _ in range(7)]
    PB = nc.alloc_psum_tensor("psb", [128, 1024], BF16)
    RPB = Res()

    def T(shape, dt=F32, name="t"):
        return A.alloc(shape, dt, name), Res()

    def v3(ap, a):
        return ap.rearrange("p (a b) -> p a b", a=a)

    cst, Rcst = T([128, NCONST], F32, "cst")
    S.dma("sp", writes=[Rcst], out=cst[:], in_=cst_d)
    ident_f = cst[:, 0:128]
    tri_f = cst[:, 128:256]
    ones_f = cst[:, 256:384]
    d0_f = cst[:, 384:512]
    iota16 = cst[:, 512:528]
    ident_b, Ridb = T([128, 128], BF16, "identb")
    S.op("dve", "tensor_copy", reads=[Rcst], writes=[Ridb], out=ident_b[:], in_=ident_f)
    flag, Rflag = T([128, 1], F32, "flag")
    S.dma("sp", writes=[Rflag], out=flag[:], in_=flag_d)
    epsT, Reps = T([128, 1], F32, "eps")
    S.op("pool", "memset", writes=[Reps], ap=epsT[:], constant=EPS)

    def bc_load(n, name):
        t, R = T([128, n], F32, name)
        S.dma("sp", writes=[R], out=t[:], in_=vec_d[name].broadcast_to([128, n]))
        return t, R

    def rstd_from(ss_ap, Rss, n, H, out_ap, Rout, lnv_ap, Rln):
        S.op("act", "activation", reads=[Rss, Reps], writes=[Rln], out=lnv_ap, in_=ss_ap, func=AF.Ln,
             bias=epsT[:, 0:1], scale=1.0 / n)
        S.op("act", "activation", reads=[Rln], writes=[Rout], out=out_ap, in_=lnv_ap, func=AF.Exp, scale=-0.5)

    nrm_junk, Rnj = T([128, 1024], BF16, "nrmjunk")
    nrm_ss, Rnss = T([128, 4], F32, "nrmss")
    nrm_ln, Rnln = T([128, 4], F32, "nrmln")
    nrm_rs, Rnrs = T([128, 4], F32, "nrmrs")

    def rmsnorm(x_ap, Rx, g_bc, Rg, out_ap, Rout):
        S.op("act", "activation", reads=[Rx], writes=[Rnj, Rnss], out=nrm_junk[:], in_=x_ap, func=AF.Square,
             accum_out=nrm_ss[:, 0:1])
        rstd_from(nrm_ss[:, 0:1], Rnss, 1024, 1, nrm_rs[:, 0:1], Rnrs, nrm_ln[:, 0:1], Rnln)
        S.op("dve", "scalar_tensor_tensor", reads=[Rx, Rnrs, Rg], writes=[Rout], out=out_ap, in0=x_ap,
             scalar=nrm_rs[:, 0:1], in1=g_bc[:], op0=ALU.mult, op1=ALU.mult)

    def transposeN(src_bf, Rsrc, n, width, dst3, Rdst, eng="act"):
        for k0 in range(0, n, 8):
            m = min(8, n - k0)
            for k in range(m):
                S.op("pe", "transpose", reads=[Rsrc, Ridb], writes=[RPB], out=PB[0:width, k * 128:(k + 1) * 128],
                     in_=src_bf[:, (k0 + k) * width:(k0 + k + 1) * width], identity=ident_b[:])
            src = v3(PB[0:width, 0:m * 128], m)
            dst = dst3[:, k0:k0 + m, :]
            if eng == "act":
                S.op("act", "copy", reads=[RPB], writes=Rdst, out=dst, in_=src)
            else:
                S.op("dve", "tensor_copy", reads=[RPB], writes=Rdst, out=dst, in_=src)

    qk_sq, Rqsq = T([128, 512], F32, "qksq")
    qk_tmp, Rqtmp = T([128, 512], F32, "qktmp")

    def qknorm(ps_ap, Rps, H, dh, g_bc, Rg, out_bf, Rout):
        n = H * dh
        S.op("act", "activation", reads=[Rps], writes=[Rqsq], out=qk_sq[:, 0:n], in_=ps_ap, func=AF.Square)
        S.op("dve", "tensor_reduce", reads=[Rqsq], writes=[Rnss], out=nrm_ss[:, 0:H], in_=v3(qk_sq[:, 0:n], H),
             axis=AX.X, op=ALU.add)
        rstd_from(nrm_ss[:, 0:H], Rnss, dh, H, nrm_rs[:, 0:H], Rnrs, nrm_ln[:, 0:H], Rnln)
        S.op("dve", "tensor_tensor", reads=[Rps, Rnrs], writes=[Rqtmp], out=v3(qk_tmp[:, 0:n], H), in0=v3(ps_ap, H),
             in1=nrm_rs[:, 0:H].unsqueeze(2).broadcast_to([128, H, dh]), op=ALU.mult)
        S.op("pool", "tensor_tensor", reads=[Rqtmp, Rg], writes=[Rout], out=out_bf, in0=qk_tmp[:, 0:n], in1=g_bc,
             op=ALU.mult)

    m_base = A.mark()
    hT, _ = T([128, 8, WIN], BF16, "hT")
    RhT = [Res() for _ in range(32)]
    m_after_hT = A.mark()

    gmix, Rgmix = bc_load(1024, "mix_norm_g")
    xb = [T([128, 1024], F32, "xb") for _ in range(2)]
    hb = [T([128, 1024], BF16, "hb") for _ in range(2)]
    for blk in range(32):
        x_t, Rx = xb[blk % 2]
        h_t, Rh = hb[blk % 2]
        S.dma("sp", writes=[Rx], out=x_t[:], in_=xw_d[blk * 128:(blk + 1) * 128, :])
        rmsnorm(x_t[:], Rx, gmix, Rgmix, h_t[:], Rh)
        transposeN(h_t, Rh, 8, 128, hT[:, :, blk * 128:(blk + 1) * 128], [RhT[blk]], eng="act" if blk % 2 else "dve")
    S.barrier()
    A.release(m_after_hT)

    wxbc, Rwxbc = T([128, 8, 2304], BF16, "wxbc")
    wz, Rwz = T([128, 8, 1280], BF16, "wz")
    wdt, Rwdt = T([128, 8, 20], BF16, "wdt")
    for k in range(8):
        S.dma("pool", writes=[Rwxbc], out=wxbc[:, k, :], in_=w_in_d[k * 128:(k + 1) * 128, OFF_Z:OFF_XBC])
        S.dma("pool", writes=[Rwz], out=wz[:, k, :], in_=w_in_d[k * 128:(k + 1) * 128, 0:OFF_Z])
        S.dma("pool", writes=[Rwdt], out=wdt[:, k, :], in_=w_in_d[k * 128:(k + 1) * 128, OFF_XBC:OFF_DT])
    convw, Rcw = T([128, 18, 4], F32, "convw")
    convb, Rcb = T([128, 18], F32, "convb")
    S.dma("sp", writes=[Rcw], out=convw[:].rearrange("p a b -> p (a b)"), in_=convw_d)
    S.dma("sp", writes=[Rcb], out=convb[:], in_=convb_d)
    dtb, Rdtb = bc_load(20, "dt_bias")
    alog, Ralog = bc_load(20, "a_log")
    dsk, Rdsk = bc_load(20, "d_skip")
    gssd, Rgssd = bc_load(1280, "ssd_norm_g")
    a_bc, Ra = T([128, 20], F32, "a_bc")
    S.op("act", "activation", reads=[Ralog], writes=[Ra], out=a_bc[:], in_=alog[:], func=AF.Exp)
    S.op("dve", "tensor_scalar", reads=[Ra], writes=[Ra], out=a_bc[:], in0=a_bc[:], scalar1=-1.0, scalar2=None,
         op0=ALU.mult)
    halo, Rhalo = T([128, 18, 3], F32, "halo")
    S.op("pool", "memset", writes=[Rhalo], ap=halo[:], constant=0.0)
    state, Rst = T([128, 1280], F32, "state")
    state_bf, Rstb = T([128, 1280], BF16, "statebf")
    S.op("pool", "memset", writes=[Rst], ap=state[:], constant=0.0)
    S.op("pool", "memset", writes=[Rstb], ap=state_bf[:], constant=0.0)
    U = [T([128, 259], F32, "U") for _ in range(2)]
    acc = [T([128, 256], F32, "acc") for _ in range(2)]
    xTf = [T([128, 256], F32, "xTf") for _ in range(2)]
    x_tm, Rxtm = T([128, 2, 1280], F32, "x_tm")
    BT, RBT = T([128, 4, 256], BF16, "BT")
    CTt, RCT = T([128, 4, 256], BF16, "CT")
    B_tm, RBtm = T([128, 2, 512], BF16, "B_tm")
    dt_tm, Rdt = T([128, 2, 20], F32, "dt_tm")
    adt, Radt = T([128, 2, 20], F32, "adt")
    ncs, Rncs = T([128, 20], F32, "ncs")
    tmp20, Rt20 = T([128, 20], F32, "tmp20")
    dec, Rdec = T([128, 20], F32, "dec")
    cd, Rcd = T([128, 20], F32, "cd")
    w2, Rw2 = T([128, 20], F32, "w2")
    ecs, Recs = T([128, 20], F32, "ecs")
    xdd, Rxdd = T([128, 1280], BF16, "xdd")
    xdt, Rxdt = T([128, 1280], BF16, "xdt")
    CBm, RCBm = T([128, 4, 128], F32, "CBm")
    adtb, Radtb = T([128, 4, 128], F32, "adtb")
    Lt = [T([128, 4, 128], F32, "Lt") for _ in range(2)]
    Wm, RWm = T([128, 20, 128], BF16, "Wm")
    yg, Ryg = T([128, 320], F32, "yg")
    ytmp, Rytmp = T([128, 320], F32, "ytmp")
    sz, Rsz = T([128, 320], F32, "sz")
    ss1, Rss1 = T([128, 1], F32, "ss1")
    ln1, Rln1 = T([128, 1], F32, "ln1")
    rs1, Rrs1 = T([128, 1], F32, "rs1")
    yn, Ryn = T([128, 1280], BF16, "yn")
    yTc = [T([128, 10, 128], BF16, "yTc") for _ in range(2)]

    for G in range(16):
        own = G >= 8
        RhG = RhT[G * 2:(G + 1) * 2]
        tokG = slice(G * 256, (G + 1) * 256)
        for cc in range(18):
            b = cc % 2
            for k in range(8):
                S.op("pe", "matmul", reads=[Rwxbc] + RhG, writes=[RPS[b]], out=PS[b][:, 0:256],
                     lhsT=wxbc[:, k, cc * 128:(cc + 1) * 128], rhs=hT[:, k, tokG], start=(k == 0), stop=(k == 7))
            u_t, Ru = U[b]
            a_t, Rac = acc[b]
            S.op("act", "copy", reads=[RPS[b]], writes=[Ru], out=u_t[:, 3:259], in_=PS[b][:, 0:256])
            S.op("pool", "tensor_copy", reads=[Rhalo], writes=[Ru], out=u_t[:, 0:3], in_=halo[:, cc, :])
            S.op("dve", "tensor_scalar", reads=[Ru, Rcw], writes=[Rac], out=a_t[:], in0=u_t[:, 3:259],
                 scalar1=convw[:, cc, 3:4], scalar2=None, op0=ALU.mult)
            for j in (2, 1, 0):
                S.op("dve", "scalar_tensor_tensor", reads=[Ru, Rcw, Rac], writes=[Rac], out=a_t[:],
                     in0=u_t[:, j:j + 256], scalar=convw[:, cc, j:j + 1], in1=a_t[:], op0=ALU.mult, op1=ALU.add)
            S.op("pool", "tensor_copy", reads=[Ru], writes=[Rhalo], out=halo[:, cc, :], in_=u_t[:, 256:259])
            if cc < 10:
                xt, Rxt = xTf[b]
                S.op("act", "activation", reads=[Rac, Rcb], writes=[Rxt], out=xt[:], in_=a_t[:], func=AF.Silu,
                     bias=convb[:, cc:cc + 1])
                pb = 2 + b
                for tb in range(2):
                    S.op("pe", "transpose", reads=[Rxt, Rcst], writes=[RPS[pb]], out=PS[pb][:, tb * 128:(tb + 1) * 128],
                         in_=xt[:, tb * 128:(tb + 1) * 128], identity=ident_f)
                S.op("dve", "tensor_copy", reads=[RPS[pb]], writes=[Rxtm], out=x_tm[:, :, cc * 128:(cc + 1) * 128],
                     in_=v3(PS[pb][:, 0:256], 2))
            elif cc < 14:
                g = cc - 10
                S.op("act", "activation", reads=[Rac, Rcb], writes=[RBT], out=BT[:, g, :], in_=a_t[:], func=AF.Silu,
                     bias=convb[:, cc:cc + 1])
                for tb in range(2):
                    S.op("pe", "transpose", reads=[RBT, Ridb], writes=[RPB], out=PB[:, tb * 128:(tb + 1) * 128],
                         in_=BT[:, g, tb * 128:(tb + 1) * 128], identity=ident_b[:])
                S.op("dve", "tensor_copy", reads=[RPB], writes=[RBtm], out=B_tm[:, :, g * 128:(g + 1) * 128],
                     in_=v3(PB[:, 0:256], 2))
            else:
                g = cc - 14
                S.op("act", "activation", reads=[Rac, Rcb], writes=[RCT], out=CTt[:, g, :], in_=a_t[:], func=AF.Silu,
                     bias=convb[:, cc:cc + 1])
        for tb in range(2):
            blk = G * 2 + tb
            for k in range(8):
                S.op("pe", "matmul", reads=[Rwdt, RhT[blk]], writes=[RPS[4]], out=PS[4][:, tb * 20:(tb + 1) * 20],
                     lhsT=hT[:, k, blk * 128:(blk + 1) * 128], rhs=wdt[:, k, :], start=(k == 0), stop=(k == 7))
        S.op("dve", "tensor_tensor", reads=[RPS[4], Rdtb], writes=[Rdt], out=dt_tm[:], in0=v3(PS[4][:, 0:40], 2),
             in1=dtb[:].unsqueeze(1).broadcast_to([128, 2, 20]), op=ALU.add)
        S.op("act", "activation", reads=[Rdt], writes=[Rdt], out=dt_tm[:], in_=dt_tm[:], func=AF.Exp)
        S.op("act", "activation", reads=[Rdt], writes=[Rdt], out=dt_tm[:], in_=dt_tm[:], func=AF.Ln, bias=1.0)
        S.op("dve", "tensor_tensor", reads=[Rdt, Ra], writes=[Radt], out=adt[:], in0=dt_tm[:],
             in1=a_bc[:].unsqueeze(1).broadcast_to([128, 2, 20]), op=ALU.mult)
        for tb in range(2):
            c = G * 2 + tb
            tk = slice(tb * 128, (tb + 1) * 128)
            S.op("pe", "matmul", reads=[Rcst, Radt], writes=[RPS[5]], out=PS[5][:, 0:20], lhsT=tri_f, rhs=adt[:, tb, :],
                 start=True, stop=True)
            S.op("pe", "matmul", reads=[Rcst, Radt], writes=[RPS[5]], out=PS[5][:, 32:52], lhsT=ones_f, rhs=adt[:, tb, :],
                 start=True, stop=True)
            S.op("dve", "tensor_scalar", reads=[RPS[5]], writes=[Rncs], out=ncs[:], in0=PS[5][:, 0:20], scalar1=-1.0,
                 scalar2=None, op0=ALU.mult)
            S.op("dve", "tensor_tensor", reads=[RPS[5], Rncs], writes=[Rt20], out=tmp20[:], in0=PS[5][:, 32:52],
                 in1=ncs[:], op=ALU.add)
            S.op("act", "activation", reads=[Rt20], writes=[Rdec], out=dec[:], in_=tmp20[:], func=AF.Exp)
            S.op("act", "activation", reads=[RPS[5]], writes=[Rcd], out=cd[:], in_=PS[5][:, 32:52], func=AF.Exp)
            S.op("dve", "tensor_tensor", reads=[Rdt, Rdec], writes=[Rw2], out=w2[:], in0=dt_tm[:, tb, :], in1=dec[:],
                 op=ALU.mult)
            S.op("dve", "tensor_tensor", reads=[Rxtm, Rw2], writes=[Rxdd], out=v3(xdd[:], 20),
                 in0=v3(x_tm[:, tb, :], 20), in1=w2[:].unsqueeze(2).broadcast_to([128, 20, 64]), op=ALU.mult)
            if own:
                blk = c
                S.op("act", "activation", reads=[Rncs], writes=[Recs], out=ecs[:], in_=ncs[:], func=AF.Exp, scale=-1.0)
                S.op("pool", "tensor_tensor", reads=[Rxtm, Rdt], writes=[Rxdt], out=v3(xdt[:], 20),
                     in0=v3(x_tm[:, tb, :], 20), in1=dt_tm[:, tb, :].unsqueeze(2).broadcast_to([128, 20, 64]),
                     op=ALU.mult)
                for g in range(4):
                    S.op("pe", "matmul", reads=[RBT, RCT], writes=[RPS[2]], out=PS[2][:, g * 128:(g + 1) * 128],
                         lhsT=BT[:, g, tk], rhs=CTt[:, g, tk], start=True, stop=True)
                S.op("dve", "tensor_tensor", reads=[RPS[2], Rcst], writes=[RCBm], out=CBm[:], in0=v3(PS[2][:, :], 4),
                     in1=tri_f.unsqueeze(1).broadcast_to([128, 4, 128]), op=ALU.mult)
                for hq in range(5):
                    pb = 3 + hq % 2
                    S.op("pool", "tensor_copy", reads=[Radt], writes=[Radtb], out=adtb[:],
                         in_=adt[:, tb, hq * 4:hq * 4 + 4].unsqueeze(2).broadcast_to([128, 4, 128]))
                    for i in range(4):
                        h = hq * 4 + i
                        S.op("pe", "matmul", reads=[Radtb, Rcst], writes=[RPS[pb]], out=PS[pb][:, i * 128:(i + 1) * 128],
                             lhsT=adtb[:, i, :], rhs=tri_f, start=True, stop=True)
                    lt, Rlt = Lt[hq % 2]
                    S.op("dve", "tensor_tensor", reads=[RPS[pb], Rncs], writes=[Rlt], out=lt[:], in0=v3(PS[pb][:, :], 4),
                         in1=ncs[:, hq * 4:hq * 4 + 4].unsqueeze(2).broadcast_to([128, 4, 128]), op=ALU.add)
                    S.op("act", "activation", reads=[Rlt], writes=[Rlt], out=lt[:], in_=lt[:], func=AF.Exp)
                    for i in range(4):
                        h = hq * 4 + i
                        if i % 2:
                            S.op("pool", "tensor_scalar", reads=[Rlt], writes=[Rlt], out=lt[:, i, :], in0=lt[:, i, :],
                                 scalar1=1.0, scalar2=None, op0=ALU.min)
                            S.op("pool", "tensor_tensor", reads=[Rlt, RCBm], writes=[RWm], out=Wm[:, h, :],
                                 in0=lt[:, i, :], in1=CBm[:, h // 5, :], op=ALU.mult)
                        else:
                            S.op("dve", "scalar_tensor_tensor", reads=[Rlt, RCBm], writes=[RWm],
                                 out=Wm[:, h, :], in0=lt[:, i, :], scalar=1.0, in1=CBm[:, h // 5, :], op0=ALU.min,
                                 op1=ALU.mult)
                for g in range(4):
                    gs = slice(g * 320, (g + 1) * 320)
                    S.op("pe", "matmul", reads=[RCT, Rstb], writes=[RPS[0]], out=PS[0][:, 0:320], lhsT=CTt[:, g, tk],
                         rhs=state_bf[:, gs], start=True, stop=True)
                    for hh in range(5):
                        h = g * 5 + hh
                        S.op("pe", "matmul", reads=[RWm, Rxdt], writes=[RPS[1]], out=PS[1][:, hh * 64:(hh + 1) * 64],
                             lhsT=Wm[:, h, :], rhs=xdt[:, h * 64:(h + 1) * 64], start=True, stop=True)
                    for k in range(8):
                        S.op("pe", "matmul", reads=[Rwz, RhT[blk]], writes=[RPS[6]], out=PS[6][:, 0:320],
                             lhsT=hT[:, k, blk * 128:(blk + 1) * 128], rhs=wz[:, k, gs], start=(k == 0), stop=(k == 7))
                    S.op("dve", "tensor_tensor", reads=[RPS[0], Recs], writes=[Ryg], out=v3(yg[:], 5),
                         in0=v3(PS[0][:, 0:320], 5), in1=ecs[:, g * 5:g * 5 + 5].unsqueeze(2).broadcast_to([128, 5, 64]),
                         op=ALU.mult)
                    S.op("dve", "tensor_tensor", reads=[RPS[1], Ryg], writes=[Ryg], out=yg[:], in0=yg[:],
                         in1=PS[1][:, 0:320], op=ALU.add)
                    S.op("pool", "tensor_tensor", reads=[Rxtm, Rdsk], writes=[Rytmp], out=v3(ytmp[:], 5),
                         in0=v3(x_tm[:, tb, gs], 5), in1=dsk[:, g * 5:g * 5 + 5].unsqueeze(2).broadcast_to([128, 5, 64]),
                         op=ALU.mult)
                    S.op("pool", "tensor_tensor", reads=[Rytmp, Ryg], writes=[Ryg], out=yg[:], in0=yg[:], in1=ytmp[:],
                         op=ALU.add)
                    S.op("act", "activation", reads=[RPS[6]], writes=[Rsz], out=sz[:], in_=PS[6][:, 0:320], func=AF.Silu)
                    S.op("dve", "tensor_tensor", reads=[Ryg, Rsz], writes=[Ryg], out=yg[:], in0=yg[:], in1=sz[:],
                         op=ALU.mult)
                    S.op("act", "activation", reads=[Ryg], writes=[Rytmp, Rss1], out=ytmp[:], in_=yg[:], func=AF.Square,
                         accum_out=ss1[:, 0:1])
                    rstd_from(ss1[:, 0:1], Rss1, 320, 1, rs1[:, 0:1], Rrs1, ln1[:, 0:1], Rln1)
                    S.op("dve", "scalar_tensor_tensor", reads=[Ryg, Rrs1, Rgssd], writes=[Ryn], out=yn[:, gs], in0=yg[:],
                         scalar=rs1[:, 0:1], in1=gssd[:, gs], op0=ALU.mult, op1=ALU.mult)
                yt, Ryt = yTc[tb % 2]
                transposeN(yn, Ryn, 10, 128, yt[:], [Ryt], eng="act")
                ob = c - 16
                S.dma("sp", reads=[Ryt], writes=[Res()], out=yT_scr[:, :, ob * 128:(ob + 1) * 128], in_=yt[:])
            for g in range(4):
                gs = slice(g * 320, (g + 1) * 320)
                S.op("pe", "matmul", reads=[RBtm, Rxdd], writes=[RPS[6]], out=PS[6][:, 0:320],
                     lhsT=B_tm[:, tb, g * 128:(g + 1) * 128], rhs=xdd[:, gs], start=True, stop=True)
                S.op("dve", "tensor_tensor", reads=[Rst, Rcd], writes=[Rst], out=v3(state[:, gs], 5),
                     in0=v3(state[:, gs], 5), in1=cd[:, g * 5:g * 5 + 5].unsqueeze(2).broadcast_to([128, 5, 64]),
                     op=ALU.mult)
                S.op("dve", "tensor_tensor", reads=[Rst, RPS[6]], writes=[Rst], out=state[:, gs], in0=state[:, gs],
                     in1=PS[6][:, 0:320], op=ALU.add)
            if c == 15:
                S.op("dve", "tensor_scalar", reads=[Rst, Rflag], writes=[Rst], out=state[:], in0=state[:],
                     scalar1=flag[:, 0:1], scalar2=None, op0=ALU.mult)
            if c >= 15:
                S.op("act", "copy", reads=[Rst], writes=[Rstb], out=state_bf[:], in_=state[:])
    S.barrier()
    A.release(m_after_hT)

    mtab, Rmtab = T([128, 17, 128], BF16, "mtab")
    S.dma("pool", writes=[Rmtab], out=mtab[:].rearrange("p a b -> p (a b)"), in_=mtab_d)
    gq64, Rgq64 = bc_load(64, "attn_q_norm_g")
    gk64, Rgk64 = bc_load(64, "attn_k_norm_g")
    gq, Rgq = T([128, 256], F32, "gq")
    gk, Rgk = T([128, 256], F32, "gk")
    S.op("dve", "tensor_scalar", reads=[Rgq64], writes=[Rgq], out=v3(gq[:], 4),
         in0=gq64[:].unsqueeze(1).broadcast_to([128, 4, 64]), scalar1=0.125, scalar2=None, op0=ALU.mult)
    S.op("dve", "tensor_copy", reads=[Rgk64], writes=[Rgk], out=v3(gk[:], 4),
         in_=gk64[:].unsqueeze(1).broadcast_to([128, 4, 64]))
    wq, Rwq = T([128, 8, 256], BF16, "wq")
    wk, Rwk = T([128, 8, 256], BF16, "wk")
    wv, Rwv = T([128, 8, 256], BF16, "wv")
    KT, _ = T([64, 4, WIN], BF16, "KT")
    RKT = [Res() for _ in range(32)]
    QT, _ = T([64, 4, HALF], BF16, "QT")
    RQT = [Res() for _ in range(16)]
    Vaug, _ = T([128, 32, 4, 65], BF16, "Vaug")
    RV = [Res() for _ in range(32)]
    kn = [T([128, 256], BF16, "kn") for _ in range(2)]
    Aexp, RAexp = T([128, 128], F32, "Aexp")
    Erev, REr = T([128, 17, 128], F32, "Erev")
    Pf = [T([128, 512], F32, "Pf") for _ in range(2)]
    Pbf = [T([128, 512], BF16, "Pbf") for _ in range(2)]
    rd, Rrd = T([128, 2], F32, "rd")
    o_all, _ = T([128, 16, 256], BF16, "o_all")
    Roall = [Res() for _ in range(16)]
    oTc = [T([128, 2, 128], BF16, "oTc") for _ in range(2)]
    for r in range(3):
        for k in range(8):
            rows = slice(k * 128, (k + 1) * 128)
            S.dma("pool", writes=[Rwq], out=wq[:, k, :], in_=w_in_d[rows, OFF_DT + r * 256:OFF_DT + (r + 1) * 256])
            S.dma("pool", writes=[Rwk], out=wk[:, k, :], in_=w_in_d[rows, OFF_Q + r * 256:OFF_Q + (r + 1) * 256])
            S.dma("pool", writes=[Rwv], out=wv[:, k, :], in_=w_in_d[rows, OFF_K + r * 256:OFF_K + (r + 1) * 256])
        for blk in range(32):
            bs = slice(blk * 128, (blk + 1) * 128)
            pa = blk % 2
            for k in range(8):
                S.op("pe", "matmul", reads=[Rwk, RhT[blk]], writes=[RPS[pa]], out=PS[pa][:, 0:256], lhsT=hT[:, k, bs],
                     rhs=wk[:, k, :], start=(k == 0), stop=(k == 7))
            kt, Rkn = kn[blk % 2]
            qknorm(PS[pa][:, 0:256], RPS[pa], 4, 64, gk[:], Rgk, kt[:], Rkn)
            transposeN(kt, Rkn, 4, 64, KT[:, :, bs], [RKT[blk]], eng="act")
            pv = 2 + blk % 2
            for k in range(8):
                S.op("pe", "matmul", reads=[Rwv, RhT[blk]], writes=[RPS[pv]], out=PS[pv][:, 0:256], lhsT=hT[:, k, bs],
                     rhs=wv[:, k, :], start=(k == 0), stop=(k == 7))
            if blk >= 16:
                S.op("act", "copy", reads=[RPS[pv]], writes=[RV[blk]], out=Vaug[:, blk, :, 0:64],
                     in_=v3(PS[pv][:, 0:256], 4))
                S.op("pool", "memset", writes=[RV[blk]], ap=Vaug[:, blk, :, 64:65], constant=1.0)
            else:
                S.op("dve", "tensor_scalar", reads=[RPS[pv], Rflag], writes=[RV[blk]], out=Vaug[:, blk, :, 0:64],
                     in0=v3(PS[pv][:, 0:256], 4), scalar1=flag[:, 0:1], scalar2=None, op0=ALU.mult)
                S.op("pool", "tensor_copy", reads=[Rflag], writes=[RV[blk]], out=Vaug[:, blk, :, 64:65],
                     in_=flag[:, 0:1].unsqueeze(1).broadcast_to([128, 4, 1]))
            if blk >= 16:
                pq = 4 + blk % 2
                for k in range(8):
                    S.op("pe", "matmul", reads=[Rwq, RhT[blk]], writes=[RPS[pq]], out=PS[pq][:, 0:256], lhsT=hT[:, k, bs],
                         rhs=wq[:, k, :], start=(k == 0), stop=(k == 7))
                qt, Rqn = kn[blk % 2]
                qknorm(PS[pq][:, 0:256], RPS[pq], 4, 64, gq[:], Rgq, qt[:], Rqn)
                transposeN(qt, Rqn, 4, 64, QT[:, :, (blk - 16) * 128:(blk - 15) * 128], [RQT[blk - 16]], eng="act")
        it = 0
        for hh in range(4):
            h = r * 4 + hh
            slope = 2.0 ** (-8.0 * (h + 1) / 12.0)
            S.op("act", "activation", reads=[Rcst], writes=[RAexp], out=Aexp[:], in_=d0_f, func=AF.Exp, scale=-slope)
            for j in range(17):
                cpow = math.exp(-slope * 128.0 * (16 - j))
                S.op("dve", "scalar_tensor_tensor", reads=[RAexp, Rmtab], writes=[REr],
                     out=Erev[:, j, :], in0=Aexp[:], scalar=float(cpow), in1=mtab[:, j, :], op0=ALU.mult, op1=ALU.mult)
            for qb in range(16, 32):
                qi = qb - 16
                kbs = list(range(qb - 16, qb + 1))
                ob = 4 + it % 2
                for gi in range(5):
                    grp = kbs[gi * 4:gi * 4 + 4]
                    n = len(grp)
                    sbk = (it * 5 + gi) % 2
                    pf, Rpf = Pf[sbk]
                    pbf, Rpbf = Pbf[sbk]
                    for i, kb in enumerate(grp):
                        S.op("pe", "matmul", reads=[RKT[kb], RQT[qi]], writes=[RPS[sbk]],
                             out=PS[sbk][:, i * 128:(i + 1) * 128], lhsT=KT[:, hh, kb * 128:(kb + 1) * 128],
                             rhs=QT[:, hh, qi * 128:(qi + 1) * 128], start=True, stop=True)
                    S.op("act", "activation", reads=[RPS[sbk]], writes=[Rpf], out=pf[:, 0:n * 128],
                         in_=PS[sbk][:, 0:n * 128], func=AF.Exp)
                    S.op("dve", "tensor_tensor", reads=[Rpf, REr], writes=[Rpbf], out=pbf[:, 0:n * 128],
                         in0=pf[:, 0:n * 128], in1=Erev[:, gi * 4:gi * 4 + n, :].rearrange("p a b -> p (a b)"),
                         op=ALU.mult)
                    for i, kb in enumerate(grp):
                        S.op("pe", "matmul", reads=[Rpbf, RV[kb]], writes=[RPS[ob]], out=PS[ob][:, 0:65],
                             lhsT=pbf[:, i * 128:(i + 1) * 128], rhs=Vaug[:, kb, hh, :], start=(gi == 0 and i == 0),
                             stop=(gi == 4))
                S.op("dve", "reciprocal", reads=[RPS[ob]], writes=[Rrd], out=rd[:, 0:1], in_=PS[ob][:, 64:65])
                S.op("dve", "tensor_scalar", reads=[RPS[ob], Rrd], writes=[Roall[qi]], out=o_all[:, qi, hh * 64:(hh + 1) * 64],
                     in0=PS[ob][:, 0:64], scalar1=rd[:, 0:1], scalar2=None, op0=ALU.mult)
                it += 1
        for qi in range(16):
            for i in range(2):
                S.op("pe", "transpose", reads=[Roall[qi], Ridb], writes=[RPB], out=PB[:, i * 128:(i + 1) * 128],
                     in_=o_all[:, qi, i * 128:(i + 1) * 128], identity=ident_b[:])
            ot, Rot = oTc[qi % 2]
            S.op("act", "copy", reads=[RPB], writes=[Rot], out=ot[:], in_=v3(PB[:, 0:256], 2))
            S.dma("sp", reads=[Rot], writes=[Res()], out=oT_scr[:, 2 * r:2 * r + 2, qi * 128:(qi + 1) * 128], in_=ot[:])
    S.barrier()
    A.release(m_base)

    x1, _ = T([128, 16, 1024], F32, "x1")
    Rx1 = [Res() for _ in range(16)]
    m_after_x1 = A.mark()
    yT_sb, RyT = T([128, 10, HALF], BF16, "yT_sb")
    for kc in range(10):
        S.dma("sp", writes=[RyT], out=yT_sb[:, kc, :], in_=yT_scr[:, kc, :])
    oT, RoTs = T([128, 6, HALF], BF16, "oT")
    for kc in range(6):
        S.dma("sp", writes=[RoTs], out=oT[:, kc, :], in_=oT_scr[:, kc, :])
    wout, Rwout = T([128, 16, 1024], BF16, "wout")
    for kc in range(16):
        S.dma("pool", writes=[Rwout], out=wout[:, kc, :], in_=w_out_d[kc * 128:(kc + 1) * 128, :])
    xo = [T([128, 1024], F32, "xo") for _ in range(2)]
    for tb in range(16):
        ts_ = slice(tb * 128, (tb + 1) * 128)
        xo_t, Rxo = xo[tb % 2]
        S.dma("sp", writes=[Rxo], out=xo_t[:], in_=xw_d[HALF + tb * 128:HALF + (tb + 1) * 128, :])
        for half in range(2):
            pb = 2 * (tb % 2) + half
            cs_ = slice(half * 512, (half + 1) * 512)
            for kc in range(16):
                lhsT = yT_sb[:, kc, ts_] if kc < 10 else oT[:, kc - 10, ts_]
                S.op("pe", "matmul", reads=[RyT, RoTs, Rwout], writes=[RPS[pb]], out=PS[pb][:, :], lhsT=lhsT,
                     rhs=wout[:, kc, cs_], start=(kc == 0), stop=(kc == 15))
            S.op("dve", "tensor_tensor", reads=[RPS[pb], Rxo], writes=[Rx1[tb]], out=x1[:, tb, cs_], in0=PS[pb][:, :],
                 in1=xo_t[:, cs_], op=ALU.add)
    S.barrier()
    A.release(m_after_x1)

    def store_x1_and_finish():
        fin = []
        for tb in range(16):
            R = Res()
            S.dma("sp", reads=[Rx1[tb]], writes=[R], out=out_d[tb * 128:(tb + 1) * 128, :], in_=x1[:, tb, :])
            fin.append(R)
        for R in fin:
            S._wait("sp", R.w)
        S.emit()
        return nc

    if stop_after == "C":
        return store_x1_and_finish()

    wqm, Rwqm = T([128, 8, 512], BF16, "wqm")
    wkv, Rwkv = T([128, 8, 1024], BF16, "wkv")
    wo, Rwo = T([128, 4, 1024], BF16, "wo")
    for k in range(8):
        S.dma("pool", writes=[Rwqm], out=wqm[:, k, :], in_=mem_w_q_d[k * 128:(k + 1) * 128, :])
        S.dma("pool", writes=[Rwkv], out=wkv[:, k, :], in_=mem_w_kv_d[k * 128:(k + 1) * 128, :])
    for k in range(4):
        S.dma("pool", writes=[Rwo], out=wo[:, k, :], in_=mem_w_o_d[k * 128:(k + 1) * 128, :])
    gx, Rgx = bc_load(1024, "xattn_norm_g")
    gm, Rgm = bc_load(1024, "mem_norm_g")
    gqm128, Rgqm128 = bc_load(128, "mem_q_norm_g")
    gkm128, Rgkm128 = bc_load(128, "mem_k_norm_g")
    gqm, Rgqm = T([128, 512], F32, "gqm")
    gkm, Rgkm = T([128, 512], F32, "gkm")
    S.op("dve", "tensor_scalar", reads=[Rgqm128], writes=[Rgqm], out=v3(gqm[:], 4),
         in0=gqm128[:].unsqueeze(1).broadcast_to([128, 4, 128]), scalar1=128.0 ** -0.5, scalar2=None, op0=ALU.mult)
    S.op("dve", "tensor_copy", reads=[Rgkm128], writes=[Rgkm], out=v3(gkm[:], 4),
         in_=gkm128[:].unsqueeze(1).broadcast_to([128, 4, 128]))
    KmT, RKmT = T([128, 4, 256], BF16, "KmT")
    Vm, RVm = T([128, 2, 4, 129], BF16, "Vm")
    S.op("pool", "memset", writes=[RVm], ap=Vm[:].rearrange("p a b c -> p (a b c)"), constant=1.0)
    memb, Rmemb = T([128, 1024], F32, "memb")
    mh, Rmh = T([128, 1024], BF16, "mh")
    mT, RmT = T([128, 8, 128], BF16, "mT")
    knm, Rknm = T([128, 512], BF16, "knm")
    for mb in range(2):
        ms = slice(mb * 128, (mb + 1) * 128)
        S.dma("sp", writes=[Rmemb], out=memb[:], in_=mem_d[ms, :])
        rmsnorm(memb[:], Rmemb, gm, Rgm, mh[:], Rmh)
        transposeN(mh, Rmh, 8, 128, mT[:], [RmT], eng="act")
        for half in range(2):
            for k in range(8):
                S.op("pe", "matmul", reads=[RmT, Rwkv], writes=[RPS[half]], out=PS[half][:, :], lhsT=mT[:, k, :],
                     rhs=wkv[:, k, half * 512:(half + 1) * 512], start=(k == 0), stop=(k == 7))
        qknorm(PS[0][:, :], RPS[0], 4, 128, gkm[:], Rgkm, knm[:], Rknm)
        transposeN(knm, Rknm, 4, 128, KmT[:, :, ms], [RKmT], eng="act")
        S.op("act", "copy", reads=[RPS[1]], writes=[RVm], out=Vm[:, mb, :, 0:128], in_=v3(PS[1][:, :], 4))
    h2, Rh2 = T([128, 1024], BF16, "h2")
    h2T, Rh2T = T([128, 8, 128], BF16, "h2T")
    qn, Rqn2 = T([128, 512], BF16, "qn")
    qT, RqT = T([128, 4, 128], BF16, "qT")
    Pm = [T([128, 256], BF16, "Pm") for _ in range(2)]
    o2, Ro2 = T([128, 512], BF16, "o2")
    o2T, Ro2T = T([128, 4, 128], BF16, "o2T")
    for tb in range(16):
        rmsnorm(x1[:, tb, :], Rx1[tb], gx, Rgx, h2[:], Rh2)
        transposeN(h2, Rh2, 8, 128, h2T[:], [Rh2T], eng="act")
        for k in range(8):
            S.op("pe", "matmul", reads=[Rh2T, Rwqm], writes=[RPS[2]], out=PS[2][:, :], lhsT=h2T[:, k, :], rhs=wqm[:, k, :],
                 start=(k == 0), stop=(k == 7))
        qknorm(PS[2][:, :], RPS[2], 4, 128, gqm[:], Rgqm, qn[:], Rqn2)
        transposeN(qn, Rqn2, 4, 128, qT[:], [RqT], eng="act")
        for hh in range(4):
            sb_ = 3 + hh % 2
            ob = 5 + hh % 2
            pm, Rpm = Pm[hh % 2]
            for mb in range(2):
                S.op("pe", "matmul", reads=[RKmT, RqT], writes=[RPS[sb_]], out=PS[sb_][:, mb * 128:(mb + 1) * 128],
                     lhsT=KmT[:, hh, mb * 128:(mb + 1) * 128], rhs=qT[:, hh, :], start=True, stop=True)
            S.op("act", "activation", reads=[RPS[sb_]], writes=[Rpm], out=pm[:], in_=PS[sb_][:, 0:256], func=AF.Exp)
            for mb in range(2):
                S.op("pe", "matmul", reads=[Rpm, RVm], writes=[RPS[ob]], out=PS[ob][:, 0:129],
                     lhsT=pm[:, mb * 128:(mb + 1) * 128], rhs=Vm[:, mb, hh, :], start=(mb == 0), stop=(mb == 1))
            S.op("dve", "reciprocal", reads=[RPS[ob]], writes=[Rrd], out=rd[:, 0:1], in_=PS[ob][:, 128:129])
            S.op("dve", "tensor_scalar", reads=[RPS[ob], Rrd], writes=[Ro2], out=o2[:, hh * 128:(hh + 1) * 128],
                 in0=PS[ob][:, 0:128], scalar1=rd[:, 0:1], scalar2=None, op0=ALU.mult)
        transposeN(o2, Ro2, 4, 128, o2T[:], [Ro2T], eng="act")
        for half in range(2):
            cs_ = slice(half * 512, (half + 1) * 512)
            for kc in range(4):
                S.op("pe", "matmul", reads=[Ro2T, Rwo], writes=[RPS[half]], out=PS[half][:, :], lhsT=o2T[:, kc, :],
                     rhs=wo[:, kc, cs_], start=(kc == 0), stop=(kc == 3))
            S.op("dve", "tensor_tensor", reads=[RPS[half], Rx1[tb]], writes=[Rx1[tb]], out=x1[:, tb, cs_],
                 in0=PS[half][:, :], in1=x1[:, tb, cs_], op=ALU.add)
    S.barrier()
    A.release(m_after_x1)
    if stop_after == "D":
        return store_x1_and_finish()

    wpq, Rwpq = T([128, 8, 2048], BF16, "wpq")
    for k in range(8):
        S.dma("pool", writes=[Rwpq], out=wpq[:, k, :], in_=peer_wq_d[k * 128:(k + 1) * 128, :])
    keysT, RkeysT = T([128, 2, 128], F32, "keysT")
    S.dma("sp", writes=[RkeysT], out=keysT[:, 0, :], in_=keys1T_d)
    S.dma("sp", writes=[RkeysT], out=keysT[:, 1, :], in_=keys2T_d)
    gf, Rgf = bc_load(1024, "ffn_norm_g")
    h3, Rh3 = T([128, 1024], F32, "h3")
    h3b, Rh3b = T([128, 1024], BF16, "h3b")
    h3T, Rh3T = T([128, 8, 128], BF16, "h3T")
    qrT, RqrT = T([128, 16, 128], F32, "qrT")
    sc, Rsc = T([128, 16, 128], F32, "sc")
    sc2, Rsc2 = T([128, 128], F32, "sc2")
    tv, Rtv = T([128, 16, 16], F32, "tv")
    ti, Rti = T([128, 16, 16], U32, "ti")
    tif, Rtif = T([128, 16, 16], F32, "tif")
    cand, Rcand = T([128, 8, 256], F32, "cand")
    cand2, Rcand2 = T([128, 256], F32, "cand2")
    cv, Rcv = T([128, 8, 16], F32, "cv")
    cp, Rcp = T([128, 8, 16], U32, "cp")
    ca, Rca = T([128, 8, 16], U32, "ca")
    cb_, Rcb_ = T([128, 8, 16], U32, "cb")
    caf, Rcaf = T([128, 8, 16], F32, "caf")
    cbf, Rcbf = T([128, 8, 16], F32, "cbf")
    eq, Req = T([128, 8, 16, 16], F32, "eq")
    i1f, Ri1f = T([128, 8, 16], F32, "i1f")
    i2f, Ri2f = T([128, 8, 16], F32, "i2f")
    ef, Ref = T([128, 128], F32, "ef")
    ei, Rei = T([128, 128], I32, "ei")
    gt, Rgt = T([128, 8, 16], F32, "gt")
    gsum, Rgsum = T([128, 8], F32, "gsum")
    actv, Ractv = T([128, 128], F32, "actv")
    wgt, Rwgt = T([128, 128], F32, "wgt")
    pacc, Rpacc = T([128, 1024], F32, "pacc")
    pjunk, Rpj = T([128, 1024], F32, "pjunk")
    pacc2, Rpacc2 = T([128, 1024], F32, "pacc2")
    ub = [T([128, 1024], F32, "ub") for _ in range(4)]
    vb = [T([128, 1024], F32, "vb") for _ in range(4)]
    fin = []
    for tb in range(16):
        rmsnorm(x1[:, tb, :], Rx1[tb], gf, Rgf, h3[:], Rh3)
        S.op("pool", "tensor_copy", reads=[Rh3], writes=[Rh3b], out=h3b[:], in_=h3[:])
        transposeN(h3b, Rh3b, 8, 128, h3T[:], [Rh3T], eng="act")
        for q4 in range(4):
            pb = q4 % 2
            for i in range(4):
                ccq = q4 * 4 + i
                for k in range(8):
                    S.op("pe", "matmul", reads=[Rwpq, Rh3T], writes=[RPS[pb]], out=PS[pb][:, i * 128:(i + 1) * 128],
                         lhsT=wpq[:, k, ccq * 128:(ccq + 1) * 128], rhs=h3T[:, k, :], start=(k == 0), stop=(k == 7))
            S.op("act", "copy", reads=[RPS[pb]], writes=[RqrT], out=qrT[:, q4 * 4:q4 * 4 + 4, :], in_=v3(PS[pb][:, :], 4))
        for q4 in range(4):
            pb = 2 + q4 % 2
            for i in range(4):
                j = q4 * 4 + i
                S.op("pe", "matmul", reads=[RqrT, RkeysT], writes=[RPS[pb]], out=PS[pb][:, i * 128:(i + 1) * 128],
                     lhsT=qrT[:, j, :], rhs=keysT[:, j % 2, :], start=True, stop=True)
            S.op("act", "copy", reads=[RPS[pb]], writes=[Rsc], out=sc[:, q4 * 4:q4 * 4 + 4, :], in_=v3(PS[pb][:, :], 4))
        for j in range(16):
            S.op("dve", "max", reads=[Rsc], writes=[Rtv], out=tv[:, j, 0:8], in_=sc[:, j, :])
            S.op("dve", "max_index", reads=[Rsc, Rtv], writes=[Rti], out=ti[:, j, 0:8], in_max=tv[:, j, 0:8],
                 in_values=sc[:, j, :])
            S.op("dve", "match_replace", reads=[Rsc, Rtv], writes=[Rsc2], out=sc2[:], in_to_replace=tv[:, j, 0:8],
                 in_values=sc[:, j, :], imm_value=-1e30)
            S.op("dve", "max", reads=[Rsc2], writes=[Rtv], out=tv[:, j, 8:16], in_=sc2[:])
            S.op("dve", "max_index", reads=[Rsc2, Rtv], writes=[Rti], out=ti[:, j, 8:16], in_max=tv[:, j, 8:16],
                 in_values=sc2[:])
        S.op("dve", "tensor_copy", reads=[Rti], writes=[Rtif], out=tif[:], in_=ti[:])
        tv4 = tv[:].rearrange("p (h two) k -> p h two k", two=2)
        tif4 = tif[:].rearrange("p (h two) k -> p h two k", two=2)
        S.op("dve", "tensor_tensor", reads=[Rtv], writes=[Rcand], out=cand[:].rearrange("p h (a b) -> p h a b", a=16),
             in0=tv4[:, :, 0, :].unsqueeze(3).broadcast_to([128, 8, 16, 16]),
             in1=tv4[:, :, 1, :].unsqueeze(2).broadcast_to([128, 8, 16, 16]), op=ALU.add)
        for h in range(8):
            S.op("dve", "max", reads=[Rcand], writes=[Rcv], out=cv[:, h, 0:8], in_=cand[:, h, :])
            S.op("dve", "max_index", reads=[Rcand, Rcv], writes=[Rcp], out=cp[:, h, 0:8], in_max=cv[:, h, 0:8],
                 in_values=cand[:, h, :])
            S.op("dve", "match_replace", reads=[Rcand, Rcv], writes=[Rcand2], out=cand2[:], in_to_replace=cv[:, h, 0:8],
                 in_values=cand[:, h, :], imm_value=-1e30)
            S.op("dve", "max", reads=[Rcand2], writes=[Rcv], out=cv[:, h, 8:16], in_=cand2[:])
            S.op("dve", "max_index", reads=[Rcand2, Rcv], writes=[Rcp], out=cp[:, h, 8:16], in_max=cv[:, h, 8:16],
                 in_values=cand2[:])
        S.op("dve", "tensor_single_scalar", reads=[Rcp], writes=[Rca], out=ca[:], in_=cp[:], scalar=4,
             op=ALU.logical_shift_right)
        S.op("dve", "tensor_single_scalar", reads=[Rcp], writes=[Rcb_], out=cb_[:], in_=cp[:], scalar=15,
             op=ALU.bitwise_and)
        S.op("dve", "tensor_copy", reads=[Rca], writes=[Rcaf], out=caf[:], in_=ca[:])
        S.op("dve", "tensor_copy", reads=[Rcb_], writes=[Rcbf], out=cbf[:], in_=cb_[:])
        io4 = iota16.unsqueeze(1).unsqueeze(1).broadcast_to([128, 8, 16, 16])
        for (sel, Rsel, half, dst, Rdst) in ((caf, Rcaf, 0, i1f, Ri1f), (cbf, Rcbf, 1, i2f, Ri2f)):
            S.op("dve", "tensor_tensor", reads=[Rsel, Rcst], writes=[Req], out=eq[:],
                 in0=sel[:].unsqueeze(3).broadcast_to([128, 8, 16, 16]), in1=io4, op=ALU.is_equal)
            S.op("dve", "tensor_tensor", reads=[Req, Rtif], writes=[Req], out=eq[:], in0=eq[:],
                 in1=tif4[:, :, half, :].unsqueeze(2).broadcast_to([128, 8, 16, 16]), op=ALU.mult)
            S.op("dve", "tensor_reduce", reads=[Req], writes=[Rdst], out=dst[:].rearrange("p h k -> p (h k)"),
                 in_=eq[:].rearrange("p h k a -> p (h k) a"), axis=AX.X, op=ALU.add)
        S.op("dve", "scalar_tensor_tensor", reads=[Ri1f, Ri2f], writes=[Ref], out=ef[:],
             in0=i1f[:].rearrange("p h k -> p (h k)"), scalar=128.0, in1=i2f[:].rearrange("p h k -> p (h k)"),
             op0=ALU.mult, op1=ALU.add)
        S.op("dve", "tensor_copy", reads=[Ref], writes=[Rei], out=ei[:], in_=ef[:])
        S.op("dve", "tensor_tensor", reads=[Rcv], writes=[Rgt], out=gt[:], in0=cv[:],
             in1=cv[:, :, 0:1].broadcast_to([128, 8, 16]), op=ALU.subtract)
        S.op("act", "activation", reads=[Rgt], writes=[Rgt], out=gt[:], in_=gt[:], func=AF.Exp)
        S.op("dve", "tensor_reduce", reads=[Rgt], writes=[Rgsum], out=gsum[:], in_=gt[:], axis=AX.X, op=ALU.add)
        S.op("dve", "reciprocal", reads=[Rgsum], writes=[Rgsum], out=gsum[:], in_=gsum[:])
        S.op("dve", "tensor_tensor", reads=[Rgt, Rgsum], writes=[Rgt], out=gt[:], in0=gt[:],
             in1=gsum[:].unsqueeze(2).broadcast_to([128, 8, 16]), op=ALU.mult)
        S.op("pool", "memset", writes=[Ractv], ap=actv[:], constant=0.0)
        for s in range(128):
            u_t, Ru_ = ub[s % 4]
            S.dma("pool", reads=[Rei], writes=[Ru_], meth="indirect_dma_start", out=u_t[:], out_offset=None,
                  in_=peer_u_d, in_offset=bass.IndirectOffsetOnAxis(ap=ei[:, s:s + 1], axis=0))
            S.op("dve", "scalar_tensor_tensor", reads=[Ru_, Rh3], writes=[Rpj, Ractv], out=pjunk[:], in0=u_t[:],
                 scalar=1.0, in1=h3[:], op0=ALU.mult, op1=ALU.mult, accum_out=actv[:, s:s + 1])
        S.op("act", "activation", reads=[Ractv], writes=[Rwgt], out=wgt[:], in_=actv[:], func=AF.Gelu)
        S.op("dve", "tensor_tensor", reads=[Rwgt, Rgt], writes=[Rwgt], out=wgt[:], in0=wgt[:],
             in1=gt[:].rearrange("p h k -> p (h k)"), op=ALU.mult)
        for s in range(128):
            v_t, Rv_ = vb[s % 4]
            S.dma("pool", reads=[Rei], writes=[Rv_], meth="indirect_dma_start", out=v_t[:], out_offset=None,
                  in_=peer_v_d, in_offset=bass.IndirectOffsetOnAxis(ap=ei[:, s:s + 1], axis=0))
            if s % 4 == 3:
                S.op("pool", "tensor_scalar", reads=[Rv_, Rwgt], writes=[Rpj], out=pjunk[:], in0=v_t[:],
                     scalar1=wgt[:, s:s + 1], scalar2=None, op0=ALU.mult)
                if s == 3:
                    S.op("pool", "tensor_copy", reads=[Rpj], writes=[Rpacc2], out=pacc2[:], in_=pjunk[:])
                else:
                    S.op("pool", "tensor_tensor", reads=[Rpj, Rpacc2], writes=[Rpacc2], out=pacc2[:], in0=pacc2[:],
                         in1=pjunk[:], op=ALU.add)
            elif s == 0:
                S.op("dve", "tensor_scalar", reads=[Rv_, Rwgt], writes=[Rpacc], out=pacc[:], in0=v_t[:],
                     scalar1=wgt[:, 0:1], scalar2=None, op0=ALU.mult)
            else:
                S.op("dve", "scalar_tensor_tensor", reads=[Rv_, Rwgt, Rpacc], writes=[Rpacc], out=pacc[:], in0=v_t[:],
                     scalar=wgt[:, s:s + 1], in1=pacc[:], op0=ALU.mult, op1=ALU.add)
        S.op("dve", "tensor_tensor", reads=[Rpacc, Rpacc2], writes=[Rpacc], out=pacc[:], in0=pacc[:], in1=pacc2[:],
             op=ALU.add)
        S.op("dve", "tensor_tensor", reads=[Rpacc, Rx1[tb]], writes=[Rx1[tb]], out=x1[:, tb, :], in0=x1[:, tb, :],
             in1=pacc[:], op=ALU.add)
        R = Res()
        S.dma("sp", reads=[Rx1[tb]], writes=[R], out=out_d[tb * 128:(tb + 1) * 128, :], in_=x1[:, tb, :])
        fin.append(R)
    for R in fin:
        S._wait("sp", R.w)
    S.emit()
    return nc


def _consts():
    c = np.zeros((128, NCONST), np.float32)
    p = np.arange(128)
    c[:, 0:128] = np.eye(128)
    c[:, 128:256] = (p[:, None] <= p[None, :])
    c[:, 256:384] = 1.0
    c[:, 384:512] = (p[None, :] - p[:, None])
    c[:, 512:528] = np.arange(16)[None, :]
    k = p[:, None, None]
    j = np.arange(17)[None, :, None]
    q = p[None, None, :]
    delta = (16 - j) * 128 + q - k
    m = ((delta >= 0) & (delta <= 128)).astype(np.float32)
    m += ((delta >= 0) & (delta <= 512) & (delta % 4 == 0))
    m += ((delta >= 0) & (delta <= 2048) & (delta % 16 == 0))
    return c, np.ascontiguousarray(m.reshape(128, 17 * 128).astype(np.float32))


_NC_CACHE = {}


def kernel(**inputs):
    stop_after = os.environ.get("KSTOP") or None
    x = np.asarray(inputs["x"], np.float32)
    mem = np.asarray(inputs["mem"], np.float32)
    cst, mtab = _consts()
    common = {
        "cst": cst, "mtab": mtab,
        "w_in": np.ascontiguousarray(inputs["w_in"][0], dtype=np.float32),
        "convw": np.ascontiguousarray(
            np.asarray(inputs["conv_w"][0], np.float32).T.reshape(18, 128, 4).transpose(1, 0, 2).reshape(128, 72)),
        "convb": np.ascontiguousarray(np.asarray(inputs["conv_b"][0], np.float32).reshape(18, 128).T),
        "w_out": np.ascontiguousarray(inputs["w_out"][0], dtype=np.float32),
    }
    for nm in ("mix_norm_g", "dt_bias", "a_log", "d_skip", "ssd_norm_g", "attn_q_norm_g", "attn_k_norm_g",
               "xattn_norm_g", "mem_norm_g", "mem_q_norm_g", "mem_k_norm_g", "ffn_norm_g"):
        common[nm] = np.ascontiguousarray(np.asarray(inputs[nm][0], np.float32).reshape(1, -1))
    if True:
        common["mem_w_q"] = np.ascontiguousarray(inputs["mem_w_q"][0], dtype=np.float32)
        common["mem_w_kv"] = np.ascontiguousarray(inputs["mem_w_kv"][0], dtype=np.float32)
        common["mem_w_o"] = np.ascontiguousarray(inputs["mem_w_o"][0], dtype=np.float32)
    if True:
        common["peer_w_query"] = np.ascontiguousarray(inputs["peer_w_query"][0], dtype=np.float32)
        common["keys1T"] = np.ascontiguousarray(np.asarray(inputs["peer_sub_keys1"][0], np.float32).T)
        common["keys2T"] = np.ascontiguousarray(np.asarray(inputs["peer_sub_keys2"][0], np.float32).T)
        common["peer_u"] = np.ascontiguousarray(inputs["peer_u"][0], dtype=np.float32)
        common["peer_v"] = np.ascontiguousarray(inputs["peer_v"][0], dtype=np.float32)
    in_maps = []
    for c in range(8):
        b, j = c // 2, c % 2
        xw = np.zeros((WIN, D), np.float32)
        if j == 1:
            xw[:HALF] = x[b, :HALF]
        xw[HALF:] = x[b, j * HALF:(j + 1) * HALF]
        m = dict(common)
        m["xw"] = xw
        m["flag"] = np.full((128, 1), float(j), np.float32)
        if True:
            m["mem"] = np.ascontiguousarray(mem[b])
        in_maps.append(m)
    nc = build(stop_after)
    res = run_bass_kernel_spmd(nc, in_maps, core_ids=list(range(8)))
    out = np.zeros((NB, SEQ, D), np.float32)
    for c in range(8):
        b, j = c // 2, c % 2
        out[b, j * HALF:(j + 1) * HALF] = np.asarray(res.results[c]["out"])
    return out
```

```python
import os
import math
import numpy as np
import concourse.bass as bass
import concourse.mybir as mybir
from concourse.bass_utils import run_bass_kernel_spmd

F32 = mybir.dt.float32
BF16 = mybir.dt.bfloat16
I32 = mybir.dt.int32
U32 = mybir.dt.uint32
AF = mybir.ActivationFunctionType
ALU = mybir.AluOpType
AX = mybir.AxisListType

D = 1024
NB = 4
SEQ = 4096
HALF = 2048
WIN = 4096
EPS = 1e-6
OFF_Z = 1280
OFF_XBC = OFF_Z + 2304
OFF_DT = OFF_XBC + 20
OFF_Q = OFF_DT + 768
OFF_K = OFF_Q + 768
IN_COLS = OFF_K + 768
NCONST = 128 * 4 + 16


class Res:
    __slots__ = ("w", "r")

    def __init__(self):
        self.w = None
        self.r = {}


class Sched:
    COMPUTE = ("pe", "act", "dve", "pool")
    NDMA = 12

    def __init__(self, nc):
        self.nc = nc
        self.streams = {k: [] for k in ("pe", "act", "dve", "pool", "sp")}
        self.esem = {k: nc.alloc_semaphore("es_" + k) for k in self.COMPUTE}
        self.ecnt = {k: 0 for k in self.COMPUTE}
        self.dsem = {q: [nc.alloc_semaphore("ds_%s%d" % (q, i)) for i in range(self.NDMA)]
                     for q in ("sp", "pool")}
        self.dcnt = {q: [0] * self.NDMA for q in ("sp", "pool")}
        self.dnext = {q: 0 for q in ("sp", "pool")}
        self.known = {k: {} for k in self.streams}
        self.nwaits = 0
        self.nops = 0

    def _sem_of(self, pid):
        if pid[0] == "e":
            return self.esem[pid[1]], pid[2], ("e", pid[1])
        return self.dsem[pid[1]][pid[2]], pid[3], ("d", pid[1], pid[2])

    def _wait(self, eng, pid):
        sem, val, key = self._sem_of(pid)
        if self.known[eng].get(key, 0) >= val:
            return
        self.known[eng][key] = val
        self.nwaits += 1
        self.streams[eng].append(lambda e, sem=sem, val=val: e.wait_ge(sem, val))

    def _deps(self, eng, reads, writes):
        deps = []
        for r in reads:
            if r.w is not None:
                deps.append(r.w)
        for w in writes:
            if w.w is not None:
                deps.append(w.w)
            deps.extend(w.r.values())
        for pid in deps:
            if pid[0] == "e" and pid[1] == eng and eng == "pe":
                continue
            self._wait(eng, pid)

    def _commit(self, pid, reads, writes):
        key = pid[:2] if pid[0] == "e" else pid[:3]
        for r in reads:
            r.r[key] = pid
        for w in writes:
            w.w = pid
            w.r = {}

    def op(self, eng, meth, reads=(), writes=(), **kw):
        fn = lambda e, meth=meth, kw=kw: getattr(e, meth)(**kw)
        self._deps(eng, reads, writes)
        self.ecnt[eng] += 1
        idx = self.ecnt[eng]
        sem = self.esem[eng]
        self.streams[eng].append(lambda e, fn=fn, sem=sem: fn(e).then_inc(sem, 1))
        self.nops += 1
        self._commit(("e", eng, idx), reads, writes)

    def dma(self, q, reads=(), writes=(), meth="dma_start", **kw):
        fn = lambda e, meth=meth, kw=kw: getattr(e, meth)(**kw)
        i = self.dnext[q]
        self.dnext[q] = (i + 1) % self.NDMA
        prev = self.dcnt[q][i]
        if prev > 0:
            self._wait(q, ("d", q, i, prev))
        self._deps(q, reads, writes)
        self.dcnt[q][i] = prev + 16
        sem = self.dsem[q][i]
        self.streams[q].append(lambda e, fn=fn, sem=sem: fn(e).then_inc(sem, 16))
        self.nops += 1
        self._commit(("d", q, i, prev + 16), reads, writes)

    def barrier(self):
        for eng in self.streams:
            for k in self.COMPUTE:
                if k != eng and self.ecnt[k] > 0:
                    self._wait(eng, ("e", k, self.ecnt[k]))
            for q in self.dsem:
                for i in range(self.NDMA):
                    if self.dcnt[q][i] > 0:
                        self._wait(eng, ("d", q, i, self.dcnt[q][i]))

    def emit(self):
        nc = self.nc
        with nc.Block() as block:
            @block.tensor
            def _(e):
                for it in self.streams["pe"]:
                    it(e)

            @block.scalar
            def _(e):
                for it in self.streams["act"]:
                    it(e)

            @block.vector
            def _(e):
                for it in self.streams["dve"]:
                    it(e)

            @block.gpsimd
            def _(e):
                for it in self.streams["pool"]:
                    it(e)

            @block.sync
            def _(e):
                for it in self.streams["sp"]:
                    it(e)


class Arena:
    def __init__(self, nc):
        self.nc = nc
        self.off = (nc.sbuf_base + 31) // 32 * 32
        self.top = nc.sbuf_top
        self.n = 0

    def alloc(self, shape, dt=F32, name="t"):
        sz = {F32: 4, BF16: 2, I32: 4, U32: 4}[dt]
        nb = int(np.prod(shape[1:])) * sz
        off = self.off
        self.off += (nb + 31) // 32 * 32
        assert self.off <= self.top, "SBUF overflow at %s: %d > %d" % (name, self.off, self.top)
        self.n += 1
        return self.nc.alloc_sbuf_tensor_at("%s_%d" % (name, self.n), list(shape), dt, offset=off)

    def mark(self):
        return self.off

    def release(self, m):
        self.off = m


def build(stop_after=None):
    nc = bass.Bass("TRN2", target_bir_lowering=False)
    S = Sched(nc)
    A = Arena(nc)

    def din(name, shape, dt=F32):
        return nc.dram_tensor(name, list(shape), dt, kind="ExternalInput").ap()

    xw_d = din("xw", [WIN, D])
    flag_d = din("flag", [128, 1])
    mem_d = din("mem", [256, D])
    cst_d = din("cst", [128, NCONST])
    mtab_d = din("mtab", [128, 17 * 128])
    w_in_d = din("w_in", [D, IN_COLS])
    convw_d = din("convw", [128, 72])
    convb_d = din("convb", [128, 18])
    vec_d = {}
    for nm, n in [("mix_norm_g", 1024), ("dt_bias", 20), ("a_log", 20), ("d_skip", 20), ("ssd_norm_g", 1280),
                  ("attn_q_norm_g", 64), ("attn_k_norm_g", 64), ("xattn_norm_g", 1024), ("mem_norm_g", 1024),
                  ("mem_q_norm_g", 128), ("mem_k_norm_g", 128), ("ffn_norm_g", 1024)]:
        vec_d[nm] = din(nm, [1, n])
    w_out_d = din("w_out", [2048, D])
    mem_w_q_d = din("mem_w_q", [D, 512])
    mem_w_kv_d = din("mem_w_kv", [D, 1024])
    mem_w_o_d = din("mem_w_o", [512, D])
    peer_wq_d = din("peer_w_query", [D, 2048])
    keys1T_d = din("keys1T", [128, 128])
    keys2T_d = din("keys2T", [128, 128])
    peer_u_d = din("peer_u", [16384, D])
    peer_v_d = din("peer_v", [16384, D])
    out_d = nc.dram_tensor("out", [HALF, D], F32, kind="ExternalOutput").ap()
    yT_scr = nc.dram_tensor("yT_scr", [128, 10, HALF], BF16).ap()
    oT_scr = nc.dram_tensor("oT_scr", [128, 6, HALF], BF16).ap()
    uv_scr = nc.dram_tensor("uv_scr", [16384, 2048], BF16).ap()
    Ruv = Res()
    conv_jobs = [(tbl, h) for h in range(32) for tbl in (0, 1)]

    def issue_conv(n):
        for _ in range(n):
            if not conv_jobs or stop_after is not None:
                return
            tbl, h = conv_jobs.pop(0)
            src = peer_u_d if tbl == 0 else peer_v_d
            S.dma("pool", writes=[Ruv], out=uv_scr[h * 512:(h + 1) * 512, tbl * 1024:(tbl + 1) * 1024],
                  in_=src[h * 512:(h + 1) * 512, :])

    PS = [nc.alloc_psum_tensor("ps%d" % i, [128, 512], F32) for i in range(7)]
    RPS = [Res() for _ in range(7)]
    PB = nc.alloc_psum_tensor("psb", [128, 1024], BF16)
    RPB = Res()

    def T(shape, dt=F32, name="t"):
        return A.alloc(shape, dt, name), Res()

    def v3(ap, a):
        return ap.rearrange("p (a b) -> p a b", a=a)

    cst, Rcst = T([128, NCONST], F32, "cst")
    S.dma("sp", writes=[Rcst], out=cst[:], in_=cst_d)
    ident_f = cst[:, 0:128]
    tri_f = cst[:, 128:256]
    ones_f = cst[:, 256:384]
    d0_f = cst[:, 384:512]
    iota16 = cst[:, 512:528]
    ident_b, Ridb = T([128, 128], BF16, "identb")
    S.op("dve", "tensor_copy", reads=[Rcst], writes=[Ridb], out=ident_b[:], in_=ident_f)
    flag, Rflag = T([128, 1], F32, "flag")
    S.dma("sp", writes=[Rflag], out=flag[:], in_=flag_d)
    epsT, Reps = T([128, 1], F32, "eps")
    S.op("pool", "memset", writes=[Reps], ap=epsT[:], constant=EPS)

    def bc_load(n, name):
        t, R = T([128, n], F32, name)
        S.dma("sp", writes=[R], out=t[:], in_=vec_d[name].broadcast_to([128, n]))
        return t, R

    def rstd_from(ss_ap, Rss, n, H, out_ap, Rout, lnv_ap, Rln):
        S.op("act", "activation", reads=[Rss, Reps], writes=[Rln], out=lnv_ap, in_=ss_ap, func=AF.Ln,
             bias=epsT[:, 0:1], scale=1.0 / n)
        S.op("act", "activation", reads=[Rln], writes=[Rout], out=out_ap, in_=lnv_ap, func=AF.Exp, scale=-0.5)

    nrm_junk, Rnj = T([128, 1024], BF16, "nrmjunk")
    nrm_ss, Rnss = T([128, 4], F32, "nrmss")
    nrm_ln, Rnln = T([128, 4], F32, "nrmln")
    nrm_rs, Rnrs = T([128, 4], F32, "nrmrs")

    def rmsnorm(x_ap, Rx, g_bc, Rg, out_ap, Rout):
        S.op("act", "activation", reads=[Rx], writes=[Rnj, Rnss], out=nrm_junk[:], in_=x_ap, func=AF.Square,
             accum_out=nrm_ss[:, 0:1])
        rstd_from(nrm_ss[:, 0:1], Rnss, 1024, 1, nrm_rs[:, 0:1], Rnrs, nrm_ln[:, 0:1], Rnln)
        S.op("dve", "scalar_tensor_tensor", reads=[Rx, Rnrs, Rg], writes=[Rout], out=out_ap, in0=x_ap,
             scalar=nrm_rs[:, 0:1], in1=g_bc[:], op0=ALU.mult, op1=ALU.mult)

    def transposeN(src_bf, Rsrc, n, width, dst3, Rdst, eng="act"):
        for k0 in range(0, n, 8):
            m = min(8, n - k0)
            for k in range(m):
                S.op("pe", "transpose", reads=[Rsrc, Ridb], writes=[RPB], out=PB[0:width, k * 128:(k + 1) * 128],
                     in_=src_bf[:, (k0 + k) * width:(k0 + k + 1) * width], identity=ident_b[:])
            src = v3(PB[0:width, 0:m * 128], m)
            dst = dst3[:, k0:k0 + m, :]
            if eng == "act":
                S.op("act", "copy", reads=[RPB], writes=Rdst, out=dst, in_=src)
            else:
                S.op("dve", "tensor_copy", reads=[RPB], writes=Rdst, out=dst, in_=src)

    qk_sq, Rqsq = T([128, 512], F32, "qksq")
    qk_tmp, Rqtmp = T([128, 512], F32, "qktmp")

    def qknorm(ps_ap, Rps, H, dh, g_bc, Rg, out_bf, Rout):
        n = H * dh
        S.op("act", "activation", reads=[Rps], writes=[Rqsq], out=qk_sq[:, 0:n], in_=ps_ap, func=AF.Square)
        S.op("dve", "tensor_reduce", reads=[Rqsq], writes=[Rnss], out=nrm_ss[:, 0:H], in_=v3(qk_sq[:, 0:n], H),
             axis=AX.X, op=ALU.add)
        rstd_from(nrm_ss[:, 0:H], Rnss, dh, H, nrm_rs[:, 0:H], Rnrs, nrm_ln[:, 0:H], Rnln)
        S.op("dve", "tensor_tensor", reads=[Rps, Rnrs], writes=[Rqtmp], out=v3(qk_tmp[:, 0:n], H), in0=v3(ps_ap, H),
             in1=nrm_rs[:, 0:H].unsqueeze(2).broadcast_to([128, H, dh]), op=ALU.mult)
        S.op("pool", "tensor_tensor", reads=[Rqtmp, Rg], writes=[Rout], out=out_bf, in0=qk_tmp[:, 0:n], in1=g_bc,
             op=ALU.mult)

    m_base = A.mark()
    hT, _ = T([128, 8, WIN], BF16, "hT")
    RhT = [Res() for _ in range(32)]
    m_after_hT = A.mark()

    gmix, Rgmix = bc_load(1024, "mix_norm_g")
    xb = [T([128, 1024], F32, "xb") for _ in range(2)]
    hb = [T([128, 1024], BF16, "hb") for _ in range(2)]
    for blk in range(32):
        x_t, Rx = xb[blk % 2]
        h_t, Rh = hb[blk % 2]
        S.dma("sp", writes=[Rx], out=x_t[:], in_=xw_d[blk * 128:(blk + 1) * 128, :])
        rmsnorm(x_t[:], Rx, gmix, Rgmix, h_t[:], Rh)
        transposeN(h_t, Rh, 8, 128, hT[:, :, blk * 128:(blk + 1) * 128], [RhT[blk]], eng="act" if blk % 2 else "dve")
    S.barrier()
    A.release(m_after_hT)

    wxbc, Rwxbc = T([128, 8, 2304], BF16, "wxbc")
    wz, Rwz = T([128, 8, 1280], BF16, "wz")
    wdt, Rwdt = T([128, 8, 20], BF16, "wdt")
    for k in range(8):
        S.dma("pool", writes=[Rwxbc], out=wxbc[:, k, :], in_=w_in_d[k * 128:(k + 1) * 128, OFF_Z:OFF_XBC])
        S.dma("pool", writes=[Rwz], out=wz[:, k, :], in_=w_in_d[k * 128:(k + 1) * 128, 0:OFF_Z])
        S.dma("pool", writes=[Rwdt], out=wdt[:, k, :], in_=w_in_d[k * 128:(k + 1) * 128, OFF_XBC:OFF_DT])
    convw, Rcw = T([128, 18, 4], F32, "convw")
    convb, Rcb = T([128, 18], F32, "convb")
    S.dma("sp", writes=[Rcw], out=convw[:].rearrange("p a b -> p (a b)"), in_=convw_d)
    S.dma("sp", writes=[Rcb], out=convb[:], in_=convb_d)
    dtb, Rdtb = bc_load(20, "dt_bias")
    alog, Ralog = bc_load(20, "a_log")
    dsk, Rdsk = bc_load(20, "d_skip")
    gssd, Rgssd = bc_load(1280, "ssd_norm_g")
    a_bc, Ra = T([128, 20], F32, "a_bc")
    S.op("act", "activation", reads=[Ralog], writes=[Ra], out=a_bc[:], in_=alog[:], func=AF.Exp)
    S.op("dve", "tensor_scalar", reads=[Ra], writes=[Ra], out=a_bc[:], in0=a_bc[:], scalar1=-1.0, scalar2=None,
         op0=ALU.mult)
    halo, _ = T([128, 18, 3], F32, "halo")
    Rhalo = [Res() for _ in range(18)]
    S.op("pool", "memset", writes=Rhalo, ap=halo[:], constant=0.0)
    state, Rst = T([128, 1280], F32, "state")
    state_bf, Rstb = T([128, 1280], BF16, "statebf")
    S.op("pool", "memset", writes=[Rst], ap=state[:], constant=0.0)
    S.op("pool", "memset", writes=[Rstb], ap=state_bf[:], constant=0.0)
    U = [T([128, 259], F32, "U") for _ in range(2)]
    acc = [T([128, 256], F32, "acc") for _ in range(2)]
    xTf = [T([128, 256], F32, "xTf") for _ in range(2)]
    x_tm, Rxtm = T([128, 2, 1280], F32, "x_tm")
    BT, _ = T([128, 4, 256], BF16, "BT")
    RBTg = [Res() for _ in range(4)]
    CTt, _ = T([128, 4, 256], BF16, "CT")
    RCTg = [Res() for _ in range(4)]
    B_tm, RBtm = T([128, 2, 512], BF16, "B_tm")
    dt_tm, Rdt = T([128, 2, 20], F32, "dt_tm")
    adt, Radt = T([128, 2, 20], F32, "adt")
    ncs, Rncs = T([128, 20], F32, "ncs")
    tmp20, Rt20 = T([128, 20], F32, "tmp20")
    dec, Rdec = T([128, 20], F32, "dec")
    cd, Rcd = T([128, 20], F32, "cd")
    w2, Rw2 = T([128, 20], F32, "w2")
    ecs, Recs = T([128, 20], F32, "ecs")
    xdd, Rxdd = T([128, 1280], BF16, "xdd")
    xdt, Rxdt = T([128, 1280], BF16, "xdt")
    CBm, RCBm = T([128, 4, 128], F32, "CBm")
    adtb, Radtb = T([128, 4, 128], F32, "adtb")
    Lt = [T([128, 4, 128], F32, "Lt") for _ in range(2)]
    Wm, RWm = T([128, 20, 128], BF16, "Wm")
    yg, Ryg = T([128, 320], F32, "yg")
    ytmp, Rytmp = T([128, 320], F32, "ytmp")
    sz, Rsz = T([128, 320], F32, "sz")
    ss1, Rss1 = T([128, 1], F32, "ss1")
    ln1, Rln1 = T([128, 1], F32, "ln1")
    rs1, Rrs1 = T([128, 1], F32, "rs1")
    yn, Ryn = T([128, 1280], BF16, "yn")
    yTc = [T([128, 10, 128], BF16, "yTc") for _ in range(2)]

    for G in range(16):
        own = G >= 8
        RhG = RhT[G * 2:(G + 1) * 2]
        tokG = slice(G * 256, (G + 1) * 256)
        def cc_main(cc):
            b = cc % 2
            for k in range(8):
                S.op("pe", "matmul", reads=[Rwxbc] + RhG, writes=[RPS[b]], out=PS[b][:, 0:256],
                     lhsT=wxbc[:, k, cc * 128:(cc + 1) * 128], rhs=hT[:, k, tokG], start=(k == 0), stop=(k == 7))
            u_t, Ru = U[b]
            a_t, Rac = acc[b]
            S.op("act", "copy", reads=[RPS[b]], writes=[Ru], out=u_t[:, 3:259], in_=PS[b][:, 0:256])
            S.op("pool", "tensor_copy", reads=[Rhalo[cc]], writes=[Ru], out=u_t[:, 0:3], in_=halo[:, cc, :])
            S.op("dve", "tensor_scalar", reads=[Ru, Rcw], writes=[Rac], out=a_t[:], in0=u_t[:, 3:259],
                 scalar1=convw[:, cc, 3:4], scalar2=None, op0=ALU.mult)
            for j in (2, 1, 0):
                S.op("dve", "scalar_tensor_tensor", reads=[Ru, Rcw, Rac], writes=[Rac], out=a_t[:],
                     in0=u_t[:, j:j + 256], scalar=convw[:, cc, j:j + 1], in1=a_t[:], op0=ALU.mult, op1=ALU.add)
            S.op("pool", "tensor_copy", reads=[Ru], writes=[Rhalo[cc]], out=halo[:, cc, :], in_=u_t[:, 256:259])
            if cc < 10:
                xt, Rxt = xTf[b]
                S.op("act", "activation", reads=[Rac, Rcb], writes=[Rxt], out=xt[:], in_=a_t[:], func=AF.Silu,
                     bias=convb[:, cc:cc + 1])
            elif cc < 14:
                g = cc - 10
                S.op("act", "activation", reads=[Rac, Rcb], writes=[RBTg[g]], out=BT[:, g, :], in_=a_t[:], func=AF.Silu,
                     bias=convb[:, cc:cc + 1])
            else:
                g = cc - 14
                S.op("act", "activation", reads=[Rac, Rcb], writes=[RCTg[g]], out=CTt[:, g, :], in_=a_t[:], func=AF.Silu,
                     bias=convb[:, cc:cc + 1])

        def cc_tr(cc):
            b = cc % 2
            if cc < 10:
                xt, Rxt = xTf[b]
                pb = 2 + b
                for tb in range(2):
                    S.op("pe", "transpose", reads=[Rxt, Rcst], writes=[RPS[pb]], out=PS[pb][:, tb * 128:(tb + 1) * 128],
                         in_=xt[:, tb * 128:(tb + 1) * 128], identity=ident_f)
                S.op("dve", "tensor_copy", reads=[RPS[pb]], writes=[Rxtm], out=x_tm[:, :, cc * 128:(cc + 1) * 128],
                     in_=v3(PS[pb][:, 0:256], 2))
            elif cc < 14:
                g = cc - 10
                for tb in range(2):
                    S.op("pe", "transpose", reads=[RBTg[g], Ridb], writes=[RPB], out=PB[:, tb * 128:(tb + 1) * 128],
                         in_=BT[:, g, tb * 128:(tb + 1) * 128], identity=ident_b[:])
                S.op("dve", "tensor_copy", reads=[RPB], writes=[RBtm], out=B_tm[:, :, g * 128:(g + 1) * 128],
                     in_=v3(PB[:, 0:256], 2))

        for cc in range(19):
            if cc < 18:
                cc_main(cc)
            if cc >= 1:
                cc_tr(cc - 1)
        issue_conv(4)
        for tb in range(2):
            blk = G * 2 + tb
            for k in range(8):
                S.op("pe", "matmul", reads=[Rwdt, RhT[blk]], writes=[RPS[4]], out=PS[4][:, tb * 20:(tb + 1) * 20],
                     lhsT=hT[:, k, blk * 128:(blk + 1) * 128], rhs=wdt[:, k, :], start=(k == 0), stop=(k == 7))
        S.op("dve", "tensor_tensor", reads=[RPS[4], Rdtb], writes=[Rdt], out=dt_tm[:], in0=v3(PS[4][:, 0:40], 2),
             in1=dtb[:].unsqueeze(1).broadcast_to([128, 2, 20]), op=ALU.add)
        S.op("act", "activation", reads=[Rdt], writes=[Rdt], out=dt_tm[:], in_=dt_tm[:], func=AF.Exp)
        S.op("act", "activation", reads=[Rdt], writes=[Rdt], out=dt_tm[:], in_=dt_tm[:], func=AF.Ln, bias=1.0)
        S.op("dve", "tensor_tensor", reads=[Rdt, Ra], writes=[Radt], out=adt[:], in0=dt_tm[:],
             in1=a_bc[:].unsqueeze(1).broadcast_to([128, 2, 20]), op=ALU.mult)
        for tb in range(2):
            c = G * 2 + tb
            tk = slice(tb * 128, (tb + 1) * 128)
            S.op("pe", "matmul", reads=[Rcst, Radt], writes=[RPS[5]], out=PS[5][:, 0:20], lhsT=tri_f, rhs=adt[:, tb, :],
                 start=True, stop=True)
            S.op("pe", "matmul", reads=[Rcst, Radt], writes=[RPS[5]], out=PS[5][:, 32:52], lhsT=ones_f, rhs=adt[:, tb, :],
                 start=True, stop=True)
            S.op("dve", "tensor_scalar", reads=[RPS[5]], writes=[Rncs], out=ncs[:], in0=PS[5][:, 0:20], scalar1=-1.0,
                 scalar2=None, op0=ALU.mult)
            S.op("dve", "tensor_tensor", reads=[RPS[5], Rncs], writes=[Rt20], out=tmp20[:], in0=PS[5][:, 32:52],
                 in1=ncs[:], op=ALU.add)
            S.op("act", "activation", reads=[Rt20], writes=[Rdec], out=dec[:], in_=tmp20[:], func=AF.Exp)
            S.op("act", "activation", reads=[RPS[5]], writes=[Rcd], out=cd[:], in_=PS[5][:, 32:52], func=AF.Exp)
            S.op("dve", "tensor_tensor", reads=[Rdt, Rdec], writes=[Rw2], out=w2[:], in0=dt_tm[:, tb, :], in1=dec[:],
                 op=ALU.mult)
            S.op("dve", "tensor_tensor", reads=[Rxtm, Rw2], writes=[Rxdd], out=v3(xdd[:], 20),
                 in0=v3(x_tm[:, tb, :], 20), in1=w2[:].unsqueeze(2).broadcast_to([128, 20, 64]), op=ALU.mult)
            if own:
                blk = c
                S.op("act", "activation", reads=[Rncs], writes=[Recs], out=ecs[:], in_=ncs[:], func=AF.Exp, scale=-1.0)
                S.op("pool", "tensor_tensor", reads=[Rxtm, Rdt], writes=[Rxdt], out=v3(xdt[:], 20),
                     in0=v3(x_tm[:, tb, :], 20), in1=dt_tm[:, tb, :].unsqueeze(2).broadcast_to([128, 20, 64]),
                     op=ALU.mult)
                for g in range(4):
                    S.op("pe", "matmul", reads=[RBTg[g], RCTg[g]], writes=[RPS[2]], out=PS[2][:, g * 128:(g + 1) * 128],
                         lhsT=BT[:, g, tk], rhs=CTt[:, g, tk], start=True, stop=True)
                S.op("dve", "tensor_tensor", reads=[RPS[2], Rcst], writes=[RCBm], out=CBm[:], in0=v3(PS[2][:, :], 4),
                     in1=tri_f.unsqueeze(1).broadcast_to([128, 4, 128]), op=ALU.mult)
                for hq in range(5):
                    pb = 3 + hq % 2
                    S.op("pool", "tensor_copy", reads=[Radt], writes=[Radtb], out=adtb[:],
                         in_=adt[:, tb, hq * 4:hq * 4 + 4].unsqueeze(2).broadcast_to([128, 4, 128]))
                    for i in range(4):
                        h = hq * 4 + i
                        S.op("pe", "matmul", reads=[Radtb, Rcst], writes=[RPS[pb]], out=PS[pb][:, i * 128:(i + 1) * 128],
                             lhsT=adtb[:, i, :], rhs=tri_f, start=True, stop=True)
                    lt, Rlt = Lt[hq % 2]
                    S.op("dve", "tensor_tensor", reads=[RPS[pb], Rncs], writes=[Rlt], out=lt[:], in0=v3(PS[pb][:, :], 4),
                         in1=ncs[:, hq * 4:hq * 4 + 4].unsqueeze(2).broadcast_to([128, 4, 128]), op=ALU.add)
                    S.op("act", "activation", reads=[Rlt], writes=[Rlt], out=lt[:], in_=lt[:], func=AF.Exp)
                    for i in range(4):
                        h = hq * 4 + i
                        if i % 2:
                            S.op("pool", "tensor_scalar", reads=[Rlt], writes=[Rlt], out=lt[:, i, :], in0=lt[:, i, :],
                                 scalar1=1.0, scalar2=None, op0=ALU.min)
                            S.op("pool", "tensor_tensor", reads=[Rlt, RCBm], writes=[RWm], out=Wm[:, h, :],
                                 in0=lt[:, i, :], in1=CBm[:, h // 5, :], op=ALU.mult)
                        else:
                            S.op("dve", "scalar_tensor_tensor", reads=[Rlt, RCBm], writes=[RWm],
                                 out=Wm[:, h, :], in0=lt[:, i, :], scalar=1.0, in1=CBm[:, h // 5, :], op0=ALU.min,
                                 op1=ALU.mult)
                for g in range(4):
                    gs = slice(g * 320, (g + 1) * 320)
                    S.op("pe", "matmul", reads=[RCTg[g], Rstb], writes=[RPS[0]], out=PS[0][:, 0:320], lhsT=CTt[:, g, tk],
                         rhs=state_bf[:, gs], start=True, stop=True)
                    for hh in range(5):
                        h = g * 5 + hh
                        S.op("pe", "matmul", reads=[RWm, Rxdt], writes=[RPS[1]], out=PS[1][:, hh * 64:(hh + 1) * 64],
                             lhsT=Wm[:, h, :], rhs=xdt[:, h * 64:(h + 1) * 64], start=True, stop=True)
                    for k in range(8):
                        S.op("pe", "matmul", reads=[Rwz, RhT[blk]], writes=[RPS[6]], out=PS[6][:, 0:320],
                             lhsT=hT[:, k, blk * 128:(blk + 1) * 128], rhs=wz[:, k, gs], start=(k == 0), stop=(k == 7))
                    S.op("dve", "tensor_tensor", reads=[RPS[0], Recs], writes=[Ryg], out=v3(yg[:], 5),
                         in0=v3(PS[0][:, 0:320], 5), in1=ecs[:, g * 5:g * 5 + 5].unsqueeze(2).broadcast_to([128, 5, 64]),
                         op=ALU.mult)
                    S.op("dve", "tensor_tensor", reads=[RPS[1], Ryg], writes=[Ryg], out=yg[:], in0=yg[:],
                         in1=PS[1][:, 0:320], op=ALU.add)
                    S.op("pool", "tensor_tensor", reads=[Rxtm, Rdsk], writes=[Rytmp], out=v3(ytmp[:], 5),
                         in0=v3(x_tm[:, tb, gs], 5), in1=dsk[:, g * 5:g * 5 + 5].unsqueeze(2).broadcast_to([128, 5, 64]),
                         op=ALU.mult)
                    S.op("pool", "tensor_tensor", reads=[Rytmp, Ryg], writes=[Ryg], out=yg[:], in0=yg[:], in1=ytmp[:],
                         op=ALU.add)
                    S.op("act", "activation", reads=[RPS[6]], writes=[Rsz], out=sz[:], in_=PS[6][:, 0:320], func=AF.Silu)
                    S.op("dve", "tensor_tensor", reads=[Ryg, Rsz], writes=[Ryg], out=yg[:], in0=yg[:], in1=sz[:],
                         op=ALU.mult)
                    S.op("act", "activation", reads=[Ryg], writes=[Rytmp, Rss1], out=ytmp[:], in_=yg[:], func=AF.Square,
                         accum_out=ss1[:, 0:1])
                    rstd_from(ss1[:, 0:1], Rss1, 320, 1, rs1[:, 0:1], Rrs1, ln1[:, 0:1], Rln1)
                    S.op("dve", "scalar_tensor_tensor", reads=[Ryg, Rrs1, Rgssd], writes=[Ryn], out=yn[:, gs], in0=yg[:],
                         scalar=rs1[:, 0:1], in1=gssd[:, gs], op0=ALU.mult, op1=ALU.mult)
                yt, Ryt = yTc[tb % 2]
                transposeN(yn, Ryn, 10, 128, yt[:], [Ryt], eng="act")
                ob = c - 16
                S.dma("sp", reads=[Ryt], writes=[Res()], out=yT_scr[:, :, ob * 128:(ob + 1) * 128], in_=yt[:])
            for g in range(4):
                gs = slice(g * 320, (g + 1) * 320)
                S.op("pe", "matmul", reads=[RBtm, Rxdd], writes=[RPS[6]], out=PS[6][:, 0:320],
                     lhsT=B_tm[:, tb, g * 128:(g + 1) * 128], rhs=xdd[:, gs], start=True, stop=True)
                S.op("dve", "tensor_tensor", reads=[Rst, Rcd], writes=[Rst], out=v3(state[:, gs], 5),
                     in0=v3(state[:, gs], 5), in1=cd[:, g * 5:g * 5 + 5].unsqueeze(2).broadcast_to([128, 5, 64]),
                     op=ALU.mult)
                S.op("dve", "tensor_tensor", reads=[Rst, RPS[6]], writes=[Rst], out=state[:, gs], in0=state[:, gs],
                     in1=PS[6][:, 0:320], op=ALU.add)
            if c == 15:
                S.op("dve", "tensor_scalar", reads=[Rst, Rflag], writes=[Rst], out=state[:], in0=state[:],
                     scalar1=flag[:, 0:1], scalar2=None, op0=ALU.mult)
            if c >= 15:
                S.op("act", "copy", reads=[Rst], writes=[Rstb], out=state_bf[:], in_=state[:])
    S.barrier()
    A.release(m_after_hT)

    mtab, Rmtab = T([128, 17, 128], BF16, "mtab")
    S.dma("pool", writes=[Rmtab], out=mtab[:].rearrange("p a b -> p (a b)"), in_=mtab_d)
    gq64, Rgq64 = bc_load(64, "attn_q_norm_g")
    gk64, Rgk64 = bc_load(64, "attn_k_norm_g")
    gq, Rgq = T([128, 256], F32, "gq")
    gk, Rgk = T([128, 256], F32, "gk")
    S.op("dve", "tensor_scalar", reads=[Rgq64], writes=[Rgq], out=v3(gq[:], 4),
         in0=gq64[:].unsqueeze(1).broadcast_to([128, 4, 64]), scalar1=0.125, scalar2=None, op0=ALU.mult)
    S.op("dve", "tensor_copy", reads=[Rgk64], writes=[Rgk], out=v3(gk[:], 4),
         in_=gk64[:].unsqueeze(1).broadcast_to([128, 4, 64]))
    wq, Rwq = T([128, 8, 256], BF16, "wq")
    wk, Rwk = T([128, 8, 256], BF16, "wk")
    wv, Rwv = T([128, 8, 256], BF16, "wv")
    KT, _ = T([64, 4, WIN], BF16, "KT")
    RKT = [Res() for _ in range(32)]
    QT, _ = T([64, 4, HALF], BF16, "QT")
    RQT = [Res() for _ in range(16)]
    Vaug, _ = T([128, 32, 4, 65], BF16, "Vaug")
    RV = [Res() for _ in range(32)]
    kn = [T([128, 256], BF16, "kn") for _ in range(2)]
    qn_ = [T([128, 256], BF16, "qn_") for _ in range(2)]
    Aexp, RAexp = T([128, 128], F32, "Aexp")
    Erev, REr = T([128, 17, 128], F32, "Erev")
    Pf = [T([128, 512], F32, "Pf") for _ in range(3)]
    Pbf = [T([128, 512], BF16, "Pbf") for _ in range(3)]
    rd, Rrd = T([128, 2], F32, "rd")
    o_all, _ = T([128, 16, 256], BF16, "o_all")
    Roall = [Res() for _ in range(16)]
    oTc = [T([128, 2, 128], BF16, "oTc") for _ in range(2)]
    for r in range(3):
        for k in range(8):
            rows = slice(k * 128, (k + 1) * 128)
            S.dma("pool", writes=[Rwq], out=wq[:, k, :], in_=w_in_d[rows, OFF_DT + r * 256:OFF_DT + (r + 1) * 256])
            S.dma("pool", writes=[Rwk], out=wk[:, k, :], in_=w_in_d[rows, OFF_Q + r * 256:OFF_Q + (r + 1) * 256])
            S.dma("pool", writes=[Rwv], out=wv[:, k, :], in_=w_in_d[rows, OFF_K + r * 256:OFF_K + (r + 1) * 256])

        def proj_mm(blk):
            bs = slice(blk * 128, (blk + 1) * 128)
            pa = blk % 2
            for k in range(8):
                S.op("pe", "matmul", reads=[Rwk, RhT[blk]], writes=[RPS[pa]], out=PS[pa][:, 0:256], lhsT=hT[:, k, bs],
                     rhs=wk[:, k, :], start=(k == 0), stop=(k == 7))
            kt, Rkn = kn[blk % 2]
            qknorm(PS[pa][:, 0:256], RPS[pa], 4, 64, gk[:], Rgk, kt[:], Rkn)
            pv = 2 + blk % 2
            for k in range(8):
                S.op("pe", "matmul", reads=[Rwv, RhT[blk]], writes=[RPS[pv]], out=PS[pv][:, 0:256], lhsT=hT[:, k, bs],
                     rhs=wv[:, k, :], start=(k == 0), stop=(k == 7))
            if blk >= 16:
                S.op("act", "copy", reads=[RPS[pv]], writes=[RV[blk]], out=Vaug[:, blk, :, 0:64],
                     in_=v3(PS[pv][:, 0:256], 4))
                S.op("pool", "memset", writes=[RV[blk]], ap=Vaug[:, blk, :, 64:65], constant=1.0)
            else:
                S.op("dve", "tensor_scalar", reads=[RPS[pv], Rflag], writes=[RV[blk]], out=Vaug[:, blk, :, 0:64],
                     in0=v3(PS[pv][:, 0:256], 4), scalar1=flag[:, 0:1], scalar2=None, op0=ALU.mult)
                S.op("pool", "tensor_copy", reads=[Rflag], writes=[RV[blk]], out=Vaug[:, blk, :, 64:65],
                     in_=flag[:, 0:1].unsqueeze(1).broadcast_to([128, 4, 1]))
            if blk >= 16:
                pq = 4 + blk % 2
                for k in range(8):
                    S.op("pe", "matmul", reads=[Rwq, RhT[blk]], writes=[RPS[pq]], out=PS[pq][:, 0:256], lhsT=hT[:, k, bs],
                         rhs=wq[:, k, :], start=(k == 0), stop=(k == 7))
                qt, Rqn = qn_[blk % 2]
                qknorm(PS[pq][:, 0:256], RPS[pq], 4, 64, gq[:], Rgq, qt[:], Rqn)

        def proj_tr(blk):
            bs = slice(blk * 128, (blk + 1) * 128)
            kt, Rkn = kn[blk % 2]
            transposeN(kt, Rkn, 4, 64, KT[:, :, bs], [RKT[blk]], eng="act")
            if blk >= 16:
                qt, Rqn = qn_[blk % 2]
                transposeN(qt, Rqn, 4, 64, QT[:, :, (blk - 16) * 128:(blk - 15) * 128], [RQT[blk - 16]], eng="act")

        for blk in range(33):
            if blk < 32:
                proj_mm(blk)
            if blk >= 1:
                proj_tr(blk - 1)

        seq = []
        for hh in range(4):
            for qb in range(16, 32):
                kbs = list(range(qb - 16, qb + 1))
                for gi in range(5):
                    seq.append((hh, qb, gi, kbs[gi * 4:gi * 4 + 4]))

        def emit_S(i):
            hh, qb, gi, grp = seq[i]
            qi = qb - 16
            sbk = i % 2
            for ii, kb in enumerate(grp):
                S.op("pe", "matmul", reads=[RKT[kb], RQT[qi]], writes=[RPS[sbk]],
                     out=PS[sbk][:, ii * 128:(ii + 1) * 128], lhsT=KT[:, hh, kb * 128:(kb + 1) * 128],
                     rhs=QT[:, hh, qi * 128:(qi + 1) * 128], start=True, stop=True)

        def emit_rest(i):
            hh, qb, gi, grp = seq[i]
            qi = qb - 16
            n = len(grp)
            sbk = i % 2
            unit = i // 5
            ob = 4 + unit % 2
            pf, Rpf = Pf[i % 3]
            pbf, Rpbf = Pbf[i % 3]
            if qb == 16 and gi == 0:
                h = r * 4 + hh
                slope = 2.0 ** (-8.0 * (h + 1) / 12.0)
                S.op("act", "activation", reads=[Rcst], writes=[RAexp], out=Aexp[:], in_=d0_f, func=AF.Exp, scale=-slope)
                for j in range(17):
                    cpow = math.exp(-slope * 128.0 * (16 - j))
                    S.op("dve", "scalar_tensor_tensor", reads=[RAexp, Rmtab], writes=[REr],
                         out=Erev[:, j, :], in0=Aexp[:], scalar=float(cpow), in1=mtab[:, j, :], op0=ALU.mult,
                         op1=ALU.mult)
            S.op("act", "activation", reads=[RPS[sbk]], writes=[Rpf], out=pf[:, 0:n * 128],
                 in_=PS[sbk][:, 0:n * 128], func=AF.Exp)
            S.op("pool" if i % 3 == 2 else "dve", "tensor_tensor", reads=[Rpf, REr], writes=[Rpbf], out=pbf[:, 0:n * 128],
                 in0=pf[:, 0:n * 128], in1=Erev[:, gi * 4:gi * 4 + n, :].rearrange("p a b -> p (a b)"),
                 op=ALU.mult)
            for ii, kb in enumerate(grp):
                S.op("pe", "matmul", reads=[Rpbf, RV[kb]], writes=[RPS[ob]], out=PS[ob][:, 0:65],
                     lhsT=pbf[:, ii * 128:(ii + 1) * 128], rhs=Vaug[:, kb, hh, :], start=(gi == 0 and ii == 0),
                     stop=(gi == 4))
            if gi == 4:
                S.op("dve", "reciprocal", reads=[RPS[ob]], writes=[Rrd], out=rd[:, 0:1], in_=PS[ob][:, 64:65])
                S.op("dve", "tensor_scalar", reads=[RPS[ob], Rrd], writes=[Roall[qi]],
                     out=o_all[:, qi, hh * 64:(hh + 1) * 64], in0=PS[ob][:, 0:64], scalar1=rd[:, 0:1], scalar2=None,
                     op0=ALU.mult)

        emit_S(0)
        for i in range(len(seq)):
            if i + 1 < len(seq):
                emit_S(i + 1)
            emit_rest(i)
        for qi in range(16):
            for i in range(2):
                S.op("pe", "transpose", reads=[Roall[qi], Ridb], writes=[RPB], out=PB[:, i * 128:(i + 1) * 128],
                     in_=o_all[:, qi, i * 128:(i + 1) * 128], identity=ident_b[:])
            ot, Rot = oTc[qi % 2]
            S.op("act", "copy", reads=[RPB], writes=[Rot], out=ot[:], in_=v3(PB[:, 0:256], 2))
            S.dma("sp", reads=[Rot], writes=[Res()], out=oT_scr[:, 2 * r:2 * r + 2, qi * 128:(qi + 1) * 128], in_=ot[:])
    S.barrier()
    A.release(m_base)

    x1, _ = T([128, 16, 1024], F32, "x1")
    Rx1 = [Res() for _ in range(16)]
    m_after_x1 = A.mark()
    yT_sb, RyT = T([128, 10, HALF], BF16, "yT_sb")
    for kc in range(10):
        S.dma("sp", writes=[RyT], out=yT_sb[:, kc, :], in_=yT_scr[:, kc, :])
    oT, RoTs = T([128, 6, HALF], BF16, "oT")
    for kc in range(6):
        S.dma("sp", writes=[RoTs], out=oT[:, kc, :], in_=oT_scr[:, kc, :])
    wout, Rwout = T([128, 16, 1024], BF16, "wout")
    for kc in range(16):
        S.dma("pool", writes=[Rwout], out=wout[:, kc, :], in_=w_out_d[kc * 128:(kc + 1) * 128, :])
    xo = [T([128, 1024], F32, "xo") for _ in range(2)]
    for tb in range(16):
        ts_ = slice(tb * 128, (tb + 1) * 128)
        xo_t, Rxo = xo[tb % 2]
        S.dma("sp", writes=[Rxo], out=xo_t[:], in_=xw_d[HALF + tb * 128:HALF + (tb + 1) * 128, :])
        for half in range(2):
            pb = 2 * (tb % 2) + half
            cs_ = slice(half * 512, (half + 1) * 512)
            for kc in range(16):
                lhsT = yT_sb[:, kc, ts_] if kc < 10 else oT[:, kc - 10, ts_]
                S.op("pe", "matmul", reads=[RyT, RoTs, Rwout], writes=[RPS[pb]], out=PS[pb][:, :], lhsT=lhsT,
                     rhs=wout[:, kc, cs_], start=(kc == 0), stop=(kc == 15))
            S.op("dve", "tensor_tensor", reads=[RPS[pb], Rxo], writes=[Rx1[tb]], out=x1[:, tb, cs_], in0=PS[pb][:, :],
                 in1=xo_t[:, cs_], op=ALU.add)
    S.barrier()
    A.release(m_after_x1)

    def store_x1_and_finish():
        fin = []
        for tb in range(16):
            R = Res()
            S.dma("sp", reads=[Rx1[tb]], writes=[R], out=out_d[tb * 128:(tb + 1) * 128, :], in_=x1[:, tb, :])
            fin.append(R)
        for R in fin:
            S._wait("sp", R.w)
        S.emit()
        return nc

    if stop_after == "C":
        return store_x1_and_finish()

    wqm, Rwqm = T([128, 8, 512], BF16, "wqm")
    wkv, Rwkv = T([128, 8, 1024], BF16, "wkv")
    wo, Rwo = T([128, 4, 1024], BF16, "wo")
    for k in range(8):
        S.dma("pool", writes=[Rwqm], out=wqm[:, k, :], in_=mem_w_q_d[k * 128:(k + 1) * 128, :])
        S.dma("pool", writes=[Rwkv], out=wkv[:, k, :], in_=mem_w_kv_d[k * 128:(k + 1) * 128, :])
    for k in range(4):
        S.dma("pool", writes=[Rwo], out=wo[:, k, :], in_=mem_w_o_d[k * 128:(k + 1) * 128, :])
    gx, Rgx = bc_load(1024, "xattn_norm_g")
    gm, Rgm = bc_load(1024, "mem_norm_g")
    gqm128, Rgqm128 = bc_load(128, "mem_q_norm_g")
    gkm128, Rgkm128 = bc_load(128, "mem_k_norm_g")
    gqm, Rgqm = T([128, 512], F32, "gqm")
    gkm, Rgkm = T([128, 512], F32, "gkm")
    S.op("dve", "tensor_scalar", reads=[Rgqm128], writes=[Rgqm], out=v3(gqm[:], 4),
         in0=gqm128[:].unsqueeze(1).broadcast_to([128, 4, 128]), scalar1=128.0 ** -0.5, scalar2=None, op0=ALU.mult)
    S.op("dve", "tensor_copy", reads=[Rgkm128], writes=[Rgkm], out=v3(gkm[:], 4),
         in_=gkm128[:].unsqueeze(1).broadcast_to([128, 4, 128]))
    KmT, RKmT = T([128, 4, 256], BF16, "KmT")
    Vm, RVm = T([128, 2, 4, 129], BF16, "Vm")
    S.op("pool", "memset", writes=[RVm], ap=Vm[:].rearrange("p a b c -> p (a b c)"), constant=1.0)
    memb, Rmemb = T([128, 1024], F32, "memb")
    mh, Rmh = T([128, 1024], BF16, "mh")
    mT, RmT = T([128, 8, 128], BF16, "mT")
    knm, Rknm = T([128, 512], BF16, "knm")
    for mb in range(2):
        ms = slice(mb * 128, (mb + 1) * 128)
        S.dma("sp", writes=[Rmemb], out=memb[:], in_=mem_d[ms, :])
        rmsnorm(memb[:], Rmemb, gm, Rgm, mh[:], Rmh)
        transposeN(mh, Rmh, 8, 128, mT[:], [RmT], eng="act")
        for half in range(2):
            for k in range(8):
                S.op("pe", "matmul", reads=[RmT, Rwkv], writes=[RPS[half]], out=PS[half][:, :], lhsT=mT[:, k, :],
                     rhs=wkv[:, k, half * 512:(half + 1) * 512], start=(k == 0), stop=(k == 7))
        qknorm(PS[0][:, :], RPS[0], 4, 128, gkm[:], Rgkm, knm[:], Rknm)
        transposeN(knm, Rknm, 4, 128, KmT[:, :, ms], [RKmT], eng="act")
        S.op("act", "copy", reads=[RPS[1]], writes=[RVm], out=Vm[:, mb, :, 0:128], in_=v3(PS[1][:, :], 4))
    h2, Rh2 = T([128, 1024], BF16, "h2")
    h2T, Rh2T = T([128, 8, 128], BF16, "h2T")
    qn, Rqn2 = T([128, 512], BF16, "qn")
    qT, RqT = T([128, 4, 128], BF16, "qT")
    Pm = [T([128, 256], BF16, "Pm") for _ in range(2)]
    o2, Ro2 = T([128, 512], BF16, "o2")
    o2T, Ro2T = T([128, 4, 128], BF16, "o2T")
    for tb in range(16):
        rmsnorm(x1[:, tb, :], Rx1[tb], gx, Rgx, h2[:], Rh2)
        transposeN(h2, Rh2, 8, 128, h2T[:], [Rh2T], eng="act")
        for k in range(8):
            S.op("pe", "matmul", reads=[Rh2T, Rwqm], writes=[RPS[2]], out=PS[2][:, :], lhsT=h2T[:, k, :], rhs=wqm[:, k, :],
                 start=(k == 0), stop=(k == 7))
        qknorm(PS[2][:, :], RPS[2], 4, 128, gqm[:], Rgqm, qn[:], Rqn2)
        transposeN(qn, Rqn2, 4, 128, qT[:], [RqT], eng="act")
        for hh in range(4):
            sb_ = 3 + hh % 2
            ob = 5 + hh % 2
            pm, Rpm = Pm[hh % 2]
            for mb in range(2):
                S.op("pe", "matmul", reads=[RKmT, RqT], writes=[RPS[sb_]], out=PS[sb_][:, mb * 128:(mb + 1) * 128],
                     lhsT=KmT[:, hh, mb * 128:(mb + 1) * 128], rhs=qT[:, hh, :], start=True, stop=True)
            S.op("act", "activation", reads=[RPS[sb_]], writes=[Rpm], out=pm[:], in_=PS[sb_][:, 0:256], func=AF.Exp)
            for mb in range(2):
                S.op("pe", "matmul", reads=[Rpm, RVm], writes=[RPS[ob]], out=PS[ob][:, 0:129],
                     lhsT=pm[:, mb * 128:(mb + 1) * 128], rhs=Vm[:, mb, hh, :], start=(mb == 0), stop=(mb == 1))
            S.op("dve", "reciprocal", reads=[RPS[ob]], writes=[Rrd], out=rd[:, 0:1], in_=PS[ob][:, 128:129])
            S.op("dve", "tensor_scalar", reads=[RPS[ob], Rrd], writes=[Ro2], out=o2[:, hh * 128:(hh + 1) * 128],
                 in0=PS[ob][:, 0:128], scalar1=rd[:, 0:1], scalar2=None, op0=ALU.mult)
        transposeN(o2, Ro2, 4, 128, o2T[:], [Ro2T], eng="act")
        for half in range(2):
            cs_ = slice(half * 512, (half + 1) * 512)
            for kc in range(4):
                S.op("pe", "matmul", reads=[Ro2T, Rwo], writes=[RPS[half]], out=PS[half][:, :], lhsT=o2T[:, kc, :],
                     rhs=wo[:, kc, cs_], start=(kc == 0), stop=(kc == 3))
            S.op("dve", "tensor_tensor", reads=[RPS[half], Rx1[tb]], writes=[Rx1[tb]], out=x1[:, tb, cs_],
                 in0=PS[half][:, :], in1=x1[:, tb, cs_], op=ALU.add)
    S.barrier()
    A.release(m_after_x1)
    if stop_after == "D":
        return store_x1_and_finish()

    wpq, Rwpq = T([128, 8, 2048], BF16, "wpq")
    for k in range(8):
        S.dma("pool", writes=[Rwpq], out=wpq[:, k, :], in_=peer_wq_d[k * 128:(k + 1) * 128, :])
    keysT, RkeysT = T([128, 2, 128], F32, "keysT")
    S.dma("sp", writes=[RkeysT], out=keysT[:, 0, :], in_=keys1T_d)
    S.dma("sp", writes=[RkeysT], out=keysT[:, 1, :], in_=keys2T_d)
    gf, Rgf = bc_load(1024, "ffn_norm_g")
    h3p = [T([128, 1024], F32, "h3") for _ in range(2)]
    h3b, Rh3b = T([128, 1024], BF16, "h3b")
    h3T, Rh3T = T([128, 8, 128], BF16, "h3T")
    off_q = A.mark()
    qrT, RqrT = T([128, 16, 128], F32, "qrT")
    eq = nc.alloc_sbuf_tensor_at("eq_alias", [128, 8, 16, 16], F32, offset=off_q)
    Req = RqrT
    off_s = A.mark()
    sc, Rsc = T([128, 16, 128], F32, "sc")
    cand = nc.alloc_sbuf_tensor_at("cand_alias", [128, 8, 256], F32, offset=off_s)
    Rcand = Rsc
    sc2, Rsc2 = T([128, 128], F32, "sc2")
    tv, Rtv = T([128, 16, 16], F32, "tv")
    ti, Rti = T([128, 16, 16], U32, "ti")
    tif, Rtif = T([128, 16, 16], F32, "tif")
    cand2, Rcand2 = T([128, 256], F32, "cand2")
    cv, Rcv = T([128, 8, 16], F32, "cv")
    cp, Rcp = T([128, 8, 16], U32, "cp")
    ca, Rca = T([128, 8, 16], U32, "ca")
    cb_, Rcb_ = T([128, 8, 16], U32, "cb")
    caf, Rcaf = T([128, 8, 16], F32, "caf")
    cbf, Rcbf = T([128, 8, 16], F32, "cbf")
    i1f, Ri1f = T([128, 8, 16], F32, "i1f")
    i2f, Ri2f = T([128, 8, 16], F32, "i2f")
    ef, Ref = T([128, 128], F32, "ef")
    eip = [T([128, 128], I32, "ei") for _ in range(2)]
    gtp = [T([128, 8, 16], F32, "gt") for _ in range(2)]
    gsum, Rgsum = T([128, 8], F32, "gsum")
    actv, Ractv = T([128, 128], F32, "actv")
    wg4, Rwg4 = T([128, 4], F32, "wg4")
    wgt, Rwgt = T([128, 128], F32, "wgt")
    djunk, Rdj = T([128, 1024], BF16, "djunk")
    NUV = 3
    UVt = [T([128, 4, 2048], BF16, "UV") for _ in range(NUV)]
    SV = [T([128, 1024], BF16, "SV") for _ in range(4)]
    fin = []

    def peer_setup(tb, p):
        h3, Rh3 = h3p[p]
        ei, Rei = eip[p]
        gt, Rgt = gtp[p]
        rmsnorm(x1[:, tb, :], Rx1[tb], gf, Rgf, h3[:], Rh3)
        S.op("pool", "tensor_copy", reads=[Rh3], writes=[Rh3b], out=h3b[:], in_=h3[:])
        transposeN(h3b, Rh3b, 8, 128, h3T[:], [Rh3T], eng="act")
        for q4 in range(4):
            pb = q4 % 2
            for i in range(4):
                ccq = q4 * 4 + i
                for k in range(8):
                    S.op("pe", "matmul", reads=[Rwpq, Rh3T], writes=[RPS[pb]], out=PS[pb][:, i * 128:(i + 1) * 128],
                         lhsT=wpq[:, k, ccq * 128:(ccq + 1) * 128], rhs=h3T[:, k, :], start=(k == 0), stop=(k == 7))
            S.op("act", "copy", reads=[RPS[pb]], writes=[RqrT], out=qrT[:, q4 * 4:q4 * 4 + 4, :], in_=v3(PS[pb][:, :], 4))
        for q4 in range(4):
            pb = 2 + q4 % 2
            for i in range(4):
                j = q4 * 4 + i
                S.op("pe", "matmul", reads=[RqrT, RkeysT], writes=[RPS[pb]], out=PS[pb][:, i * 128:(i + 1) * 128],
                     lhsT=qrT[:, j, :], rhs=keysT[:, j % 2, :], start=True, stop=True)
            S.op("act", "copy", reads=[RPS[pb]], writes=[Rsc], out=sc[:, q4 * 4:q4 * 4 + 4, :], in_=v3(PS[pb][:, :], 4))
        for j in range(16):
            S.op("dve", "max", reads=[Rsc], writes=[Rtv], out=tv[:, j, 0:8], in_=sc[:, j, :])
            S.op("dve", "max_index", reads=[Rsc, Rtv], writes=[Rti], out=ti[:, j, 0:8], in_max=tv[:, j, 0:8],
                 in_values=sc[:, j, :])
            S.op("dve", "match_replace", reads=[Rsc, Rtv], writes=[Rsc2], out=sc2[:], in_to_replace=tv[:, j, 0:8],
                 in_values=sc[:, j, :], imm_value=-1e30)
            S.op("dve", "max", reads=[Rsc2], writes=[Rtv], out=tv[:, j, 8:16], in_=sc2[:])
            S.op("dve", "max_index", reads=[Rsc2, Rtv], writes=[Rti], out=ti[:, j, 8:16], in_max=tv[:, j, 8:16],
                 in_values=sc2[:])
        S.op("dve", "tensor_copy", reads=[Rti], writes=[Rtif], out=tif[:], in_=ti[:])
        tv4 = tv[:].rearrange("p (h two) k -> p h two k", two=2)
        tif4 = tif[:].rearrange("p (h two) k -> p h two k", two=2)
        S.op("dve", "tensor_tensor", reads=[Rtv], writes=[Rcand], out=cand[:].rearrange("p h (a b) -> p h a b", a=16),
             in0=tv4[:, :, 0, :].unsqueeze(3).broadcast_to([128, 8, 16, 16]),
             in1=tv4[:, :, 1, :].unsqueeze(2).broadcast_to([128, 8, 16, 16]), op=ALU.add)
        for h in range(8):
            S.op("dve", "max", reads=[Rcand], writes=[Rcv], out=cv[:, h, 0:8], in_=cand[:, h, :])
            S.op("dve", "max_index", reads=[Rcand, Rcv], writes=[Rcp], out=cp[:, h, 0:8], in_max=cv[:, h, 0:8],
                 in_values=cand[:, h, :])
            S.op("dve", "match_replace", reads=[Rcand, Rcv], writes=[Rcand2], out=cand2[:], in_to_replace=cv[:, h, 0:8],
                 in_values=cand[:, h, :], imm_value=-1e30)
            S.op("dve", "max", reads=[Rcand2], writes=[Rcv], out=cv[:, h, 8:16], in_=cand2[:])
            S.op("dve", "max_index", reads=[Rcand2, Rcv], writes=[Rcp], out=cp[:, h, 8:16], in_max=cv[:, h, 8:16],
                 in_values=cand2[:])
        S.op("dve", "tensor_single_scalar", reads=[Rcp], writes=[Rca], out=ca[:], in_=cp[:], scalar=4,
             op=ALU.logical_shift_right)
        S.op("dve", "tensor_single_scalar", reads=[Rcp], writes=[Rcb_], out=cb_[:], in_=cp[:], scalar=15,
             op=ALU.bitwise_and)
        S.op("dve", "tensor_copy", reads=[Rca], writes=[Rcaf], out=caf[:], in_=ca[:])
        S.op("dve", "tensor_copy", reads=[Rcb_], writes=[Rcbf], out=cbf[:], in_=cb_[:])
        io4 = iota16.unsqueeze(1).unsqueeze(1).broadcast_to([128, 8, 16, 16])
        for (sel, Rsel, half, dst, Rdst) in ((caf, Rcaf, 0, i1f, Ri1f), (cbf, Rcbf, 1, i2f, Ri2f)):
            S.op("dve", "tensor_tensor", reads=[Rsel, Rcst], writes=[Req], out=eq[:],
                 in0=sel[:].unsqueeze(3).broadcast_to([128, 8, 16, 16]), in1=io4, op=ALU.is_equal)
            S.op("dve", "tensor_tensor", reads=[Req, Rtif], writes=[Req], out=eq[:], in0=eq[:],
                 in1=tif4[:, :, half, :].unsqueeze(2).broadcast_to([128, 8, 16, 16]), op=ALU.mult)
            S.op("dve", "tensor_reduce", reads=[Req], writes=[Rdst], out=dst[:].rearrange("p h k -> p (h k)"),
                 in_=eq[:].rearrange("p h k a -> p (h k) a"), axis=AX.X, op=ALU.add)
        S.op("dve", "scalar_tensor_tensor", reads=[Ri1f, Ri2f], writes=[Ref], out=ef[:],
             in0=i1f[:].rearrange("p h k -> p (h k)"), scalar=128.0, in1=i2f[:].rearrange("p h k -> p (h k)"),
             op0=ALU.mult, op1=ALU.add)
        S.op("dve", "tensor_copy", reads=[Ref], writes=[Rei], out=ei[:], in_=ef[:])
        S.op("dve", "tensor_tensor", reads=[Rcv], writes=[Rgt], out=gt[:], in0=cv[:],
             in1=cv[:, :, 0:1].broadcast_to([128, 8, 16]), op=ALU.subtract)
        S.op("act", "activation", reads=[Rgt], writes=[Rgt], out=gt[:], in_=gt[:], func=AF.Exp)
        S.op("dve", "tensor_reduce", reads=[Rgt], writes=[Rgsum], out=gsum[:], in_=gt[:], axis=AX.X, op=ALU.add)
        S.op("dve", "reciprocal", reads=[Rgsum], writes=[Rgsum], out=gsum[:], in_=gsum[:])
        S.op("dve", "tensor_tensor", reads=[Rgt, Rgsum], writes=[Rgt], out=gt[:], in0=gt[:],
             in1=gsum[:].unsqueeze(2).broadcast_to([128, 8, 16]), op=ALU.mult)

    def peer_gather(tb, p):
        h3, Rh3 = h3p[p]
        ei, Rei = eip[p]
        gt, Rgt = gtp[p]
        gtf = gt[:].rearrange("p h k -> p (h k)")
        S.op("pool", "memset", writes=[Ractv], ap=actv[:], constant=0.0)
        for g4 in range(32):
            uvt, Ruvt = UVt[g4 % NUV]
            for i in range(4):
                s = g4 * 4 + i
                S.dma("pool", reads=[Rei, Ruv], writes=[Ruvt], meth="indirect_dma_start", out=uvt[:, i, :],
                      out_offset=None, in_=uv_scr, in_offset=bass.IndirectOffsetOnAxis(ap=ei[:, s:s + 1], axis=0))
            for i in range(4):
                s = g4 * 4 + i
                S.op("dve", "scalar_tensor_tensor", reads=[Ruvt, Rh3], writes=[Rdj, Ractv], out=djunk[:],
                     in0=uvt[:, i, 0:1024], scalar=1.0, in1=h3[:], op0=ALU.mult, op1=ALU.mult,
                     accum_out=actv[:, s:s + 1])
            S.op("act", "activation", reads=[Ractv], writes=[Rwg4], out=wg4[:], in_=actv[:, g4 * 4:g4 * 4 + 4],
                 func=AF.Gelu)
            S.op("dve", "tensor_tensor", reads=[Rwg4, Rgt], writes=[Rwgt], out=wgt[:, g4 * 4:g4 * 4 + 4], in0=wg4[:],
                 in1=gtf[:, g4 * 4:g4 * 4 + 4], op=ALU.mult)
            for i in range(4):
                s = g4 * 4 + i
                sv, Rsv = SV[s % 4]
                S.op("act", "activation", reads=[Ruvt, Rwgt], writes=[Rsv], out=sv[:], in_=uvt[:, i, 1024:2048],
                     func=AF.Copy, scale=wgt[:, s:s + 1])
                for half in range(2):
                    S.op("pe", "matmul", reads=[Rsv, Ridb], writes=[RPS[5 + half]], out=PS[5 + half][:, :],
                         lhsT=ident_b[:], rhs=sv[:, half * 512:(half + 1) * 512], start=(s == 0), stop=(s == 127))
        for half in range(2):
            cs_ = slice(half * 512, (half + 1) * 512)
            S.op("dve", "tensor_tensor", reads=[RPS[5 + half], Rx1[tb]], writes=[Rx1[tb]], out=x1[:, tb, cs_],
                 in0=x1[:, tb, cs_], in1=PS[5 + half][:, :], op=ALU.add)
        R = Res()
        S.dma("sp", reads=[Rx1[tb]], writes=[R], out=out_d[tb * 128:(tb + 1) * 128, :], in_=x1[:, tb, :])
        fin.append(R)

    peer_setup(0, 0)
    for tb in range(16):
        if tb + 1 < 16:
            peer_setup(tb + 1, (tb + 1) % 2)
        peer_gather(tb, tb % 2)
    for R in fin:
        S._wait("sp", R.w)
    S.emit()
    return nc


def _consts():
    c = np.zeros((128, NCONST), np.float32)
    p = np.arange(128)
    c[:, 0:128] = np.eye(128)
    c[:, 128:256] = (p[:, None] <= p[None, :])
    c[:, 256:384] = 1.0
    c[:, 384:512] = (p[None, :] - p[:, None])
    c[:, 512:528] = np.arange(16)[None, :]
    k = p[:, None, None]
    j = np.arange(17)[None, :, None]
    q = p[None, None, :]
    delta = (16 - j) * 128 + q - k
    m = ((delta >= 0) & (delta <= 128)).astype(np.float32)
    m += ((delta >= 0) & (delta <= 512) & (delta % 4 == 0))
    m += ((delta >= 0) & (delta <= 2048) & (delta % 16 == 0))
    return c, np.ascontiguousarray(m.reshape(128, 17 * 128).astype(np.float32))


_NC_CACHE = {}


def kernel(**inputs):
    stop_after = os.environ.get("KSTOP") or None
    x = np.asarray(inputs["x"], np.float32)
    mem = np.asarray(inputs["mem"], np.float32)
    cst, mtab = _consts()
    common = {
        "cst": cst, "mtab": mtab,
        "w_in": np.ascontiguousarray(inputs["w_in"][0], dtype=np.float32),
        "convw": np.ascontiguousarray(
            np.asarray(inputs["conv_w"][0], np.float32).T.reshape(18, 128, 4).transpose(1, 0, 2).reshape(128, 72)),
        "convb": np.ascontiguousarray(np.asarray(inputs["conv_b"][0], np.float32).reshape(18, 128).T),
        "w_out": np.ascontiguousarray(inputs["w_out"][0], dtype=np.float32),
    }
    for nm in ("mix_norm_g", "dt_bias", "a_log", "d_skip", "ssd_norm_g", "attn_q_norm_g", "attn_k_norm_g",
               "xattn_norm_g", "mem_norm_g", "mem_q_norm_g", "mem_k_norm_g", "ffn_norm_g"):
        common[nm] = np.ascontiguousarray(np.asarray(inputs[nm][0], np.float32).reshape(1, -1))
    if True:
        common["mem_w_q"] = np.ascontiguousarray(inputs["mem_w_q"][0], dtype=np.float32)
        common["mem_w_kv"] = np.ascontiguousarray(inputs["mem_w_kv"][0], dtype=np.float32)
        common["mem_w_o"] = np.ascontiguousarray(inputs["mem_w_o"][0], dtype=np.float32)
    if True:
        common["peer_w_query"] = np.ascontiguousarray(inputs["peer_w_query"][0], dtype=np.float32)
        common["keys1T"] = np.ascontiguousarray(np.asarray(inputs["peer_sub_keys1"][0], np.float32).T)
        common["keys2T"] = np.ascontiguousarray(np.asarray(inputs["peer_sub_keys2"][0], np.float32).T)
        common["peer_u"] = np.ascontiguousarray(inputs["peer_u"][0], dtype=np.float32)
        common["peer_v"] = np.ascontiguousarray(inputs["peer_v"][0], dtype=np.float32)
    in_maps = []
    for c in range(8):
        b, j = c // 2, c % 2
        xw = np.zeros((WIN, D), np.float32)
        if j == 1:
            xw[:HALF] = x[b, :HALF]
        xw[HALF:] = x[b, j * HALF:(j + 1) * HALF]
        m = dict(common)
        m["xw"] = xw
        m["flag"] = np.full((128, 1), float(j), np.float32)
        if True:
            m["mem"] = np.ascontiguousarray(mem[b])
        in_maps.append(m)
    nc = build(stop_after)
    res = run_bass_kernel_spmd(nc, in_maps, core_ids=list(range(8)))
    out = np.zeros((NB, SEQ, D), np.float32)
    for c in range(8):
        b, j = c // 2, c % 2
        out[b, j * HALF:(j + 1) * HALF] = np.asarray(res.results[c]["out"])
    return out
```

```python
import os
import math
import numpy as np
import concourse.bass as bass
import concourse.mybir as mybir
from concourse.bass_utils import run_bass_kernel_spmd

F32 = mybir.dt.float32
BF16 = mybir.dt.bfloat16
I32 = mybir.dt.int32
U32 = mybir.dt.uint32
AF = mybir.ActivationFunctionType
ALU = mybir.AluOpType
AX = mybir.AxisListType

D = 1024
NB = 4
SEQ = 4096
HALF = 2048
WIN = 4096
EPS = 1e-6
OFF_Z = 1280
OFF_XBC = OFF_Z + 2304
OFF_DT = OFF_XBC + 20
OFF_Q = OFF_DT + 768
OFF_K = OFF_Q + 768
IN_COLS = OFF_K + 768
NCONST = 128 * 4 + 16


class Res:
    __slots__ = ("w", "r")

    def __init__(self):
        self.w = None
        self.r = {}


class Sched:
    COMPUTE = ("pe", "act", "dve", "pool")
    NDMA = 12

    def __init__(self, nc):
        self.nc = nc
        self.streams = {k: [] for k in ("pe", "act", "dve", "pool", "sp")}
        self.esem = {k: nc.alloc_semaphore("es_" + k) for k in self.COMPUTE}
        self.ecnt = {k: 0 for k in self.COMPUTE}
        self.dsem = {q: [nc.alloc_semaphore("ds_%s%d" % (q, i)) for i in range(self.NDMA)]
                     for q in ("sp", "pool")}
        self.dcnt = {q: [0] * self.NDMA for q in ("sp", "pool")}
        self.dnext = {q: 0 for q in ("sp", "pool")}
        self.known = {k: {} for k in self.streams}
        self.nwaits = 0
        self.nops = 0

    def _sem_of(self, pid):
        if pid[0] == "e":
            return self.esem[pid[1]], pid[2], ("e", pid[1])
        return self.dsem[pid[1]][pid[2]], pid[3], ("d", pid[1], pid[2])

    def _wait(self, eng, pid):
        sem, val, key = self._sem_of(pid)
        if self.known[eng].get(key, 0) >= val:
            return
        self.known[eng][key] = val
        self.nwaits += 1
        self.streams[eng].append(lambda e, sem=sem, val=val: e.wait_ge(sem, val))

    def _deps(self, eng, reads, writes):
        deps = []
        for r in reads:
            if r.w is not None:
                deps.append(r.w)
        for w in writes:
            if w.w is not None:
                deps.append(w.w)
            deps.extend(w.r.values())
        for pid in deps:
            if pid[0] == "e" and pid[1] == eng and eng == "pe":
                continue
            self._wait(eng, pid)

    def _commit(self, pid, reads, writes):
        key = pid[:2] if pid[0] == "e" else pid[:3]
        for r in reads:
            r.r[key] = pid
        for w in writes:
            w.w = pid
            w.r = {}

    def op(self, eng, meth, reads=(), writes=(), **kw):
        fn = lambda e, meth=meth, kw=kw: getattr(e, meth)(**kw)
        self._deps(eng, reads, writes)
        self.ecnt[eng] += 1
        idx = self.ecnt[eng]
        sem = self.esem[eng]
        self.streams[eng].append(lambda e, fn=fn, sem=sem: fn(e).then_inc(sem, 1))
        self.nops += 1
        self._commit(("e", eng, idx), reads, writes)

    def dma(self, q, reads=(), writes=(), meth="dma_start", **kw):
        fn = lambda e, meth=meth, kw=kw: getattr(e, meth)(**kw)
        i = self.dnext[q]
        self.dnext[q] = (i + 1) % self.NDMA
        prev = self.dcnt[q][i]
        if prev > 0:
            self._wait(q, ("d", q, i, prev))
        self._deps(q, reads, writes)
        self.dcnt[q][i] = prev + 16
        sem = self.dsem[q][i]
        self.streams[q].append(lambda e, fn=fn, sem=sem: fn(e).then_inc(sem, 16))
        self.nops += 1
        self._commit(("d", q, i, prev + 16), reads, writes)

    def barrier(self):
        for eng in self.streams:
            for k in self.COMPUTE:
                if k != eng and self.ecnt[k] > 0:
                    self._wait(eng, ("e", k, self.ecnt[k]))
            for q in self.dsem:
                for i in range(self.NDMA):
                    if self.dcnt[q][i] > 0:
                        self._wait(eng, ("d", q, i, self.dcnt[q][i]))

    def emit(self):
        nc = self.nc
        with nc.Block() as block:
            @block.tensor
            def _(e):
                for it in self.streams["pe"]:
                    it(e)

            @block.scalar
            def _(e):
                for it in self.streams["act"]:
                    it(e)

            @block.vector
            def _(e):
                for it in self.streams["dve"]:
                    it(e)

            @block.gpsimd
            def _(e):
                for it in self.streams["pool"]:
                    it(e)

            @block.sync
            def _(e):
                for it in self.streams["sp"]:
                    it(e)


class Arena:
    def __init__(self, nc):
        self.nc = nc
        self.off = (nc.sbuf_base + 31) // 32 * 32
        self.top = nc.sbuf_top
        self.n = 0

    def alloc(self, shape, dt=F32, name="t"):
        sz = {F32: 4, BF16: 2, I32: 4, U32: 4}[dt]
        nb = int(np.prod(shape[1:])) * sz
        off = self.off
        self.off += (nb + 31) // 32 * 32
        assert self.off <= self.top, "SBUF overflow at %s: %d > %d" % (name, self.off, self.top)
        self.n += 1
        return self.nc.alloc_sbuf_tensor_at("%s_%d" % (name, self.n), list(shape), dt, offset=off)

    def mark(self):
        return self.off

    def release(self, m):
        self.off = m


def build(stop_after=None):
    nc = bass.Bass("TRN2", target_bir_lowering=False)
    S = Sched(nc)
    A = Arena(nc)

    def din(name, shape, dt=F32):
        return nc.dram_tensor(name, list(shape), dt, kind="ExternalInput").ap()

    xw_d = din("xw", [WIN, D])
    flag_d = din("flag", [128, 1])
    mem_d = din("mem", [256, D])
    cst_d = din("cst", [128, NCONST])
    mtab_d = din("mtab", [128, 17 * 128])
    w_in_d = din("w_in", [D, IN_COLS])
    convw_d = din("convw", [128, 72])
    convb_d = din("convb", [128, 18])
    vec_d = {}
    for nm, n in [("mix_norm_g", 1024), ("dt_bias", 20), ("a_log", 20), ("d_skip", 20), ("ssd_norm_g", 1280),
                  ("attn_q_norm_g", 64), ("attn_k_norm_g", 64), ("xattn_norm_g", 1024), ("mem_norm_g", 1024),
                  ("mem_q_norm_g", 128), ("mem_k_norm_g", 128), ("ffn_norm_g", 1024)]:
        vec_d[nm] = din(nm, [1, n])
    w_out_d = din("w_out", [2048, D])
    mem_w_q_d = din("mem_w_q", [D, 512])
    mem_w_kv_d = din("mem_w_kv", [D, 1024])
    mem_w_o_d = din("mem_w_o", [512, D])
    peer_wq_d = din("peer_w_query", [D, 2048])
    keys1T_d = din("keys1T", [128, 128])
    keys2T_d = din("keys2T", [128, 128])
    peer_u_d = din("peer_u", [16384, D])
    peer_v_d = din("peer_v", [16384, D])
    out_d = nc.dram_tensor("out", [HALF, D], F32, kind="ExternalOutput").ap()
    yT_scr = nc.dram_tensor("yT_scr", [128, 10, HALF], BF16).ap()
    oT_scr = nc.dram_tensor("oT_scr", [128, 6, HALF], BF16).ap()
    uv_scr = nc.dram_tensor("uv_scr", [16384, 2048], BF16).ap()
    Ruv = Res()
    conv_jobs = [(tbl, h) for h in range(32) for tbl in (0, 1)]

    def issue_conv(n):
        for _ in range(n):
            if not conv_jobs or stop_after is not None:
                return
            tbl, h = conv_jobs.pop(0)
            src = peer_u_d if tbl == 0 else peer_v_d
            S.dma("pool", writes=[Ruv], out=uv_scr[h * 512:(h + 1) * 512, tbl * 1024:(tbl + 1) * 1024],
                  in_=src[h * 512:(h + 1) * 512, :])

    PS = [nc.alloc_psum_tensor("ps%d" % i, [128, 512], F32) for i in range(7)]
    RPS = [Res() for _ in range(7)]
    PB = nc.alloc_psum_tensor("psb", [128, 1024], BF16)
    RPB = Res()

    def T(shape, dt=F32, name="t"):
        return A.alloc(shape, dt, name), Res()

    def v3(ap, a):
        return ap.rearrange("p (a b) -> p a b", a=a)

    cst, Rcst = T([128, NCONST], F32, "cst")
    S.dma("sp", writes=[Rcst], out=cst[:], in_=cst_d)
    ident_f = cst[:, 0:128]
    tri_f = cst[:, 128:256]
    ones_f = cst[:, 256:384]
    d0_f = cst[:, 384:512]
    iota16 = cst[:, 512:528]
    ident_b, Ridb = T([128, 128], BF16, "identb")
    S.op("dve", "tensor_copy", reads=[Rcst], writes=[Ridb], out=ident_b[:], in_=ident_f)
    flag, Rflag = T([128, 1], F32, "flag")
    S.dma("sp", writes=[Rflag], out=flag[:], in_=flag_d)
    epsT, Reps = T([128, 1], F32, "eps")
    S.op("pool", "memset", writes=[Reps], ap=epsT[:], constant=EPS)

    def bc_load(n, name):
        t, R = T([128, n], F32, name)
        S.dma("sp", writes=[R], out=t[:], in_=vec_d[name].broadcast_to([128, n]))
        return t, R

    def rstd_from(ss_ap, Rss, n, H, out_ap, Rout, lnv_ap, Rln):
        S.op("act", "activation", reads=[Rss, Reps], writes=[Rln], out=lnv_ap, in_=ss_ap, func=AF.Ln,
             bias=epsT[:, 0:1], scale=1.0 / n)
        S.op("act", "activation", reads=[Rln], writes=[Rout], out=out_ap, in_=lnv_ap, func=AF.Exp, scale=-0.5)

    nrm_junk, Rnj = T([128, 1024], BF16, "nrmjunk")
    nrm_ss, Rnss = T([128, 4], F32, "nrmss")
    nrm_ln, Rnln = T([128, 4], F32, "nrmln")
    nrm_rs, Rnrs = T([128, 4], F32, "nrmrs")

    def rmsnorm(x_ap, Rx, g_bc, Rg, out_ap, Rout):
        S.op("act", "activation", reads=[Rx], writes=[Rnj, Rnss], out=nrm_junk[:], in_=x_ap, func=AF.Square,
             accum_out=nrm_ss[:, 0:1])
        rstd_from(nrm_ss[:, 0:1], Rnss, 1024, 1, nrm_rs[:, 0:1], Rnrs, nrm_ln[:, 0:1], Rnln)
        S.op("dve", "scalar_tensor_tensor", reads=[Rx, Rnrs, Rg], writes=[Rout], out=out_ap, in0=x_ap,
             scalar=nrm_rs[:, 0:1], in1=g_bc[:], op0=ALU.mult, op1=ALU.mult)

    def transposeN(src_bf, Rsrc, n, width, dst3, Rdst, eng="act"):
        for k0 in range(0, n, 8):
            m = min(8, n - k0)
            for k in range(m):
                S.op("pe", "transpose", reads=[Rsrc, Ridb], writes=[RPB], out=PB[0:width, k * 128:(k + 1) * 128],
                     in_=src_bf[:, (k0 + k) * width:(k0 + k + 1) * width], identity=ident_b[:])
            src = v3(PB[0:width, 0:m * 128], m)
            dst = dst3[:, k0:k0 + m, :]
            if eng == "act":
                S.op("act", "copy", reads=[RPB], writes=Rdst, out=dst, in_=src)
            else:
                S.op("dve", "tensor_copy", reads=[RPB], writes=Rdst, out=dst, in_=src)

    qk_sq, Rqsq = T([128, 512], F32, "qksq")
    qk_tmp, Rqtmp = T([128, 512], F32, "qktmp")

    def qknorm(ps_ap, Rps, H, dh, g_bc, Rg, out_bf, Rout):
        n = H * dh
        S.op("act", "activation", reads=[Rps], writes=[Rqsq], out=qk_sq[:, 0:n], in_=ps_ap, func=AF.Square)
        S.op("dve", "tensor_reduce", reads=[Rqsq], writes=[Rnss], out=nrm_ss[:, 0:H], in_=v3(qk_sq[:, 0:n], H),
             axis=AX.X, op=ALU.add)
        rstd_from(nrm_ss[:, 0:H], Rnss, dh, H, nrm_rs[:, 0:H], Rnrs, nrm_ln[:, 0:H], Rnln)
        S.op("dve", "tensor_tensor", reads=[Rps, Rnrs], writes=[Rqtmp], out=v3(qk_tmp[:, 0:n], H), in0=v3(ps_ap, H),
             in1=nrm_rs[:, 0:H].unsqueeze(2).broadcast_to([128, H, dh]), op=ALU.mult)
        S.op("pool", "tensor_tensor", reads=[Rqtmp, Rg], writes=[Rout], out=out_bf, in0=qk_tmp[:, 0:n], in1=g_bc,
             op=ALU.mult)

    m_base = A.mark()
    hT, _ = T([128, 8, WIN], BF16, "hT")
    RhT = [Res() for _ in range(32)]
    m_after_hT = A.mark()

    gmix, Rgmix = bc_load(1024, "mix_norm_g")
    xb = [T([128, 1024], F32, "xb") for _ in range(2)]
    hb = [T([128, 1024], BF16, "hb") for _ in range(2)]
    for blk in range(32):
        x_t, Rx = xb[blk % 2]
        h_t, Rh = hb[blk % 2]
        S.dma("sp", writes=[Rx], out=x_t[:], in_=xw_d[blk * 128:(blk + 1) * 128, :])
        rmsnorm(x_t[:], Rx, gmix, Rgmix, h_t[:], Rh)
        transposeN(h_t, Rh, 8, 128, hT[:, :, blk * 128:(blk + 1) * 128], [RhT[blk]], eng="act" if blk % 2 else "dve")
    S.barrier()
    A.release(m_after_hT)

    wxbc, Rwxbc = T([128, 8, 2304], BF16, "wxbc")
    wz, Rwz = T([128, 8, 1280], BF16, "wz")
    wdt, Rwdt = T([128, 8, 20], BF16, "wdt")
    for k in range(8):
        S.dma("pool", writes=[Rwxbc], out=wxbc[:, k, :], in_=w_in_d[k * 128:(k + 1) * 128, OFF_Z:OFF_XBC])
        S.dma("pool", writes=[Rwz], out=wz[:, k, :], in_=w_in_d[k * 128:(k + 1) * 128, 0:OFF_Z])
        S.dma("pool", writes=[Rwdt], out=wdt[:, k, :], in_=w_in_d[k * 128:(k + 1) * 128, OFF_XBC:OFF_DT])
    convw, Rcw = T([128, 18, 4], F32, "convw")
    convb, Rcb = T([128, 18], F32, "convb")
    S.dma("sp", writes=[Rcw], out=convw[:].rearrange("p a b -> p (a b)"), in_=convw_d)
    S.dma("sp", writes=[Rcb], out=convb[:], in_=convb_d)
    dtb, Rdtb = bc_load(20, "dt_bias")
    alog, Ralog = bc_load(20, "a_log")
    dsk, Rdsk = bc_load(20, "d_skip")
    gssd, Rgssd = bc_load(1280, "ssd_norm_g")
    a_bc, Ra = T([128, 20], F32, "a_bc")
    S.op("act", "activation", reads=[Ralog], writes=[Ra], out=a_bc[:], in_=alog[:], func=AF.Exp)
    S.op("dve", "tensor_scalar", reads=[Ra], writes=[Ra], out=a_bc[:], in0=a_bc[:], scalar1=-1.0, scalar2=None,
         op0=ALU.mult)
    halo, _ = T([128, 18, 3], F32, "halo")
    Rhalo = [Res() for _ in range(18)]
    S.op("pool", "memset", writes=Rhalo, ap=halo[:], constant=0.0)
    state, Rst = T([128, 1280], F32, "state")
    state_bf, Rstb = T([128, 1280], BF16, "statebf")
    S.op("pool", "memset", writes=[Rst], ap=state[:], constant=0.0)
    S.op("pool", "memset", writes=[Rstb], ap=state_bf[:], constant=0.0)
    U = [T([128, 259], F32, "U") for _ in range(2)]
    acc = [T([128, 256], F32, "acc") for _ in range(2)]
    xTf = [T([128, 256], F32, "xTf") for _ in range(2)]
    x_tm, Rxtm = T([128, 2, 1280], F32, "x_tm")
    BT, _ = T([128, 4, 256], BF16, "BT")
    RBTg = [Res() for _ in range(4)]
    CTt, _ = T([128, 4, 256], BF16, "CT")
    RCTg = [Res() for _ in range(4)]
    B_tm, RBtm = T([128, 2, 512], BF16, "B_tm")
    dt_tm, Rdt = T([128, 2, 20], F32, "dt_tm")
    adt, Radt = T([128, 2, 20], F32, "adt")
    ncs, Rncs = T([128, 20], F32, "ncs")
    tmp20, Rt20 = T([128, 20], F32, "tmp20")
    dec, Rdec = T([128, 20], F32, "dec")
    cd, Rcd = T([128, 20], F32, "cd")
    w2, Rw2 = T([128, 20], F32, "w2")
    ecs, Recs = T([128, 20], F32, "ecs")
    xdd, Rxdd = T([128, 1280], BF16, "xdd")
    xdt, Rxdt = T([128, 1280], BF16, "xdt")
    CBm, RCBm = T([128, 4, 128], F32, "CBm")
    adtb, Radtb = T([128, 4, 128], F32, "adtb")
    Lt = [T([128, 4, 128], F32, "Lt") for _ in range(2)]
    Wm, RWm = T([128, 20, 128], BF16, "Wm")
    yg, Ryg = T([128, 320], F32, "yg")
    ytmp, Rytmp = T([128, 320], F32, "ytmp")
    sz, Rsz = T([128, 320], F32, "sz")
    ss1, Rss1 = T([128, 1], F32, "ss1")
    ln1, Rln1 = T([128, 1], F32, "ln1")
    rs1, Rrs1 = T([128, 1], F32, "rs1")
    yn, Ryn = T([128, 1280], BF16, "yn")
    yTc = [T([128, 10, 128], BF16, "yTc") for _ in range(2)]

    for G in range(16):
        own = G >= 8
        RhG = RhT[G * 2:(G + 1) * 2]
        tokG = slice(G * 256, (G + 1) * 256)
        def cc_main(cc):
            b = cc % 2
            for k in range(8):
                S.op("pe", "matmul", reads=[Rwxbc] + RhG, writes=[RPS[b]], out=PS[b][:, 0:256],
                     lhsT=wxbc[:, k, cc * 128:(cc + 1) * 128], rhs=hT[:, k, tokG], start=(k == 0), stop=(k == 7))
            u_t, Ru = U[b]
            a_t, Rac = acc[b]
            S.op("act", "copy", reads=[RPS[b]], writes=[Ru], out=u_t[:, 3:259], in_=PS[b][:, 0:256])
            S.op("pool", "tensor_copy", reads=[Rhalo[cc]], writes=[Ru], out=u_t[:, 0:3], in_=halo[:, cc, :])
            S.op("dve", "tensor_scalar", reads=[Ru, Rcw], writes=[Rac], out=a_t[:], in0=u_t[:, 3:259],
                 scalar1=convw[:, cc, 3:4], scalar2=None, op0=ALU.mult)
            for j in (2, 1, 0):
                S.op("dve", "scalar_tensor_tensor", reads=[Ru, Rcw, Rac], writes=[Rac], out=a_t[:],
                     in0=u_t[:, j:j + 256], scalar=convw[:, cc, j:j + 1], in1=a_t[:], op0=ALU.mult, op1=ALU.add)
            S.op("pool", "tensor_copy", reads=[Ru], writes=[Rhalo[cc]], out=halo[:, cc, :], in_=u_t[:, 256:259])
            if cc < 10:
                xt, Rxt = xTf[b]
                S.op("act", "activation", reads=[Rac, Rcb], writes=[Rxt], out=xt[:], in_=a_t[:], func=AF.Silu,
                     bias=convb[:, cc:cc + 1])
            elif cc < 14:
                g = cc - 10
                S.op("act", "activation", reads=[Rac, Rcb], writes=[RBTg[g]], out=BT[:, g, :], in_=a_t[:], func=AF.Silu,
                     bias=convb[:, cc:cc + 1])
            else:
                g = cc - 14
                S.op("act", "activation", reads=[Rac, Rcb], writes=[RCTg[g]], out=CTt[:, g, :], in_=a_t[:], func=AF.Silu,
                     bias=convb[:, cc:cc + 1])

        def cc_tr(cc):
            b = cc % 2
            if cc < 10:
                xt, Rxt = xTf[b]
                pb = 2 + b
                for tb in range(2):
                    S.op("pe", "transpose", reads=[Rxt, Rcst], writes=[RPS[pb]], out=PS[pb][:, tb * 128:(tb + 1) * 128],
                         in_=xt[:, tb * 128:(tb + 1) * 128], identity=ident_f)
                S.op("dve", "tensor_copy", reads=[RPS[pb]], writes=[Rxtm], out=x_tm[:, :, cc * 128:(cc + 1) * 128],
                     in_=v3(PS[pb][:, 0:256], 2))
            elif cc < 14:
                g = cc - 10
                for tb in range(2):
                    S.op("pe", "transpose", reads=[RBTg[g], Ridb], writes=[RPB], out=PB[:, tb * 128:(tb + 1) * 128],
                         in_=BT[:, g, tb * 128:(tb + 1) * 128], identity=ident_b[:])
                S.op("dve", "tensor_copy", reads=[RPB], writes=[RBtm], out=B_tm[:, :, g * 128:(g + 1) * 128],
                     in_=v3(PB[:, 0:256], 2))

        for cc in range(19):
            if cc < 18:
                cc_main(cc)
            if cc >= 1:
                cc_tr(cc - 1)
        issue_conv(4)
        for tb in range(2):
            blk = G * 2 + tb
            for k in range(8):
                S.op("pe", "matmul", reads=[Rwdt, RhT[blk]], writes=[RPS[4]], out=PS[4][:, tb * 20:(tb + 1) * 20],
                     lhsT=hT[:, k, blk * 128:(blk + 1) * 128], rhs=wdt[:, k, :], start=(k == 0), stop=(k == 7))
        S.op("dve", "tensor_tensor", reads=[RPS[4], Rdtb], writes=[Rdt], out=dt_tm[:], in0=v3(PS[4][:, 0:40], 2),
             in1=dtb[:].unsqueeze(1).broadcast_to([128, 2, 20]), op=ALU.add)
        S.op("act", "activation", reads=[Rdt], writes=[Rdt], out=dt_tm[:], in_=dt_tm[:], func=AF.Exp)
        S.op("act", "activation", reads=[Rdt], writes=[Rdt], out=dt_tm[:], in_=dt_tm[:], func=AF.Ln, bias=1.0)
        S.op("dve", "tensor_tensor", reads=[Rdt, Ra], writes=[Radt], out=adt[:], in0=dt_tm[:],
             in1=a_bc[:].unsqueeze(1).broadcast_to([128, 2, 20]), op=ALU.mult)
        for tb in range(2):
            c = G * 2 + tb
            tk = slice(tb * 128, (tb + 1) * 128)
            S.op("pe", "matmul", reads=[Rcst, Radt], writes=[RPS[5]], out=PS[5][:, 0:20], lhsT=tri_f, rhs=adt[:, tb, :],
                 start=True, stop=True)
            S.op("pe", "matmul", reads=[Rcst, Radt], writes=[RPS[5]], out=PS[5][:, 32:52], lhsT=ones_f, rhs=adt[:, tb, :],
                 start=True, stop=True)
            S.op("dve", "tensor_scalar", reads=[RPS[5]], writes=[Rncs], out=ncs[:], in0=PS[5][:, 0:20], scalar1=-1.0,
                 scalar2=None, op0=ALU.mult)
            S.op("dve", "tensor_tensor", reads=[RPS[5], Rncs], writes=[Rt20], out=tmp20[:], in0=PS[5][:, 32:52],
                 in1=ncs[:], op=ALU.add)
            S.op("act", "activation", reads=[Rt20], writes=[Rdec], out=dec[:], in_=tmp20[:], func=AF.Exp)
            S.op("act", "activation", reads=[RPS[5]], writes=[Rcd], out=cd[:], in_=PS[5][:, 32:52], func=AF.Exp)
            S.op("dve", "tensor_tensor", reads=[Rdt, Rdec], writes=[Rw2], out=w2[:], in0=dt_tm[:, tb, :], in1=dec[:],
                 op=ALU.mult)
            S.op("dve", "tensor_tensor", reads=[Rxtm, Rw2], writes=[Rxdd], out=v3(xdd[:], 20),
                 in0=v3(x_tm[:, tb, :], 20), in1=w2[:].unsqueeze(2).broadcast_to([128, 20, 64]), op=ALU.mult)
            if own:
                blk = c
                S.op("act", "activation", reads=[Rncs], writes=[Recs], out=ecs[:], in_=ncs[:], func=AF.Exp, scale=-1.0)
                S.op("pool", "tensor_tensor", reads=[Rxtm, Rdt], writes=[Rxdt], out=v3(xdt[:], 20),
                     in0=v3(x_tm[:, tb, :], 20), in1=dt_tm[:, tb, :].unsqueeze(2).broadcast_to([128, 20, 64]),
                     op=ALU.mult)
                for g in range(4):
                    S.op("pe", "matmul", reads=[RBTg[g], RCTg[g]], writes=[RPS[2]], out=PS[2][:, g * 128:(g + 1) * 128],
                         lhsT=BT[:, g, tk], rhs=CTt[:, g, tk], start=True, stop=True)
                S.op("dve", "tensor_tensor", reads=[RPS[2], Rcst], writes=[RCBm], out=CBm[:], in0=v3(PS[2][:, :], 4),
                     in1=tri_f.unsqueeze(1).broadcast_to([128, 4, 128]), op=ALU.mult)
                for hq in range(5):
                    pb = 3 + hq % 2
                    S.op("pool", "tensor_copy", reads=[Radt], writes=[Radtb], out=adtb[:],
                         in_=adt[:, tb, hq * 4:hq * 4 + 4].unsqueeze(2).broadcast_to([128, 4, 128]))
                    for i in range(4):
                        h = hq * 4 + i
                        S.op("pe", "matmul", reads=[Radtb, Rcst], writes=[RPS[pb]], out=PS[pb][:, i * 128:(i + 1) * 128],
                             lhsT=adtb[:, i, :], rhs=tri_f, start=True, stop=True)
                    lt, Rlt = Lt[hq % 2]
                    S.op("dve", "tensor_tensor", reads=[RPS[pb], Rncs], writes=[Rlt], out=lt[:], in0=v3(PS[pb][:, :], 4),
                         in1=ncs[:, hq * 4:hq * 4 + 4].unsqueeze(2).broadcast_to([128, 4, 128]), op=ALU.add)
                    S.op("act", "activation", reads=[Rlt], writes=[Rlt], out=lt[:], in_=lt[:], func=AF.Exp)
                    for i in range(4):
                        h = hq * 4 + i
                        if i % 2:
                            S.op("pool", "tensor_scalar", reads=[Rlt], writes=[Rlt], out=lt[:, i, :], in0=lt[:, i, :],
                                 scalar1=1.0, scalar2=None, op0=ALU.min)
                            S.op("pool", "tensor_tensor", reads=[Rlt, RCBm], writes=[RWm], out=Wm[:, h, :],
                                 in0=lt[:, i, :], in1=CBm[:, h // 5, :], op=ALU.mult)
                        else:
                            S.op("dve", "scalar_tensor_tensor", reads=[Rlt, RCBm], writes=[RWm],
                                 out=Wm[:, h, :], in0=lt[:, i, :], scalar=1.0, in1=CBm[:, h // 5, :], op0=ALU.min,
                                 op1=ALU.mult)
                for g in range(4):
                    gs = slice(g * 320, (g + 1) * 320)
                    S.op("pe", "matmul", reads=[RCTg[g], Rstb], writes=[RPS[0]], out=PS[0][:, 0:320], lhsT=CTt[:, g, tk],
                         rhs=state_bf[:, gs], start=True, stop=True)
                    for hh in range(5):
                        h = g * 5 + hh
                        S.op("pe", "matmul", reads=[RWm, Rxdt], writes=[RPS[1]], out=PS[1][:, hh * 64:(hh + 1) * 64],
                             lhsT=Wm[:, h, :], rhs=xdt[:, h * 64:(h + 1) * 64], start=True, stop=True)
                    for k in range(8):
                        S.op("pe", "matmul", reads=[Rwz, RhT[blk]], writes=[RPS[6]], out=PS[6][:, 0:320],
                             lhsT=hT[:, k, blk * 128:(blk + 1) * 128], rhs=wz[:, k, gs], start=(k == 0), stop=(k == 7))
                    S.op("dve", "tensor_tensor", reads=[RPS[0], Recs], writes=[Ryg], out=v3(yg[:], 5),
                         in0=v3(PS[0][:, 0:320], 5), in1=ecs[:, g * 5:g * 5 + 5].unsqueeze(2).broadcast_to([128, 5, 64]),
                         op=ALU.mult)
                    S.op("dve", "tensor_tensor", reads=[RPS[1], Ryg], writes=[Ryg], out=yg[:], in0=yg[:],
                         in1=PS[1][:, 0:320], op=ALU.add)
                    S.op("pool", "tensor_tensor", reads=[Rxtm, Rdsk], writes=[Rytmp], out=v3(ytmp[:], 5),
                         in0=v3(x_tm[:, tb, gs], 5), in1=dsk[:, g * 5:g * 5 + 5].unsqueeze(2).broadcast_to([128, 5, 64]),
                         op=ALU.mult)
                    S.op("pool", "tensor_tensor", reads=[Rytmp, Ryg], writes=[Ryg], out=yg[:], in0=yg[:], in1=ytmp[:],
                         op=ALU.add)
                    S.op("act", "activation", reads=[RPS[6]], writes=[Rsz], out=sz[:], in_=PS[6][:, 0:320], func=AF.Silu)
                    S.op("dve", "tensor_tensor", reads=[Ryg, Rsz], writes=[Ryg], out=yg[:], in0=yg[:], in1=sz[:],
                         op=ALU.mult)
                    S.op("act", "activation", reads=[Ryg], writes=[Rytmp, Rss1], out=ytmp[:], in_=yg[:], func=AF.Square,
                         accum_out=ss1[:, 0:1])
                    rstd_from(ss1[:, 0:1], Rss1, 320, 1, rs1[:, 0:1], Rrs1, ln1[:, 0:1], Rln1)
                    S.op("dve", "scalar_tensor_tensor", reads=[Ryg, Rrs1, Rgssd], writes=[Ryn], out=yn[:, gs], in0=yg[:],
                         scalar=rs1[:, 0:1], in1=gssd[:, gs], op0=ALU.mult, op1=ALU.mult)
                yt, Ryt = yTc[tb % 2]
                transposeN(yn, Ryn, 10, 128, yt[:], [Ryt], eng="act")
                ob = c - 16
                S.dma("sp", reads=[Ryt], writes=[Res()], out=yT_scr[:, :, ob * 128:(ob + 1) * 128], in_=yt[:])
            for g in range(4):
                gs = slice(g * 320, (g + 1) * 320)
                S.op("pe", "matmul", reads=[RBtm, Rxdd], writes=[RPS[6]], out=PS[6][:, 0:320],
                     lhsT=B_tm[:, tb, g * 128:(g + 1) * 128], rhs=xdd[:, gs], start=True, stop=True)
                S.op("dve", "tensor_tensor", reads=[Rst, Rcd], writes=[Rst], out=v3(state[:, gs], 5),
                     in0=v3(state[:, gs], 5), in1=cd[:, g * 5:g * 5 + 5].unsqueeze(2).broadcast_to([128, 5, 64]),
                     op=ALU.mult)
                S.op("dve", "tensor_tensor", reads=[Rst, RPS[6]], writes=[Rst], out=state[:, gs], in0=state[:, gs],
                     in1=PS[6][:, 0:320], op=ALU.add)
            if c == 15:
                S.op("dve", "tensor_scalar", reads=[Rst, Rflag], writes=[Rst], out=state[:], in0=state[:],
                     scalar1=flag[:, 0:1], scalar2=None, op0=ALU.mult)
            if c >= 15:
                S.op("act", "copy", reads=[Rst], writes=[Rstb], out=state_bf[:], in_=state[:])
    S.barrier()
    A.release(m_after_hT)

    mtab, Rmtab = T([128, 17, 128], BF16, "mtab")
    S.dma("pool", writes=[Rmtab], out=mtab[:].rearrange("p a b -> p (a b)"), in_=mtab_d)
    gq64, Rgq64 = bc_load(64, "attn_q_norm_g")
    gk64, Rgk64 = bc_load(64, "attn_k_norm_g")
    gq, Rgq = T([128, 256], F32, "gq")
    gk, Rgk = T([128, 256], F32, "gk")
    S.op("dve", "tensor_scalar", reads=[Rgq64], writes=[Rgq], out=v3(gq[:], 4),
         in0=gq64[:].unsqueeze(1).broadcast_to([128, 4, 64]), scalar1=0.125, scalar2=None, op0=ALU.mult)
    S.op("dve", "tensor_copy", reads=[Rgk64], writes=[Rgk], out=v3(gk[:], 4),
         in_=gk64[:].unsqueeze(1).broadcast_to([128, 4, 64]))
    wq, Rwq = T([128, 8, 256], BF16, "wq")
    wk, Rwk = T([128, 8, 256], BF16, "wk")
    wv, Rwv = T([128, 8, 256], BF16, "wv")
    KT, _ = T([64, 4, WIN], BF16, "KT")
    RKT = [Res() for _ in range(32)]
    QT, _ = T([64, 4, HALF], BF16, "QT")
    RQT = [Res() for _ in range(16)]
    Vaug, _ = T([128, 32, 4, 65], BF16, "Vaug")
    RV = [Res() for _ in range(32)]
    kn = [T([128, 256], BF16, "kn") for _ in range(2)]
    qn_ = [T([128, 256], BF16, "qn_") for _ in range(2)]
    Aexp, RAexp = T([128, 128], F32, "Aexp")
    Erev, REr = T([128, 17, 128], F32, "Erev")
    Pf = [T([128, 512], F32, "Pf") for _ in range(3)]
    Pbf = [T([128, 512], BF16, "Pbf") for _ in range(3)]
    rd, Rrd = T([128, 2], F32, "rd")
    o_all, _ = T([128, 16, 256], BF16, "o_all")
    Roall = [Res() for _ in range(16)]
    oTc = [T([128, 2, 128], BF16, "oTc") for _ in range(2)]
    for r in range(3):
        for k in range(8):
            rows = slice(k * 128, (k + 1) * 128)
            S.dma("pool", writes=[Rwq], out=wq[:, k, :], in_=w_in_d[rows, OFF_DT + r * 256:OFF_DT + (r + 1) * 256])
            S.dma("pool", writes=[Rwk], out=wk[:, k, :], in_=w_in_d[rows, OFF_Q + r * 256:OFF_Q + (r + 1) * 256])
            S.dma("pool", writes=[Rwv], out=wv[:, k, :], in_=w_in_d[rows, OFF_K + r * 256:OFF_K + (r + 1) * 256])

        def proj_mm(blk):
            bs = slice(blk * 128, (blk + 1) * 128)
            pa = blk % 2
            for k in range(8):
                S.op("pe", "matmul", reads=[Rwk, RhT[blk]], writes=[RPS[pa]], out=PS[pa][:, 0:256], lhsT=hT[:, k, bs],
                     rhs=wk[:, k, :], start=(k == 0), stop=(k == 7))
            kt, Rkn = kn[blk % 2]
            qknorm(PS[pa][:, 0:256], RPS[pa], 4, 64, gk[:], Rgk, kt[:], Rkn)
            pv = 2 + blk % 2
            for k in range(8):
                S.op("pe", "matmul", reads=[Rwv, RhT[blk]], writes=[RPS[pv]], out=PS[pv][:, 0:256], lhsT=hT[:, k, bs],
                     rhs=wv[:, k, :], start=(k == 0), stop=(k == 7))
            if blk >= 16:
                S.op("act", "copy", reads=[RPS[pv]], writes=[RV[blk]], out=Vaug[:, blk, :, 0:64],
                     in_=v3(PS[pv][:, 0:256], 4))
                S.op("pool", "memset", writes=[RV[blk]], ap=Vaug[:, blk, :, 64:65], constant=1.0)
            else:
                S.op("dve", "tensor_scalar", reads=[RPS[pv], Rflag], writes=[RV[blk]], out=Vaug[:, blk, :, 0:64],
                     in0=v3(PS[pv][:, 0:256], 4), scalar1=flag[:, 0:1], scalar2=None, op0=ALU.mult)
                S.op("pool", "tensor_copy", reads=[Rflag], writes=[RV[blk]], out=Vaug[:, blk, :, 64:65],
                     in_=flag[:, 0:1].unsqueeze(1).broadcast_to([128, 4, 1]))
            if blk >= 16:
                pq = 4 + blk % 2
                for k in range(8):
                    S.op("pe", "matmul", reads=[Rwq, RhT[blk]], writes=[RPS[pq]], out=PS[pq][:, 0:256], lhsT=hT[:, k, bs],
                         rhs=wq[:, k, :], start=(k == 0), stop=(k == 7))
                qt, Rqn = qn_[blk % 2]
                qknorm(PS[pq][:, 0:256], RPS[pq], 4, 64, gq[:], Rgq, qt[:], Rqn)

        def proj_tr(blk):
            bs = slice(blk * 128, (blk + 1) * 128)
            kt, Rkn = kn[blk % 2]
            transposeN(kt, Rkn, 4, 64, KT[:, :, bs], [RKT[blk]], eng="act")
            if blk >= 16:
                qt, Rqn = qn_[blk % 2]
                transposeN(qt, Rqn, 4, 64, QT[:, :, (blk - 16) * 128:(blk - 15) * 128], [RQT[blk - 16]], eng="act")

        for blk in range(33):
            if blk < 32:
                proj_mm(blk)
            if blk >= 1:
                proj_tr(blk - 1)

        seq = []
        for hh in range(4):
            for qb in range(16, 32):
                kbs = list(range(qb - 16, qb + 1))
                for gi in range(5):
                    seq.append((hh, qb, gi, kbs[gi * 4:gi * 4 + 4]))

        def emit_S(i):
            hh, qb, gi, grp = seq[i]
            qi = qb - 16
            sbk = i % 2
            for ii, kb in enumerate(grp):
                S.op("pe", "matmul", reads=[RKT[kb], RQT[qi]], writes=[RPS[sbk]],
                     out=PS[sbk][:, ii * 128:(ii + 1) * 128], lhsT=KT[:, hh, kb * 128:(kb + 1) * 128],
                     rhs=QT[:, hh, qi * 128:(qi + 1) * 128], start=True, stop=True)

        def emit_rest(i):
            hh, qb, gi, grp = seq[i]
            qi = qb - 16
            n = len(grp)
            sbk = i % 2
            unit = i // 5
            ob = 4 + unit % 2
            pf, Rpf = Pf[i % 3]
            pbf, Rpbf = Pbf[i % 3]
            if qb == 16 and gi == 0:
                h = r * 4 + hh
                slope = 2.0 ** (-8.0 * (h + 1) / 12.0)
                S.op("act", "activation", reads=[Rcst], writes=[RAexp], out=Aexp[:], in_=d0_f, func=AF.Exp, scale=-slope)
                for j in range(17):
                    cpow = math.exp(-slope * 128.0 * (16 - j))
                    S.op("dve", "scalar_tensor_tensor", reads=[RAexp, Rmtab], writes=[REr],
                         out=Erev[:, j, :], in0=Aexp[:], scalar=float(cpow), in1=mtab[:, j, :], op0=ALU.mult,
                         op1=ALU.mult)
            S.op("act", "activation", reads=[RPS[sbk]], writes=[Rpf], out=pf[:, 0:n * 128],
                 in_=PS[sbk][:, 0:n * 128], func=AF.Exp)
            S.op("pool" if i % 3 == 2 else "dve", "tensor_tensor", reads=[Rpf, REr], writes=[Rpbf], out=pbf[:, 0:n * 128],
                 in0=pf[:, 0:n * 128], in1=Erev[:, gi * 4:gi * 4 + n, :].rearrange("p a b -> p (a b)"),
                 op=ALU.mult)
            for ii, kb in enumerate(grp):
                S.op("pe", "matmul", reads=[Rpbf, RV[kb]], writes=[RPS[ob]], out=PS[ob][:, 0:65],
                     lhsT=pbf[:, ii * 128:(ii + 1) * 128], rhs=Vaug[:, kb, hh, :], start=(gi == 0 and ii == 0),
                     stop=(gi == 4))
            if gi == 4:
                S.op("dve", "reciprocal", reads=[RPS[ob]], writes=[Rrd], out=rd[:, 0:1], in_=PS[ob][:, 64:65])
                S.op("dve", "tensor_scalar", reads=[RPS[ob], Rrd], writes=[Roall[qi]],
                     out=o_all[:, qi, hh * 64:(hh + 1) * 64], in0=PS[ob][:, 0:64], scalar1=rd[:, 0:1], scalar2=None,
                     op0=ALU.mult)

        emit_S(0)
        for i in range(len(seq)):
            if i + 1 < len(seq):
                emit_S(i + 1)
            emit_rest(i)
        for qi in range(16):
            for i in range(2):
                S.op("pe", "transpose", reads=[Roall[qi], Ridb], writes=[RPB], out=PB[:, i * 128:(i + 1) * 128],
                     in_=o_all[:, qi, i * 128:(i + 1) * 128], identity=ident_b[:])
            ot, Rot = oTc[qi % 2]
            S.op("act", "copy", reads=[RPB], writes=[Rot], out=ot[:], in_=v3(PB[:, 0:256], 2))
            S.dma("sp", reads=[Rot], writes=[Res()], out=oT_scr[:, 2 * r:2 * r + 2, qi * 128:(qi + 1) * 128], in_=ot[:])
    S.barrier()
    A.release(m_base)

    x1, _ = T([128, 16, 1024], F32, "x1")
    Rx1 = [Res() for _ in range(16)]
    m_after_x1 = A.mark()
    yT_sb, RyT = T([128, 10, HALF], BF16, "yT_sb")
    for kc in range(10):
        S.dma("sp", writes=[RyT], out=yT_sb[:, kc, :], in_=yT_scr[:, kc, :])
    oT, RoTs = T([128, 6, HALF], BF16, "oT")
    for kc in range(6):
        S.dma("sp", writes=[RoTs], out=oT[:, kc, :], in_=oT_scr[:, kc, :])
    wout, Rwout = T([128, 16, 1024], BF16, "wout")
    for kc in range(16):
        S.dma("pool", writes=[Rwout], out=wout[:, kc, :], in_=w_out_d[kc * 128:(kc + 1) * 128, :])
    xo = [T([128, 1024], F32, "xo") for _ in range(2)]
    for tb in range(16):
        ts_ = slice(tb * 128, (tb + 1) * 128)
        xo_t, Rxo = xo[tb % 2]
        S.dma("sp", writes=[Rxo], out=xo_t[:], in_=xw_d[HALF + tb * 128:HALF + (tb + 1) * 128, :])
        for half in range(2):
            pb = 2 * (tb % 2) + half
            cs_ = slice(half * 512, (half + 1) * 512)
            for kc in range(16):
                lhsT = yT_sb[:, kc, ts_] if kc < 10 else oT[:, kc - 10, ts_]
                S.op("pe", "matmul", reads=[RyT, RoTs, Rwout], writes=[RPS[pb]], out=PS[pb][:, :], lhsT=lhsT,
                     rhs=wout[:, kc, cs_], start=(kc == 0), stop=(kc == 15))
            S.op("dve", "tensor_tensor", reads=[RPS[pb], Rxo], writes=[Rx1[tb]], out=x1[:, tb, cs_], in0=PS[pb][:, :],
                 in1=xo_t[:, cs_], op=ALU.add)
    S.barrier()
    A.release(m_after_x1)

    def store_x1_and_finish():
        fin = []
        for tb in range(16):
            R = Res()
            S.dma("sp", reads=[Rx1[tb]], writes=[R], out=out_d[tb * 128:(tb + 1) * 128, :], in_=x1[:, tb, :])
            fin.append(R)
        for R in fin:
            S._wait("sp", R.w)
        S.emit()
        return nc

    if stop_after == "C":
        return store_x1_and_finish()

    wqm, Rwqm = T([128, 8, 512], BF16, "wqm")
    wkv, Rwkv = T([128, 8, 1024], BF16, "wkv")
    wo, Rwo = T([128, 4, 1024], BF16, "wo")
    for k in range(8):
        S.dma("pool", writes=[Rwqm], out=wqm[:, k, :], in_=mem_w_q_d[k * 128:(k + 1) * 128, :])
        S.dma("pool", writes=[Rwkv], out=wkv[:, k, :], in_=mem_w_kv_d[k * 128:(k + 1) * 128, :])
    for k in range(4):
        S.dma("pool", writes=[Rwo], out=wo[:, k, :], in_=mem_w_o_d[k * 128:(k + 1) * 128, :])
    gx, Rgx = bc_load(1024, "xattn_norm_g")
    gm, Rgm = bc_load(1024, "mem_norm_g")
    gqm128, Rgqm128 = bc_load(128, "mem_q_norm_g")
    gkm128, Rgkm128 = bc_load(128, "mem_k_norm_g")
    gqm, Rgqm = T([128, 512], F32, "gqm")
    gkm, Rgkm = T([128, 512], F32, "gkm")
    S.op("dve", "tensor_scalar", reads=[Rgqm128], writes=[Rgqm], out=v3(gqm[:], 4),
         in0=gqm128[:].unsqueeze(1).broadcast_to([128, 4, 128]), scalar1=128.0 ** -0.5, scalar2=None, op0=ALU.mult)
    S.op("dve", "tensor_copy", reads=[Rgkm128], writes=[Rgkm], out=v3(gkm[:], 4),
         in_=gkm128[:].unsqueeze(1).broadcast_to([128, 4, 128]))
    KmT, RKmT = T([128, 4, 256], BF16, "KmT")
    Vm, RVm = T([128, 2, 4, 129], BF16, "Vm")
    S.op("pool", "memset", writes=[RVm], ap=Vm[:].rearrange("p a b c -> p (a b c)"), constant=1.0)
    memb, Rmemb = T([128, 1024], F32, "memb")
    mh, Rmh = T([128, 1024], BF16, "mh")
    mT, RmT = T([128, 8, 128], BF16, "mT")
    knm, Rknm = T([128, 512], BF16, "knm")
    for mb in range(2):
        ms = slice(mb * 128, (mb + 1) * 128)
        S.dma("sp", writes=[Rmemb], out=memb[:], in_=mem_d[ms, :])
        rmsnorm(memb[:], Rmemb, gm, Rgm, mh[:], Rmh)
        transposeN(mh, Rmh, 8, 128, mT[:], [RmT], eng="act")
        for half in range(2):
            for k in range(8):
                S.op("pe", "matmul", reads=[RmT, Rwkv], writes=[RPS[half]], out=PS[half][:, :], lhsT=mT[:, k, :],
                     rhs=wkv[:, k, half * 512:(half + 1) * 512], start=(k == 0), stop=(k == 7))
        qknorm(PS[0][:, :], RPS[0], 4, 128, gkm[:], Rgkm, knm[:], Rknm)
        transposeN(knm, Rknm, 4, 128, KmT[:, :, ms], [RKmT], eng="act")
        S.op("act", "copy", reads=[RPS[1]], writes=[RVm], out=Vm[:, mb, :, 0:128], in_=v3(PS[1][:, :], 4))
    h2, Rh2 = T([128, 1024], BF16, "h2")
    h2T, Rh2T = T([128, 8, 128], BF16, "h2T")
    qn, Rqn2 = T([128, 512], BF16, "qn")
    qT, RqT = T([128, 4, 128], BF16, "qT")
    Pm = [T([128, 256], BF16, "Pm") for _ in range(2)]
    o2, Ro2 = T([128, 512], BF16, "o2")
    o2T, Ro2T = T([128, 4, 128], BF16, "o2T")
    for tb in range(16):
        rmsnorm(x1[:, tb, :], Rx1[tb], gx, Rgx, h2[:], Rh2)
        transposeN(h2, Rh2, 8, 128, h2T[:], [Rh2T], eng="act")
        for k in range(8):
            S.op("pe", "matmul", reads=[Rh2T, Rwqm], writes=[RPS[2]], out=PS[2][:, :], lhsT=h2T[:, k, :], rhs=wqm[:, k, :],
                 start=(k == 0), stop=(k == 7))
        qknorm(PS[2][:, :], RPS[2], 4, 128, gqm[:], Rgqm, qn[:], Rqn2)
        transposeN(qn, Rqn2, 4, 128, qT[:], [RqT], eng="act")
        for hh in range(4):
            sb_ = 3 + hh % 2
            ob = 5 + hh % 2
            pm, Rpm = Pm[hh % 2]
            for mb in range(2):
                S.op("pe", "matmul", reads=[RKmT, RqT], writes=[RPS[sb_]], out=PS[sb_][:, mb * 128:(mb + 1) * 128],
                     lhsT=KmT[:, hh, mb * 128:(mb + 1) * 128], rhs=qT[:, hh, :], start=True, stop=True)
            S.op("act", "activation", reads=[RPS[sb_]], writes=[Rpm], out=pm[:], in_=PS[sb_][:, 0:256], func=AF.Exp)
            for mb in range(2):
                S.op("pe", "matmul", reads=[Rpm, RVm], writes=[RPS[ob]], out=PS[ob][:, 0:129],
                     lhsT=pm[:, mb * 128:(mb + 1) * 128], rhs=Vm[:, mb, hh, :], start=(mb == 0), stop=(mb == 1))
            S.op("dve", "reciprocal", reads=[RPS[ob]], writes=[Rrd], out=rd[:, 0:1], in_=PS[ob][:, 128:129])
            S.op("dve", "tensor_scalar", reads=[RPS[ob], Rrd], writes=[Ro2], out=o2[:, hh * 128:(hh + 1) * 128],
                 in0=PS[ob][:, 0:128], scalar1=rd[:, 0:1], scalar2=None, op0=ALU.mult)
        transposeN(o2, Ro2, 4, 128, o2T[:], [Ro2T], eng="act")
        for half in range(2):
            cs_ = slice(half * 512, (half + 1) * 512)
            for kc in range(4):
                S.op("pe", "matmul", reads=[Ro2T, Rwo], writes=[RPS[half]], out=PS[half][:, :], lhsT=o2T[:, kc, :],
                     rhs=wo[:, kc, cs_], start=(kc == 0), stop=(kc == 3))
            S.op("dve", "tensor_tensor", reads=[RPS[half], Rx1[tb]], writes=[Rx1[tb]], out=x1[:, tb, cs_],
                 in0=PS[half][:, :], in1=x1[:, tb, cs_], op=ALU.add)
    if stop_after == "D":
        S.barrier()
        return store_x1_and_finish()
    Rx2d = [Res() for _ in range(16)]
    for tb in range(16):
        S.dma("sp", reads=[Rx1[tb]], writes=[Rx2d[tb]], out=out_d[tb * 128:(tb + 1) * 128, :], in_=x1[:, tb, :])
    S.barrier()
    A.release(m_base)

    wpq, Rwpq = T([128, 8, 2048], BF16, "wpq")
    for k in range(8):
        S.dma("pool", writes=[Rwpq], out=wpq[:, k, :], in_=peer_wq_d[k * 128:(k + 1) * 128, :])
    keysT, RkeysT = T([128, 2, 128], F32, "keysT")
    S.dma("sp", writes=[RkeysT], out=keysT[:, 0, :], in_=keys1T_d)
    S.dma("sp", writes=[RkeysT], out=keysT[:, 1, :], in_=keys2T_d)
    gf, Rgf = bc_load(1024, "ffn_norm_g")
    xrp = [T([128, 1024], F32, "xr") for _ in range(2)]
    h3p = [T([128, 1024], F32, "h3") for _ in range(2)]
    h3bp = [T([128, 1024], BF16, "h3b") for _ in range(2)]
    h3T, Rh3T = T([128, 8, 128], BF16, "h3T")
    off_q = A.mark()
    qrT, RqrT = T([128, 16, 128], F32, "qrT")
    eq = nc.alloc_sbuf_tensor_at("eq_alias", [128, 8, 16, 16], F32, offset=off_q)
    Req = RqrT
    off_s = A.mark()
    sc, Rsc = T([128, 16, 128], F32, "sc")
    cand = nc.alloc_sbuf_tensor_at("cand_alias", [128, 8, 256], F32, offset=off_s)
    Rcand = Rsc
    sc2, Rsc2 = T([128, 128], F32, "sc2")
    tv, Rtv = T([128, 16, 16], F32, "tv")
    ti, Rti = T([128, 16, 16], U32, "ti")
    tif, Rtif = T([128, 16, 16], F32, "tif")
    cand2, Rcand2 = T([128, 256], F32, "cand2")
    cv, Rcv = T([128, 8, 16], F32, "cv")
    cp, Rcp = T([128, 8, 16], U32, "cp")
    ca, Rca = T([128, 8, 16], U32, "ca")
    cb_, Rcb_ = T([128, 8, 16], U32, "cb")
    caf, Rcaf = T([128, 8, 16], F32, "caf")
    cbf, Rcbf = T([128, 8, 16], F32, "cbf")
    i1f, Ri1f = T([128, 8, 16], F32, "i1f")
    i2f, Ri2f = T([128, 8, 16], F32, "i2f")
    ef, Ref = T([128, 128], F32, "ef")
    eip = [T([128, 128], I32, "ei") for _ in range(2)]
    gtp = [T([128, 8, 16], F32, "gt") for _ in range(2)]
    gsum, Rgsum = T([128, 8], F32, "gsum")
    actv, _ = T([128, 128], F32, "actv")
    Ractv = [Res() for _ in range(32)]
    WG = [T([128, 4], F32, "wg4") for _ in range(2)]
    wgt, _ = T([128, 128], F32, "wgt")
    Rwgt = [Res() for _ in range(32)]
    djunk, Rdj = T([128, 1024], BF16, "djunk")
    NUV = 6
    UVt = [T([128, 4, 2048], BF16, "UV") for _ in range(NUV)]
    SV = [T([128, 1024], BF16, "SV") for _ in range(4)]
    fin = []

    def peer_setup(tb, p):
        xr, Rxr = xrp[p]
        h3, Rh3 = h3p[p]
        h3b, Rh3b = h3bp[p]
        ei, Rei = eip[p]
        gt, Rgt = gtp[p]
        S.dma("sp", reads=[Rx2d[tb]], writes=[Rxr], out=xr[:], in_=out_d[tb * 128:(tb + 1) * 128, :])
        rmsnorm(xr[:], Rxr, gf, Rgf, h3[:], Rh3)
        S.op("pool", "tensor_copy", reads=[Rh3], writes=[Rh3b], out=h3b[:], in_=h3[:])
        transposeN(h3b, Rh3b, 8, 128, h3T[:], [Rh3T], eng="act")
        yield
        for q4 in range(4):
            pb = q4 % 2
            for i in range(4):
                ccq = q4 * 4 + i
                for k in range(8):
                    S.op("pe", "matmul", reads=[Rwpq, Rh3T], writes=[RPS[pb]], out=PS[pb][:, i * 128:(i + 1) * 128],
                         lhsT=wpq[:, k, ccq * 128:(ccq + 1) * 128], rhs=h3T[:, k, :], start=(k == 0), stop=(k == 7))
            S.op("act", "copy", reads=[RPS[pb]], writes=[RqrT], out=qrT[:, q4 * 4:q4 * 4 + 4, :], in_=v3(PS[pb][:, :], 4))
            yield
        for q4 in range(4):
            pb = 2 + q4 % 2
            for i in range(4):
                j = q4 * 4 + i
                S.op("pe", "matmul", reads=[RqrT, RkeysT], writes=[RPS[pb]], out=PS[pb][:, i * 128:(i + 1) * 128],
                     lhsT=qrT[:, j, :], rhs=keysT[:, j % 2, :], start=True, stop=True)
            S.op("act", "copy", reads=[RPS[pb]], writes=[Rsc], out=sc[:, q4 * 4:q4 * 4 + 4, :], in_=v3(PS[pb][:, :], 4))
        for j in range(16):
            S.op("dve", "max", reads=[Rsc], writes=[Rtv], out=tv[:, j, 0:8], in_=sc[:, j, :])
            S.op("dve", "max_index", reads=[Rsc, Rtv], writes=[Rti], out=ti[:, j, 0:8], in_max=tv[:, j, 0:8],
                 in_values=sc[:, j, :])
            S.op("dve", "match_replace", reads=[Rsc, Rtv], writes=[Rsc2], out=sc2[:], in_to_replace=tv[:, j, 0:8],
                 in_values=sc[:, j, :], imm_value=-1e30)
            S.op("dve", "max", reads=[Rsc2], writes=[Rtv], out=tv[:, j, 8:16], in_=sc2[:])
            S.op("dve", "max_index", reads=[Rsc2, Rtv], writes=[Rti], out=ti[:, j, 8:16], in_max=tv[:, j, 8:16],
                 in_values=sc2[:])
            yield
        S.op("dve", "tensor_copy", reads=[Rti], writes=[Rtif], out=tif[:], in_=ti[:])
        tv4 = tv[:].rearrange("p (h two) k -> p h two k", two=2)
        tif4 = tif[:].rearrange("p (h two) k -> p h two k", two=2)
        S.op("dve", "tensor_tensor", reads=[Rtv], writes=[Rcand], out=cand[:].rearrange("p h (a b) -> p h a b", a=16),
             in0=tv4[:, :, 0, :].unsqueeze(3).broadcast_to([128, 8, 16, 16]),
             in1=tv4[:, :, 1, :].unsqueeze(2).broadcast_to([128, 8, 16, 16]), op=ALU.add)
        for h in range(8):
            S.op("dve", "max", reads=[Rcand], writes=[Rcv], out=cv[:, h, 0:8], in_=cand[:, h, :])
            S.op("dve", "max_index", reads=[Rcand, Rcv], writes=[Rcp], out=cp[:, h, 0:8], in_max=cv[:, h, 0:8],
                 in_values=cand[:, h, :])
            S.op("dve", "match_replace", reads=[Rcand, Rcv], writes=[Rcand2], out=cand2[:], in_to_replace=cv[:, h, 0:8],
                 in_values=cand[:, h, :], imm_value=-1e30)
            S.op("dve", "max", reads=[Rcand2], writes=[Rcv], out=cv[:, h, 8:16], in_=cand2[:])
            S.op("dve", "max_index", reads=[Rcand2, Rcv], writes=[Rcp], out=cp[:, h, 8:16], in_max=cv[:, h, 8:16],
                 in_values=cand2[:])
            yield
        S.op("dve", "tensor_single_scalar", reads=[Rcp], writes=[Rca], out=ca[:], in_=cp[:], scalar=4,
             op=ALU.logical_shift_right)
        S.op("dve", "tensor_single_scalar", reads=[Rcp], writes=[Rcb_], out=cb_[:], in_=cp[:], scalar=15,
             op=ALU.bitwise_and)
        S.op("dve", "tensor_copy", reads=[Rca], writes=[Rcaf], out=caf[:], in_=ca[:])
        S.op("dve", "tensor_copy", reads=[Rcb_], writes=[Rcbf], out=cbf[:], in_=cb_[:])
        io4 = iota16.unsqueeze(1).unsqueeze(1).broadcast_to([128, 8, 16, 16])
        for (sel, Rsel, half, dst, Rdst) in ((caf, Rcaf, 0, i1f, Ri1f), (cbf, Rcbf, 1, i2f, Ri2f)):
            S.op("dve", "tensor_tensor", reads=[Rsel, Rcst], writes=[Req], out=eq[:],
                 in0=sel[:].unsqueeze(3).broadcast_to([128, 8, 16, 16]), in1=io4, op=ALU.is_equal)
            S.op("dve", "tensor_tensor", reads=[Req, Rtif], writes=[Req], out=eq[:], in0=eq[:],
                 in1=tif4[:, :, half, :].unsqueeze(2).broadcast_to([128, 8, 16, 16]), op=ALU.mult)
            S.op("dve", "tensor_reduce", reads=[Req], writes=[Rdst], out=dst[:].rearrange("p h k -> p (h k)"),
                 in_=eq[:].rearrange("p h k a -> p (h k) a"), axis=AX.X, op=ALU.add)
        S.op("dve", "scalar_tensor_tensor", reads=[Ri1f, Ri2f], writes=[Ref], out=ef[:],
             in0=i1f[:].rearrange("p h k -> p (h k)"), scalar=128.0, in1=i2f[:].rearrange("p h k -> p (h k)"),
             op0=ALU.mult, op1=ALU.add)
        S.op("dve", "tensor_copy", reads=[Ref], writes=[Rei], out=ei[:], in_=ef[:])
        S.op("dve", "tensor_tensor", reads=[Rcv], writes=[Rgt], out=gt[:], in0=cv[:],
             in1=cv[:, :, 0:1].broadcast_to([128, 8, 16]), op=ALU.subtract)
        S.op("act", "activation", reads=[Rgt], writes=[Rgt], out=gt[:], in_=gt[:], func=AF.Exp)
        S.op("dve", "tensor_reduce", reads=[Rgt], writes=[Rgsum], out=gsum[:], in_=gt[:], axis=AX.X, op=ALU.add)
        S.op("dve", "reciprocal", reads=[Rgsum], writes=[Rgsum], out=gsum[:], in_=gsum[:])
        S.op("dve", "tensor_tensor", reads=[Rgt, Rgsum], writes=[Rgt], out=gt[:], in0=gt[:],
             in1=gsum[:].unsqueeze(2).broadcast_to([128, 8, 16]), op=ALU.mult)

    def peer_gather(tb, p, bg):
        xr, Rxr = xrp[p]
        h3b, Rh3b = h3bp[p]
        ei, Rei = eip[p]
        gt, Rgt = gtp[p]
        gtf = gt[:].rearrange("p h k -> p (h k)")
        S.op("pool", "memset", writes=Ractv, ap=actv[:], constant=0.0)

        def stA(g4):
            uvt, Ruvt = UVt[g4 % NUV]
            for i in range(4):
                s_ = g4 * 4 + i
                S.dma("pool", reads=[Rei, Ruv], writes=[Ruvt], meth="indirect_dma_start", out=uvt[:, i, :],
                      out_offset=None, in_=uv_scr, in_offset=bass.IndirectOffsetOnAxis(ap=ei[:, s_:s_ + 1], axis=0))

        def stBC(g4):
            uvt, Ruvt = UVt[g4 % NUV]
            for i in range(4):
                s_ = g4 * 4 + i
                S.op("dve", "scalar_tensor_tensor", reads=[Ruvt, Rh3b], writes=[Ractv[g4]], out=djunk[:],
                     in0=uvt[:, i, 0:1024], scalar=1.0, in1=h3b[:], op0=ALU.mult, op1=ALU.mult,
                     accum_out=actv[:, s_:s_ + 1])
            wg, Rwg = WG[g4 % 2]
            S.op("act", "activation", reads=[Ractv[g4]], writes=[Rwg], out=wg[:], in_=actv[:, g4 * 4:g4 * 4 + 4],
                 func=AF.Gelu)

        def stDE(g4):
            uvt, Ruvt = UVt[g4 % NUV]
            wg, Rwg = WG[g4 % 2]
            S.op("dve", "tensor_tensor", reads=[Rwg, Rgt], writes=[Rwgt[g4]], out=wgt[:, g4 * 4:g4 * 4 + 4], in0=wg[:],
                 in1=gtf[:, g4 * 4:g4 * 4 + 4], op=ALU.mult)
            for i in range(4):
                s_ = g4 * 4 + i
                sv, Rsv = SV[s_ % 4]
                S.op("act", "activation", reads=[Ruvt, Rwgt[g4]], writes=[Rsv], out=sv[:], in_=uvt[:, i, 1024:2048],
                     func=AF.Copy, scale=wgt[:, s_:s_ + 1])
                for half in range(2):
                    S.op("pe", "matmul", reads=[Rsv, Ridb], writes=[RPS[5 + half]], out=PS[5 + half][:, :],
                         lhsT=ident_b[:], rhs=sv[:, half * 512:(half + 1) * 512], start=(s_ == 0), stop=(s_ == 127))

        for g4 in range(NUV - 2):
            stA(g4)
        for g4 in range(33):
            if g4 + NUV - 2 < 32:
                stA(g4 + NUV - 2)
            if g4 < 32:
                stBC(g4)
            if g4 >= 1:
                stDE(g4 - 1)
            if bg is not None:
                next(bg, None)
                next(bg, None)
        if bg is not None:
            for _ in bg:
                pass
        for half in range(2):
            cs_ = slice(half * 512, (half + 1) * 512)
            S.op("dve", "tensor_tensor", reads=[RPS[5 + half], Rxr], writes=[Rxr], out=xr[:, cs_],
                 in0=xr[:, cs_], in1=PS[5 + half][:, :], op=ALU.add)
        R = Res()
        S.dma("sp", reads=[Rxr], writes=[R, Rx2d[tb]], out=out_d[tb * 128:(tb + 1) * 128, :], in_=xr[:])
        fin.append(R)

    for _ in peer_setup(0, 0):
        pass
    for tb in range(16):
        bg = peer_setup(tb + 1, (tb + 1) % 2) if tb + 1 < 16 else None
        peer_gather(tb, tb % 2, bg)
    for R in fin:
        S._wait("sp", R.w)
    S.emit()
    return nc


def _consts():
    c = np.zeros((128, NCONST), np.float32)
    p = np.arange(128)
    c[:, 0:128] = np.eye(128)
    c[:, 128:256] = (p[:, None] <= p[None, :])
    c[:, 256:384] = 1.0
    c[:, 384:512] = (p[None, :] - p[:, None])
    c[:, 512:528] = np.arange(16)[None, :]
    k = p[:, None, None]
    j = np.arange(17)[None, :, None]
    q = p[None, None, :]
    delta = (16 - j) * 128 + q - k
    m = ((delta >= 0) & (delta <= 128)).astype(np.float32)
    m += ((delta >= 0) & (delta <= 512) & (delta % 4 == 0))
    m += ((delta >= 0) & (delta <= 2048) & (delta % 16 == 0))
    return c, np.ascontiguousarray(m.reshape(128, 17 * 128).astype(np.float32))


_NC_CACHE = {}


def kernel(**inputs):
    stop_after = os.environ.get("KSTOP") or None
    x = np.asarray(inputs["x"], np.float32)
    mem = np.asarray(inputs["mem"], np.float32)
    cst, mtab = _consts()
    common = {
        "cst": cst, "mtab": mtab,
        "w_in": np.ascontiguousarray(inputs["w_in"][0], dtype=np.float32),
        "convw": np.ascontiguousarray(
            np.asarray(inputs["conv_w"][0], np.float32).T.reshape(18, 128, 4).transpose(1, 0, 2).reshape(128, 72)),
        "convb": np.ascontiguousarray(np.asarray(inputs["conv_b"][0], np.float32).reshape(18, 128).T),
        "w_out": np.ascontiguousarray(inputs["w_out"][0], dtype=np.float32),
    }
    for nm in ("mix_norm_g", "dt_bias", "a_log", "d_skip", "ssd_norm_g", "attn_q_norm_g", "attn_k_norm_g",
               "xattn_norm_g", "mem_norm_g", "mem_q_norm_g", "mem_k_norm_g", "ffn_norm_g"):
        common[nm] = np.ascontiguousarray(np.asarray(inputs[nm][0], np.float32).reshape(1, -1))
    if True:
        common["mem_w_q"] = np.ascontiguousarray(inputs["mem_w_q"][0], dtype=np.float32)
        common["mem_w_kv"] = np.ascontiguousarray(inputs["mem_w_kv"][0], dtype=np.float32)
        common["mem_w_o"] = np.ascontiguousarray(inputs["mem_w_o"][0], dtype=np.float32)
    if True:
        common["peer_w_query"] = np.ascontiguousarray(inputs["peer_w_query"][0], dtype=np.float32)
        common["keys1T"] = np.ascontiguousarray(np.asarray(inputs["peer_sub_keys1"][0], np.float32).T)
        common["keys2T"] = np.ascontiguousarray(np.asarray(inputs["peer_sub_keys2"][0], np.float32).T)
        common["peer_u"] = np.ascontiguousarray(inputs["peer_u"][0], dtype=np.float32)
        common["peer_v"] = np.ascontiguousarray(inputs["peer_v"][0], dtype=np.float32)
    in_maps = []
    for c in range(8):
        b, j = c // 2, c % 2
        xw = np.zeros((WIN, D), np.float32)
        if j == 1:
            xw[:HALF] = x[b, :HALF]
        xw[HALF:] = x[b, j * HALF:(j + 1) * HALF]
        m = dict(common)
        m["xw"] = xw
        m["flag"] = np.full((128, 1), float(j), np.float32)
        if True:
            m["mem"] = np.ascontiguousarray(mem[b])
        in_maps.append(m)
    nc = build(stop_after)
    res = run_bass_kernel_spmd(nc, in_maps, core_ids=list(range(8)))
    out = np.zeros((NB, SEQ, D), np.float32)
    for c in range(8):
        b, j = c // 2, c % 2
        out[b, j * HALF:(j + 1) * HALF] = np.asarray(res.results[c]["out"])
    return out
```

```python
import os
import math
import numpy as np
import concourse.bass as bass
import concourse.mybir as mybir
from concourse.bass_utils import run_bass_kernel_spmd

F32 = mybir.dt.float32
BF16 = mybir.dt.bfloat16
I32 = mybir.dt.int32
U32 = mybir.dt.uint32
AF = mybir.ActivationFunctionType
ALU = mybir.AluOpType
AX = mybir.AxisListType

D = 1024
NB = 4
SEQ = 4096
HALF = 2048
WIN = 4096
EPS = 1e-6
OFF_Z = 1280
OFF_XBC = OFF_Z + 2304
OFF_DT = OFF_XBC + 20
OFF_Q = OFF_DT + 768
OFF_K = OFF_Q + 768
IN_COLS = OFF_K + 768
NCONST = 128 * 4 + 16


class Res:
    __slots__ = ("w", "r")

    def __init__(self):
        self.w = None
        self.r = {}


class Sched:
    COMPUTE = ("pe", "act", "dve", "pool")
    NDMA = 12

    def __init__(self, nc):
        self.nc = nc
        self.streams = {k: [] for k in ("pe", "act", "dve", "pool", "sp")}
        self.esem = {k: nc.alloc_semaphore("es_" + k) for k in self.COMPUTE}
        self.ecnt = {k: 0 for k in self.COMPUTE}
        self.dsem = {q: [nc.alloc_semaphore("ds_%s%d" % (q, i)) for i in range(self.NDMA)]
                     for q in ("sp", "pool")}
        self.dcnt = {q: [0] * self.NDMA for q in ("sp", "pool")}
        self.dnext = {q: 0 for q in ("sp", "pool")}
        self.known = {k: {} for k in self.streams}
        self.nwaits = 0
        self.nops = 0

    def _sem_of(self, pid):
        if pid[0] == "e":
            return self.esem[pid[1]], pid[2], ("e", pid[1])
        return self.dsem[pid[1]][pid[2]], pid[3], ("d", pid[1], pid[2])

    def _wait(self, eng, pid):
        sem, val, key = self._sem_of(pid)
        if self.known[eng].get(key, 0) >= val:
            return
        self.known[eng][key] = val
        self.nwaits += 1
        self.streams[eng].append(lambda e, sem=sem, val=val: e.wait_ge(sem, val))

    def _deps(self, eng, reads, writes):
        deps = []
        for r in reads:
            if r.w is not None:
                deps.append(r.w)
        for w in writes:
            if w.w is not None:
                deps.append(w.w)
            deps.extend(w.r.values())
        for pid in deps:
            if pid[0] == "e" and pid[1] == eng and eng == "pe":
                continue
            self._wait(eng, pid)

    def _commit(self, pid, reads, writes):
        key = pid[:2] if pid[0] == "e" else pid[:3]
        for r in reads:
            r.r[key] = pid
        for w in writes:
            w.w = pid
            w.r = {}

    def op(self, eng, meth, reads=(), writes=(), **kw):
        fn = lambda e, meth=meth, kw=kw: getattr(e, meth)(**kw)
        self._deps(eng, reads, writes)
        self.ecnt[eng] += 1
        idx = self.ecnt[eng]
        sem = self.esem[eng]
        self.streams[eng].append(lambda e, fn=fn, sem=sem: fn(e).then_inc(sem, 1))
        self.nops += 1
        self._commit(("e", eng, idx), reads, writes)

    def dma(self, q, reads=(), writes=(), meth="dma_start", **kw):
        fn = lambda e, meth=meth, kw=kw: getattr(e, meth)(**kw)
        i = self.dnext[q]
        self.dnext[q] = (i + 1) % self.NDMA
        prev = self.dcnt[q][i]
        if prev > 0:
            self._wait(q, ("d", q, i, prev))
        self._deps(q, reads, writes)
        self.dcnt[q][i] = prev + 16
        sem = self.dsem[q][i]
        self.streams[q].append(lambda e, fn=fn, sem=sem: fn(e).then_inc(sem, 16))
        self.nops += 1
        self._commit(("d", q, i, prev + 16), reads, writes)

    def barrier(self):
        for eng in self.streams:
            for k in self.COMPUTE:
                if k != eng and self.ecnt[k] > 0:
                    self._wait(eng, ("e", k, self.ecnt[k]))
            for q in self.dsem:
                for i in range(self.NDMA):
                    if self.dcnt[q][i] > 0:
                        self._wait(eng, ("d", q, i, self.dcnt[q][i]))

    def emit(self):
        nc = self.nc
        with nc.Block() as block:
            @block.tensor
            def _(e):
                for it in self.streams["pe"]:
                    it(e)

            @block.scalar
            def _(e):
                for it in self.streams["act"]:
                    it(e)

            @block.vector
            def _(e):
                for it in self.streams["dve"]:
                    it(e)

            @block.gpsimd
            def _(e):
                for it in self.streams["pool"]:
                    it(e)

            @block.sync
            def _(e):
                for it in self.streams["sp"]:
                    it(e)


class Arena:
    def __init__(self, nc):
        self.nc = nc
        self.off = (nc.sbuf_base + 31) // 32 * 32
        self.top = nc.sbuf_top
        self.n = 0

    def alloc(self, shape, dt=F32, name="t"):
        sz = {F32: 4, BF16: 2, I32: 4, U32: 4}[dt]
        nb = int(np.prod(shape[1:])) * sz
        off = self.off
        self.off += (nb + 31) // 32 * 32
        assert self.off <= self.top, "SBUF overflow at %s: %d > %d" % (name, self.off, self.top)
        self.n += 1
        return self.nc.alloc_sbuf_tensor_at("%s_%d" % (name, self.n), list(shape), dt, offset=off)

    def mark(self):
        return self.off

    def release(self, m):
        self.off = m


def build(stop_after=None):
    nc = bass.Bass("TRN2", target_bir_lowering=False)
    S = Sched(nc)
    A = Arena(nc)

    def din(name, shape, dt=F32):
        return nc.dram_tensor(name, list(shape), dt, kind="ExternalInput").ap()

    xw_d = din("xw", [WIN, D])
    flag_d = din("flag", [128, 1])
    mem_d = din("mem", [256, D])
    cst_d = din("cst", [128, NCONST])
    mtab_d = din("mtab", [128, 17 * 128])
    w_in_d = din("w_in", [D, IN_COLS])
    convw_d = din("convw", [128, 72])
    convb_d = din("convb", [128, 18])
    vec_d = {}
    for nm, n in [("mix_norm_g", 1024), ("dt_bias", 20), ("a_log", 20), ("d_skip", 20), ("ssd_norm_g", 1280),
                  ("attn_q_norm_g", 64), ("attn_k_norm_g", 64), ("xattn_norm_g", 1024), ("mem_norm_g", 1024),
                  ("mem_q_norm_g", 128), ("mem_k_norm_g", 128), ("ffn_norm_g", 1024)]:
        vec_d[nm] = din(nm, [1, n])
    w_out_d = din("w_out", [2048, D])
    mem_w_q_d = din("mem_w_q", [D, 512])
    mem_w_kv_d = din("mem_w_kv", [D, 1024])
    mem_w_o_d = din("mem_w_o", [512, D])
    peer_wq_d = din("peer_w_query", [D, 2048])
    keys1T_d = din("keys1T", [128, 128])
    keys2T_d = din("keys2T", [128, 128])
    peer_u_d = din("peer_u", [16384, D])
    peer_v_d = din("peer_v", [16384, D])
    out_d = nc.dram_tensor("out", [HALF, D], F32, kind="ExternalOutput").ap()
    yT_scr = nc.dram_tensor("yT_scr", [128, 10, HALF], BF16).ap()
    oT_scr = nc.dram_tensor("oT_scr", [128, 6, HALF], BF16).ap()
    uv_scr = nc.dram_tensor("uv_scr", [16384, 2048], BF16).ap()
    Ruv = Res()
    conv_jobs = [(tbl, h) for h in range(32) for tbl in (0, 1)]

    def issue_conv(n):
        for _ in range(n):
            if not conv_jobs or stop_after is not None:
                return
            tbl, h = conv_jobs.pop(0)
            src = peer_u_d if tbl == 0 else peer_v_d
            S.dma("pool", writes=[Ruv], out=uv_scr[h * 512:(h + 1) * 512, tbl * 1024:(tbl + 1) * 1024],
                  in_=src[h * 512:(h + 1) * 512, :])

    PS = [nc.alloc_psum_tensor("ps%d" % i, [128, 512], F32) for i in range(7)]
    RPS = [Res() for _ in range(7)]
    PB = nc.alloc_psum_tensor("psb", [128, 1024], BF16)
    RPB = Res()

    def T(shape, dt=F32, name="t"):
        return A.alloc(shape, dt, name), Res()

    def v3(ap, a):
        return ap.rearrange("p (a b) -> p a b", a=a)

    cst, Rcst = T([128, NCONST], F32, "cst")
    S.dma("sp", writes=[Rcst], out=cst[:], in_=cst_d)
    ident_f = cst[:, 0:128]
    tri_f = cst[:, 128:256]
    ones_f = cst[:, 256:384]
    d0_f = cst[:, 384:512]
    iota16 = cst[:, 512:528]
    ident_b, Ridb = T([128, 128], BF16, "identb")
    S.op("dve", "tensor_copy", reads=[Rcst], writes=[Ridb], out=ident_b[:], in_=ident_f)
    flag, Rflag = T([128, 1], F32, "flag")
    S.dma("sp", writes=[Rflag], out=flag[:], in_=flag_d)
    epsT, Reps = T([128, 1], F32, "eps")
    S.op("pool", "memset", writes=[Reps], ap=epsT[:], constant=EPS)

    def bc_load(n, name):
        t, R = T([128, n], F32, name)
        S.dma("sp", writes=[R], out=t[:], in_=vec_d[name].broadcast_to([128, n]))
        return t, R

    def rstd_from(ss_ap, Rss, n, H, out_ap, Rout, lnv_ap, Rln):
        S.op("act", "activation", reads=[Rss, Reps], writes=[Rln], out=lnv_ap, in_=ss_ap, func=AF.Ln,
             bias=epsT[:, 0:1], scale=1.0 / n)
        S.op("act", "activation", reads=[Rln], writes=[Rout], out=out_ap, in_=lnv_ap, func=AF.Exp, scale=-0.5)

    nrm_junk, Rnj = T([128, 1024], BF16, "nrmjunk")
    nrm_ss, Rnss = T([128, 4], F32, "nrmss")
    nrm_ln, Rnln = T([128, 4], F32, "nrmln")
    nrm_rs, Rnrs = T([128, 4], F32, "nrmrs")

    def rmsnorm(x_ap, Rx, g_bc, Rg, out_ap, Rout):
        S.op("act", "activation", reads=[Rx], writes=[Rnj, Rnss], out=nrm_junk[:], in_=x_ap, func=AF.Square,
             accum_out=nrm_ss[:, 0:1])
        rstd_from(nrm_ss[:, 0:1], Rnss, 1024, 1, nrm_rs[:, 0:1], Rnrs, nrm_ln[:, 0:1], Rnln)
        S.op("dve", "scalar_tensor_tensor", reads=[Rx, Rnrs, Rg], writes=[Rout], out=out_ap, in0=x_ap,
             scalar=nrm_rs[:, 0:1], in1=g_bc[:], op0=ALU.mult, op1=ALU.mult)

    def transposeN(src_bf, Rsrc, n, width, dst3, Rdst, eng="act"):
        for k0 in range(0, n, 8):
            m = min(8, n - k0)
            for k in range(m):
                S.op("pe", "transpose", reads=[Rsrc, Ridb], writes=[RPB], out=PB[0:width, k * 128:(k + 1) * 128],
                     in_=src_bf[:, (k0 + k) * width:(k0 + k + 1) * width], identity=ident_b[:])
            src = v3(PB[0:width, 0:m * 128], m)
            dst = dst3[:, k0:k0 + m, :]
            if eng == "act":
                S.op("act", "copy", reads=[RPB], writes=Rdst, out=dst, in_=src)
            else:
                S.op("dve", "tensor_copy", reads=[RPB], writes=Rdst, out=dst, in_=src)

    qk_sq, Rqsq = T([128, 512], F32, "qksq")
    qk_tmp, Rqtmp = T([128, 512], F32, "qktmp")

    def qknorm(ps_ap, Rps, H, dh, g_bc, Rg, out_bf, Rout):
        n = H * dh
        S.op("act", "activation", reads=[Rps], writes=[Rqsq], out=qk_sq[:, 0:n], in_=ps_ap, func=AF.Square)
        S.op("dve", "tensor_reduce", reads=[Rqsq], writes=[Rnss], out=nrm_ss[:, 0:H], in_=v3(qk_sq[:, 0:n], H),
             axis=AX.X, op=ALU.add)
        rstd_from(nrm_ss[:, 0:H], Rnss, dh, H, nrm_rs[:, 0:H], Rnrs, nrm_ln[:, 0:H], Rnln)
        S.op("dve", "tensor_tensor", reads=[Rps, Rnrs], writes=[Rqtmp], out=v3(qk_tmp[:, 0:n], H), in0=v3(ps_ap, H),
             in1=nrm_rs[:, 0:H].unsqueeze(2).broadcast_to([128, H, dh]), op=ALU.mult)
        S.op("pool", "tensor_tensor", reads=[Rqtmp, Rg], writes=[Rout], out=out_bf, in0=qk_tmp[:, 0:n], in1=g_bc,
             op=ALU.mult)

    m_base = A.mark()
    hT, _ = T([128, 8, WIN], BF16, "hT")
    RhT = [Res() for _ in range(32)]
    m_after_hT = A.mark()

    gmix, Rgmix = bc_load(1024, "mix_norm_g")
    xb = [T([128, 1024], F32, "xb") for _ in range(2)]
    hb = [T([128, 1024], BF16, "hb") for _ in range(2)]
    for blk in range(32):
        x_t, Rx = xb[blk % 2]
        h_t, Rh = hb[blk % 2]
        S.dma("sp", writes=[Rx], out=x_t[:], in_=xw_d[blk * 128:(blk + 1) * 128, :])
        rmsnorm(x_t[:], Rx, gmix, Rgmix, h_t[:], Rh)
        transposeN(h_t, Rh, 8, 128, hT[:, :, blk * 128:(blk + 1) * 128], [RhT[blk]], eng="act" if blk % 2 else "dve")
    S.barrier()
    A.release(m_after_hT)

    wxbc, _ = T([128, 8, 2304], BF16, "wxbc")
    Rwxbc = [Res() for _ in range(8)]
    wz, _ = T([128, 8, 1280], BF16, "wz")
    Rwz = [Res() for _ in range(8)]
    wdt, _ = T([128, 8, 20], BF16, "wdt")
    Rwdt = [Res() for _ in range(8)]
    for k in range(8):
        S.dma("pool", writes=[Rwxbc[k]], out=wxbc[:, k, :], in_=w_in_d[k * 128:(k + 1) * 128, OFF_Z:OFF_XBC])
        S.dma("pool", writes=[Rwz[k]], out=wz[:, k, :], in_=w_in_d[k * 128:(k + 1) * 128, 0:OFF_Z])
        S.dma("pool", writes=[Rwdt[k]], out=wdt[:, k, :], in_=w_in_d[k * 128:(k + 1) * 128, OFF_XBC:OFF_DT])
    convw, Rcw = T([128, 18, 4], F32, "convw")
    convb, Rcb = T([128, 18], F32, "convb")
    S.dma("sp", writes=[Rcw], out=convw[:].rearrange("p a b -> p (a b)"), in_=convw_d)
    S.dma("sp", writes=[Rcb], out=convb[:], in_=convb_d)
    dtb, Rdtb = bc_load(20, "dt_bias")
    alog, Ralog = bc_load(20, "a_log")
    dsk, Rdsk = bc_load(20, "d_skip")
    gssd, Rgssd = bc_load(1280, "ssd_norm_g")
    a_bc, Ra = T([128, 20], F32, "a_bc")
    S.op("act", "activation", reads=[Ralog], writes=[Ra], out=a_bc[:], in_=alog[:], func=AF.Exp)
    S.op("dve", "tensor_scalar", reads=[Ra], writes=[Ra], out=a_bc[:], in0=a_bc[:], scalar1=-1.0, scalar2=None,
         op0=ALU.mult)
    halo, _ = T([128, 18, 3], F32, "halo")
    Rhalo = [Res() for _ in range(18)]
    S.op("pool", "memset", writes=Rhalo, ap=halo[:], constant=0.0)
    state, Rst = T([128, 1280], F32, "state")
    state_bf, Rstb = T([128, 1280], BF16, "statebf")
    S.op("pool", "memset", writes=[Rst], ap=state[:], constant=0.0)
    S.op("pool", "memset", writes=[Rstb], ap=state_bf[:], constant=0.0)
    U = [T([128, 259], F32, "U") for _ in range(2)]
    acc = [T([128, 256], F32, "acc") for _ in range(2)]
    xTf = [T([128, 256], F32, "xTf") for _ in range(2)]
    x_tm, Rxtm = T([128, 2, 1280], F32, "x_tm")
    BT, _ = T([128, 4, 256], BF16, "BT")
    RBTg = [Res() for _ in range(4)]
    CTt, _ = T([128, 4, 256], BF16, "CT")
    RCTg = [Res() for _ in range(4)]
    B_tm, RBtm = T([128, 2, 512], BF16, "B_tm")
    dt_tm, Rdt = T([128, 2, 20], F32, "dt_tm")
    adt, Radt = T([128, 2, 20], F32, "adt")
    ncs, Rncs = T([128, 20], F32, "ncs")
    tmp20, Rt20 = T([128, 20], F32, "tmp20")
    dec, Rdec = T([128, 20], F32, "dec")
    cd, Rcd = T([128, 20], F32, "cd")
    w2, Rw2 = T([128, 20], F32, "w2")
    ecs, Recs = T([128, 20], F32, "ecs")
    xdd, Rxdd = T([128, 1280], BF16, "xdd")
    xdt, Rxdt = T([128, 1280], BF16, "xdt")
    CBm, RCBm = T([128, 4, 128], F32, "CBm")
    adtb, Radtb = T([128, 4, 128], F32, "adtb")
    Lt = [T([128, 4, 128], F32, "Lt") for _ in range(2)]
    Wm, RWm = T([128, 20, 128], BF16, "Wm")
    yg, Ryg = T([128, 320], F32, "yg")
    ytmp, Rytmp = T([128, 320], F32, "ytmp")
    sz, Rsz = T([128, 320], F32, "sz")
    ss1, Rss1 = T([128, 1], F32, "ss1")
    ln1, Rln1 = T([128, 1], F32, "ln1")
    rs1, Rrs1 = T([128, 1], F32, "rs1")
    yn, Ryn = T([128, 1280], BF16, "yn")
    yTc = [T([128, 10, 128], BF16, "yTc") for _ in range(2)]

    for G in range(16):
        own = G >= 8
        RhG = RhT[G * 2:(G + 1) * 2]
        tokG = slice(G * 256, (G + 1) * 256)
        def cc_main(cc):
            b = cc % 2
            for k in range(8):
                S.op("pe", "matmul", reads=[Rwxbc[k]] + RhG, writes=[RPS[b]], out=PS[b][:, 0:256],
                     lhsT=wxbc[:, k, cc * 128:(cc + 1) * 128], rhs=hT[:, k, tokG], start=(k == 0), stop=(k == 7))
            u_t, Ru = U[b]
            a_t, Rac = acc[b]
            S.op("act", "copy", reads=[RPS[b]], writes=[Ru], out=u_t[:, 3:259], in_=PS[b][:, 0:256])
            S.op("pool", "tensor_copy", reads=[Rhalo[cc]], writes=[Ru], out=u_t[:, 0:3], in_=halo[:, cc, :])
            S.op("dve", "tensor_scalar", reads=[Ru, Rcw], writes=[Rac], out=a_t[:], in0=u_t[:, 3:259],
                 scalar1=convw[:, cc, 3:4], scalar2=None, op0=ALU.mult)
            for j in (2, 1, 0):
                S.op("dve", "scalar_tensor_tensor", reads=[Ru, Rcw, Rac], writes=[Rac], out=a_t[:],
                     in0=u_t[:, j:j + 256], scalar=convw[:, cc, j:j + 1], in1=a_t[:], op0=ALU.mult, op1=ALU.add)
            S.op("pool", "tensor_copy", reads=[Ru], writes=[Rhalo[cc]], out=halo[:, cc, :], in_=u_t[:, 256:259])
            if cc < 10:
                xt, Rxt = xTf[b]
                S.op("act", "activation", reads=[Rac, Rcb], writes=[Rxt], out=xt[:], in_=a_t[:], func=AF.Silu,
                     bias=convb[:, cc:cc + 1])
            elif cc < 14:
                g = cc - 10
                S.op("act", "activation", reads=[Rac, Rcb], writes=[RBTg[g]], out=BT[:, g, :], in_=a_t[:], func=AF.Silu,
                     bias=convb[:, cc:cc + 1])
            else:
                g = cc - 14
                S.op("act", "activation", reads=[Rac, Rcb], writes=[RCTg[g]], out=CTt[:, g, :], in_=a_t[:], func=AF.Silu,
                     bias=convb[:, cc:cc + 1])

        def cc_tr(cc):
            b = cc % 2
            if cc < 10:
                xt, Rxt = xTf[b]
                pb = 2 + b
                for tb in range(2):
                    S.op("pe", "transpose", reads=[Rxt, Rcst], writes=[RPS[pb]], out=PS[pb][:, tb * 128:(tb + 1) * 128],
                         in_=xt[:, tb * 128:(tb + 1) * 128], identity=ident_f)
                S.op("dve", "tensor_copy", reads=[RPS[pb]], writes=[Rxtm], out=x_tm[:, :, cc * 128:(cc + 1) * 128],
                     in_=v3(PS[pb][:, 0:256], 2))
            elif cc < 14:
                g = cc - 10
                for tb in range(2):
                    S.op("pe", "transpose", reads=[RBTg[g], Ridb], writes=[RPB], out=PB[:, tb * 128:(tb + 1) * 128],
                         in_=BT[:, g, tb * 128:(tb + 1) * 128], identity=ident_b[:])
                S.op("dve", "tensor_copy", reads=[RPB], writes=[RBtm], out=B_tm[:, :, g * 128:(g + 1) * 128],
                     in_=v3(PB[:, 0:256], 2))

        for cc in range(19):
            if cc < 18:
                cc_main(cc)
            if cc >= 1:
                cc_tr(cc - 1)
        issue_conv(4)
        for tb in range(2):
            blk = G * 2 + tb
            for k in range(8):
                S.op("pe", "matmul", reads=[Rwdt[k], RhT[blk]], writes=[RPS[4]], out=PS[4][:, tb * 20:(tb + 1) * 20],
                     lhsT=hT[:, k, blk * 128:(blk + 1) * 128], rhs=wdt[:, k, :], start=(k == 0), stop=(k == 7))
        S.op("dve", "tensor_tensor", reads=[RPS[4], Rdtb], writes=[Rdt], out=dt_tm[:], in0=v3(PS[4][:, 0:40], 2),
             in1=dtb[:].unsqueeze(1).broadcast_to([128, 2, 20]), op=ALU.add)
        S.op("act", "activation", reads=[Rdt], writes=[Rdt], out=dt_tm[:], in_=dt_tm[:], func=AF.Exp)
        S.op("act", "activation", reads=[Rdt], writes=[Rdt], out=dt_tm[:], in_=dt_tm[:], func=AF.Ln, bias=1.0)
        S.op("dve", "tensor_tensor", reads=[Rdt, Ra], writes=[Radt], out=adt[:], in0=dt_tm[:],
             in1=a_bc[:].unsqueeze(1).broadcast_to([128, 2, 20]), op=ALU.mult)
        for tb in range(2):
            c = G * 2 + tb
            tk = slice(tb * 128, (tb + 1) * 128)
            S.op("pe", "matmul", reads=[Rcst, Radt], writes=[RPS[5]], out=PS[5][:, 0:20], lhsT=tri_f, rhs=adt[:, tb, :],
                 start=True, stop=True)
            S.op("pe", "matmul", reads=[Rcst, Radt], writes=[RPS[5]], out=PS[5][:, 32:52], lhsT=ones_f, rhs=adt[:, tb, :],
                 start=True, stop=True)
            S.op("dve", "tensor_scalar", reads=[RPS[5]], writes=[Rncs], out=ncs[:], in0=PS[5][:, 0:20], scalar1=-1.0,
                 scalar2=None, op0=ALU.mult)
            S.op("dve", "tensor_tensor", reads=[RPS[5], Rncs], writes=[Rt20], out=tmp20[:], in0=PS[5][:, 32:52],
                 in1=ncs[:], op=ALU.add)
            S.op("act", "activation", reads=[Rt20], writes=[Rdec], out=dec[:], in_=tmp20[:], func=AF.Exp)
            S.op("act", "activation", reads=[RPS[5]], writes=[Rcd], out=cd[:], in_=PS[5][:, 32:52], func=AF.Exp)
            S.op("dve", "tensor_tensor", reads=[Rdt, Rdec], writes=[Rw2], out=w2[:], in0=dt_tm[:, tb, :], in1=dec[:],
                 op=ALU.mult)
            S.op("dve", "tensor_tensor", reads=[Rxtm, Rw2], writes=[Rxdd], out=v3(xdd[:], 20),
                 in0=v3(x_tm[:, tb, :], 20), in1=w2[:].unsqueeze(2).broadcast_to([128, 20, 64]), op=ALU.mult)
            if own:
                blk = c
                S.op("act", "activation", reads=[Rncs], writes=[Recs], out=ecs[:], in_=ncs[:], func=AF.Exp, scale=-1.0)
                S.op("pool", "tensor_tensor", reads=[Rxtm, Rdt], writes=[Rxdt], out=v3(xdt[:], 20),
                     in0=v3(x_tm[:, tb, :], 20), in1=dt_tm[:, tb, :].unsqueeze(2).broadcast_to([128, 20, 64]),
                     op=ALU.mult)
                for g in range(4):
                    S.op("pe", "matmul", reads=[RBTg[g], RCTg[g]], writes=[RPS[2]], out=PS[2][:, g * 128:(g + 1) * 128],
                         lhsT=BT[:, g, tk], rhs=CTt[:, g, tk], start=True, stop=True)
                S.op("dve", "tensor_tensor", reads=[RPS[2], Rcst], writes=[RCBm], out=CBm[:], in0=v3(PS[2][:, :], 4),
                     in1=tri_f.unsqueeze(1).broadcast_to([128, 4, 128]), op=ALU.mult)
                for hq in range(5):
                    pb = 3 + hq % 2
                    S.op("pool", "tensor_copy", reads=[Radt], writes=[Radtb], out=adtb[:],
                         in_=adt[:, tb, hq * 4:hq * 4 + 4].unsqueeze(2).broadcast_to([128, 4, 128]))
                    for i in range(4):
                        h = hq * 4 + i
                        S.op("pe", "matmul", reads=[Radtb, Rcst], writes=[RPS[pb]], out=PS[pb][:, i * 128:(i + 1) * 128],
                             lhsT=adtb[:, i, :], rhs=tri_f, start=True, stop=True)
                    lt, Rlt = Lt[hq % 2]
                    S.op("dve", "tensor_tensor", reads=[RPS[pb], Rncs], writes=[Rlt], out=lt[:], in0=v3(PS[pb][:, :], 4),
                         in1=ncs[:, hq * 4:hq * 4 + 4].unsqueeze(2).broadcast_to([128, 4, 128]), op=ALU.add)
                    S.op("act", "activation", reads=[Rlt], writes=[Rlt], out=lt[:], in_=lt[:], func=AF.Exp)
                    for i in range(4):
                        h = hq * 4 + i
                        if i % 2:
                            S.op("pool", "tensor_scalar", reads=[Rlt], writes=[Rlt], out=lt[:, i, :], in0=lt[:, i, :],
                                 scalar1=1.0, scalar2=None, op0=ALU.min)
                            S.op("pool", "tensor_tensor", reads=[Rlt, RCBm], writes=[RWm], out=Wm[:, h, :],
                                 in0=lt[:, i, :], in1=CBm[:, h // 5, :], op=ALU.mult)
                        else:
                            S.op("dve", "scalar_tensor_tensor", reads=[Rlt, RCBm], writes=[RWm],
                                 out=Wm[:, h, :], in0=lt[:, i, :], scalar=1.0, in1=CBm[:, h // 5, :], op0=ALU.min,
                                 op1=ALU.mult)
                for g in range(4):
                    gs = slice(g * 320, (g + 1) * 320)
                    S.op("pe", "matmul", reads=[RCTg[g], Rstb], writes=[RPS[0]], out=PS[0][:, 0:320], lhsT=CTt[:, g, tk],
                         rhs=state_bf[:, gs], start=True, stop=True)
                    for hh in range(5):
                        h = g * 5 + hh
                        S.op("pe", "matmul", reads=[RWm, Rxdt], writes=[RPS[1]], out=PS[1][:, hh * 64:(hh + 1) * 64],
                             lhsT=Wm[:, h, :], rhs=xdt[:, h * 64:(h + 1) * 64], start=True, stop=True)
                    for k in range(8):
                        S.op("pe", "matmul", reads=[Rwz[k], RhT[blk]], writes=[RPS[6]], out=PS[6][:, 0:320],
                             lhsT=hT[:, k, blk * 128:(blk + 1) * 128], rhs=wz[:, k, gs], start=(k == 0), stop=(k == 7))
                    S.op("dve", "tensor_tensor", reads=[RPS[0], Recs], writes=[Ryg], out=v3(yg[:], 5),
                         in0=v3(PS[0][:, 0:320], 5), in1=ecs[:, g * 5:g * 5 + 5].unsqueeze(2).broadcast_to([128, 5, 64]),
                         op=ALU.mult)
                    S.op("dve", "tensor_tensor", reads=[RPS[1], Ryg], writes=[Ryg], out=yg[:], in0=yg[:],
                         in1=PS[1][:, 0:320], op=ALU.add)
                    S.op("pool", "tensor_tensor", reads=[Rxtm, Rdsk], writes=[Rytmp], out=v3(ytmp[:], 5),
                         in0=v3(x_tm[:, tb, gs], 5), in1=dsk[:, g * 5:g * 5 + 5].unsqueeze(2).broadcast_to([128, 5, 64]),
                         op=ALU.mult)
                    S.op("pool", "tensor_tensor", reads=[Rytmp, Ryg], writes=[Ryg], out=yg[:], in0=yg[:], in1=ytmp[:],
                         op=ALU.add)
                    S.op("act", "activation", reads=[RPS[6]], writes=[Rsz], out=sz[:], in_=PS[6][:, 0:320], func=AF.Silu)
                    S.op("dve", "tensor_tensor", reads=[Ryg, Rsz], writes=[Ryg], out=yg[:], in0=yg[:], in1=sz[:],
                         op=ALU.mult)
                    S.op("act", "activation", reads=[Ryg], writes=[Rytmp, Rss1], out=ytmp[:], in_=yg[:], func=AF.Square,
                         accum_out=ss1[:, 0:1])
                    rstd_from(ss1[:, 0:1], Rss1, 320, 1, rs1[:, 0:1], Rrs1, ln1[:, 0:1], Rln1)
                    S.op("dve", "scalar_tensor_tensor", reads=[Ryg, Rrs1, Rgssd], writes=[Ryn], out=yn[:, gs], in0=yg[:],
                         scalar=rs1[:, 0:1], in1=gssd[:, gs], op0=ALU.mult, op1=ALU.mult)
                yt, Ryt = yTc[tb % 2]
                transposeN(yn, Ryn, 10, 128, yt[:], [Ryt], eng="act")
                ob = c - 16
                S.dma("sp", reads=[Ryt], writes=[Res()], out=yT_scr[:, :, ob * 128:(ob + 1) * 128], in_=yt[:])
            for g in range(4):
                gs = slice(g * 320, (g + 1) * 320)
                S.op("pe", "matmul", reads=[RBtm, Rxdd], writes=[RPS[6]], out=PS[6][:, 0:320],
                     lhsT=B_tm[:, tb, g * 128:(g + 1) * 128], rhs=xdd[:, gs], start=True, stop=True)
                S.op("dve", "tensor_tensor", reads=[Rst, Rcd], writes=[Rst], out=v3(state[:, gs], 5),
                     in0=v3(state[:, gs], 5), in1=cd[:, g * 5:g * 5 + 5].unsqueeze(2).broadcast_to([128, 5, 64]),
                     op=ALU.mult)
                S.op("dve", "tensor_tensor", reads=[Rst, RPS[6]], writes=[Rst], out=state[:, gs], in0=state[:, gs],
                     in1=PS[6][:, 0:320], op=ALU.add)
            if c == 15:
                S.op("dve", "tensor_scalar", reads=[Rst, Rflag], writes=[Rst], out=state[:], in0=state[:],
                     scalar1=flag[:, 0:1], scalar2=None, op0=ALU.mult)
            if c >= 15:
                S.op("act", "copy", reads=[Rst], writes=[Rstb], out=state_bf[:], in_=state[:])
    S.barrier()
    A.release(m_after_hT)

    mtab, Rmtab = T([128, 17, 128], BF16, "mtab")
    S.dma("pool", writes=[Rmtab], out=mtab[:].rearrange("p a b -> p (a b)"), in_=mtab_d)
    gq64, Rgq64 = bc_load(64, "attn_q_norm_g")
    gk64, Rgk64 = bc_load(64, "attn_k_norm_g")
    gq, Rgq = T([128, 256], F32, "gq")
    gk, Rgk = T([128, 256], F32, "gk")
    S.op("dve", "tensor_scalar", reads=[Rgq64], writes=[Rgq], out=v3(gq[:], 4),
         in0=gq64[:].unsqueeze(1).broadcast_to([128, 4, 64]), scalar1=0.125, scalar2=None, op0=ALU.mult)
    S.op("dve", "tensor_copy", reads=[Rgk64], writes=[Rgk], out=v3(gk[:], 4),
         in_=gk64[:].unsqueeze(1).broadcast_to([128, 4, 64]))
    wq, _ = T([128, 8, 256], BF16, "wq")
    Rwq = [Res() for _ in range(8)]
    wk, _ = T([128, 8, 256], BF16, "wk")
    Rwk = [Res() for _ in range(8)]
    wv, _ = T([128, 8, 256], BF16, "wv")
    Rwv = [Res() for _ in range(8)]
    KT, _ = T([64, 4, WIN], BF16, "KT")
    RKT = [Res() for _ in range(32)]
    QT, _ = T([64, 4, HALF], BF16, "QT")
    RQT = [Res() for _ in range(16)]
    Vaug, _ = T([128, 32, 4, 65], BF16, "Vaug")
    RV = [Res() for _ in range(32)]
    kn = [T([128, 256], BF16, "kn") for _ in range(2)]
    qn_ = [T([128, 256], BF16, "qn_") for _ in range(2)]
    Aexp, RAexp = T([128, 128], F32, "Aexp")
    Erev, REr = T([128, 17, 128], F32, "Erev")
    Pf = [T([128, 512], F32, "Pf") for _ in range(3)]
    Pbf = [T([128, 512], BF16, "Pbf") for _ in range(3)]
    rd, Rrd = T([128, 2], F32, "rd")
    o_all, _ = T([128, 16, 256], BF16, "o_all")
    Roall = [Res() for _ in range(16)]
    oTc = [T([128, 2, 128], BF16, "oTc") for _ in range(2)]
    for r in range(3):
        for k in range(8):
            rows = slice(k * 128, (k + 1) * 128)
            S.dma("pool", writes=[Rwq[k]], out=wq[:, k, :], in_=w_in_d[rows, OFF_DT + r * 256:OFF_DT + (r + 1) * 256])
            S.dma("pool", writes=[Rwk[k]], out=wk[:, k, :], in_=w_in_d[rows, OFF_Q + r * 256:OFF_Q + (r + 1) * 256])
            S.dma("pool", writes=[Rwv[k]], out=wv[:, k, :], in_=w_in_d[rows, OFF_K + r * 256:OFF_K + (r + 1) * 256])

        def proj_mm(blk):
            bs = slice(blk * 128, (blk + 1) * 128)
            pa = blk % 2
            for k in range(8):
                S.op("pe", "matmul", reads=[Rwk[k], RhT[blk]], writes=[RPS[pa]], out=PS[pa][:, 0:256], lhsT=hT[:, k, bs],
                     rhs=wk[:, k, :], start=(k == 0), stop=(k == 7))
            kt, Rkn = kn[blk % 2]
            qknorm(PS[pa][:, 0:256], RPS[pa], 4, 64, gk[:], Rgk, kt[:], Rkn)
            pv = 2 + blk % 2
            for k in range(8):
                S.op("pe", "matmul", reads=[Rwv[k], RhT[blk]], writes=[RPS[pv]], out=PS[pv][:, 0:256], lhsT=hT[:, k, bs],
                     rhs=wv[:, k, :], start=(k == 0), stop=(k == 7))
            if blk >= 16:
                S.op("act", "copy", reads=[RPS[pv]], writes=[RV[blk]], out=Vaug[:, blk, :, 0:64],
                     in_=v3(PS[pv][:, 0:256], 4))
                S.op("pool", "memset", writes=[RV[blk]], ap=Vaug[:, blk, :, 64:65], constant=1.0)
            else:
                S.op("dve", "tensor_scalar", reads=[RPS[pv], Rflag], writes=[RV[blk]], out=Vaug[:, blk, :, 0:64],
                     in0=v3(PS[pv][:, 0:256], 4), scalar1=flag[:, 0:1], scalar2=None, op0=ALU.mult)
                S.op("pool", "tensor_copy", reads=[Rflag], writes=[RV[blk]], out=Vaug[:, blk, :, 64:65],
                     in_=flag[:, 0:1].unsqueeze(1).broadcast_to([128, 4, 1]))
            if blk >= 16:
                pq = 4 + blk % 2
                for k in range(8):
                    S.op("pe", "matmul", reads=[Rwq[k], RhT[blk]], writes=[RPS[pq]], out=PS[pq][:, 0:256], lhsT=hT[:, k, bs],
                         rhs=wq[:, k, :], start=(k == 0), stop=(k == 7))
                qt, Rqn = qn_[blk % 2]
                qknorm(PS[pq][:, 0:256], RPS[pq], 4, 64, gq[:], Rgq, qt[:], Rqn)

        def proj_tr(blk):
            bs = slice(blk * 128, (blk + 1) * 128)
            kt, Rkn = kn[blk % 2]
            transposeN(kt, Rkn, 4, 64, KT[:, :, bs], [RKT[blk]], eng="act")
            if blk >= 16:
                qt, Rqn = qn_[blk % 2]
                transposeN(qt, Rqn, 4, 64, QT[:, :, (blk - 16) * 128:(blk - 15) * 128], [RQT[blk - 16]], eng="act")

        for blk in range(33):
            if blk < 32:
                proj_mm(blk)
            if blk >= 1:
                proj_tr(blk - 1)

        seq = []
        for hh in range(4):
            for qb in range(16, 32):
                kbs = list(range(qb - 16, qb + 1))
                for gi in range(5):
                    seq.append((hh, qb, gi, kbs[gi * 4:gi * 4 + 4]))

        def emit_S(i):
            hh, qb, gi, grp = seq[i]
            qi = qb - 16
            sbk = i % 2
            for ii, kb in enumerate(grp):
                S.op("pe", "matmul", reads=[RKT[kb], RQT[qi]], writes=[RPS[sbk]],
                     out=PS[sbk][:, ii * 128:(ii + 1) * 128], lhsT=KT[:, hh, kb * 128:(kb + 1) * 128],
                     rhs=QT[:, hh, qi * 128:(qi + 1) * 128], start=True, stop=True)

        def emit_rest(i):
            hh, qb, gi, grp = seq[i]
            qi = qb - 16
            n = len(grp)
            sbk = i % 2
            unit = i // 5
            ob = 4 + unit % 2
            pf, Rpf = Pf[i % 3]
            pbf, Rpbf = Pbf[i % 3]
            if qb == 16 and gi == 0:
                h = r * 4 + hh
                slope = 2.0 ** (-8.0 * (h + 1) / 12.0)
                S.op("act", "activation", reads=[Rcst], writes=[RAexp], out=Aexp[:], in_=d0_f, func=AF.Exp, scale=-slope)
                for j in range(17):
                    cpow = math.exp(-slope * 128.0 * (16 - j))
                    S.op("dve", "scalar_tensor_tensor", reads=[RAexp, Rmtab], writes=[REr],
                         out=Erev[:, j, :], in0=Aexp[:], scalar=float(cpow), in1=mtab[:, j, :], op0=ALU.mult,
                         op1=ALU.mult)
            S.op("act", "activation", reads=[RPS[sbk]], writes=[Rpf], out=pf[:, 0:n * 128],
                 in_=PS[sbk][:, 0:n * 128], func=AF.Exp)
            S.op("pool" if i % 3 == 2 else "dve", "tensor_tensor", reads=[Rpf, REr], writes=[Rpbf], out=pbf[:, 0:n * 128],
                 in0=pf[:, 0:n * 128], in1=Erev[:, gi * 4:gi * 4 + n, :].rearrange("p a b -> p (a b)"),
                 op=ALU.mult)
            for ii, kb in enumerate(grp):
                S.op("pe", "matmul", reads=[Rpbf, RV[kb]], writes=[RPS[ob]], out=PS[ob][:, 0:65],
                     lhsT=pbf[:, ii * 128:(ii + 1) * 128], rhs=Vaug[:, kb, hh, :], start=(gi == 0 and ii == 0),
                     stop=(gi == 4))
            if gi == 4:
                S.op("dve", "reciprocal", reads=[RPS[ob]], writes=[Rrd], out=rd[:, 0:1], in_=PS[ob][:, 64:65])
                S.op("dve", "tensor_scalar", reads=[RPS[ob], Rrd], writes=[Roall[qi]],
                     out=o_all[:, qi, hh * 64:(hh + 1) * 64], in0=PS[ob][:, 0:64], scalar1=rd[:, 0:1], scalar2=None,
                     op0=ALU.mult)

        emit_S(0)
        for i in range(len(seq)):
            if i + 1 < len(seq):
                emit_S(i + 1)
            emit_rest(i)
        for qi in range(16):
            for i in range(2):
                S.op("pe", "transpose", reads=[Roall[qi], Ridb], writes=[RPB], out=PB[:, i * 128:(i + 1) * 128],
                     in_=o_all[:, qi, i * 128:(i + 1) * 128], identity=ident_b[:])
            ot, Rot = oTc[qi % 2]
            S.op("act", "copy", reads=[RPB], writes=[Rot], out=ot[:], in_=v3(PB[:, 0:256], 2))
            S.dma("sp", reads=[Rot], writes=[Res()], out=oT_scr[:, 2 * r:2 * r + 2, qi * 128:(qi + 1) * 128], in_=ot[:])
    S.barrier()
    A.release(m_base)

    x1, _ = T([128, 16, 1024], F32, "x1")
    Rx1 = [Res() for _ in range(16)]
    m_after_x1 = A.mark()
    yT_sb, _ = T([128, 10, HALF], BF16, "yT_sb")
    RyT = [Res() for _ in range(10)]
    for kc in range(10):
        S.dma("sp", writes=[RyT[kc]], out=yT_sb[:, kc, :], in_=yT_scr[:, kc, :])
    oT, _ = T([128, 6, HALF], BF16, "oT")
    RoTs = [Res() for _ in range(6)]
    for kc in range(6):
        S.dma("sp", writes=[RoTs[kc]], out=oT[:, kc, :], in_=oT_scr[:, kc, :])
    wout, _ = T([128, 16, 1024], BF16, "wout")
    Rwout = [Res() for _ in range(16)]
    for kc in range(16):
        S.dma("pool", writes=[Rwout[kc]], out=wout[:, kc, :], in_=w_out_d[kc * 128:(kc + 1) * 128, :])
    xo = [T([128, 1024], F32, "xo") for _ in range(2)]
    for tb in range(16):
        ts_ = slice(tb * 128, (tb + 1) * 128)
        xo_t, Rxo = xo[tb % 2]
        S.dma("sp", writes=[Rxo], out=xo_t[:], in_=xw_d[HALF + tb * 128:HALF + (tb + 1) * 128, :])
        for half in range(2):
            pb = 2 * (tb % 2) + half
            cs_ = slice(half * 512, (half + 1) * 512)
            for kc in range(16):
                lhsT = yT_sb[:, kc, ts_] if kc < 10 else oT[:, kc - 10, ts_]
                S.op("pe", "matmul", reads=[RyT[kc] if kc < 10 else RoTs[kc - 10], Rwout[kc]], writes=[RPS[pb]], out=PS[pb][:, :], lhsT=lhsT,
                     rhs=wout[:, kc, cs_], start=(kc == 0), stop=(kc == 15))
            S.op("dve", "tensor_tensor", reads=[RPS[pb], Rxo], writes=[Rx1[tb]], out=x1[:, tb, cs_], in0=PS[pb][:, :],
                 in1=xo_t[:, cs_], op=ALU.add)
    S.barrier()
    A.release(m_after_x1)

    def store_x1_and_finish():
        fin = []
        for tb in range(16):
            R = Res()
            S.dma("sp", reads=[Rx1[tb]], writes=[R], out=out_d[tb * 128:(tb + 1) * 128, :], in_=x1[:, tb, :])
            fin.append(R)
        for R in fin:
            S._wait("sp", R.w)
        S.emit()
        return nc

    if stop_after == "C":
        return store_x1_and_finish()

    wqm, _ = T([128, 8, 512], BF16, "wqm")
    Rwqm = [Res() for _ in range(8)]
    wkv, _ = T([128, 8, 1024], BF16, "wkv")
    Rwkv = [Res() for _ in range(8)]
    wo, _ = T([128, 4, 1024], BF16, "wo")
    Rwo = [Res() for _ in range(4)]
    for k in range(8):
        S.dma("pool", writes=[Rwqm[k]], out=wqm[:, k, :], in_=mem_w_q_d[k * 128:(k + 1) * 128, :])
        S.dma("pool", writes=[Rwkv[k]], out=wkv[:, k, :], in_=mem_w_kv_d[k * 128:(k + 1) * 128, :])
    for k in range(4):
        S.dma("pool", writes=[Rwo[k]], out=wo[:, k, :], in_=mem_w_o_d[k * 128:(k + 1) * 128, :])
    gx, Rgx = bc_load(1024, "xattn_norm_g")
    gm, Rgm = bc_load(1024, "mem_norm_g")
    gqm128, Rgqm128 = bc_load(128, "mem_q_norm_g")
    gkm128, Rgkm128 = bc_load(128, "mem_k_norm_g")
    gqm, Rgqm = T([128, 512], F32, "gqm")
    gkm, Rgkm = T([128, 512], F32, "gkm")
    S.op("dve", "tensor_scalar", reads=[Rgqm128], writes=[Rgqm], out=v3(gqm[:], 4),
         in0=gqm128[:].unsqueeze(1).broadcast_to([128, 4, 128]), scalar1=128.0 ** -0.5, scalar2=None, op0=ALU.mult)
    S.op("dve", "tensor_copy", reads=[Rgkm128], writes=[Rgkm], out=v3(gkm[:], 4),
         in_=gkm128[:].unsqueeze(1).broadcast_to([128, 4, 128]))
    KmT, RKmT = T([128, 4, 256], BF16, "KmT")
    Vm, RVm = T([128, 2, 4, 129], BF16, "Vm")
    S.op("pool", "memset", writes=[RVm], ap=Vm[:].rearrange("p a b c -> p (a b c)"), constant=1.0)
    memb, Rmemb = T([128, 1024], F32, "memb")
    mh, Rmh = T([128, 1024], BF16, "mh")
    mT, RmT = T([128, 8, 128], BF16, "mT")
    knm, Rknm = T([128, 512], BF16, "knm")
    for mb in range(2):
        ms = slice(mb * 128, (mb + 1) * 128)
        S.dma("sp", writes=[Rmemb], out=memb[:], in_=mem_d[ms, :])
        rmsnorm(memb[:], Rmemb, gm, Rgm, mh[:], Rmh)
        transposeN(mh, Rmh, 8, 128, mT[:], [RmT], eng="act")
        for half in range(2):
            for k in range(8):
                S.op("pe", "matmul", reads=[RmT, Rwkv[k]], writes=[RPS[half]], out=PS[half][:, :], lhsT=mT[:, k, :],
                     rhs=wkv[:, k, half * 512:(half + 1) * 512], start=(k == 0), stop=(k == 7))
        qknorm(PS[0][:, :], RPS[0], 4, 128, gkm[:], Rgkm, knm[:], Rknm)
        transposeN(knm, Rknm, 4, 128, KmT[:, :, ms], [RKmT], eng="act")
        S.op("act", "copy", reads=[RPS[1]], writes=[RVm], out=Vm[:, mb, :, 0:128], in_=v3(PS[1][:, :], 4))
    h2, Rh2 = T([128, 1024], BF16, "h2")
    h2T, Rh2T = T([128, 8, 128], BF16, "h2T")
    qn, Rqn2 = T([128, 512], BF16, "qn")
    qT, RqT = T([128, 4, 128], BF16, "qT")
    Pm = [T([128, 256], BF16, "Pm") for _ in range(2)]
    o2, Ro2 = T([128, 512], BF16, "o2")
    o2T, Ro2T = T([128, 4, 128], BF16, "o2T")
    for tb in range(16):
        rmsnorm(x1[:, tb, :], Rx1[tb], gx, Rgx, h2[:], Rh2)
        transposeN(h2, Rh2, 8, 128, h2T[:], [Rh2T], eng="act")
        for k in range(8):
            S.op("pe", "matmul", reads=[Rh2T, Rwqm[k]], writes=[RPS[2]], out=PS[2][:, :], lhsT=h2T[:, k, :], rhs=wqm[:, k, :],
                 start=(k == 0), stop=(k == 7))
        qknorm(PS[2][:, :], RPS[2], 4, 128, gqm[:], Rgqm, qn[:], Rqn2)
        transposeN(qn, Rqn2, 4, 128, qT[:], [RqT], eng="act")
        for hh in range(4):
            sb_ = 3 + hh % 2
            ob = 5 + hh % 2
            pm, Rpm = Pm[hh % 2]
            for mb in range(2):
                S.op("pe", "matmul", reads=[RKmT, RqT], writes=[RPS[sb_]], out=PS[sb_][:, mb * 128:(mb + 1) * 128],
                     lhsT=KmT[:, hh, mb * 128:(mb + 1) * 128], rhs=qT[:, hh, :], start=True, stop=True)
            S.op("act", "activation", reads=[RPS[sb_]], writes=[Rpm], out=pm[:], in_=PS[sb_][:, 0:256], func=AF.Exp)
            for mb in range(2):
                S.op("pe", "matmul", reads=[Rpm, RVm], writes=[RPS[ob]], out=PS[ob][:, 0:129],
                     lhsT=pm[:, mb * 128:(mb + 1) * 128], rhs=Vm[:, mb, hh, :], start=(mb == 0), stop=(mb == 1))
            S.op("dve", "reciprocal", reads=[RPS[ob]], writes=[Rrd], out=rd[:, 0:1], in_=PS[ob][:, 128:129])
            S.op("dve", "tensor_scalar", reads=[RPS[ob], Rrd], writes=[Ro2], out=o2[:, hh * 128:(hh + 1) * 128],
                 in0=PS[ob][:, 0:128], scalar1=rd[:, 0:1], scalar2=None, op0=ALU.mult)
        transposeN(o2, Ro2, 4, 128, o2T[:], [Ro2T], eng="act")
        for half in range(2):
            cs_ = slice(half * 512, (half + 1) * 512)
            for kc in range(4):
                S.op("pe", "matmul", reads=[Ro2T, Rwo[kc]], writes=[RPS[half]], out=PS[half][:, :], lhsT=o2T[:, kc, :],
                     rhs=wo[:, kc, cs_], start=(kc == 0), stop=(kc == 3))
            S.op("dve", "tensor_tensor", reads=[RPS[half], Rx1[tb]], writes=[Rx1[tb]], out=x1[:, tb, cs_],
                 in0=PS[half][:, :], in1=x1[:, tb, cs_], op=ALU.add)
    if stop_after == "D":
        S.barrier()
        return store_x1_and_finish()
    Rx2d = [Res() for _ in range(16)]
    for tb in range(16):
        S.dma("sp", reads=[Rx1[tb]], writes=[Rx2d[tb]], out=out_d[tb * 128:(tb + 1) * 128, :], in_=x1[:, tb, :])
    S.barrier()
    A.release(m_base)

    wpq, _ = T([128, 8, 2048], BF16, "wpq")
    Rwpq = [Res() for _ in range(8)]
    for k in range(8):
        S.dma("pool", writes=[Rwpq[k]], out=wpq[:, k, :], in_=peer_wq_d[k * 128:(k + 1) * 128, :])
    keysT, RkeysT = T([128, 2, 128], F32, "keysT")
    S.dma("sp", writes=[RkeysT], out=keysT[:, 0, :], in_=keys1T_d)
    S.dma("sp", writes=[RkeysT], out=keysT[:, 1, :], in_=keys2T_d)
    gf, Rgf = bc_load(1024, "ffn_norm_g")
    xrp = [T([128, 1024], F32, "xr") for _ in range(2)]
    h3p = [T([128, 1024], F32, "h3") for _ in range(2)]
    h3bp = [T([128, 1024], BF16, "h3b") for _ in range(2)]
    h3T, Rh3T = T([128, 8, 128], BF16, "h3T")
    off_q = A.mark()
    qrT, RqrT = T([128, 16, 128], F32, "qrT")
    eq = nc.alloc_sbuf_tensor_at("eq_alias", [128, 8, 16, 16], F32, offset=off_q)
    Req = RqrT
    off_s = A.mark()
    sc, Rsc = T([128, 16, 128], F32, "sc")
    cand = nc.alloc_sbuf_tensor_at("cand_alias", [128, 8, 256], F32, offset=off_s)
    Rcand = Rsc
    sc2, Rsc2 = T([128, 128], F32, "sc2")
    tv, Rtv = T([128, 16, 16], F32, "tv")
    ti, Rti = T([128, 16, 16], U32, "ti")
    tif, Rtif = T([128, 16, 16], F32, "tif")
    cand2, Rcand2 = T([128, 256], F32, "cand2")
    cv, Rcv = T([128, 8, 16], F32, "cv")
    cp, Rcp = T([128, 8, 16], U32, "cp")
    ca, Rca = T([128, 8, 16], U32, "ca")
    cb_, Rcb_ = T([128, 8, 16], U32, "cb")
    caf, Rcaf = T([128, 8, 16], F32, "caf")
    cbf, Rcbf = T([128, 8, 16], F32, "cbf")
    i1f, Ri1f = T([128, 8, 16], F32, "i1f")
    i2f, Ri2f = T([128, 8, 16], F32, "i2f")
    ef, Ref = T([128, 128], F32, "ef")
    eip = [T([128, 128], I32, "ei") for _ in range(2)]
    gtp = [T([128, 8, 16], F32, "gt") for _ in range(2)]
    gsum, Rgsum = T([128, 8], F32, "gsum")
    actv, _ = T([128, 128], F32, "actv")
    Ractv = [Res() for _ in range(32)]
    WG = [T([128, 4], F32, "wg4") for _ in range(2)]
    wgt, _ = T([128, 128], F32, "wgt")
    Rwgt = [Res() for _ in range(32)]
    djunk, Rdj = T([128, 1024], BF16, "djunk")
    NUV = 6
    UVt = [(T([128, 4, 2048], BF16, "UV")[0], [Res() for _ in range(4)]) for _ in range(NUV)]
    SV = [T([128, 1024], BF16, "SV") for _ in range(4)]
    fin = []

    def peer_setup(tb, p):
        xr, Rxr = xrp[p]
        h3, Rh3 = h3p[p]
        h3b, Rh3b = h3bp[p]
        ei, Rei = eip[p]
        gt, Rgt = gtp[p]
        S.dma("sp", reads=[Rx2d[tb]], writes=[Rxr], out=xr[:], in_=out_d[tb * 128:(tb + 1) * 128, :])
        rmsnorm(xr[:], Rxr, gf, Rgf, h3[:], Rh3)
        S.op("pool", "tensor_copy", reads=[Rh3], writes=[Rh3b], out=h3b[:], in_=h3[:])
        transposeN(h3b, Rh3b, 8, 128, h3T[:], [Rh3T], eng="act")
        yield
        for q4 in range(4):
            pb = q4 % 2
            for i in range(4):
                ccq = q4 * 4 + i
                for k in range(8):
                    S.op("pe", "matmul", reads=[Rwpq[k], Rh3T], writes=[RPS[pb]], out=PS[pb][:, i * 128:(i + 1) * 128],
                         lhsT=wpq[:, k, ccq * 128:(ccq + 1) * 128], rhs=h3T[:, k, :], start=(k == 0), stop=(k == 7))
            S.op("act", "copy", reads=[RPS[pb]], writes=[RqrT], out=qrT[:, q4 * 4:q4 * 4 + 4, :], in_=v3(PS[pb][:, :], 4))
            yield
        for q4 in range(4):
            pb = 2 + q4 % 2
            for i in range(4):
                j = q4 * 4 + i
                S.op("pe", "matmul", reads=[RqrT, RkeysT], writes=[RPS[pb]], out=PS[pb][:, i * 128:(i + 1) * 128],
                     lhsT=qrT[:, j, :], rhs=keysT[:, j % 2, :], start=True, stop=True)
            S.op("act", "copy", reads=[RPS[pb]], writes=[Rsc], out=sc[:, q4 * 4:q4 * 4 + 4, :], in_=v3(PS[pb][:, :], 4))
        for j in range(16):
            S.op("dve", "max", reads=[Rsc], writes=[Rtv], out=tv[:, j, 0:8], in_=sc[:, j, :])
            S.op("dve", "max_index", reads=[Rsc, Rtv], writes=[Rti], out=ti[:, j, 0:8], in_max=tv[:, j, 0:8],
                 in_values=sc[:, j, :])
            S.op("dve", "match_replace", reads=[Rsc, Rtv], writes=[Rsc2], out=sc2[:], in_to_replace=tv[:, j, 0:8],
                 in_values=sc[:, j, :], imm_value=-1e30)
            S.op("dve", "max", reads=[Rsc2], writes=[Rtv], out=tv[:, j, 8:16], in_=sc2[:])
            S.op("dve", "max_index", reads=[Rsc2, Rtv], writes=[Rti], out=ti[:, j, 8:16], in_max=tv[:, j, 8:16],
                 in_values=sc2[:])
            yield
        S.op("dve", "tensor_copy", reads=[Rti], writes=[Rtif], out=tif[:], in_=ti[:])
        tv4 = tv[:].rearrange("p (h two) k -> p h two k", two=2)
        tif4 = tif[:].rearrange("p (h two) k -> p h two k", two=2)
        S.op("dve", "tensor_tensor", reads=[Rtv], writes=[Rcand], out=cand[:].rearrange("p h (a b) -> p h a b", a=16),
             in0=tv4[:, :, 0, :].unsqueeze(3).broadcast_to([128, 8, 16, 16]),
             in1=tv4[:, :, 1, :].unsqueeze(2).broadcast_to([128, 8, 16, 16]), op=ALU.add)
        for h in range(8):
            S.op("dve", "max", reads=[Rcand], writes=[Rcv], out=cv[:, h, 0:8], in_=cand[:, h, :])
            S.op("dve", "max_index", reads=[Rcand, Rcv], writes=[Rcp], out=cp[:, h, 0:8], in_max=cv[:, h, 0:8],
                 in_values=cand[:, h, :])
            S.op("dve", "match_replace", reads=[Rcand, Rcv], writes=[Rcand2], out=cand2[:], in_to_replace=cv[:, h, 0:8],
                 in_values=cand[:, h, :], imm_value=-1e30)
            S.op("dve", "max", reads=[Rcand2], writes=[Rcv], out=cv[:, h, 8:16], in_=cand2[:])
            S.op("dve", "max_index", reads=[Rcand2, Rcv], writes=[Rcp], out=cp[:, h, 8:16], in_max=cv[:, h, 8:16],
                 in_values=cand2[:])
            yield
        S.op("dve", "tensor_single_scalar", reads=[Rcp], writes=[Rca], out=ca[:], in_=cp[:], scalar=4,
             op=ALU.logical_shift_right)
        S.op("dve", "tensor_single_scalar", reads=[Rcp], writes=[Rcb_], out=cb_[:], in_=cp[:], scalar=15,
             op=ALU.bitwise_and)
        S.op("dve", "tensor_copy", reads=[Rca], writes=[Rcaf], out=caf[:], in_=ca[:])
        S.op("dve", "tensor_copy", reads=[Rcb_], writes=[Rcbf], out=cbf[:], in_=cb_[:])
        io4 = iota16.unsqueeze(1).unsqueeze(1).broadcast_to([128, 8, 16, 16])
        for (sel, Rsel, half, dst, Rdst) in ((caf, Rcaf, 0, i1f, Ri1f), (cbf, Rcbf, 1, i2f, Ri2f)):
            S.op("dve", "tensor_tensor", reads=[Rsel, Rcst], writes=[Req], out=eq[:],
                 in0=sel[:].unsqueeze(3).broadcast_to([128, 8, 16, 16]), in1=io4, op=ALU.is_equal)
            S.op("dve", "tensor_tensor", reads=[Req, Rtif], writes=[Req], out=eq[:], in0=eq[:],
                 in1=tif4[:, :, half, :].unsqueeze(2).broadcast_to([128, 8, 16, 16]), op=ALU.mult)
            S.op("dve", "tensor_reduce", reads=[Req], writes=[Rdst], out=dst[:].rearrange("p h k -> p (h k)"),
                 in_=eq[:].rearrange("p h k a -> p (h k) a"), axis=AX.X, op=ALU.add)
        S.op("dve", "scalar_tensor_tensor", reads=[Ri1f, Ri2f], writes=[Ref], out=ef[:],
             in0=i1f[:].rearrange("p h k -> p (h k)"), scalar=128.0, in1=i2f[:].rearrange("p h k -> p (h k)"),
             op0=ALU.mult, op1=ALU.add)
        S.op("dve", "tensor_copy", reads=[Ref], writes=[Rei], out=ei[:], in_=ef[:])
        S.op("dve", "tensor_tensor", reads=[Rcv], writes=[Rgt], out=gt[:], in0=cv[:],
             in1=cv[:, :, 0:1].broadcast_to([128, 8, 16]), op=ALU.subtract)
        S.op("act", "activation", reads=[Rgt], writes=[Rgt], out=gt[:], in_=gt[:], func=AF.Exp)
        S.op("dve", "tensor_reduce", reads=[Rgt], writes=[Rgsum], out=gsum[:], in_=gt[:], axis=AX.X, op=ALU.add)
        S.op("dve", "reciprocal", reads=[Rgsum], writes=[Rgsum], out=gsum[:], in_=gsum[:])
        S.op("dve", "tensor_tensor", reads=[Rgt, Rgsum], writes=[Rgt], out=gt[:], in0=gt[:],
             in1=gsum[:].unsqueeze(2).broadcast_to([128, 8, 16]), op=ALU.mult)

    def peer_gather(tb, p, bg):
        xr, Rxr = xrp[p]
        h3b, Rh3b = h3bp[p]
        ei, Rei = eip[p]
        gt, Rgt = gtp[p]
        gtf = gt[:].rearrange("p h k -> p (h k)")
        S.op("pool", "memset", writes=Ractv, ap=actv[:], constant=0.0)

        def stA(g4):
            uvt, Ruvt = UVt[g4 % NUV]
            for i in range(4):
                s_ = g4 * 4 + i
                S.dma("pool", reads=[Rei, Ruv], writes=[Ruvt[i]], meth="indirect_dma_start", out=uvt[:, i, :],
                      out_offset=None, in_=uv_scr, in_offset=bass.IndirectOffsetOnAxis(ap=ei[:, s_:s_ + 1], axis=0))

        def stBC(g4):
            uvt, Ruvt = UVt[g4 % NUV]
            for i in range(4):
                s_ = g4 * 4 + i
                S.op("dve", "scalar_tensor_tensor", reads=[Ruvt[i], Rh3b], writes=[Ractv[g4]], out=djunk[:],
                     in0=uvt[:, i, 0:1024], scalar=1.0, in1=h3b[:], op0=ALU.mult, op1=ALU.mult,
                     accum_out=actv[:, s_:s_ + 1])
            wg, Rwg = WG[g4 % 2]
            S.op("act", "activation", reads=[Ractv[g4]], writes=[Rwg], out=wg[:], in_=actv[:, g4 * 4:g4 * 4 + 4],
                 func=AF.Gelu)

        def stDE(g4):
            uvt, Ruvt = UVt[g4 % NUV]
            wg, Rwg = WG[g4 % 2]
            S.op("dve", "tensor_tensor", reads=[Rwg, Rgt], writes=[Rwgt[g4]], out=wgt[:, g4 * 4:g4 * 4 + 4], in0=wg[:],
                 in1=gtf[:, g4 * 4:g4 * 4 + 4], op=ALU.mult)
            for i in range(4):
                s_ = g4 * 4 + i
                sv, Rsv = SV[s_ % 4]
                S.op("act", "activation", reads=[Ruvt[i], Rwgt[g4]], writes=[Rsv], out=sv[:], in_=uvt[:, i, 1024:2048],
                     func=AF.Copy, scale=wgt[:, s_:s_ + 1])
                for half in range(2):
                    S.op("pe", "matmul", reads=[Rsv, Ridb], writes=[RPS[5 + half]], out=PS[5 + half][:, :],
                         lhsT=ident_b[:], rhs=sv[:, half * 512:(half + 1) * 512], start=(s_ == 0), stop=(s_ == 127))

        for g4 in range(NUV - 2):
            stA(g4)
        for g4 in range(33):
            if g4 + NUV - 2 < 32:
                stA(g4 + NUV - 2)
            if g4 < 32:
                stBC(g4)
            if g4 >= 1:
                stDE(g4 - 1)
            if bg is not None:
                next(bg, None)
                next(bg, None)
        if bg is not None:
            for _ in bg:
                pass
        for half in range(2):
            cs_ = slice(half * 512, (half + 1) * 512)
            S.op("dve", "tensor_tensor", reads=[RPS[5 + half], Rxr], writes=[Rxr], out=xr[:, cs_],
                 in0=xr[:, cs_], in1=PS[5 + half][:, :], op=ALU.add)
        R = Res()
        S.dma("sp", reads=[Rxr], writes=[R, Rx2d[tb]], out=out_d[tb * 128:(tb + 1) * 128, :], in_=xr[:])
        fin.append(R)

    for _ in peer_setup(0, 0):
        pass
    for tb in range(16):
        bg = peer_setup(tb + 1, (tb + 1) % 2) if tb + 1 < 16 else None
        peer_gather(tb, tb % 2, bg)
    for R in fin:
        S._wait("sp", R.w)
    S.emit()
    return nc


def _consts():
    c = np.zeros((128, NCONST), np.float32)
    p = np.arange(128)
    c[:, 0:128] = np.eye(128)
    c[:, 128:256] = (p[:, None] <= p[None, :])
    c[:, 256:384] = 1.0
    c[:, 384:512] = (p[None, :] - p[:, None])
    c[:, 512:528] = np.arange(16)[None, :]
    k = p[:, None, None]
    j = np.arange(17)[None, :, None]
    q = p[None, None, :]
    delta = (16 - j) * 128 + q - k
    m = ((delta >= 0) & (delta <= 128)).astype(np.float32)
    m += ((delta >= 0) & (delta <= 512) & (delta % 4 == 0))
    m += ((delta >= 0) & (delta <= 2048) & (delta % 16 == 0))
    return c, np.ascontiguousarray(m.reshape(128, 17 * 128).astype(np.float32))


_NC_CACHE = {}


def kernel(**inputs):
    stop_after = os.environ.get("KSTOP") or None
    x = np.asarray(inputs["x"], np.float32)
    mem = np.asarray(inputs["mem"], np.float32)
    cst, mtab = _consts()
    common = {
        "cst": cst, "mtab": mtab,
        "w_in": np.ascontiguousarray(inputs["w_in"][0], dtype=np.float32),
        "convw": np.ascontiguousarray(
            np.asarray(inputs["conv_w"][0], np.float32).T.reshape(18, 128, 4).transpose(1, 0, 2).reshape(128, 72)),
        "convb": np.ascontiguousarray(np.asarray(inputs["conv_b"][0], np.float32).reshape(18, 128).T),
        "w_out": np.ascontiguousarray(inputs["w_out"][0], dtype=np.float32),
    }
    for nm in ("mix_norm_g", "dt_bias", "a_log", "d_skip", "ssd_norm_g", "attn_q_norm_g", "attn_k_norm_g",
               "xattn_norm_g", "mem_norm_g", "mem_q_norm_g", "mem_k_norm_g", "ffn_norm_g"):
        common[nm] = np.ascontiguousarray(np.asarray(inputs[nm][0], np.float32).reshape(1, -1))
    if True:
        common["mem_w_q"] = np.ascontiguousarray(inputs["mem_w_q"][0], dtype=np.float32)
        common["mem_w_kv"] = np.ascontiguousarray(inputs["mem_w_kv"][0], dtype=np.float32)
        common["mem_w_o"] = np.ascontiguousarray(inputs["mem_w_o"][0], dtype=np.float32)
    if True:
        common["peer_w_query"] = np.ascontiguousarray(inputs["peer_w_query"][0], dtype=np.float32)
        common["keys1T"] = np.ascontiguousarray(np.asarray(inputs["peer_sub_keys1"][0], np.float32).T)
        common["keys2T"] = np.ascontiguousarray(np.asarray(inputs["peer_sub_keys2"][0], np.float32).T)
        common["peer_u"] = np.ascontiguousarray(inputs["peer_u"][0], dtype=np.float32)
        common["peer_v"] = np.ascontiguousarray(inputs["peer_v"][0], dtype=np.float32)
    in_maps = []
    for c in range(8):
        b, j = c // 2, c % 2
        xw = np.zeros((WIN, D), np.float32)
        if j == 1:
            xw[:HALF] = x[b, :HALF]
        xw[HALF:] = x[b, j * HALF:(j + 1) * HALF]
        m = dict(common)
        m["xw"] = xw
        m["flag"] = np.full((128, 1), float(j), np.float32)
        if True:
            m["mem"] = np.ascontiguousarray(mem[b])
        in_maps.append(m)
    nc = build(stop_after)
    res = run_bass_kernel_spmd(nc, in_maps, core_ids=list(range(8)))
    out = np.zeros((NB, SEQ, D), np.float32)
    for c in range(8):
        b, j = c // 2, c % 2
        out[b, j * HALF:(j + 1) * HALF] = np.asarray(res.results[c]["out"])
    return out
```

```python
import os
import math
import numpy as np
import concourse.bass as bass
import concourse.mybir as mybir
from concourse.bass_utils import run_bass_kernel_spmd

F32 = mybir.dt.float32
BF16 = mybir.dt.bfloat16
I32 = mybir.dt.int32
U32 = mybir.dt.uint32
AF = mybir.ActivationFunctionType
ALU = mybir.AluOpType
AX = mybir.AxisListType

D = 1024
NB = 4
SEQ = 4096
HALF = 2048
WIN = 4096
EPS = 1e-6
OFF_Z = 1280
OFF_XBC = OFF_Z + 2304
OFF_DT = OFF_XBC + 20
OFF_Q = OFF_DT + 768
OFF_K = OFF_Q + 768
IN_COLS = OFF_K + 768
NCONST = 128 * 4 + 16


class Res:
    __slots__ = ("w", "r")

    def __init__(self):
        self.w = None
        self.r = {}


class Sched:
    COMPUTE = ("pe", "act", "dve", "pool")
    NDMA = 12

    def __init__(self, nc):
        self.nc = nc
        self.streams = {k: [] for k in ("pe", "act", "dve", "pool", "sp")}
        self.esem = {k: nc.alloc_semaphore("es_" + k) for k in self.COMPUTE}
        self.ecnt = {k: 0 for k in self.COMPUTE}
        self.dsem = {q: [nc.alloc_semaphore("ds_%s%d" % (q, i)) for i in range(self.NDMA)]
                     for q in ("sp", "pool")}
        self.dcnt = {q: [0] * self.NDMA for q in ("sp", "pool")}
        self.dnext = {q: 0 for q in ("sp", "pool")}
        self.known = {k: {} for k in self.streams}
        self.nwaits = 0
        self.nops = 0

    def _sem_of(self, pid):
        if pid[0] == "e":
            return self.esem[pid[1]], pid[2], ("e", pid[1])
        return self.dsem[pid[1]][pid[2]], pid[3], ("d", pid[1], pid[2])

    def _wait(self, eng, pid):
        sem, val, key = self._sem_of(pid)
        if self.known[eng].get(key, 0) >= val:
            return
        self.known[eng][key] = val
        self.nwaits += 1
        self.streams[eng].append(lambda e, sem=sem, val=val: e.wait_ge(sem, val))

    def _deps(self, eng, reads, writes):
        deps = []
        for r in reads:
            if r.w is not None:
                deps.append(r.w)
        for w in writes:
            if w.w is not None:
                deps.append(w.w)
            deps.extend(w.r.values())
        for pid in deps:
            if pid[0] == "e" and pid[1] == eng and eng == "pe":
                continue
            self._wait(eng, pid)

    def _commit(self, pid, reads, writes):
        key = pid[:2] if pid[0] == "e" else pid[:3]
        for r in reads:
            r.r[key] = pid
        for w in writes:
            w.w = pid
            w.r = {}

    def op(self, eng, meth, reads=(), writes=(), **kw):
        fn = lambda e, meth=meth, kw=kw: getattr(e, meth)(**kw)
        self._deps(eng, reads, writes)
        self.ecnt[eng] += 1
        idx = self.ecnt[eng]
        sem = self.esem[eng]
        self.streams[eng].append(lambda e, fn=fn, sem=sem: fn(e).then_inc(sem, 1))
        self.nops += 1
        self._commit(("e", eng, idx), reads, writes)

    def dma(self, q, reads=(), writes=(), meth="dma_start", **kw):
        fn = lambda e, meth=meth, kw=kw: getattr(e, meth)(**kw)
        i = self.dnext[q]
        self.dnext[q] = (i + 1) % self.NDMA
        prev = self.dcnt[q][i]
        if prev > 0:
            self._wait(q, ("d", q, i, prev))
        self._deps(q, reads, writes)
        self.dcnt[q][i] = prev + 16
        sem = self.dsem[q][i]
        self.streams[q].append(lambda e, fn=fn, sem=sem: fn(e).then_inc(sem, 16))
        self.nops += 1
        self._commit(("d", q, i, prev + 16), reads, writes)

    def barrier(self):
        for eng in self.streams:
            for k in self.COMPUTE:
                if k != eng and self.ecnt[k] > 0:
                    self._wait(eng, ("e", k, self.ecnt[k]))
            for q in self.dsem:
                for i in range(self.NDMA):
                    if self.dcnt[q][i] > 0:
                        self._wait(eng, ("d", q, i, self.dcnt[q][i]))

    def emit(self):
        nc = self.nc
        with nc.Block() as block:
            @block.tensor
            def _(e):
                for it in self.streams["pe"]:
                    it(e)

            @block.scalar
            def _(e):
                for it in self.streams["act"]:
                    it(e)

            @block.vector
            def _(e):
                for it in self.streams["dve"]:
                    it(e)

            @block.gpsimd
            def _(e):
                for it in self.streams["pool"]:
                    it(e)

            @block.sync
            def _(e):
                for it in self.streams["sp"]:
                    it(e)


class Arena:
    def __init__(self, nc):
        self.nc = nc
        self.off = (nc.sbuf_base + 31) // 32 * 32
        self.top = nc.sbuf_top
        self.n = 0

    def alloc(self, shape, dt=F32, name="t"):
        sz = {F32: 4, BF16: 2, I32: 4, U32: 4}[dt]
        nb = int(np.prod(shape[1:])) * sz
        off = self.off
        self.off += (nb + 31) // 32 * 32
        assert self.off <= self.top, "SBUF overflow at %s: %d > %d" % (name, self.off, self.top)
        self.n += 1
        return self.nc.alloc_sbuf_tensor_at("%s_%d" % (name, self.n), list(shape), dt, offset=off)

    def mark(self):
        return self.off

    def release(self, m):
        self.off = m


def build(stop_after=None):
    nc = bass.Bass("TRN2", target_bir_lowering=False)
    S = Sched(nc)
    A = Arena(nc)

    def din(name, shape, dt=F32):
        return nc.dram_tensor(name, list(shape), dt, kind="ExternalInput").ap()

    xw_d = din("xw", [WIN, D])
    flag_d = din("flag", [128, 1])
    mem_d = din("mem", [256, D])
    cst_d = din("cst", [128, NCONST])
    mtab_d = din("mtab", [128, 17 * 128])
    w_in_d = din("w_in", [D, IN_COLS])
    convw_d = din("convw", [128, 72])
    convb_d = din("convb", [128, 18])
    vec_d = {}
    for nm, n in [("mix_norm_g", 1024), ("dt_bias", 20), ("a_log", 20), ("d_skip", 20), ("ssd_norm_g", 1280),
                  ("attn_q_norm_g", 64), ("attn_k_norm_g", 64), ("xattn_norm_g", 1024), ("mem_norm_g", 1024),
                  ("mem_q_norm_g", 128), ("mem_k_norm_g", 128), ("ffn_norm_g", 1024)]:
        vec_d[nm] = din(nm, [1, n])
    w_out_d = din("w_out", [2048, D])
    mem_w_q_d = din("mem_w_q", [D, 512])
    mem_w_kv_d = din("mem_w_kv", [D, 1024])
    mem_w_o_d = din("mem_w_o", [512, D])
    peer_wq_d = din("peer_w_query", [D, 2048])
    keys1T_d = din("keys1T", [128, 128])
    keys2T_d = din("keys2T", [128, 128])
    peer_u_d = din("peer_u", [16384, D])
    peer_v_d = din("peer_v", [16384, D])
    out_d = nc.dram_tensor("out", [HALF, D], F32, kind="ExternalOutput").ap()
    yT_scr = nc.dram_tensor("yT_scr", [128, 10, HALF], BF16).ap()
    oT_scr = nc.dram_tensor("oT_scr", [128, 6, HALF], BF16).ap()
    uv_scr = nc.dram_tensor("uv_scr", [16384, 2048], BF16).ap()
    Ruv = Res()
    conv_jobs = [(tbl, h) for h in range(32) for tbl in (0, 1)]

    def issue_conv(n):
        for _ in range(n):
            if not conv_jobs or stop_after is not None:
                return
            tbl, h = conv_jobs.pop(0)
            src = peer_u_d if tbl == 0 else peer_v_d
            S.dma("pool", writes=[Ruv], out=uv_scr[h * 512:(h + 1) * 512, tbl * 1024:(tbl + 1) * 1024],
                  in_=src[h * 512:(h + 1) * 512, :])

    PS = [nc.alloc_psum_tensor("ps%d" % i, [128, 512], F32) for i in range(7)]
    RPS = [Res() for _ in range(7)]
    PB = nc.alloc_psum_tensor("psb", [128, 1024], BF16)
    RPB = Res()

    def T(shape, dt=F32, name="t"):
        return A.alloc(shape, dt, name), Res()

    def v3(ap, a):
        return ap.rearrange("p (a b) -> p a b", a=a)

    cst, Rcst = T([128, NCONST], F32, "cst")
    S.dma("sp", writes=[Rcst], out=cst[:], in_=cst_d)
    ident_f = cst[:, 0:128]
    tri_f = cst[:, 128:256]
    ones_f = cst[:, 256:384]
    d0_f = cst[:, 384:512]
    iota16 = cst[:, 512:528]
    ident_b, Ridb = T([128, 128], BF16, "identb")
    S.op("dve", "tensor_copy", reads=[Rcst], writes=[Ridb], out=ident_b[:], in_=ident_f)
    flag, Rflag = T([128, 1], F32, "flag")
    S.dma("sp", writes=[Rflag], out=flag[:], in_=flag_d)
    epsT, Reps = T([128, 1], F32, "eps")
    S.op("pool", "memset", writes=[Reps], ap=epsT[:], constant=EPS)

    def bc_load(n, name):
        t, R = T([128, n], F32, name)
        S.dma("sp", writes=[R], out=t[:], in_=vec_d[name].broadcast_to([128, n]))
        return t, R

    def rstd_from(ss_ap, Rss, n, H, out_ap, Rout, lnv_ap, Rln):
        Rss_l = Rss if isinstance(Rss, list) else [Rss]
        S.op("act", "activation", reads=Rss_l + [Reps], writes=[Rln], out=lnv_ap, in_=ss_ap, func=AF.Ln,
             bias=epsT[:, 0:1], scale=1.0 / n)
        S.op("act", "activation", reads=[Rln], writes=[Rout], out=out_ap, in_=lnv_ap, func=AF.Exp, scale=-0.5)

    nrm_junk, Rnj = T([128, 1024], BF16, "nrmjunk")
    nrm_ss, Rnss = T([128, 4], F32, "nrmss")
    nrm_ln, Rnln = T([128, 4], F32, "nrmln")
    nrm_rs, Rnrs = T([128, 4], F32, "nrmrs")

    def rmsnorm(x_ap, Rx, g_bc, Rg, out_ap, Rout):
        S.op("act", "activation", reads=[Rx], writes=[Rnj, Rnss], out=nrm_junk[:], in_=x_ap, func=AF.Square,
             accum_out=nrm_ss[:, 0:1])
        rstd_from(nrm_ss[:, 0:1], Rnss, 1024, 1, nrm_rs[:, 0:1], Rnrs, nrm_ln[:, 0:1], Rnln)
        S.op("dve", "scalar_tensor_tensor", reads=[Rx, Rnrs, Rg], writes=[Rout], out=out_ap, in0=x_ap,
             scalar=nrm_rs[:, 0:1], in1=g_bc[:], op0=ALU.mult, op1=ALU.mult)

    def transposeN(src_bf, Rsrc, n, width, dst3, Rdst, eng="act"):
        for k0 in range(0, n, 8):
            m = min(8, n - k0)
            for k in range(m):
                S.op("pe", "transpose", reads=[Rsrc, Ridb], writes=[RPB], out=PB[0:width, k * 128:(k + 1) * 128],
                     in_=src_bf[:, (k0 + k) * width:(k0 + k + 1) * width], identity=ident_b[:])
            src = v3(PB[0:width, 0:m * 128], m)
            dst = dst3[:, k0:k0 + m, :]
            if eng == "act":
                S.op("act", "copy", reads=[RPB], writes=Rdst, out=dst, in_=src)
            else:
                S.op("dve", "tensor_copy", reads=[RPB], writes=Rdst, out=dst, in_=src)

    qk_sq, Rqsq = T([128, 512], F32, "qksq")
    qk_tmp, Rqtmp = T([128, 512], F32, "qktmp")

    def qknorm(ps_ap, Rps, H, dh, g_bc, Rg, out_bf, Rout):
        n = H * dh
        S.op("act", "activation", reads=[Rps], writes=[Rqsq], out=qk_sq[:, 0:n], in_=ps_ap, func=AF.Square)
        S.op("dve", "tensor_reduce", reads=[Rqsq], writes=[Rnss], out=nrm_ss[:, 0:H], in_=v3(qk_sq[:, 0:n], H),
             axis=AX.X, op=ALU.add)
        rstd_from(nrm_ss[:, 0:H], Rnss, dh, H, nrm_rs[:, 0:H], Rnrs, nrm_ln[:, 0:H], Rnln)
        S.op("dve", "tensor_tensor", reads=[Rps, Rnrs], writes=[Rqtmp], out=v3(qk_tmp[:, 0:n], H), in0=v3(ps_ap, H),
             in1=nrm_rs[:, 0:H].unsqueeze(2).broadcast_to([128, H, dh]), op=ALU.mult)
        S.op("dve", "tensor_tensor", reads=[Rqtmp, Rg], writes=[Rout], out=out_bf, in0=qk_tmp[:, 0:n], in1=g_bc,
             op=ALU.mult)

    m_base = A.mark()
    hT, _ = T([128, 8, WIN], BF16, "hT")
    RhT = [Res() for _ in range(32)]
    m_after_hT = A.mark()

    gmix, Rgmix = bc_load(1024, "mix_norm_g")
    xb = [T([128, 1024], F32, "xb") for _ in range(2)]
    hb = [T([128, 1024], BF16, "hb") for _ in range(2)]
    for blk in range(32):
        x_t, Rx = xb[blk % 2]
        h_t, Rh = hb[blk % 2]
        S.dma("sp", writes=[Rx], out=x_t[:], in_=xw_d[blk * 128:(blk + 1) * 128, :])
        rmsnorm(x_t[:], Rx, gmix, Rgmix, h_t[:], Rh)
        transposeN(h_t, Rh, 8, 128, hT[:, :, blk * 128:(blk + 1) * 128], [RhT[blk]], eng="act" if blk % 2 else "dve")
    S.barrier()
    A.release(m_after_hT)

    wxbc, _ = T([128, 8, 2304], BF16, "wxbc")
    Rwxbc = [Res() for _ in range(8)]
    wz, _ = T([128, 8, 1280], BF16, "wz")
    Rwz = [Res() for _ in range(8)]
    wdt, _ = T([128, 8, 20], BF16, "wdt")
    Rwdt = [Res() for _ in range(8)]
    for k in range(8):
        S.dma("pool", writes=[Rwxbc[k]], out=wxbc[:, k, :], in_=w_in_d[k * 128:(k + 1) * 128, OFF_Z:OFF_XBC])
        S.dma("pool", writes=[Rwz[k]], out=wz[:, k, :], in_=w_in_d[k * 128:(k + 1) * 128, 0:OFF_Z])
        S.dma("pool", writes=[Rwdt[k]], out=wdt[:, k, :], in_=w_in_d[k * 128:(k + 1) * 128, OFF_XBC:OFF_DT])
    convw, Rcw = T([128, 18, 4], F32, "convw")
    convb, Rcb = T([128, 18], F32, "convb")
    S.dma("sp", writes=[Rcw], out=convw[:].rearrange("p a b -> p (a b)"), in_=convw_d)
    S.dma("sp", writes=[Rcb], out=convb[:], in_=convb_d)
    dtb, Rdtb = bc_load(20, "dt_bias")
    alog, Ralog = bc_load(20, "a_log")
    dsk, Rdsk = bc_load(20, "d_skip")
    gssd, Rgssd = bc_load(1280, "ssd_norm_g")
    a_bc, Ra = T([128, 20], F32, "a_bc")
    S.op("act", "activation", reads=[Ralog], writes=[Ra], out=a_bc[:], in_=alog[:], func=AF.Exp)
    S.op("dve", "tensor_scalar", reads=[Ra], writes=[Ra], out=a_bc[:], in0=a_bc[:], scalar1=-1.0, scalar2=None,
         op0=ALU.mult)
    halo, _ = T([128, 18, 3], F32, "halo")
    Rhalo = [Res() for _ in range(18)]
    S.op("pool", "memset", writes=Rhalo, ap=halo[:], constant=0.0)
    state, Rst = T([128, 1280], F32, "state")
    state_bf, Rstb = T([128, 1280], BF16, "statebf")
    S.op("pool", "memset", writes=[Rst], ap=state[:], constant=0.0)
    S.op("pool", "memset", writes=[Rstb], ap=state_bf[:], constant=0.0)
    U = [T([128, 259], F32, "U") for _ in range(2)]
    acc = [T([128, 256], F32, "acc") for _ in range(2)]
    xTf = [T([128, 256], F32, "xTf") for _ in range(2)]
    x_tm, Rxtm = T([128, 2, 1280], F32, "x_tm")
    BT, _ = T([128, 4, 256], BF16, "BT")
    RBTg = [Res() for _ in range(4)]
    CTt, _ = T([128, 4, 256], BF16, "CT")
    RCTg = [Res() for _ in range(4)]
    B_tm, RBtm = T([128, 2, 512], BF16, "B_tm")
    dt_tm, Rdt = T([128, 2, 20], F32, "dt_tm")
    adt, Radt = T([128, 2, 20], F32, "adt")
    ncs, Rncs = T([128, 20], F32, "ncs")
    tmp20, Rt20 = T([128, 20], F32, "tmp20")
    dec, Rdec = T([128, 20], F32, "dec")
    cd, Rcd = T([128, 20], F32, "cd")
    w2, Rw2 = T([128, 20], F32, "w2")
    ecs, Recs = T([128, 20], F32, "ecs")
    xdd, Rxdd = T([128, 1280], BF16, "xdd")
    xdt, Rxdt = T([128, 1280], BF16, "xdt")
    CBm, RCBm = T([128, 4, 128], F32, "CBm")
    adtb2 = [T([128, 4, 128], F32, "adtb") for _ in range(2)]
    Lt = [T([128, 4, 128], F32, "Lt") for _ in range(2)]
    Wm, RWm = T([128, 20, 128], BF16, "Wm")
    yg4, _ = T([128, 4, 320], F32, "yg4")
    Ryg4 = [Res() for _ in range(4)]
    sz4, _ = T([128, 4, 320], F32, "sz4")
    Rsz4 = [Res() for _ in range(4)]
    sqj, _ = T([128, 320], BF16, "sqj")
    ss4, _ = T([128, 4], F32, "ss4")
    Rss4 = [Res() for _ in range(4)]
    ln4, Rln4 = T([128, 4], F32, "ln4")
    rs4, Rrs4 = T([128, 4], F32, "rs4")
    ytmp, Rytmp = T([128, 320], F32, "ytmp")
    ss1, Rss1 = T([128, 1], F32, "ss1")
    ln1, Rln1 = T([128, 1], F32, "ln1")
    rs1, Rrs1 = T([128, 1], F32, "rs1")
    yn, Ryn = T([128, 1280], BF16, "yn")
    yTc = [T([128, 10, 128], BF16, "yTc") for _ in range(2)]

    for G in range(16):
        own = G >= 8
        RhG = RhT[G * 2:(G + 1) * 2]
        tokG = slice(G * 256, (G + 1) * 256)
        def cc_main(cc):
            b = cc % 2
            for k in range(8):
                S.op("pe", "matmul", reads=[Rwxbc[k]] + RhG, writes=[RPS[b]], out=PS[b][:, 0:256],
                     lhsT=wxbc[:, k, cc * 128:(cc + 1) * 128], rhs=hT[:, k, tokG], start=(k == 0), stop=(k == 7))
            u_t, Ru = U[b]
            a_t, Rac = acc[b]
            S.op("act", "copy", reads=[RPS[b]], writes=[Ru], out=u_t[:, 3:259], in_=PS[b][:, 0:256])
            S.op("act", "copy", reads=[Rhalo[cc]], writes=[Ru], out=u_t[:, 0:3], in_=halo[:, cc, :])
            S.op("dve", "tensor_scalar", reads=[Ru, Rcw], writes=[Rac], out=a_t[:], in0=u_t[:, 3:259],
                 scalar1=convw[:, cc, 3:4], scalar2=None, op0=ALU.mult)
            for j in (2, 1, 0):
                S.op("dve", "scalar_tensor_tensor", reads=[Ru, Rcw, Rac], writes=[Rac], out=a_t[:],
                     in0=u_t[:, j:j + 256], scalar=convw[:, cc, j:j + 1], in1=a_t[:], op0=ALU.mult, op1=ALU.add)
            S.op("act", "copy", reads=[Ru], writes=[Rhalo[cc]], out=halo[:, cc, :], in_=u_t[:, 256:259])
            if cc < 10:
                xt, Rxt = xTf[b]
                S.op("act", "activation", reads=[Rac, Rcb], writes=[Rxt], out=xt[:], in_=a_t[:], func=AF.Silu,
                     bias=convb[:, cc:cc + 1])
            elif cc < 14:
                g = cc - 10
                S.op("act", "activation", reads=[Rac, Rcb], writes=[RBTg[g]], out=BT[:, g, :], in_=a_t[:], func=AF.Silu,
                     bias=convb[:, cc:cc + 1])
            else:
                g = cc - 14
                S.op("act", "activation", reads=[Rac, Rcb], writes=[RCTg[g]], out=CTt[:, g, :], in_=a_t[:], func=AF.Silu,
                     bias=convb[:, cc:cc + 1])

        def cc_tr(cc):
            b = cc % 2
            if cc < 10:
                xt, Rxt = xTf[b]
                pb = 2 + b
                for tb in range(2):
                    S.op("pe", "transpose", reads=[Rxt, Rcst], writes=[RPS[pb]], out=PS[pb][:, tb * 128:(tb + 1) * 128],
                         in_=xt[:, tb * 128:(tb + 1) * 128], identity=ident_f)
                S.op("dve", "tensor_copy", reads=[RPS[pb]], writes=[Rxtm], out=x_tm[:, :, cc * 128:(cc + 1) * 128],
                     in_=v3(PS[pb][:, 0:256], 2))
            elif cc < 14:
                g = cc - 10
                for tb in range(2):
                    S.op("pe", "transpose", reads=[RBTg[g], Ridb], writes=[RPB], out=PB[:, tb * 128:(tb + 1) * 128],
                         in_=BT[:, g, tb * 128:(tb + 1) * 128], identity=ident_b[:])
                S.op("dve", "tensor_copy", reads=[RPB], writes=[RBtm], out=B_tm[:, :, g * 128:(g + 1) * 128],
                     in_=v3(PB[:, 0:256], 2))

        for cc in range(19):
            if cc < 18:
                cc_main(cc)
            if cc >= 1:
                cc_tr(cc - 1)
        issue_conv(4)
        for tb in range(2):
            blk = G * 2 + tb
            for k in range(8):
                S.op("pe", "matmul", reads=[Rwdt[k], RhT[blk]], writes=[RPS[4]], out=PS[4][:, tb * 20:(tb + 1) * 20],
                     lhsT=hT[:, k, blk * 128:(blk + 1) * 128], rhs=wdt[:, k, :], start=(k == 0), stop=(k == 7))
        S.op("dve", "tensor_tensor", reads=[RPS[4], Rdtb], writes=[Rdt], out=dt_tm[:], in0=v3(PS[4][:, 0:40], 2),
             in1=dtb[:].unsqueeze(1).broadcast_to([128, 2, 20]), op=ALU.add)
        S.op("act", "activation", reads=[Rdt], writes=[Rdt], out=dt_tm[:], in_=dt_tm[:], func=AF.Exp)
        S.op("act", "activation", reads=[Rdt], writes=[Rdt], out=dt_tm[:], in_=dt_tm[:], func=AF.Ln, bias=1.0)
        S.op("dve", "tensor_tensor", reads=[Rdt, Ra], writes=[Radt], out=adt[:], in0=dt_tm[:],
             in1=a_bc[:].unsqueeze(1).broadcast_to([128, 2, 20]), op=ALU.mult)
        for tb in range(2):
            c = G * 2 + tb
            tk = slice(tb * 128, (tb + 1) * 128)
            if own:
                for g in range(4):
                    pz = 6 if g % 2 == 0 else 4
                    for k in range(8):
                        S.op("pe", "matmul", reads=[Rwz[k], RhT[c]], writes=[RPS[pz]], out=PS[pz][:, 0:320],
                             lhsT=hT[:, k, c * 128:(c + 1) * 128], rhs=wz[:, k, g * 320:(g + 1) * 320], start=(k == 0),
                             stop=(k == 7))
                    S.op("act", "activation", reads=[RPS[pz]], writes=[Rsz4[g]], out=sz4[:, g, :], in_=PS[pz][:, 0:320],
                         func=AF.Silu)
            S.op("pe", "matmul", reads=[Rcst, Radt], writes=[RPS[5]], out=PS[5][:, 0:20], lhsT=tri_f, rhs=adt[:, tb, :],
                 start=True, stop=True)
            S.op("pe", "matmul", reads=[Rcst, Radt], writes=[RPS[5]], out=PS[5][:, 32:52], lhsT=ones_f, rhs=adt[:, tb, :],
                 start=True, stop=True)
            S.op("dve", "tensor_scalar", reads=[RPS[5]], writes=[Rncs], out=ncs[:], in0=PS[5][:, 0:20], scalar1=-1.0,
                 scalar2=None, op0=ALU.mult)
            S.op("dve", "tensor_tensor", reads=[RPS[5], Rncs], writes=[Rt20], out=tmp20[:], in0=PS[5][:, 32:52],
                 in1=ncs[:], op=ALU.add)
            S.op("act", "activation", reads=[Rt20], writes=[Rdec], out=dec[:], in_=tmp20[:], func=AF.Exp)
            S.op("act", "activation", reads=[RPS[5]], writes=[Rcd], out=cd[:], in_=PS[5][:, 32:52], func=AF.Exp)
            S.op("dve", "tensor_tensor", reads=[Rdt, Rdec], writes=[Rw2], out=w2[:], in0=dt_tm[:, tb, :], in1=dec[:],
                 op=ALU.mult)
            S.op("dve", "tensor_tensor", reads=[Rxtm, Rw2], writes=[Rxdd], out=v3(xdd[:], 20),
                 in0=v3(x_tm[:, tb, :], 20), in1=w2[:].unsqueeze(2).broadcast_to([128, 20, 64]), op=ALU.mult)
            if own:
                blk = c
                S.op("act", "activation", reads=[Rncs], writes=[Recs], out=ecs[:], in_=ncs[:], func=AF.Exp, scale=-1.0)
                S.op("dve", "tensor_tensor", reads=[Rxtm, Rdt], writes=[Rxdt], out=v3(xdt[:], 20),
                     in0=v3(x_tm[:, tb, :], 20), in1=dt_tm[:, tb, :].unsqueeze(2).broadcast_to([128, 20, 64]),
                     op=ALU.mult)
                for g in range(4):
                    S.op("pe", "matmul", reads=[RBTg[g], RCTg[g]], writes=[RPS[2]], out=PS[2][:, g * 128:(g + 1) * 128],
                         lhsT=BT[:, g, tk], rhs=CTt[:, g, tk], start=True, stop=True)
                S.op("dve", "tensor_tensor", reads=[RPS[2], Rcst], writes=[RCBm], out=CBm[:], in0=v3(PS[2][:, :], 4),
                     in1=tri_f.unsqueeze(1).broadcast_to([128, 4, 128]), op=ALU.mult)
                for hq in range(5):
                    pb = 3 + hq % 2
                    adtb, Radtb = adtb2[hq % 2]
                    S.op("dve", "tensor_copy", reads=[Radt], writes=[Radtb], out=adtb[:],
                         in_=adt[:, tb, hq * 4:hq * 4 + 4].unsqueeze(2).broadcast_to([128, 4, 128]))
                    for i in range(4):
                        h = hq * 4 + i
                        S.op("pe", "matmul", reads=[Radtb, Rcst], writes=[RPS[pb]], out=PS[pb][:, i * 128:(i + 1) * 128],
                             lhsT=adtb[:, i, :], rhs=tri_f, start=True, stop=True)
                    lt, Rlt = Lt[hq % 2]
                    S.op("dve", "tensor_tensor", reads=[RPS[pb], Rncs], writes=[Rlt], out=lt[:], in0=v3(PS[pb][:, :], 4),
                         in1=ncs[:, hq * 4:hq * 4 + 4].unsqueeze(2).broadcast_to([128, 4, 128]), op=ALU.add)
                    S.op("act", "activation", reads=[Rlt], writes=[Rlt], out=lt[:], in_=lt[:], func=AF.Exp)
                    for i in range(4):
                        h = hq * 4 + i
                        S.op("dve", "scalar_tensor_tensor", reads=[Rlt, RCBm], writes=[RWm],
                             out=Wm[:, h, :], in0=lt[:, i, :], scalar=1.0, in1=CBm[:, h // 5, :], op0=ALU.min,
                             op1=ALU.mult)
                for g in range(4):
                    gs = slice(g * 320, (g + 1) * 320)
                    po = 0 if g % 2 == 0 else 2
                    pd = 1 if g % 2 == 0 else 3
                    S.op("pe", "matmul", reads=[RCTg[g], Rstb], writes=[RPS[po]], out=PS[po][:, 0:320], lhsT=CTt[:, g, tk],
                         rhs=state_bf[:, gs], start=True, stop=True)
                    for hh in range(5):
                        h = g * 5 + hh
                        S.op("pe", "matmul", reads=[RWm, Rxdt], writes=[RPS[pd]], out=PS[pd][:, hh * 64:(hh + 1) * 64],
                             lhsT=Wm[:, h, :], rhs=xdt[:, h * 64:(h + 1) * 64], start=True, stop=True)
                    ygg = yg4[:, g, :]
                    S.op("dve", "tensor_tensor", reads=[RPS[po], Recs], writes=[Ryg4[g]], out=v3(ygg, 5),
                         in0=v3(PS[po][:, 0:320], 5), in1=ecs[:, g * 5:g * 5 + 5].unsqueeze(2).broadcast_to([128, 5, 64]),
                         op=ALU.mult)
                    S.op("dve", "tensor_tensor", reads=[RPS[pd], Ryg4[g]], writes=[Ryg4[g]], out=ygg, in0=ygg,
                         in1=PS[pd][:, 0:320], op=ALU.add)
                    S.op("dve", "tensor_tensor", reads=[Rxtm, Rdsk], writes=[Rytmp], out=v3(ytmp[:], 5),
                         in0=v3(x_tm[:, tb, gs], 5), in1=dsk[:, g * 5:g * 5 + 5].unsqueeze(2).broadcast_to([128, 5, 64]),
                         op=ALU.mult)
                    S.op("dve", "tensor_tensor", reads=[Rytmp, Ryg4[g]], writes=[Ryg4[g]], out=ygg, in0=ygg, in1=ytmp[:],
                         op=ALU.add)
                    S.op("dve", "tensor_tensor", reads=[Ryg4[g], Rsz4[g]], writes=[Ryg4[g]], out=ygg, in0=ygg,
                         in1=sz4[:, g, :], op=ALU.mult)
                    S.op("act", "activation", reads=[Ryg4[g]], writes=[Rss4[g]], out=sqj[:], in_=ygg, func=AF.Square,
                         accum_out=ss4[:, g:g + 1])
                rstd_from(ss4[:, 0:4], Rss4, 320, 4, rs4[:, 0:4], Rrs4, ln4[:, 0:4], Rln4)
                for g in range(4):
                    gs = slice(g * 320, (g + 1) * 320)
                    S.op("dve", "scalar_tensor_tensor", reads=[Ryg4[g], Rrs4, Rgssd], writes=[Ryn], out=yn[:, gs],
                         in0=yg4[:, g, :], scalar=rs4[:, g:g + 1], in1=gssd[:, gs], op0=ALU.mult, op1=ALU.mult)
                yt, Ryt = yTc[tb % 2]
                transposeN(yn, Ryn, 10, 128, yt[:], [Ryt], eng="act")
                ob = c - 16
                S.dma("sp", reads=[Ryt], writes=[Res()], out=yT_scr[:, :, ob * 128:(ob + 1) * 128], in_=yt[:])
            for g in range(4):
                gs = slice(g * 320, (g + 1) * 320)
                S.op("pe", "matmul", reads=[RBtm, Rxdd], writes=[RPS[6]], out=PS[6][:, 0:320],
                     lhsT=B_tm[:, tb, g * 128:(g + 1) * 128], rhs=xdd[:, gs], start=True, stop=True)
                S.op("dve", "tensor_tensor", reads=[Rst, Rcd], writes=[Rst], out=v3(state[:, gs], 5),
                     in0=v3(state[:, gs], 5), in1=cd[:, g * 5:g * 5 + 5].unsqueeze(2).broadcast_to([128, 5, 64]),
                     op=ALU.mult)
                S.op("dve", "tensor_tensor", reads=[Rst, RPS[6]], writes=[Rst], out=state[:, gs], in0=state[:, gs],
                     in1=PS[6][:, 0:320], op=ALU.add)
            if c == 15:
                S.op("dve", "tensor_scalar", reads=[Rst, Rflag], writes=[Rst], out=state[:], in0=state[:],
                     scalar1=flag[:, 0:1], scalar2=None, op0=ALU.mult)
            if c >= 15:
                S.op("act", "copy", reads=[Rst], writes=[Rstb], out=state_bf[:], in_=state[:])
    S.barrier()
    A.release(m_after_hT)

    mtab, Rmtab = T([128, 17, 128], BF16, "mtab")
    S.dma("pool", writes=[Rmtab], out=mtab[:].rearrange("p a b -> p (a b)"), in_=mtab_d)
    gq64, Rgq64 = bc_load(64, "attn_q_norm_g")
    gk64, Rgk64 = bc_load(64, "attn_k_norm_g")
    gq, Rgq = T([128, 256], F32, "gq")
    gk, Rgk = T([128, 256], F32, "gk")
    S.op("dve", "tensor_scalar", reads=[Rgq64], writes=[Rgq], out=v3(gq[:], 4),
         in0=gq64[:].unsqueeze(1).broadcast_to([128, 4, 64]), scalar1=0.125, scalar2=None, op0=ALU.mult)
    S.op("dve", "tensor_copy", reads=[Rgk64], writes=[Rgk], out=v3(gk[:], 4),
         in_=gk64[:].unsqueeze(1).broadcast_to([128, 4, 64]))
    wq, _ = T([128, 8, 256], BF16, "wq")
    Rwq = [Res() for _ in range(8)]
    wk, _ = T([128, 8, 256], BF16, "wk")
    Rwk = [Res() for _ in range(8)]
    wv, _ = T([128, 8, 256], BF16, "wv")
    Rwv = [Res() for _ in range(8)]
    KT, _ = T([64, 4, WIN], BF16, "KT")
    RKT = [Res() for _ in range(32)]
    QT, _ = T([64, 4, HALF], BF16, "QT")
    RQT = [Res() for _ in range(16)]
    Vaug, _ = T([128, 32, 4, 65], BF16, "Vaug")
    RV = [Res() for _ in range(32)]
    kn = [T([128, 256], BF16, "kn") for _ in range(2)]
    qn_ = [T([128, 256], BF16, "qn_") for _ in range(2)]
    Aexp, RAexp = T([128, 128], F32, "Aexp")
    Erev, REr = T([128, 17, 128], F32, "Erev")
    Pf = [T([128, 512], F32, "Pf") for _ in range(3)]
    Pbf = [T([128, 512], BF16, "Pbf") for _ in range(3)]
    rd, Rrd = T([128, 2], F32, "rd")
    o_all, _ = T([128, 16, 256], BF16, "o_all")
    Roall = [Res() for _ in range(16)]
    oTc = [T([128, 2, 128], BF16, "oTc") for _ in range(2)]
    for r in range(3):
        for k in range(8):
            rows = slice(k * 128, (k + 1) * 128)
            S.dma("pool", writes=[Rwq[k]], out=wq[:, k, :], in_=w_in_d[rows, OFF_DT + r * 256:OFF_DT + (r + 1) * 256])
            S.dma("pool", writes=[Rwk[k]], out=wk[:, k, :], in_=w_in_d[rows, OFF_Q + r * 256:OFF_Q + (r + 1) * 256])
            S.dma("pool", writes=[Rwv[k]], out=wv[:, k, :], in_=w_in_d[rows, OFF_K + r * 256:OFF_K + (r + 1) * 256])

        def proj_mm(blk):
            bs = slice(blk * 128, (blk + 1) * 128)
            pa = blk % 2
            for k in range(8):
                S.op("pe", "matmul", reads=[Rwk[k], RhT[blk]], writes=[RPS[pa]], out=PS[pa][:, 0:256], lhsT=hT[:, k, bs],
                     rhs=wk[:, k, :], start=(k == 0), stop=(k == 7))
            kt, Rkn = kn[blk % 2]
            qknorm(PS[pa][:, 0:256], RPS[pa], 4, 64, gk[:], Rgk, kt[:], Rkn)
            pv = 2 + blk % 2
            for k in range(8):
                S.op("pe", "matmul", reads=[Rwv[k], RhT[blk]], writes=[RPS[pv]], out=PS[pv][:, 0:256], lhsT=hT[:, k, bs],
                     rhs=wv[:, k, :], start=(k == 0), stop=(k == 7))
            if blk >= 16:
                S.op("act", "copy", reads=[RPS[pv]], writes=[RV[blk]], out=Vaug[:, blk, :, 0:64],
                     in_=v3(PS[pv][:, 0:256], 4))
                S.op("pool", "memset", writes=[RV[blk]], ap=Vaug[:, blk, :, 64:65], constant=1.0)
            else:
                S.op("dve", "tensor_scalar", reads=[RPS[pv], Rflag], writes=[RV[blk]], out=Vaug[:, blk, :, 0:64],
                     in0=v3(PS[pv][:, 0:256], 4), scalar1=flag[:, 0:1], scalar2=None, op0=ALU.mult)
                S.op("pool", "tensor_copy", reads=[Rflag], writes=[RV[blk]], out=Vaug[:, blk, :, 64:65],
                     in_=flag[:, 0:1].unsqueeze(1).broadcast_to([128, 4, 1]))
            if blk >= 16:
                pq = 4 + blk % 2
                for k in range(8):
                    S.op("pe", "matmul", reads=[Rwq[k], RhT[blk]], writes=[RPS[pq]], out=PS[pq][:, 0:256], lhsT=hT[:, k, bs],
                         rhs=wq[:, k, :], start=(k == 0), stop=(k == 7))
                qt, Rqn = qn_[blk % 2]
                qknorm(PS[pq][:, 0:256], RPS[pq], 4, 64, gq[:], Rgq, qt[:], Rqn)

        def proj_tr(blk):
            bs = slice(blk * 128, (blk + 1) * 128)
            kt, Rkn = kn[blk % 2]
            transposeN(kt, Rkn, 4, 64, KT[:, :, bs], [RKT[blk]], eng="act")
            if blk >= 16:
                qt, Rqn = qn_[blk % 2]
                transposeN(qt, Rqn, 4, 64, QT[:, :, (blk - 16) * 128:(blk - 15) * 128], [RQT[blk - 16]], eng="act")

        for blk in range(33):
            if blk < 32:
                proj_mm(blk)
            if blk >= 1:
                proj_tr(blk - 1)

        seq = []
        for hh in range(4):
            for qb in range(16, 32):
                kbs = list(range(qb - 16, qb + 1))
                for gi in range(5):
                    seq.append((hh, qb, gi, kbs[gi * 4:gi * 4 + 4]))

        def emit_S(i):
            hh, qb, gi, grp = seq[i]
            qi = qb - 16
            sbk = i % 2
            for ii, kb in enumerate(grp):
                S.op("pe", "matmul", reads=[RKT[kb], RQT[qi]], writes=[RPS[sbk]],
                     out=PS[sbk][:, ii * 128:(ii + 1) * 128], lhsT=KT[:, hh, kb * 128:(kb + 1) * 128],
                     rhs=QT[:, hh, qi * 128:(qi + 1) * 128], start=True, stop=True)

        def emit_rest(i):
            hh, qb, gi, grp = seq[i]
            qi = qb - 16
            n = len(grp)
            sbk = i % 2
            unit = i // 5
            ob = 4 + unit % 2
            pf, Rpf = Pf[i % 3]
            pbf, Rpbf = Pbf[i % 3]
            if qb == 16 and gi == 0:
                h = r * 4 + hh
                slope = 2.0 ** (-8.0 * (h + 1) / 12.0)
                S.op("act", "activation", reads=[Rcst], writes=[RAexp], out=Aexp[:], in_=d0_f, func=AF.Exp, scale=-slope)
                for j in range(17):
                    cpow = math.exp(-slope * 128.0 * (16 - j))
                    S.op("dve", "scalar_tensor_tensor", reads=[RAexp, Rmtab], writes=[REr],
                         out=Erev[:, j, :], in0=Aexp[:], scalar=float(cpow), in1=mtab[:, j, :], op0=ALU.mult,
                         op1=ALU.mult)
            S.op("act", "activation", reads=[RPS[sbk]], writes=[Rpf], out=pf[:, 0:n * 128],
                 in_=PS[sbk][:, 0:n * 128], func=AF.Exp)
            S.op("dve", "tensor_tensor", reads=[Rpf, REr], writes=[Rpbf], out=pbf[:, 0:n * 128],
                 in0=pf[:, 0:n * 128], in1=Erev[:, gi * 4:gi * 4 + n, :].rearrange("p a b -> p (a b)"),
                 op=ALU.mult)
            for ii, kb in enumerate(grp):
                S.op("pe", "matmul", reads=[Rpbf, RV[kb]], writes=[RPS[ob]], out=PS[ob][:, 0:65],
                     lhsT=pbf[:, ii * 128:(ii + 1) * 128], rhs=Vaug[:, kb, hh, :], start=(gi == 0 and ii == 0),
                     stop=(gi == 4))
            if gi == 4:
                S.op("dve", "reciprocal", reads=[RPS[ob]], writes=[Rrd], out=rd[:, 0:1], in_=PS[ob][:, 64:65])
                S.op("dve", "tensor_scalar", reads=[RPS[ob], Rrd], writes=[Roall[qi]],
                     out=o_all[:, qi, hh * 64:(hh + 1) * 64], in0=PS[ob][:, 0:64], scalar1=rd[:, 0:1], scalar2=None,
                     op0=ALU.mult)

        emit_S(0)
        for i in range(len(seq)):
            if i + 1 < len(seq):
                emit_S(i + 1)
            emit_rest(i)
        for qi in range(16):
            for i in range(2):
                S.op("pe", "transpose", reads=[Roall[qi], Ridb], writes=[RPB], out=PB[:, i * 128:(i + 1) * 128],
                     in_=o_all[:, qi, i * 128:(i + 1) * 128], identity=ident_b[:])
            ot, Rot = oTc[qi % 2]
            S.op("act", "copy", reads=[RPB], writes=[Rot], out=ot[:], in_=v3(PB[:, 0:256], 2))
            S.dma("sp", reads=[Rot], writes=[Res()], out=oT_scr[:, 2 * r:2 * r + 2, qi * 128:(qi + 1) * 128], in_=ot[:])
    S.barrier()
    A.release(m_base)

    x1, _ = T([128, 16, 1024], F32, "x1")
    Rx1 = [Res() for _ in range(16)]
    m_after_x1 = A.mark()
    yT_sb, _ = T([128, 10, HALF], BF16, "yT_sb")
    RyT = [Res() for _ in range(10)]
    for kc in range(10):
        S.dma("sp", writes=[RyT[kc]], out=yT_sb[:, kc, :], in_=yT_scr[:, kc, :])
    oT, _ = T([128, 6, HALF], BF16, "oT")
    RoTs = [Res() for _ in range(6)]
    for kc in range(6):
        S.dma("sp", writes=[RoTs[kc]], out=oT[:, kc, :], in_=oT_scr[:, kc, :])
    wout, _ = T([128, 16, 1024], BF16, "wout")
    Rwout = [Res() for _ in range(16)]
    for kc in range(16):
        S.dma("pool", writes=[Rwout[kc]], out=wout[:, kc, :], in_=w_out_d[kc * 128:(kc + 1) * 128, :])
    xo = [T([128, 1024], F32, "xo") for _ in range(2)]
    for tb in range(16):
        ts_ = slice(tb * 128, (tb + 1) * 128)
        xo_t, Rxo = xo[tb % 2]
        S.dma("sp", writes=[Rxo], out=xo_t[:], in_=xw_d[HALF + tb * 128:HALF + (tb + 1) * 128, :])
        for half in range(2):
            pb = 2 * (tb % 2) + half
            cs_ = slice(half * 512, (half + 1) * 512)
            for kc in range(16):
                lhsT = yT_sb[:, kc, ts_] if kc < 10 else oT[:, kc - 10, ts_]
                S.op("pe", "matmul", reads=[RyT[kc] if kc < 10 else RoTs[kc - 10], Rwout[kc]], writes=[RPS[pb]], out=PS[pb][:, :], lhsT=lhsT,
                     rhs=wout[:, kc, cs_], start=(kc == 0), stop=(kc == 15))
            S.op("dve", "tensor_tensor", reads=[RPS[pb], Rxo], writes=[Rx1[tb]], out=x1[:, tb, cs_], in0=PS[pb][:, :],
                 in1=xo_t[:, cs_], op=ALU.add)
    S.barrier()
    A.release(m_after_x1)

    def store_x1_and_finish():
        fin = []
        for tb in range(16):
            R = Res()
            S.dma("sp", reads=[Rx1[tb]], writes=[R], out=out_d[tb * 128:(tb + 1) * 128, :], in_=x1[:, tb, :])
            fin.append(R)
        for R in fin:
            S._wait("sp", R.w)
        S.emit()
        return nc

    if stop_after == "C":
        return store_x1_and_finish()

    wqm, _ = T([128, 8, 512], BF16, "wqm")
    Rwqm = [Res() for _ in range(8)]
    wkv, _ = T([128, 8, 1024], BF16, "wkv")
    Rwkv = [Res() for _ in range(8)]
    wo, _ = T([128, 4, 1024], BF16, "wo")
    Rwo = [Res() for _ in range(4)]
    for k in range(8):
        S.dma("pool", writes=[Rwqm[k]], out=wqm[:, k, :], in_=mem_w_q_d[k * 128:(k + 1) * 128, :])
        S.dma("pool", writes=[Rwkv[k]], out=wkv[:, k, :], in_=mem_w_kv_d[k * 128:(k + 1) * 128, :])
    for k in range(4):
        S.dma("pool", writes=[Rwo[k]], out=wo[:, k, :], in_=mem_w_o_d[k * 128:(k + 1) * 128, :])
    gx, Rgx = bc_load(1024, "xattn_norm_g")
    gm, Rgm = bc_load(1024, "mem_norm_g")
    gqm128, Rgqm128 = bc_load(128, "mem_q_norm_g")
    gkm128, Rgkm128 = bc_load(128, "mem_k_norm_g")
    gqm, Rgqm = T([128, 512], F32, "gqm")
    gkm, Rgkm = T([128, 512], F32, "gkm")
    S.op("dve", "tensor_scalar", reads=[Rgqm128], writes=[Rgqm], out=v3(gqm[:], 4),
         in0=gqm128[:].unsqueeze(1).broadcast_to([128, 4, 128]), scalar1=128.0 ** -0.5, scalar2=None, op0=ALU.mult)
    S.op("dve", "tensor_copy", reads=[Rgkm128], writes=[Rgkm], out=v3(gkm[:], 4),
         in_=gkm128[:].unsqueeze(1).broadcast_to([128, 4, 128]))
    KmT, RKmT = T([128, 4, 256], BF16, "KmT")
    Vm, RVm = T([128, 2, 4, 129], BF16, "Vm")
    S.op("pool", "memset", writes=[RVm], ap=Vm[:].rearrange("p a b c -> p (a b c)"), constant=1.0)
    memb, Rmemb = T([128, 1024], F32, "memb")
    mh, Rmh = T([128, 1024], BF16, "mh")
    mT, RmT = T([128, 8, 128], BF16, "mT")
    knm, Rknm = T([128, 512], BF16, "knm")
    for mb in range(2):
        ms = slice(mb * 128, (mb + 1) * 128)
        S.dma("sp", writes=[Rmemb], out=memb[:], in_=mem_d[ms, :])
        rmsnorm(memb[:], Rmemb, gm, Rgm, mh[:], Rmh)
        transposeN(mh, Rmh, 8, 128, mT[:], [RmT], eng="act")
        for half in range(2):
            for k in range(8):
                S.op("pe", "matmul", reads=[RmT, Rwkv[k]], writes=[RPS[half]], out=PS[half][:, :], lhsT=mT[:, k, :],
                     rhs=wkv[:, k, half * 512:(half + 1) * 512], start=(k == 0), stop=(k == 7))
        qknorm(PS[0][:, :], RPS[0], 4, 128, gkm[:], Rgkm, knm[:], Rknm)
        transposeN(knm, Rknm, 4, 128, KmT[:, :, ms], [RKmT], eng="act")
        S.op("act", "copy", reads=[RPS[1]], writes=[RVm], out=Vm[:, mb, :, 0:128], in_=v3(PS[1][:, :], 4))
    h2, Rh2 = T([128, 1024], BF16, "h2")
    h2T, Rh2T = T([128, 8, 128], BF16, "h2T")
    qn, Rqn2 = T([128, 512], BF16, "qn")
    qT, RqT = T([128, 4, 128], BF16, "qT")
    Pm = [T([128, 256], BF16, "Pm") for _ in range(2)]
    o2, Ro2 = T([128, 512], BF16, "o2")
    o2T, Ro2T = T([128, 4, 128], BF16, "o2T")
    for tb in range(16):
        rmsnorm(x1[:, tb, :], Rx1[tb], gx, Rgx, h2[:], Rh2)
        transposeN(h2, Rh2, 8, 128, h2T[:], [Rh2T], eng="act")
        for k in range(8):
            S.op("pe", "matmul", reads=[Rh2T, Rwqm[k]], writes=[RPS[2]], out=PS[2][:, :], lhsT=h2T[:, k, :], rhs=wqm[:, k, :],
                 start=(k == 0), stop=(k == 7))
        qknorm(PS[2][:, :], RPS[2], 4, 128, gqm[:], Rgqm, qn[:], Rqn2)
        transposeN(qn, Rqn2, 4, 128, qT[:], [RqT], eng="act")
        for hh in range(4):
            sb_ = 3 + hh % 2
            ob = 5 + hh % 2
            pm, Rpm = Pm[hh % 2]
            for mb in range(2):
                S.op("pe", "matmul", reads=[RKmT, RqT], writes=[RPS[sb_]], out=PS[sb_][:, mb * 128:(mb + 1) * 128],
                     lhsT=KmT[:, hh, mb * 128:(mb + 1) * 128], rhs=qT[:, hh, :], start=True, stop=True)
            S.op("act", "activation", reads=[RPS[sb_]], writes=[Rpm], out=pm[:], in_=PS[sb_][:, 0:256], func=AF.Exp)
            for mb in range(2):
                S.op("pe", "matmul", reads=[Rpm, RVm], writes=[RPS[ob]], out=PS[ob][:, 0:129],
                     lhsT=pm[:, mb * 128:(mb + 1) * 128], rhs=Vm[:, mb, hh, :], start=(mb == 0), stop=(mb == 1))
            S.op("dve", "reciprocal", reads=[RPS[ob]], writes=[Rrd], out=rd[:, 0:1], in_=PS[ob][:, 128:129])
            S.op("dve", "tensor_scalar", reads=[RPS[ob], Rrd], writes=[Ro2], out=o2[:, hh * 128:(hh + 1) * 128],
                 in0=PS[ob][:, 0:128], scalar1=rd[:, 0:1], scalar2=None, op0=ALU.mult)
        transposeN(o2, Ro2, 4, 128, o2T[:], [Ro2T], eng="act")
        for half in range(2):
            cs_ = slice(half * 512, (half + 1) * 512)
            for kc in range(4):
                S.op("pe", "matmul", reads=[Ro2T, Rwo[kc]], writes=[RPS[half]], out=PS[half][:, :], lhsT=o2T[:, kc, :],
                     rhs=wo[:, kc, cs_], start=(kc == 0), stop=(kc == 3))
            S.op("dve", "tensor_tensor", reads=[RPS[half], Rx1[tb]], writes=[Rx1[tb]], out=x1[:, tb, cs_],
                 in0=PS[half][:, :], in1=x1[:, tb, cs_], op=ALU.add)
    if stop_after == "D":
        S.barrier()
        return store_x1_and_finish()
    Rx2d = [Res() for _ in range(16)]
    for tb in range(16):
        S.dma("sp", reads=[Rx1[tb]], writes=[Rx2d[tb]], out=out_d[tb * 128:(tb + 1) * 128, :], in_=x1[:, tb, :])
    S.barrier()
    A.release(m_base)

    wpq, _ = T([128, 8, 2048], BF16, "wpq")
    Rwpq = [Res() for _ in range(8)]
    for k in range(8):
        S.dma("pool", writes=[Rwpq[k]], out=wpq[:, k, :], in_=peer_wq_d[k * 128:(k + 1) * 128, :])
    keysT, RkeysT = T([128, 2, 128], F32, "keysT")
    S.dma("sp", writes=[RkeysT], out=keysT[:, 0, :], in_=keys1T_d)
    S.dma("sp", writes=[RkeysT], out=keysT[:, 1, :], in_=keys2T_d)
    gf, Rgf = bc_load(1024, "ffn_norm_g")
    xrp = [T([128, 1024], F32, "xr") for _ in range(2)]
    h3p = [T([128, 1024], F32, "h3") for _ in range(2)]
    h3bp = [T([128, 1024], BF16, "h3b") for _ in range(2)]
    h3T, Rh3T = T([128, 8, 128], BF16, "h3T")
    off_q = A.mark()
    qrT, RqrT = T([128, 16, 128], F32, "qrT")
    eq = nc.alloc_sbuf_tensor_at("eq_alias", [128, 8, 16, 16], F32, offset=off_q)
    Req = RqrT
    off_s = A.mark()
    sc, Rsc = T([128, 16, 128], F32, "sc")
    cand = nc.alloc_sbuf_tensor_at("cand_alias", [128, 8, 256], F32, offset=off_s)
    Rcand = Rsc
    sc2, Rsc2 = T([128, 128], F32, "sc2")
    tv, Rtv = T([128, 16, 16], F32, "tv")
    ti, Rti = T([128, 16, 16], U32, "ti")
    tif, Rtif = T([128, 16, 16], F32, "tif")
    cand2, Rcand2 = T([128, 256], F32, "cand2")
    cv, Rcv = T([128, 8, 16], F32, "cv")
    cp, Rcp = T([128, 8, 16], U32, "cp")
    ca, Rca = T([128, 8, 16], U32, "ca")
    cb_, Rcb_ = T([128, 8, 16], U32, "cb")
    caf, Rcaf = T([128, 8, 16], F32, "caf")
    cbf, Rcbf = T([128, 8, 16], F32, "cbf")
    i1f, Ri1f = T([128, 8, 16], F32, "i1f")
    i2f, Ri2f = T([128, 8, 16], F32, "i2f")
    ef, Ref = T([128, 128], F32, "ef")
    eip = [T([128, 128], I32, "ei") for _ in range(2)]
    gtp = [T([128, 8, 16], F32, "gt") for _ in range(2)]
    gsum, Rgsum = T([128, 8], F32, "gsum")
    actv, _ = T([128, 128], F32, "actv")
    Ractv = [Res() for _ in range(32)]
    WG = [T([128, 4], F32, "wg4") for _ in range(2)]
    wgt, _ = T([128, 128], F32, "wgt")
    Rwgt = [Res() for _ in range(32)]
    djunk, Rdj = T([128, 1024], BF16, "djunk")
    NUV = 6
    UVt = [(T([128, 4, 2048], BF16, "UV")[0], [Res() for _ in range(4)]) for _ in range(NUV)]
    SV = [T([128, 1024], BF16, "SV") for _ in range(4)]
    fin = []

    def peer_setup(tb, p):
        xr, Rxr = xrp[p]
        h3, Rh3 = h3p[p]
        h3b, Rh3b = h3bp[p]
        ei, Rei = eip[p]
        gt, Rgt = gtp[p]
        S.dma("sp", reads=[Rx2d[tb]], writes=[Rxr], out=xr[:], in_=out_d[tb * 128:(tb + 1) * 128, :])
        rmsnorm(xr[:], Rxr, gf, Rgf, h3[:], Rh3)
        S.op("act", "copy", reads=[Rh3], writes=[Rh3b], out=h3b[:], in_=h3[:])
        transposeN(h3b, Rh3b, 8, 128, h3T[:], [Rh3T], eng="act")
        yield
        for q4 in range(4):
            pb = q4 % 2
            for i in range(4):
                ccq = q4 * 4 + i
                for k in range(8):
                    S.op("pe", "matmul", reads=[Rwpq[k], Rh3T], writes=[RPS[pb]], out=PS[pb][:, i * 128:(i + 1) * 128],
                         lhsT=wpq[:, k, ccq * 128:(ccq + 1) * 128], rhs=h3T[:, k, :], start=(k == 0), stop=(k == 7))
            S.op("act", "copy", reads=[RPS[pb]], writes=[RqrT], out=qrT[:, q4 * 4:q4 * 4 + 4, :], in_=v3(PS[pb][:, :], 4))
            yield
        for q4 in range(4):
            pb = 2 + q4 % 2
            for i in range(4):
                j = q4 * 4 + i
                S.op("pe", "matmul", reads=[RqrT, RkeysT], writes=[RPS[pb]], out=PS[pb][:, i * 128:(i + 1) * 128],
                     lhsT=qrT[:, j, :], rhs=keysT[:, j % 2, :], start=True, stop=True)
            S.op("act", "copy", reads=[RPS[pb]], writes=[Rsc], out=sc[:, q4 * 4:q4 * 4 + 4, :], in_=v3(PS[pb][:, :], 4))
        for j in range(16):
            S.op("dve", "max", reads=[Rsc], writes=[Rtv], out=tv[:, j, 0:8], in_=sc[:, j, :])
            S.op("dve", "max_index", reads=[Rsc, Rtv], writes=[Rti], out=ti[:, j, 0:8], in_max=tv[:, j, 0:8],
                 in_values=sc[:, j, :])
            S.op("dve", "match_replace", reads=[Rsc, Rtv], writes=[Rsc2], out=sc2[:], in_to_replace=tv[:, j, 0:8],
                 in_values=sc[:, j, :], imm_value=-1e30)
            S.op("dve", "max", reads=[Rsc2], writes=[Rtv], out=tv[:, j, 8:16], in_=sc2[:])
            S.op("dve", "max_index", reads=[Rsc2, Rtv], writes=[Rti], out=ti[:, j, 8:16], in_max=tv[:, j, 8:16],
                 in_values=sc2[:])
            yield
        S.op("dve", "tensor_copy", reads=[Rti], writes=[Rtif], out=tif[:], in_=ti[:])
        tv4 = tv[:].rearrange("p (h two) k -> p h two k", two=2)
        tif4 = tif[:].rearrange("p (h two) k -> p h two k", two=2)
        S.op("dve", "tensor_tensor", reads=[Rtv], writes=[Rcand], out=cand[:].rearrange("p h (a b) -> p h a b", a=16),
             in0=tv4[:, :, 0, :].unsqueeze(3).broadcast_to([128, 8, 16, 16]),
             in1=tv4[:, :, 1, :].unsqueeze(2).broadcast_to([128, 8, 16, 16]), op=ALU.add)
        for h in range(8):
            S.op("dve", "max", reads=[Rcand], writes=[Rcv], out=cv[:, h, 0:8], in_=cand[:, h, :])
            S.op("dve", "max_index", reads=[Rcand, Rcv], writes=[Rcp], out=cp[:, h, 0:8], in_max=cv[:, h, 0:8],
                 in_values=cand[:, h, :])
            S.op("dve", "match_replace", reads=[Rcand, Rcv], writes=[Rcand2], out=cand2[:], in_to_replace=cv[:, h, 0:8],
                 in_values=cand[:, h, :], imm_value=-1e30)
            S.op("dve", "max", reads=[Rcand2], writes=[Rcv], out=cv[:, h, 8:16], in_=cand2[:])
            S.op("dve", "max_index", reads=[Rcand2, Rcv], writes=[Rcp], out=cp[:, h, 8:16], in_max=cv[:, h, 8:16],
                 in_values=cand2[:])
            yield
        S.op("dve", "tensor_single_scalar", reads=[Rcp], writes=[Rca], out=ca[:], in_=cp[:], scalar=4,
             op=ALU.logical_shift_right)
        S.op("dve", "tensor_single_scalar", reads=[Rcp], writes=[Rcb_], out=cb_[:], in_=cp[:], scalar=15,
             op=ALU.bitwise_and)
        S.op("dve", "tensor_copy", reads=[Rca], writes=[Rcaf], out=caf[:], in_=ca[:])
        S.op("dve", "tensor_copy", reads=[Rcb_], writes=[Rcbf], out=cbf[:], in_=cb_[:])
        io4 = iota16.unsqueeze(1).unsqueeze(1).broadcast_to([128, 8, 16, 16])
        for (sel, Rsel, half, dst, Rdst) in ((caf, Rcaf, 0, i1f, Ri1f), (cbf, Rcbf, 1, i2f, Ri2f)):
            S.op("dve", "tensor_tensor", reads=[Rsel, Rcst], writes=[Req], out=eq[:],
                 in0=sel[:].unsqueeze(3).broadcast_to([128, 8, 16, 16]), in1=io4, op=ALU.is_equal)
            S.op("dve", "tensor_tensor", reads=[Req, Rtif], writes=[Req], out=eq[:], in0=eq[:],
                 in1=tif4[:, :, half, :].unsqueeze(2).broadcast_to([128, 8, 16, 16]), op=ALU.mult)
            S.op("dve", "tensor_reduce", reads=[Req], writes=[Rdst], out=dst[:].rearrange("p h k -> p (h k)"),
                 in_=eq[:].rearrange("p h k a -> p (h k) a"), axis=AX.X, op=ALU.add)
        S.op("dve", "scalar_tensor_tensor", reads=[Ri1f, Ri2f], writes=[Ref], out=ef[:],
             in0=i1f[:].rearrange("p h k -> p (h k)"), scalar=128.0, in1=i2f[:].rearrange("p h k -> p (h k)"),
             op0=ALU.mult, op1=ALU.add)
        S.op("dve", "tensor_copy", reads=[Ref], writes=[Rei], out=ei[:], in_=ef[:])
        S.op("dve", "tensor_tensor", reads=[Rcv], writes=[Rgt], out=gt[:], in0=cv[:],
             in1=cv[:, :, 0:1].broadcast_to([128, 8, 16]), op=ALU.subtract)
        S.op("act", "activation", reads=[Rgt], writes=[Rgt], out=gt[:], in_=gt[:], func=AF.Exp)
        S.op("dve", "tensor_reduce", reads=[Rgt], writes=[Rgsum], out=gsum[:], in_=gt[:], axis=AX.X, op=ALU.add)
        S.op("dve", "reciprocal", reads=[Rgsum], writes=[Rgsum], out=gsum[:], in_=gsum[:])
        S.op("dve", "tensor_tensor", reads=[Rgt, Rgsum], writes=[Rgt], out=gt[:], in0=gt[:],
             in1=gsum[:].unsqueeze(2).broadcast_to([128, 8, 16]), op=ALU.mult)

    def peer_gather(tb, p, bg):
        xr, Rxr = xrp[p]
        h3b, Rh3b = h3bp[p]
        ei, Rei = eip[p]
        gt, Rgt = gtp[p]
        gtf = gt[:].rearrange("p h k -> p (h k)")
        S.op("pool", "memset", writes=Ractv, ap=actv[:], constant=0.0)

        def stA(g4):
            uvt, Ruvt = UVt[g4 % NUV]
            for i in range(4):
                s_ = g4 * 4 + i
                S.dma("pool", reads=[Rei, Ruv], writes=[Ruvt[i]], meth="indirect_dma_start", out=uvt[:, i, :],
                      out_offset=None, in_=uv_scr, in_offset=bass.IndirectOffsetOnAxis(ap=ei[:, s_:s_ + 1], axis=0))

        def stBC(g4):
            uvt, Ruvt = UVt[g4 % NUV]
            for i in range(4):
                s_ = g4 * 4 + i
                S.op("dve", "scalar_tensor_tensor", reads=[Ruvt[i], Rh3b], writes=[Ractv[g4]], out=djunk[:],
                     in0=uvt[:, i, 0:1024], scalar=1.0, in1=h3b[:], op0=ALU.mult, op1=ALU.mult,
                     accum_out=actv[:, s_:s_ + 1])
            wg, Rwg = WG[g4 % 2]
            S.op("act", "activation", reads=[Ractv[g4]], writes=[Rwg], out=wg[:], in_=actv[:, g4 * 4:g4 * 4 + 4],
                 func=AF.Gelu)

        def stDE(g4):
            uvt, Ruvt = UVt[g4 % NUV]
            wg, Rwg = WG[g4 % 2]
            S.op("dve", "tensor_tensor", reads=[Rwg, Rgt], writes=[Rwgt[g4]], out=wgt[:, g4 * 4:g4 * 4 + 4], in0=wg[:],
                 in1=gtf[:, g4 * 4:g4 * 4 + 4], op=ALU.mult)
            for i in range(4):
                s_ = g4 * 4 + i
                sv, Rsv = SV[s_ % 4]
                S.op("act", "activation", reads=[Ruvt[i], Rwgt[g4]], writes=[Rsv], out=sv[:], in_=uvt[:, i, 1024:2048],
                     func=AF.Copy, scale=wgt[:, s_:s_ + 1])
                for half in range(2):
                    S.op("pe", "matmul", reads=[Rsv, Ridb], writes=[RPS[5 + half]], out=PS[5 + half][:, :],
                         lhsT=ident_b[:], rhs=sv[:, half * 512:(half + 1) * 512], start=(s_ == 0), stop=(s_ == 127))

        for g4 in range(NUV - 2):
            stA(g4)
        for g4 in range(33):
            if g4 + NUV - 2 < 32:
                stA(g4 + NUV - 2)
            if g4 < 32:
                stBC(g4)
            if g4 >= 1:
                stDE(g4 - 1)
            if bg is not None:
                next(bg, None)
                next(bg, None)
        if bg is not None:
            for _ in bg:
                pass
        for half in range(2):
            cs_ = slice(half * 512, (half + 1) * 512)
            S.op("dve", "tensor_tensor", reads=[RPS[5 + half], Rxr], writes=[Rxr], out=xr[:, cs_],
                 in0=xr[:, cs_], in1=PS[5 + half][:, :], op=ALU.add)
        R = Res()
        S.dma("sp", reads=[Rxr], writes=[R, Rx2d[tb]], out=out_d[tb * 128:(tb + 1) * 128, :], in_=xr[:])
        fin.append(R)

    for _ in peer_setup(0, 0):
        pass
    for tb in range(16):
        bg = peer_setup(tb + 1, (tb + 1) % 2) if tb + 1 < 16 else None
        peer_gather(tb, tb % 2, bg)
    for R in fin:
        S._wait("sp", R.w)
    S.emit()
    return nc


def _consts():
    c = np.zeros((128, NCONST), np.float32)
    p = np.arange(128)
    c[:, 0:128] = np.eye(128)
    c[:, 128:256] = (p[:, None] <= p[None, :])
    c[:, 256:384] = 1.0
    c[:, 384:512] = (p[None, :] - p[:, None])
    c[:, 512:528] = np.arange(16)[None, :]
    k = p[:, None, None]
    j = np.arange(17)[None, :, None]
    q = p[None, None, :]
    delta = (16 - j) * 128 + q - k
    m = ((delta >= 0) & (delta <= 128)).astype(np.float32)
    m += ((delta >= 0) & (delta <= 512) & (delta % 4 == 0))
    m += ((delta >= 0) & (delta <= 2048) & (delta % 16 == 0))
    return c, np.ascontiguousarray(m.reshape(128, 17 * 128).astype(np.float32))


_NC_CACHE = {}


def kernel(**inputs):
    stop_after = os.environ.get("KSTOP") or None
    x = np.asarray(inputs["x"], np.float32)
    mem = np.asarray(inputs["mem"], np.float32)
    cst, mtab = _consts()
    common = {
        "cst": cst, "mtab": mtab,
        "w_in": np.ascontiguousarray(inputs["w_in"][0], dtype=np.float32),
        "convw": np.ascontiguousarray(
            np.asarray(inputs["conv_w"][0], np.float32).T.reshape(18, 128, 4).transpose(1, 0, 2).reshape(128, 72)),
        "convb": np.ascontiguousarray(np.asarray(inputs["conv_b"][0], np.float32).reshape(18, 128).T),
        "w_out": np.ascontiguousarray(inputs["w_out"][0], dtype=np.float32),
    }
    for nm in ("mix_norm_g", "dt_bias", "a_log", "d_skip", "ssd_norm_g", "attn_q_norm_g", "attn_k_norm_g",
               "xattn_norm_g", "mem_norm_g", "mem_q_norm_g", "mem_k_norm_g", "ffn_norm_g"):
        common[nm] = np.ascontiguousarray(np.asarray(inputs[nm][0], np.float32).reshape(1, -1))
    if True:
        common["mem_w_q"] = np.ascontiguousarray(inputs["mem_w_q"][0], dtype=np.float32)
        common["mem_w_kv"] = np.ascontiguousarray(inputs["mem_w_kv"][0], dtype=np.float32)
        common["mem_w_o"] = np.ascontiguousarray(inputs["mem_w_o"][0], dtype=np.float32)
    if True:
        common["peer_w_query"] = np.ascontiguousarray(inputs["peer_w_query"][0], dtype=np.float32)
        common["keys1T"] = np.ascontiguousarray(np.asarray(inputs["peer_sub_keys1"][0], np.float32).T)
        common["keys2T"] = np.ascontiguousarray(np.asarray(inputs["peer_sub_keys2"][0], np.float32).T)
        common["peer_u"] = np.ascontiguousarray(inputs["peer_u"][0], dtype=np.float32)
        common["peer_v"] = np.ascontiguousarray(inputs["peer_v"][0], dtype=np.float32)
    in_maps = []
    for c in range(8):
        b, j = c // 2, c % 2
        xw = np.zeros((WIN, D), np.float32)
        if j == 1:
            xw[:HALF] = x[b, :HALF]
        xw[HALF:] = x[b, j * HALF:(j + 1) * HALF]
        m = dict(common)
        m["xw"] = xw
        m["flag"] = np.full((128, 1), float(j), np.float32)
        if True:
            m["mem"] = np.ascontiguousarray(mem[b])
        in_maps.append(m)
    nc = build(stop_after)
    res = run_bass_kernel_spmd(nc, in_maps, core_ids=list(range(8)))
    out = np.zeros((NB, SEQ, D), np.float32)
    for c in range(8):
        b, j = c // 2, c % 2
        out[b, j * HALF:(j + 1) * HALF] = np.asarray(res.results[c]["out"])
    return out
```

```python
import os
import math
import numpy as np
import concourse.bass as bass
import concourse.mybir as mybir
from concourse.bass_utils import run_bass_kernel_spmd

F32 = mybir.dt.float32
BF16 = mybir.dt.bfloat16
I32 = mybir.dt.int32
U32 = mybir.dt.uint32
AF = mybir.ActivationFunctionType
ALU = mybir.AluOpType
AX = mybir.AxisListType

D = 1024
NB = 4
SEQ = 4096
HALF = 2048
WIN = 4096
EPS = 1e-6
OFF_Z = 1280
OFF_XBC = OFF_Z + 2304
OFF_DT = OFF_XBC + 20
OFF_Q = OFF_DT + 768
OFF_K = OFF_Q + 768
IN_COLS = OFF_K + 768
NCONST = 128 * 4 + 16


class Res:
    __slots__ = ("w", "r")

    def __init__(self):
        self.w = None
        self.r = {}


class Sched:
    COMPUTE = ("pe", "act", "dve", "pool")
    NDMA = 12

    def __init__(self, nc):
        self.nc = nc
        self.streams = {k: [] for k in ("pe", "act", "dve", "pool", "sp")}
        self.esem = {k: nc.alloc_semaphore("es_" + k) for k in self.COMPUTE}
        self.ecnt = {k: 0 for k in self.COMPUTE}
        self.dsem = {q: [nc.alloc_semaphore("ds_%s%d" % (q, i)) for i in range(self.NDMA)]
                     for q in ("sp", "pool")}
        self.dcnt = {q: [0] * self.NDMA for q in ("sp", "pool")}
        self.dnext = {q: 0 for q in ("sp", "pool")}
        self.known = {k: {} for k in self.streams}
        self.nwaits = 0
        self.nops = 0

    def _sem_of(self, pid):
        if pid[0] == "e":
            return self.esem[pid[1]], pid[2], ("e", pid[1])
        return self.dsem[pid[1]][pid[2]], pid[3], ("d", pid[1], pid[2])

    def _wait(self, eng, pid):
        sem, val, key = self._sem_of(pid)
        if self.known[eng].get(key, 0) >= val:
            return
        self.known[eng][key] = val
        self.nwaits += 1
        self.streams[eng].append(lambda e, sem=sem, val=val: e.wait_ge(sem, val))

    def _deps(self, eng, reads, writes):
        deps = []
        for r in reads:
            if r.w is not None:
                deps.append(r.w)
        for w in writes:
            if w.w is not None:
                deps.append(w.w)
            deps.extend(w.r.values())
        for pid in deps:
            if pid[0] == "e" and pid[1] == eng and eng == "pe":
                continue
            self._wait(eng, pid)

    def _commit(self, pid, reads, writes):
        key = pid[:2] if pid[0] == "e" else pid[:3]
        for r in reads:
            r.r[key] = pid
        for w in writes:
            w.w = pid
            w.r = {}

    def op(self, eng, meth, reads=(), writes=(), **kw):
        fn = lambda e, meth=meth, kw=kw: getattr(e, meth)(**kw)
        self._deps(eng, reads, writes)
        self.ecnt[eng] += 1
        idx = self.ecnt[eng]
        sem = self.esem[eng]
        self.streams[eng].append(lambda e, fn=fn, sem=sem: fn(e).then_inc(sem, 1))
        self.nops += 1
        self._commit(("e", eng, idx), reads, writes)

    def dma(self, q, reads=(), writes=(), meth="dma_start", **kw):
        fn = lambda e, meth=meth, kw=kw: getattr(e, meth)(**kw)
        i = self.dnext[q]
        self.dnext[q] = (i + 1) % self.NDMA
        prev = self.dcnt[q][i]
        if prev > 0:
            self._wait(q, ("d", q, i, prev))
        self._deps(q, reads, writes)
        self.dcnt[q][i] = prev + 16
        sem = self.dsem[q][i]
        self.streams[q].append(lambda e, fn=fn, sem=sem: fn(e).then_inc(sem, 16))
        self.nops += 1
        self._commit(("d", q, i, prev + 16), reads, writes)

    def barrier(self):
        for eng in self.streams:
            for k in self.COMPUTE:
                if k != eng and self.ecnt[k] > 0:
                    self._wait(eng, ("e", k, self.ecnt[k]))
            for q in self.dsem:
                for i in range(self.NDMA):
                    if self.dcnt[q][i] > 0:
                        self._wait(eng, ("d", q, i, self.dcnt[q][i]))

    def emit(self):
        nc = self.nc
        with nc.Block() as block:
            @block.tensor
            def _(e):
                for it in self.streams["pe"]:
                    it(e)

            @block.scalar
            def _(e):
                for it in self.streams["act"]:
                    it(e)

            @block.vector
            def _(e):
                for it in self.streams["dve"]:
                    it(e)

            @block.gpsimd
            def _(e):
                for it in self.streams["pool"]:
                    it(e)

            @block.sync
            def _(e):
                for it in self.streams["sp"]:
                    it(e)


class Arena:
    def __init__(self, nc):
        self.nc = nc
        self.off = (nc.sbuf_base + 31) // 32 * 32
        self.top = nc.sbuf_top
        self.n = 0

    def alloc(self, shape, dt=F32, name="t"):
        sz = {F32: 4, BF16: 2, I32: 4, U32: 4}[dt]
        nb = int(np.prod(shape[1:])) * sz
        off = self.off
        self.off += (nb + 31) // 32 * 32
        assert self.off <= self.top, "SBUF overflow at %s: %d > %d" % (name, self.off, self.top)
        self.n += 1
        return self.nc.alloc_sbuf_tensor_at("%s_%d" % (name, self.n), list(shape), dt, offset=off)

    def mark(self):
        return self.off

    def release(self, m):
        self.off = m


def build(stop_after=None):
    nc = bass.Bass("TRN2", target_bir_lowering=False)
    S = Sched(nc)
    A = Arena(nc)

    def din(name, shape, dt=F32):
        return nc.dram_tensor(name, list(shape), dt, kind="ExternalInput").ap()

    xw_d = din("xw", [WIN, D])
    flag_d = din("flag", [128, 1])
    mem_d = din("mem", [256, D])
    cst_d = din("cst", [128, NCONST])
    mtab_d = din("mtab", [128, 17 * 128])
    w_in_d = din("w_in", [D, IN_COLS])
    convw_d = din("convw", [128, 72])
    convb_d = din("convb", [128, 18])
    vec_d = {}
    for nm, n in [("mix_norm_g", 1024), ("dt_bias", 20), ("a_log", 20), ("d_skip", 20), ("ssd_norm_g", 1280),
                  ("attn_q_norm_g", 64), ("attn_k_norm_g", 64), ("xattn_norm_g", 1024), ("mem_norm_g", 1024),
                  ("mem_q_norm_g", 128), ("mem_k_norm_g", 128), ("ffn_norm_g", 1024)]:
        vec_d[nm] = din(nm, [1, n])
    w_out_d = din("w_out", [2048, D])
    mem_w_q_d = din("mem_w_q", [D, 512])
    mem_w_kv_d = din("mem_w_kv", [D, 1024])
    mem_w_o_d = din("mem_w_o", [512, D])
    peer_wq_d = din("peer_w_query", [D, 2048])
    keys1T_d = din("keys1T", [128, 128])
    keys2T_d = din("keys2T", [128, 128])
    peer_u_d = din("peer_u", [16384, D])
    peer_v_d = din("peer_v", [16384, D])
    out_d = nc.dram_tensor("out", [HALF, D], F32, kind="ExternalOutput").ap()
    yT_scr = nc.dram_tensor("yT_scr", [128, 10, HALF], BF16).ap()
    oT_scr = nc.dram_tensor("oT_scr", [128, 6, HALF], BF16).ap()
    uv_scr = nc.dram_tensor("uv_scr", [16384, 2048], BF16).ap()
    Ruv = Res()
    conv_jobs = [(tbl, h) for h in range(32) for tbl in (0, 1)]

    def issue_conv(n):
        for _ in range(n):
            if not conv_jobs or stop_after is not None:
                return
            tbl, h = conv_jobs.pop(0)
            src = peer_u_d if tbl == 0 else peer_v_d
            S.dma("pool", writes=[Ruv], out=uv_scr[h * 512:(h + 1) * 512, tbl * 1024:(tbl + 1) * 1024],
                  in_=src[h * 512:(h + 1) * 512, :])

    PS = [nc.alloc_psum_tensor("ps%d" % i, [128, 512], F32) for i in range(7)]
    RPS = [Res() for _ in range(7)]
    PB = nc.alloc_psum_tensor("psb", [128, 1024], BF16)
    RPB = Res()

    def T(shape, dt=F32, name="t"):
        return A.alloc(shape, dt, name), Res()

    def v3(ap, a):
        return ap.rearrange("p (a b) -> p a b", a=a)

    cst, Rcst = T([128, NCONST], F32, "cst")
    S.dma("sp", writes=[Rcst], out=cst[:], in_=cst_d)
    ident_f = cst[:, 0:128]
    tri_f = cst[:, 128:256]
    ones_f = cst[:, 256:384]
    d0_f = cst[:, 384:512]
    iota16 = cst[:, 512:528]
    ident_b, Ridb = T([128, 128], BF16, "identb")
    S.op("dve", "tensor_copy", reads=[Rcst], writes=[Ridb], out=ident_b[:], in_=ident_f)
    flag, Rflag = T([128, 1], F32, "flag")
    S.dma("sp", writes=[Rflag], out=flag[:], in_=flag_d)
    epsT, Reps = T([128, 1], F32, "eps")
    S.op("pool", "memset", writes=[Reps], ap=epsT[:], constant=EPS)

    def bc_load(n, name):
        t, R = T([128, n], F32, name)
        S.dma("sp", writes=[R], out=t[:], in_=vec_d[name].broadcast_to([128, n]))
        return t, R

    def rstd_from(ss_ap, Rss, n, H, out_ap, Rout, lnv_ap, Rln):
        Rss_l = Rss if isinstance(Rss, list) else [Rss]
        S.op("act", "activation", reads=Rss_l + [Reps], writes=[Rln], out=lnv_ap, in_=ss_ap, func=AF.Ln,
             bias=epsT[:, 0:1], scale=1.0 / n)
        S.op("act", "activation", reads=[Rln], writes=[Rout], out=out_ap, in_=lnv_ap, func=AF.Exp, scale=-0.5)

    nrm_junk, Rnj = T([128, 1024], BF16, "nrmjunk")
    nrm_ss, Rnss = T([128, 4], F32, "nrmss")
    nrm_ln, Rnln = T([128, 4], F32, "nrmln")
    nrm_rs, Rnrs = T([128, 4], F32, "nrmrs")

    def rmsnorm(x_ap, Rx, g_bc, Rg, out_ap, Rout):
        S.op("act", "activation", reads=[Rx], writes=[Rnj, Rnss], out=nrm_junk[:], in_=x_ap, func=AF.Square,
             accum_out=nrm_ss[:, 0:1])
        rstd_from(nrm_ss[:, 0:1], Rnss, 1024, 1, nrm_rs[:, 0:1], Rnrs, nrm_ln[:, 0:1], Rnln)
        S.op("dve", "scalar_tensor_tensor", reads=[Rx, Rnrs, Rg], writes=[Rout], out=out_ap, in0=x_ap,
             scalar=nrm_rs[:, 0:1], in1=g_bc[:], op0=ALU.mult, op1=ALU.mult)

    def transposeN(src_bf, Rsrc, n, width, dst3, Rdst, eng="act"):
        for k0 in range(0, n, 8):
            m = min(8, n - k0)
            for k in range(m):
                S.op("pe", "transpose", reads=[Rsrc, Ridb], writes=[RPB], out=PB[0:width, k * 128:(k + 1) * 128],
                     in_=src_bf[:, (k0 + k) * width:(k0 + k + 1) * width], identity=ident_b[:])
            src = v3(PB[0:width, 0:m * 128], m)
            dst = dst3[:, k0:k0 + m, :]
            if eng == "act":
                S.op("act", "copy", reads=[RPB], writes=Rdst, out=dst, in_=src)
            else:
                S.op("dve", "tensor_copy", reads=[RPB], writes=Rdst, out=dst, in_=src)

    qk_sq, Rqsq = T([128, 512], F32, "qksq")
    qk_tmp, Rqtmp = T([128, 512], F32, "qktmp")

    def qknorm(ps_ap, Rps, H, dh, g_bc, Rg, out_bf, Rout):
        n = H * dh
        S.op("act", "activation", reads=[Rps], writes=[Rqsq], out=qk_sq[:, 0:n], in_=ps_ap, func=AF.Square)
        S.op("dve", "tensor_reduce", reads=[Rqsq], writes=[Rnss], out=nrm_ss[:, 0:H], in_=v3(qk_sq[:, 0:n], H),
             axis=AX.X, op=ALU.add)
        rstd_from(nrm_ss[:, 0:H], Rnss, dh, H, nrm_rs[:, 0:H], Rnrs, nrm_ln[:, 0:H], Rnln)
        S.op("dve", "tensor_tensor", reads=[Rps, Rnrs], writes=[Rqtmp], out=v3(qk_tmp[:, 0:n], H), in0=v3(ps_ap, H),
             in1=nrm_rs[:, 0:H].unsqueeze(2).broadcast_to([128, H, dh]), op=ALU.mult)
        S.op("dve", "tensor_tensor", reads=[Rqtmp, Rg], writes=[Rout], out=out_bf, in0=qk_tmp[:, 0:n], in1=g_bc,
             op=ALU.mult)

    m_base = A.mark()
    hT, _ = T([128, 8, WIN], BF16, "hT")
    RhT = [Res() for _ in range(32)]
    m_after_hT = A.mark()

    gmix, Rgmix = bc_load(1024, "mix_norm_g")
    xb = [T([128, 1024], F32, "xb") for _ in range(2)]
    hb = [T([128, 1024], BF16, "hb") for _ in range(2)]
    for blk in range(32):
        x_t, Rx = xb[blk % 2]
        h_t, Rh = hb[blk % 2]
        S.dma("sp", writes=[Rx], out=x_t[:], in_=xw_d[blk * 128:(blk + 1) * 128, :])
        rmsnorm(x_t[:], Rx, gmix, Rgmix, h_t[:], Rh)
        transposeN(h_t, Rh, 8, 128, hT[:, :, blk * 128:(blk + 1) * 128], [RhT[blk]], eng="act" if blk % 2 else "dve")
    S.barrier()
    A.release(m_after_hT)

    wxbc, _ = T([128, 8, 2304], BF16, "wxbc")
    Rwxbc = [Res() for _ in range(8)]
    wz, _ = T([128, 8, 1280], BF16, "wz")
    Rwz = [Res() for _ in range(8)]
    wdt, _ = T([128, 8, 20], BF16, "wdt")
    Rwdt = [Res() for _ in range(8)]
    for k in range(8):
        S.dma("pool", writes=[Rwxbc[k]], out=wxbc[:, k, :], in_=w_in_d[k * 128:(k + 1) * 128, OFF_Z:OFF_XBC])
        S.dma("pool", writes=[Rwz[k]], out=wz[:, k, :], in_=w_in_d[k * 128:(k + 1) * 128, 0:OFF_Z])
        S.dma("pool", writes=[Rwdt[k]], out=wdt[:, k, :], in_=w_in_d[k * 128:(k + 1) * 128, OFF_XBC:OFF_DT])
    convw, Rcw = T([128, 18, 4], F32, "convw")
    convb, Rcb = T([128, 18], F32, "convb")
    S.dma("sp", writes=[Rcw], out=convw[:].rearrange("p a b -> p (a b)"), in_=convw_d)
    S.dma("sp", writes=[Rcb], out=convb[:], in_=convb_d)
    dtb, Rdtb = bc_load(20, "dt_bias")
    alog, Ralog = bc_load(20, "a_log")
    dsk, Rdsk = bc_load(20, "d_skip")
    gssd, Rgssd = bc_load(1280, "ssd_norm_g")
    a_bc, Ra = T([128, 20], F32, "a_bc")
    S.op("act", "activation", reads=[Ralog], writes=[Ra], out=a_bc[:], in_=alog[:], func=AF.Exp)
    S.op("dve", "tensor_scalar", reads=[Ra], writes=[Ra], out=a_bc[:], in0=a_bc[:], scalar1=-1.0, scalar2=None,
         op0=ALU.mult)
    halo, _ = T([128, 18, 3], F32, "halo")
    Rhalo = [Res() for _ in range(18)]
    S.op("pool", "memset", writes=Rhalo, ap=halo[:], constant=0.0)
    state, Rst = T([128, 1280], F32, "state")
    state_bf, Rstb = T([128, 1280], BF16, "statebf")
    S.op("pool", "memset", writes=[Rst], ap=state[:], constant=0.0)
    S.op("pool", "memset", writes=[Rstb], ap=state_bf[:], constant=0.0)
    U = [T([128, 259], F32, "U") for _ in range(2)]
    acc = [T([128, 256], F32, "acc") for _ in range(2)]
    xTf = [T([128, 256], F32, "xTf") for _ in range(2)]
    x_tm, Rxtm = T([128, 2, 1280], F32, "x_tm")
    BT, _ = T([128, 4, 256], BF16, "BT")
    RBTg = [Res() for _ in range(4)]
    CTt, _ = T([128, 4, 256], BF16, "CT")
    RCTg = [Res() for _ in range(4)]
    B_tm, RBtm = T([128, 2, 512], BF16, "B_tm")
    dt_tm, Rdt = T([128, 2, 20], F32, "dt_tm")
    adt, Radt = T([128, 2, 20], F32, "adt")
    ncs, Rncs = T([128, 20], F32, "ncs")
    tmp20, Rt20 = T([128, 20], F32, "tmp20")
    dec, Rdec = T([128, 20], F32, "dec")
    cd, Rcd = T([128, 20], F32, "cd")
    w2, Rw2 = T([128, 20], F32, "w2")
    ecs, Recs = T([128, 20], F32, "ecs")
    xdd, Rxdd = T([128, 1280], BF16, "xdd")
    xdt, Rxdt = T([128, 1280], BF16, "xdt")
    CBm, RCBm = T([128, 4, 128], F32, "CBm")
    adtb2 = [T([128, 4, 128], F32, "adtb") for _ in range(2)]
    Lt = [T([128, 4, 128], F32, "Lt") for _ in range(2)]
    Wm, RWm = T([128, 20, 128], BF16, "Wm")
    yg4, _ = T([128, 4, 320], F32, "yg4")
    Ryg4 = [Res() for _ in range(4)]
    sz4, _ = T([128, 4, 320], F32, "sz4")
    Rsz4 = [Res() for _ in range(4)]
    sqj, _ = T([128, 320], BF16, "sqj")
    ss4, _ = T([128, 4], F32, "ss4")
    Rss4 = [Res() for _ in range(4)]
    ln4, Rln4 = T([128, 4], F32, "ln4")
    rs4, Rrs4 = T([128, 4], F32, "rs4")
    ytmp, Rytmp = T([128, 320], F32, "ytmp")
    ss1, Rss1 = T([128, 1], F32, "ss1")
    ln1, Rln1 = T([128, 1], F32, "ln1")
    rs1, Rrs1 = T([128, 1], F32, "rs1")
    yn, Ryn = T([128, 1280], BF16, "yn")
    yTc = [T([128, 10, 128], BF16, "yTc") for _ in range(2)]

    for G in range(16):
        own = G >= 8
        RhG = RhT[G * 2:(G + 1) * 2]
        tokG = slice(G * 256, (G + 1) * 256)
        def cc_s1(cc):
            b = cc % 2
            for k in range(8):
                S.op("pe", "matmul", reads=[Rwxbc[k]] + RhG, writes=[RPS[b]], out=PS[b][:, 0:256],
                     lhsT=wxbc[:, k, cc * 128:(cc + 1) * 128], rhs=hT[:, k, tokG], start=(k == 0), stop=(k == 7))
            u_t, Ru = U[b]
            S.op("act", "copy", reads=[RPS[b]], writes=[Ru], out=u_t[:, 3:259], in_=PS[b][:, 0:256])
            S.op("act", "copy", reads=[Rhalo[cc]], writes=[Ru], out=u_t[:, 0:3], in_=halo[:, cc, :])
            S.op("act", "copy", reads=[Ru], writes=[Rhalo[cc]], out=halo[:, cc, :], in_=u_t[:, 256:259])

        def cc_s2(cc):
            b = cc % 2
            u_t, Ru = U[b]
            a_t, Rac = acc[b]
            S.op("dve", "tensor_scalar", reads=[Ru, Rcw], writes=[Rac], out=a_t[:], in0=u_t[:, 3:259],
                 scalar1=convw[:, cc, 3:4], scalar2=None, op0=ALU.mult)
            for j in (2, 1, 0):
                S.op("dve", "scalar_tensor_tensor", reads=[Ru, Rcw, Rac], writes=[Rac], out=a_t[:],
                     in0=u_t[:, j:j + 256], scalar=convw[:, cc, j:j + 1], in1=a_t[:], op0=ALU.mult, op1=ALU.add)
            if cc < 10:
                xt, Rxt = xTf[b]
                S.op("act", "activation", reads=[Rac, Rcb], writes=[Rxt], out=xt[:], in_=a_t[:], func=AF.Silu,
                     bias=convb[:, cc:cc + 1])
            elif cc < 14:
                g = cc - 10
                S.op("act", "activation", reads=[Rac, Rcb], writes=[RBTg[g]], out=BT[:, g, :], in_=a_t[:], func=AF.Silu,
                     bias=convb[:, cc:cc + 1])
            else:
                g = cc - 14
                S.op("act", "activation", reads=[Rac, Rcb], writes=[RCTg[g]], out=CTt[:, g, :], in_=a_t[:], func=AF.Silu,
                     bias=convb[:, cc:cc + 1])

        def cc_tr(cc):
            b = cc % 2
            if cc < 10:
                xt, Rxt = xTf[b]
                pb = 2 + b
                for tb in range(2):
                    S.op("pe", "transpose", reads=[Rxt, Rcst], writes=[RPS[pb]], out=PS[pb][:, tb * 128:(tb + 1) * 128],
                         in_=xt[:, tb * 128:(tb + 1) * 128], identity=ident_f)
                S.op("dve", "tensor_copy", reads=[RPS[pb]], writes=[Rxtm], out=x_tm[:, :, cc * 128:(cc + 1) * 128],
                     in_=v3(PS[pb][:, 0:256], 2))
            elif cc < 14:
                g = cc - 10
                for tb in range(2):
                    S.op("pe", "transpose", reads=[RBTg[g], Ridb], writes=[RPB], out=PB[:, tb * 128:(tb + 1) * 128],
                         in_=BT[:, g, tb * 128:(tb + 1) * 128], identity=ident_b[:])
                S.op("dve", "tensor_copy", reads=[RPB], writes=[RBtm], out=B_tm[:, :, g * 128:(g + 1) * 128],
                     in_=v3(PB[:, 0:256], 2))

        cc_s1(0)
        for cc in range(19):
            if cc + 1 < 18:
                cc_s1(cc + 1)
            if cc < 18:
                cc_s2(cc)
            if cc >= 1:
                cc_tr(cc - 1)
        issue_conv(4)
        for tb in range(2):
            blk = G * 2 + tb
            for k in range(8):
                S.op("pe", "matmul", reads=[Rwdt[k], RhT[blk]], writes=[RPS[4]], out=PS[4][:, tb * 20:(tb + 1) * 20],
                     lhsT=hT[:, k, blk * 128:(blk + 1) * 128], rhs=wdt[:, k, :], start=(k == 0), stop=(k == 7))
        S.op("dve", "tensor_tensor", reads=[RPS[4], Rdtb], writes=[Rdt], out=dt_tm[:], in0=v3(PS[4][:, 0:40], 2),
             in1=dtb[:].unsqueeze(1).broadcast_to([128, 2, 20]), op=ALU.add)
        S.op("act", "activation", reads=[Rdt], writes=[Rdt], out=dt_tm[:], in_=dt_tm[:], func=AF.Exp)
        S.op("act", "activation", reads=[Rdt], writes=[Rdt], out=dt_tm[:], in_=dt_tm[:], func=AF.Ln, bias=1.0)
        S.op("dve", "tensor_tensor", reads=[Rdt, Ra], writes=[Radt], out=adt[:], in0=dt_tm[:],
             in1=a_bc[:].unsqueeze(1).broadcast_to([128, 2, 20]), op=ALU.mult)
        for tb in range(2):
            c = G * 2 + tb
            tk = slice(tb * 128, (tb + 1) * 128)
            S.op("pe", "matmul", reads=[Rcst, Radt], writes=[RPS[5]], out=PS[5][:, 0:20], lhsT=tri_f, rhs=adt[:, tb, :],
                 start=True, stop=True)
            S.op("pe", "matmul", reads=[Rcst, Radt], writes=[RPS[5]], out=PS[5][:, 32:52], lhsT=ones_f, rhs=adt[:, tb, :],
                 start=True, stop=True)
            if own:
                for g in range(4):
                    pz = 6 if g % 2 == 0 else 4
                    for k in range(8):
                        S.op("pe", "matmul", reads=[Rwz[k], RhT[c]], writes=[RPS[pz]], out=PS[pz][:, 0:320],
                             lhsT=hT[:, k, c * 128:(c + 1) * 128], rhs=wz[:, k, g * 320:(g + 1) * 320], start=(k == 0),
                             stop=(k == 7))
                    S.op("act", "activation", reads=[RPS[pz]], writes=[Rsz4[g]], out=sz4[:, g, :], in_=PS[pz][:, 0:320],
                         func=AF.Silu)
            S.op("dve", "tensor_scalar", reads=[RPS[5]], writes=[Rncs], out=ncs[:], in0=PS[5][:, 0:20], scalar1=-1.0,
                 scalar2=None, op0=ALU.mult)
            S.op("dve", "tensor_tensor", reads=[RPS[5], Rncs], writes=[Rt20], out=tmp20[:], in0=PS[5][:, 32:52],
                 in1=ncs[:], op=ALU.add)
            S.op("act", "activation", reads=[Rt20], writes=[Rdec], out=dec[:], in_=tmp20[:], func=AF.Exp)
            S.op("act", "activation", reads=[RPS[5]], writes=[Rcd], out=cd[:], in_=PS[5][:, 32:52], func=AF.Exp)
            S.op("dve", "tensor_tensor", reads=[Rdt, Rdec], writes=[Rw2], out=w2[:], in0=dt_tm[:, tb, :], in1=dec[:],
                 op=ALU.mult)
            S.op("dve", "tensor_tensor", reads=[Rxtm, Rw2], writes=[Rxdd], out=v3(xdd[:], 20),
                 in0=v3(x_tm[:, tb, :], 20), in1=w2[:].unsqueeze(2).broadcast_to([128, 20, 64]), op=ALU.mult)
            if own:
                blk = c
                S.op("act", "activation", reads=[Rncs], writes=[Recs], out=ecs[:], in_=ncs[:], func=AF.Exp, scale=-1.0)
                S.op("dve", "tensor_tensor", reads=[Rxtm, Rdt], writes=[Rxdt], out=v3(xdt[:], 20),
                     in0=v3(x_tm[:, tb, :], 20), in1=dt_tm[:, tb, :].unsqueeze(2).broadcast_to([128, 20, 64]),
                     op=ALU.mult)
                for g in range(4):
                    S.op("pe", "matmul", reads=[RBTg[g], RCTg[g]], writes=[RPS[2]], out=PS[2][:, g * 128:(g + 1) * 128],
                         lhsT=BT[:, g, tk], rhs=CTt[:, g, tk], start=True, stop=True)
                S.op("dve", "tensor_tensor", reads=[RPS[2], Rcst], writes=[RCBm], out=CBm[:], in0=v3(PS[2][:, :], 4),
                     in1=tri_f.unsqueeze(1).broadcast_to([128, 4, 128]), op=ALU.mult)
                def hq_copy(hq):
                    adtb, Radtb = adtb2[hq % 2]
                    S.op("dve", "tensor_copy", reads=[Radt], writes=[Radtb], out=adtb[:],
                         in_=adt[:, tb, hq * 4:hq * 4 + 4].unsqueeze(2).broadcast_to([128, 4, 128]))
                    pb = 3 + hq % 2
                    for i in range(4):
                        S.op("pe", "matmul", reads=[Radtb, Rcst], writes=[RPS[pb]], out=PS[pb][:, i * 128:(i + 1) * 128],
                             lhsT=adtb[:, i, :], rhs=tri_f, start=True, stop=True)

                def hq_rest(hq):
                    pb = 3 + hq % 2
                    lt, Rlt = Lt[hq % 2]
                    S.op("dve", "tensor_tensor", reads=[RPS[pb], Rncs], writes=[Rlt], out=lt[:], in0=v3(PS[pb][:, :], 4),
                         in1=ncs[:, hq * 4:hq * 4 + 4].unsqueeze(2).broadcast_to([128, 4, 128]), op=ALU.add)
                    S.op("act", "activation", reads=[Rlt], writes=[Rlt], out=lt[:], in_=lt[:], func=AF.Exp)
                    for i in range(4):
                        h = hq * 4 + i
                        S.op("dve", "scalar_tensor_tensor", reads=[Rlt, RCBm], writes=[RWm],
                             out=Wm[:, h, :], in0=lt[:, i, :], scalar=1.0, in1=CBm[:, h // 5, :], op0=ALU.min,
                             op1=ALU.mult)

                hq_copy(0)
                for hq in range(5):
                    if hq + 1 < 5:
                        hq_copy(hq + 1)
                    hq_rest(hq)
                for g in range(4):
                    gs = slice(g * 320, (g + 1) * 320)
                    po = 0 if g % 2 == 0 else 2
                    pd = 1 if g % 2 == 0 else 3
                    S.op("pe", "matmul", reads=[RCTg[g], Rstb], writes=[RPS[po]], out=PS[po][:, 0:320], lhsT=CTt[:, g, tk],
                         rhs=state_bf[:, gs], start=True, stop=True)
                    for hh in range(5):
                        h = g * 5 + hh
                        S.op("pe", "matmul", reads=[RWm, Rxdt], writes=[RPS[pd]], out=PS[pd][:, hh * 64:(hh + 1) * 64],
                             lhsT=Wm[:, h, :], rhs=xdt[:, h * 64:(h + 1) * 64], start=True, stop=True)
                    ygg = yg4[:, g, :]
                    S.op("dve", "tensor_tensor", reads=[RPS[po], Recs], writes=[Ryg4[g]], out=v3(ygg, 5),
                         in0=v3(PS[po][:, 0:320], 5), in1=ecs[:, g * 5:g * 5 + 5].unsqueeze(2).broadcast_to([128, 5, 64]),
                         op=ALU.mult)
                    S.op("dve", "tensor_tensor", reads=[RPS[pd], Ryg4[g]], writes=[Ryg4[g]], out=ygg, in0=ygg,
                         in1=PS[pd][:, 0:320], op=ALU.add)
                    S.op("dve", "tensor_tensor", reads=[Rxtm, Rdsk], writes=[Rytmp], out=v3(ytmp[:], 5),
                         in0=v3(x_tm[:, tb, gs], 5), in1=dsk[:, g * 5:g * 5 + 5].unsqueeze(2).broadcast_to([128, 5, 64]),
                         op=ALU.mult)
                    S.op("dve", "tensor_tensor", reads=[Rytmp, Ryg4[g]], writes=[Ryg4[g]], out=ygg, in0=ygg, in1=ytmp[:],
                         op=ALU.add)
                    S.op("dve", "tensor_tensor", reads=[Ryg4[g], Rsz4[g]], writes=[Ryg4[g]], out=ygg, in0=ygg,
                         in1=sz4[:, g, :], op=ALU.mult)
                    S.op("act", "activation", reads=[Ryg4[g]], writes=[Rss4[g]], out=sqj[:], in_=ygg, func=AF.Square,
                         accum_out=ss4[:, g:g + 1])
                rstd_from(ss4[:, 0:4], Rss4, 320, 4, rs4[:, 0:4], Rrs4, ln4[:, 0:4], Rln4)
                for g in range(4):
                    gs = slice(g * 320, (g + 1) * 320)
                    S.op("dve", "scalar_tensor_tensor", reads=[Ryg4[g], Rrs4, Rgssd], writes=[Ryn], out=yn[:, gs],
                         in0=yg4[:, g, :], scalar=rs4[:, g:g + 1], in1=gssd[:, gs], op0=ALU.mult, op1=ALU.mult)
                yt, Ryt = yTc[tb % 2]
                transposeN(yn, Ryn, 10, 128, yt[:], [Ryt], eng="act")
                ob = c - 16
                S.dma("sp", reads=[Ryt], writes=[Res()], out=yT_scr[:, :, ob * 128:(ob + 1) * 128], in_=yt[:])
            for g in range(4):
                gs = slice(g * 320, (g + 1) * 320)
                S.op("pe", "matmul", reads=[RBtm, Rxdd], writes=[RPS[6]], out=PS[6][:, 0:320],
                     lhsT=B_tm[:, tb, g * 128:(g + 1) * 128], rhs=xdd[:, gs], start=True, stop=True)
                S.op("dve", "tensor_tensor", reads=[Rst, Rcd], writes=[Rst], out=v3(state[:, gs], 5),
                     in0=v3(state[:, gs], 5), in1=cd[:, g * 5:g * 5 + 5].unsqueeze(2).broadcast_to([128, 5, 64]),
                     op=ALU.mult)
                S.op("dve", "tensor_tensor", reads=[Rst, RPS[6]], writes=[Rst], out=state[:, gs], in0=state[:, gs],
                     in1=PS[6][:, 0:320], op=ALU.add)
            if c == 15:
                S.op("dve", "tensor_scalar", reads=[Rst, Rflag], writes=[Rst], out=state[:], in0=state[:],
                     scalar1=flag[:, 0:1], scalar2=None, op0=ALU.mult)
            if c >= 15:
                S.op("act", "copy", reads=[Rst], writes=[Rstb], out=state_bf[:], in_=state[:])
    S.barrier()
    A.release(m_after_hT)

    mtab, Rmtab = T([128, 17, 128], BF16, "mtab")
    S.dma("pool", writes=[Rmtab], out=mtab[:].rearrange("p a b -> p (a b)"), in_=mtab_d)
    gq64, Rgq64 = bc_load(64, "attn_q_norm_g")
    gk64, Rgk64 = bc_load(64, "attn_k_norm_g")
    gq, Rgq = T([128, 256], F32, "gq")
    gk, Rgk = T([128, 256], F32, "gk")
    S.op("dve", "tensor_scalar", reads=[Rgq64], writes=[Rgq], out=v3(gq[:], 4),
         in0=gq64[:].unsqueeze(1).broadcast_to([128, 4, 64]), scalar1=0.125, scalar2=None, op0=ALU.mult)
    S.op("dve", "tensor_copy", reads=[Rgk64], writes=[Rgk], out=v3(gk[:], 4),
         in_=gk64[:].unsqueeze(1).broadcast_to([128, 4, 64]))
    wq, _ = T([128, 8, 256], BF16, "wq")
    Rwq = [Res() for _ in range(8)]
    wk, _ = T([128, 8, 256], BF16, "wk")
    Rwk = [Res() for _ in range(8)]
    wv, _ = T([128, 8, 256], BF16, "wv")
    Rwv = [Res() for _ in range(8)]
    KT, _ = T([64, 4, WIN], BF16, "KT")
    RKT = [Res() for _ in range(32)]
    QT, _ = T([64, 4, HALF], BF16, "QT")
    RQT = [Res() for _ in range(16)]
    Vaug, _ = T([128, 32, 4, 65], BF16, "Vaug")
    RV = [Res() for _ in range(32)]
    kn = [T([128, 256], BF16, "kn") for _ in range(2)]
    qn_ = [T([128, 256], BF16, "qn_") for _ in range(2)]
    Aexp, RAexp = T([128, 128], F32, "Aexp")
    Erev, REr = T([128, 17, 128], F32, "Erev")
    Pf = [T([128, 512], F32, "Pf") for _ in range(3)]
    Pbf = [T([128, 512], BF16, "Pbf") for _ in range(3)]
    rd, Rrd = T([128, 2], F32, "rd")
    o_all, _ = T([128, 16, 256], BF16, "o_all")
    Roall = [Res() for _ in range(16)]
    oTc = [T([128, 2, 128], BF16, "oTc") for _ in range(2)]
    for r in range(3):
        for k in range(8):
            rows = slice(k * 128, (k + 1) * 128)
            S.dma("pool", writes=[Rwq[k]], out=wq[:, k, :], in_=w_in_d[rows, OFF_DT + r * 256:OFF_DT + (r + 1) * 256])
            S.dma("pool", writes=[Rwk[k]], out=wk[:, k, :], in_=w_in_d[rows, OFF_Q + r * 256:OFF_Q + (r + 1) * 256])
            S.dma("pool", writes=[Rwv[k]], out=wv[:, k, :], in_=w_in_d[rows, OFF_K + r * 256:OFF_K + (r + 1) * 256])

        def proj_mm(blk):
            bs = slice(blk * 128, (blk + 1) * 128)
            pa = blk % 2
            for k in range(8):
                S.op("pe", "matmul", reads=[Rwk[k], RhT[blk]], writes=[RPS[pa]], out=PS[pa][:, 0:256], lhsT=hT[:, k, bs],
                     rhs=wk[:, k, :], start=(k == 0), stop=(k == 7))
            kt, Rkn = kn[blk % 2]
            qknorm(PS[pa][:, 0:256], RPS[pa], 4, 64, gk[:], Rgk, kt[:], Rkn)
            pv = 2 + blk % 2
            for k in range(8):
                S.op("pe", "matmul", reads=[Rwv[k], RhT[blk]], writes=[RPS[pv]], out=PS[pv][:, 0:256], lhsT=hT[:, k, bs],
                     rhs=wv[:, k, :], start=(k == 0), stop=(k == 7))
            if blk >= 16:
                S.op("act", "copy", reads=[RPS[pv]], writes=[RV[blk]], out=Vaug[:, blk, :, 0:64],
                     in_=v3(PS[pv][:, 0:256], 4))
                S.op("pool", "memset", writes=[RV[blk]], ap=Vaug[:, blk, :, 64:65], constant=1.0)
            else:
                S.op("dve", "tensor_scalar", reads=[RPS[pv], Rflag], writes=[RV[blk]], out=Vaug[:, blk, :, 0:64],
                     in0=v3(PS[pv][:, 0:256], 4), scalar1=flag[:, 0:1], scalar2=None, op0=ALU.mult)
                S.op("pool", "tensor_copy", reads=[Rflag], writes=[RV[blk]], out=Vaug[:, blk, :, 64:65],
                     in_=flag[:, 0:1].unsqueeze(1).broadcast_to([128, 4, 1]))
            if blk >= 16:
                pq = 4 + blk % 2
                for k in range(8):
                    S.op("pe", "matmul", reads=[Rwq[k], RhT[blk]], writes=[RPS[pq]], out=PS[pq][:, 0:256], lhsT=hT[:, k, bs],
                         rhs=wq[:, k, :], start=(k == 0), stop=(k == 7))
                qt, Rqn = qn_[blk % 2]
                qknorm(PS[pq][:, 0:256], RPS[pq], 4, 64, gq[:], Rgq, qt[:], Rqn)

        def proj_tr(blk):
            bs = slice(blk * 128, (blk + 1) * 128)
            kt, Rkn = kn[blk % 2]
            transposeN(kt, Rkn, 4, 64, KT[:, :, bs], [RKT[blk]], eng="act")
            if blk >= 16:
                qt, Rqn = qn_[blk % 2]
                transposeN(qt, Rqn, 4, 64, QT[:, :, (blk - 16) * 128:(blk - 15) * 128], [RQT[blk - 16]], eng="act")

        for blk in range(33):
            if blk < 32:
                proj_mm(blk)
            if blk >= 1:
                proj_tr(blk - 1)

        seq = []
        for hh in range(4):
            for qb in range(16, 32):
                kbs = list(range(qb - 16, qb + 1))
                for gi in range(5):
                    seq.append((hh, qb, gi, kbs[gi * 4:gi * 4 + 4]))

        def emit_S(i):
            hh, qb, gi, grp = seq[i]
            qi = qb - 16
            sbk = i % 4
            for ii, kb in enumerate(grp):
                S.op("pe", "matmul", reads=[RKT[kb], RQT[qi]], writes=[RPS[sbk]],
                     out=PS[sbk][:, ii * 128:(ii + 1) * 128], lhsT=KT[:, hh, kb * 128:(kb + 1) * 128],
                     rhs=QT[:, hh, qi * 128:(qi + 1) * 128], start=True, stop=True)

        def emit_rest(i):
            hh, qb, gi, grp = seq[i]
            qi = qb - 16
            n = len(grp)
            sbk = i % 4
            unit = i // 5
            ob = 4 + unit % 2
            pf, Rpf = Pf[i % 3]
            pbf, Rpbf = Pbf[i % 3]
            if qb == 16 and gi == 0:
                h = r * 4 + hh
                slope = 2.0 ** (-8.0 * (h + 1) / 12.0)
                S.op("act", "activation", reads=[Rcst], writes=[RAexp], out=Aexp[:], in_=d0_f, func=AF.Exp, scale=-slope)
                for j in range(17):
                    cpow = math.exp(-slope * 128.0 * (16 - j))
                    S.op("dve", "scalar_tensor_tensor", reads=[RAexp, Rmtab], writes=[REr],
                         out=Erev[:, j, :], in0=Aexp[:], scalar=float(cpow), in1=mtab[:, j, :], op0=ALU.mult,
                         op1=ALU.mult)
            S.op("act", "activation", reads=[RPS[sbk]], writes=[Rpf], out=pf[:, 0:n * 128],
                 in_=PS[sbk][:, 0:n * 128], func=AF.Exp)
            S.op("dve", "tensor_tensor", reads=[Rpf, REr], writes=[Rpbf], out=pbf[:, 0:n * 128],
                 in0=pf[:, 0:n * 128], in1=Erev[:, gi * 4:gi * 4 + n, :].rearrange("p a b -> p (a b)"),
                 op=ALU.mult)
            for ii, kb in enumerate(grp):
                S.op("pe", "matmul", reads=[Rpbf, RV[kb]], writes=[RPS[ob]], out=PS[ob][:, 0:65],
                     lhsT=pbf[:, ii * 128:(ii + 1) * 128], rhs=Vaug[:, kb, hh, :], start=(gi == 0 and ii == 0),
                     stop=(gi == 4))
            if gi == 4:
                S.op("dve", "reciprocal", reads=[RPS[ob]], writes=[Rrd], out=rd[:, 0:1], in_=PS[ob][:, 64:65])
                S.op("dve", "tensor_scalar", reads=[RPS[ob], Rrd], writes=[Roall[qi]],
                     out=o_all[:, qi, hh * 64:(hh + 1) * 64], in0=PS[ob][:, 0:64], scalar1=rd[:, 0:1], scalar2=None,
                     op0=ALU.mult)

        LA = 3
        for i in range(LA):
            emit_S(i)
        for i in range(len(seq)):
            if i + LA < len(seq):
                emit_S(i + LA)
            emit_rest(i)
        for qi in range(16):
            for i in range(2):
                S.op("pe", "transpose", reads=[Roall[qi], Ridb], writes=[RPB], out=PB[:, i * 128:(i + 1) * 128],
                     in_=o_all[:, qi, i * 128:(i + 1) * 128], identity=ident_b[:])
            ot, Rot = oTc[qi % 2]
            S.op("act", "copy", reads=[RPB], writes=[Rot], out=ot[:], in_=v3(PB[:, 0:256], 2))
            S.dma("sp", reads=[Rot], writes=[Res()], out=oT_scr[:, 2 * r:2 * r + 2, qi * 128:(qi + 1) * 128], in_=ot[:])
    S.barrier()
    A.release(m_base)

    x1, _ = T([128, 16, 1024], F32, "x1")
    Rx1 = [Res() for _ in range(16)]
    m_after_x1 = A.mark()
    yT_sb, _ = T([128, 10, HALF], BF16, "yT_sb")
    RyT = [Res() for _ in range(10)]
    for kc in range(10):
        S.dma("sp", writes=[RyT[kc]], out=yT_sb[:, kc, :], in_=yT_scr[:, kc, :])
    oT, _ = T([128, 6, HALF], BF16, "oT")
    RoTs = [Res() for _ in range(6)]
    for kc in range(6):
        S.dma("sp", writes=[RoTs[kc]], out=oT[:, kc, :], in_=oT_scr[:, kc, :])
    wout, _ = T([128, 16, 1024], BF16, "wout")
    Rwout = [Res() for _ in range(16)]
    for kc in range(16):
        S.dma("pool", writes=[Rwout[kc]], out=wout[:, kc, :], in_=w_out_d[kc * 128:(kc + 1) * 128, :])
    xo = [T([128, 1024], F32, "xo") for _ in range(2)]
    for tb in range(16):
        ts_ = slice(tb * 128, (tb + 1) * 128)
        xo_t, Rxo = xo[tb % 2]
        S.dma("sp", writes=[Rxo], out=xo_t[:], in_=xw_d[HALF + tb * 128:HALF + (tb + 1) * 128, :])
        for half in range(2):
            pb = 2 * (tb % 2) + half
            cs_ = slice(half * 512, (half + 1) * 512)
            for kc in range(16):
                lhsT = yT_sb[:, kc, ts_] if kc < 10 else oT[:, kc - 10, ts_]
                S.op("pe", "matmul", reads=[RyT[kc] if kc < 10 else RoTs[kc - 10], Rwout[kc]], writes=[RPS[pb]], out=PS[pb][:, :], lhsT=lhsT,
                     rhs=wout[:, kc, cs_], start=(kc == 0), stop=(kc == 15))
            S.op("dve", "tensor_tensor", reads=[RPS[pb], Rxo], writes=[Rx1[tb]], out=x1[:, tb, cs_], in0=PS[pb][:, :],
                 in1=xo_t[:, cs_], op=ALU.add)
    S.barrier()
    A.release(m_after_x1)

    def store_x1_and_finish():
        fin = []
        for tb in range(16):
            R = Res()
            S.dma("sp", reads=[Rx1[tb]], writes=[R], out=out_d[tb * 128:(tb + 1) * 128, :], in_=x1[:, tb, :])
            fin.append(R)
        for R in fin:
            S._wait("sp", R.w)
        S.emit()
        return nc

    if stop_after == "C":
        return store_x1_and_finish()

    wqm, _ = T([128, 8, 512], BF16, "wqm")
    Rwqm = [Res() for _ in range(8)]
    wkv, _ = T([128, 8, 1024], BF16, "wkv")
    Rwkv = [Res() for _ in range(8)]
    wo, _ = T([128, 4, 1024], BF16, "wo")
    Rwo = [Res() for _ in range(4)]
    for k in range(8):
        S.dma("pool", writes=[Rwqm[k]], out=wqm[:, k, :], in_=mem_w_q_d[k * 128:(k + 1) * 128, :])
        S.dma("pool", writes=[Rwkv[k]], out=wkv[:, k, :], in_=mem_w_kv_d[k * 128:(k + 1) * 128, :])
    for k in range(4):
        S.dma("pool", writes=[Rwo[k]], out=wo[:, k, :], in_=mem_w_o_d[k * 128:(k + 1) * 128, :])
    gx, Rgx = bc_load(1024, "xattn_norm_g")
    gm, Rgm = bc_load(1024, "mem_norm_g")
    gqm128, Rgqm128 = bc_load(128, "mem_q_norm_g")
    gkm128, Rgkm128 = bc_load(128, "mem_k_norm_g")
    gqm, Rgqm = T([128, 512], F32, "gqm")
    gkm, Rgkm = T([128, 512], F32, "gkm")
    S.op("dve", "tensor_scalar", reads=[Rgqm128], writes=[Rgqm], out=v3(gqm[:], 4),
         in0=gqm128[:].unsqueeze(1).broadcast_to([128, 4, 128]), scalar1=128.0 ** -0.5, scalar2=None, op0=ALU.mult)
    S.op("dve", "tensor_copy", reads=[Rgkm128], writes=[Rgkm], out=v3(gkm[:], 4),
         in_=gkm128[:].unsqueeze(1).broadcast_to([128, 4, 128]))
    KmT, RKmT = T([128, 4, 256], BF16, "KmT")
    Vm, RVm = T([128, 2, 4, 129], BF16, "Vm")
    S.op("pool", "memset", writes=[RVm], ap=Vm[:].rearrange("p a b c -> p (a b c)"), constant=1.0)
    memb, Rmemb = T([128, 1024], F32, "memb")
    mh, Rmh = T([128, 1024], BF16, "mh")
    mT, RmT = T([128, 8, 128], BF16, "mT")
    knm, Rknm = T([128, 512], BF16, "knm")
    for mb in range(2):
        ms = slice(mb * 128, (mb + 1) * 128)
        S.dma("sp", writes=[Rmemb], out=memb[:], in_=mem_d[ms, :])
        rmsnorm(memb[:], Rmemb, gm, Rgm, mh[:], Rmh)
        transposeN(mh, Rmh, 8, 128, mT[:], [RmT], eng="act")
        for half in range(2):
            for k in range(8):
                S.op("pe", "matmul", reads=[RmT, Rwkv[k]], writes=[RPS[half]], out=PS[half][:, :], lhsT=mT[:, k, :],
                     rhs=wkv[:, k, half * 512:(half + 1) * 512], start=(k == 0), stop=(k == 7))
        qknorm(PS[0][:, :], RPS[0], 4, 128, gkm[:], Rgkm, knm[:], Rknm)
        transposeN(knm, Rknm, 4, 128, KmT[:, :, ms], [RKmT], eng="act")
        S.op("act", "copy", reads=[RPS[1]], writes=[RVm], out=Vm[:, mb, :, 0:128], in_=v3(PS[1][:, :], 4))
    h2, Rh2 = T([128, 1024], BF16, "h2")
    h2T, Rh2T = T([128, 8, 128], BF16, "h2T")
    qn, Rqn2 = T([128, 512], BF16, "qn")
    qT, RqT = T([128, 4, 128], BF16, "qT")
    Pm = [T([128, 256], BF16, "Pm") for _ in range(2)]
    o2, Ro2 = T([128, 512], BF16, "o2")
    o2T, Ro2T = T([128, 4, 128], BF16, "o2T")
    for tb in range(16):
        rmsnorm(x1[:, tb, :], Rx1[tb], gx, Rgx, h2[:], Rh2)
        transposeN(h2, Rh2, 8, 128, h2T[:], [Rh2T], eng="act")
        for k in range(8):
            S.op("pe", "matmul", reads=[Rh2T, Rwqm[k]], writes=[RPS[2]], out=PS[2][:, :], lhsT=h2T[:, k, :], rhs=wqm[:, k, :],
                 start=(k == 0), stop=(k == 7))
        qknorm(PS[2][:, :], RPS[2], 4, 128, gqm[:], Rgqm, qn[:], Rqn2)
        transposeN(qn, Rqn2, 4, 128, qT[:], [RqT], eng="act")
        for hh in range(4):
            sb_ = 3 + hh % 2
            ob = 5 + hh % 2
            pm, Rpm = Pm[hh % 2]
            for mb in range(2):
                S.op("pe", "matmul", reads=[RKmT, RqT], writes=[RPS[sb_]], out=PS[sb_][:, mb * 128:(mb + 1) * 128],
                     lhsT=KmT[:, hh, mb * 128:(mb + 1) * 128], rhs=qT[:, hh, :], start=True, stop=True)
            S.op("act", "activation", reads=[RPS[sb_]], writes=[Rpm], out=pm[:], in_=PS[sb_][:, 0:256], func=AF.Exp)
            for mb in range(2):
                S.op("pe", "matmul", reads=[Rpm, RVm], writes=[RPS[ob]], out=PS[ob][:, 0:129],
                     lhsT=pm[:, mb * 128:(mb + 1) * 128], rhs=Vm[:, mb, hh, :], start=(mb == 0), stop=(mb == 1))
            S.op("dve", "reciprocal", reads=[RPS[ob]], writes=[Rrd], out=rd[:, 0:1], in_=PS[ob][:, 128:129])
            S.op("dve", "tensor_scalar", reads=[RPS[ob], Rrd], writes=[Ro2], out=o2[:, hh * 128:(hh + 1) * 128],
                 in0=PS[ob][:, 0:128], scalar1=rd[:, 0:1], scalar2=None, op0=ALU.mult)
        transposeN(o2, Ro2, 4, 128, o2T[:], [Ro2T], eng="act")
        for half in range(2):
            cs_ = slice(half * 512, (half + 1) * 512)
            for kc in range(4):
                S.op("pe", "matmul", reads=[Ro2T, Rwo[kc]], writes=[RPS[half]], out=PS[half][:, :], lhsT=o2T[:, kc, :],
                     rhs=wo[:, kc, cs_], start=(kc == 0), stop=(kc == 3))
            S.op("dve", "tensor_tensor", reads=[RPS[half], Rx1[tb]], writes=[Rx1[tb]], out=x1[:, tb, cs_],
                 in0=PS[half][:, :], in1=x1[:, tb, cs_], op=ALU.add)
    if stop_after == "D":
        S.barrier()
        return store_x1_and_finish()
    Rx2d = [Res() for _ in range(16)]
    for tb in range(16):
        S.dma("sp", reads=[Rx1[tb]], writes=[Rx2d[tb]], out=out_d[tb * 128:(tb + 1) * 128, :], in_=x1[:, tb, :])
    S.barrier()
    A.release(m_base)

    wpq, _ = T([128, 8, 2048], BF16, "wpq")
    Rwpq = [Res() for _ in range(8)]
    for k in range(8):
        S.dma("pool", writes=[Rwpq[k]], out=wpq[:, k, :], in_=peer_wq_d[k * 128:(k + 1) * 128, :])
    keysT, RkeysT = T([128, 2, 128], F32, "keysT")
    S.dma("sp", writes=[RkeysT], out=keysT[:, 0, :], in_=keys1T_d)
    S.dma("sp", writes=[RkeysT], out=keysT[:, 1, :], in_=keys2T_d)
    gf, Rgf = bc_load(1024, "ffn_norm_g")
    xrp = [T([128, 1024], F32, "xr") for _ in range(2)]
    h3p = [T([128, 1024], F32, "h3") for _ in range(2)]
    h3bp = [T([128, 1024], BF16, "h3b") for _ in range(2)]
    h3T, Rh3T = T([128, 8, 128], BF16, "h3T")
    off_q = A.mark()
    qrT, RqrT = T([128, 16, 128], F32, "qrT")
    eq = nc.alloc_sbuf_tensor_at("eq_alias", [128, 8, 16, 16], F32, offset=off_q)
    Req = RqrT
    off_s = A.mark()
    sc, Rsc = T([128, 16, 128], F32, "sc")
    cand = nc.alloc_sbuf_tensor_at("cand_alias", [128, 8, 256], F32, offset=off_s)
    Rcand = Rsc
    sc2, Rsc2 = T([128, 128], F32, "sc2")
    tv, Rtv = T([128, 16, 16], F32, "tv")
    ti, Rti = T([128, 16, 16], U32, "ti")
    tif, Rtif = T([128, 16, 16], F32, "tif")
    cand2, Rcand2 = T([128, 256], F32, "cand2")
    cv, Rcv = T([128, 8, 16], F32, "cv")
    cp, Rcp = T([128, 8, 16], U32, "cp")
    ca, Rca = T([128, 8, 16], U32, "ca")
    cb_, Rcb_ = T([128, 8, 16], U32, "cb")
    caf, Rcaf = T([128, 8, 16], F32, "caf")
    cbf, Rcbf = T([128, 8, 16], F32, "cbf")
    i1f, Ri1f = T([128, 8, 16], F32, "i1f")
    i2f, Ri2f = T([128, 8, 16], F32, "i2f")
    ef, Ref = T([128, 128], F32, "ef")
    eip = [T([128, 128], I32, "ei") for _ in range(2)]
    gtp = [T([128, 8, 16], F32, "gt") for _ in range(2)]
    gsum, Rgsum = T([128, 8], F32, "gsum")
    actv, _ = T([128, 128], F32, "actv")
    Ractv = [Res() for _ in range(32)]
    WG = [T([128, 4], F32, "wg4") for _ in range(2)]
    wgt, _ = T([128, 128], F32, "wgt")
    Rwgt = [Res() for _ in range(32)]
    djunk, Rdj = T([128, 1024], BF16, "djunk")
    NUV = 6
    UVt = [(T([128, 4, 2048], BF16, "UV")[0], [Res() for _ in range(4)]) for _ in range(NUV)]
    SV = [T([128, 1024], BF16, "SV") for _ in range(4)]
    fin = []

    def peer_setup(tb, p):
        xr, Rxr = xrp[p]
        h3, Rh3 = h3p[p]
        h3b, Rh3b = h3bp[p]
        ei, Rei = eip[p]
        gt, Rgt = gtp[p]
        S.dma("sp", reads=[Rx2d[tb]], writes=[Rxr], out=xr[:], in_=out_d[tb * 128:(tb + 1) * 128, :])
        rmsnorm(xr[:], Rxr, gf, Rgf, h3[:], Rh3)
        S.op("act", "copy", reads=[Rh3], writes=[Rh3b], out=h3b[:], in_=h3[:])
        transposeN(h3b, Rh3b, 8, 128, h3T[:], [Rh3T], eng="act")
        yield
        for q4 in range(4):
            pb = q4 % 2
            for i in range(4):
                ccq = q4 * 4 + i
                for k in range(8):
                    S.op("pe", "matmul", reads=[Rwpq[k], Rh3T], writes=[RPS[pb]], out=PS[pb][:, i * 128:(i + 1) * 128],
                         lhsT=wpq[:, k, ccq * 128:(ccq + 1) * 128], rhs=h3T[:, k, :], start=(k == 0), stop=(k == 7))
            S.op("act", "copy", reads=[RPS[pb]], writes=[RqrT], out=qrT[:, q4 * 4:q4 * 4 + 4, :], in_=v3(PS[pb][:, :], 4))
            yield
        for q4 in range(4):
            pb = 2 + q4 % 2
            for i in range(4):
                j = q4 * 4 + i
                S.op("pe", "matmul", reads=[RqrT, RkeysT], writes=[RPS[pb]], out=PS[pb][:, i * 128:(i + 1) * 128],
                     lhsT=qrT[:, j, :], rhs=keysT[:, j % 2, :], start=True, stop=True)
            S.op("act", "copy", reads=[RPS[pb]], writes=[Rsc], out=sc[:, q4 * 4:q4 * 4 + 4, :], in_=v3(PS[pb][:, :], 4))
        for j in range(16):
            S.op("dve", "max", reads=[Rsc], writes=[Rtv], out=tv[:, j, 0:8], in_=sc[:, j, :])
            S.op("dve", "max_index", reads=[Rsc, Rtv], writes=[Rti], out=ti[:, j, 0:8], in_max=tv[:, j, 0:8],
                 in_values=sc[:, j, :])
            S.op("dve", "match_replace", reads=[Rsc, Rtv], writes=[Rsc2], out=sc2[:], in_to_replace=tv[:, j, 0:8],
                 in_values=sc[:, j, :], imm_value=-1e30)
            S.op("dve", "max", reads=[Rsc2], writes=[Rtv], out=tv[:, j, 8:16], in_=sc2[:])
            S.op("dve", "max_index", reads=[Rsc2, Rtv], writes=[Rti], out=ti[:, j, 8:16], in_max=tv[:, j, 8:16],
                 in_values=sc2[:])
            yield
        S.op("dve", "tensor_copy", reads=[Rti], writes=[Rtif], out=tif[:], in_=ti[:])
        tv4 = tv[:].rearrange("p (h two) k -> p h two k", two=2)
        tif4 = tif[:].rearrange("p (h two) k -> p h two k", two=2)
        S.op("dve", "tensor_tensor", reads=[Rtv], writes=[Rcand], out=cand[:].rearrange("p h (a b) -> p h a b", a=16),
             in0=tv4[:, :, 0, :].unsqueeze(3).broadcast_to([128, 8, 16, 16]),
             in1=tv4[:, :, 1, :].unsqueeze(2).broadcast_to([128, 8, 16, 16]), op=ALU.add)
        for h in range(8):
            S.op("dve", "max", reads=[Rcand], writes=[Rcv], out=cv[:, h, 0:8], in_=cand[:, h, :])
            S.op("dve", "max_index", reads=[Rcand, Rcv], writes=[Rcp], out=cp[:, h, 0:8], in_max=cv[:, h, 0:8],
                 in_values=cand[:, h, :])
            S.op("dve", "match_replace", reads=[Rcand, Rcv], writes=[Rcand2], out=cand2[:], in_to_replace=cv[:, h, 0:8],
                 in_values=cand[:, h, :], imm_value=-1e30)
            S.op("dve", "max", reads=[Rcand2], writes=[Rcv], out=cv[:, h, 8:16], in_=cand2[:])
            S.op("dve", "max_index", reads=[Rcand2, Rcv], writes=[Rcp], out=cp[:, h, 8:16], in_max=cv[:, h, 8:16],
                 in_values=cand2[:])
            yield
        S.op("dve", "tensor_single_scalar", reads=[Rcp], writes=[Rca], out=ca[:], in_=cp[:], scalar=4,
             op=ALU.logical_shift_right)
        S.op("dve", "tensor_single_scalar", reads=[Rcp], writes=[Rcb_], out=cb_[:], in_=cp[:], scalar=15,
             op=ALU.bitwise_and)
        S.op("dve", "tensor_copy", reads=[Rca], writes=[Rcaf], out=caf[:], in_=ca[:])
        S.op("dve", "tensor_copy", reads=[Rcb_], writes=[Rcbf], out=cbf[:], in_=cb_[:])
        io4 = iota16.unsqueeze(1).unsqueeze(1).broadcast_to([128, 8, 16, 16])
        for (sel, Rsel, half, dst, Rdst) in ((caf, Rcaf, 0, i1f, Ri1f), (cbf, Rcbf, 1, i2f, Ri2f)):
            S.op("dve", "tensor_tensor", reads=[Rsel, Rcst], writes=[Req], out=eq[:],
                 in0=sel[:].unsqueeze(3).broadcast_to([128, 8, 16, 16]), in1=io4, op=ALU.is_equal)
            S.op("dve", "tensor_tensor", reads=[Req, Rtif], writes=[Req], out=eq[:], in0=eq[:],
                 in1=tif4[:, :, half, :].unsqueeze(2).broadcast_to([128, 8, 16, 16]), op=ALU.mult)
            S.op("dve", "tensor_reduce", reads=[Req], writes=[Rdst], out=dst[:].rearrange("p h k -> p (h k)"),
                 in_=eq[:].rearrange("p h k a -> p (h k) a"), axis=AX.X, op=ALU.add)
        S.op("dve", "scalar_tensor_tensor", reads=[Ri1f, Ri2f], writes=[Ref], out=ef[:],
             in0=i1f[:].rearrange("p h k -> p (h k)"), scalar=128.0, in1=i2f[:].rearrange("p h k -> p (h k)"),
             op0=ALU.mult, op1=ALU.add)
        S.op("dve", "tensor_copy", reads=[Ref], writes=[Rei], out=ei[:], in_=ef[:])
        S.op("dve", "tensor_tensor", reads=[Rcv], writes=[Rgt], out=gt[:], in0=cv[:],
             in1=cv[:, :, 0:1].broadcast_to([128, 8, 16]), op=ALU.subtract)
        S.op("act", "activation", reads=[Rgt], writes=[Rgt], out=gt[:], in_=gt[:], func=AF.Exp)
        S.op("dve", "tensor_reduce", reads=[Rgt], writes=[Rgsum], out=gsum[:], in_=gt[:], axis=AX.X, op=ALU.add)
        S.op("dve", "reciprocal", reads=[Rgsum], writes=[Rgsum], out=gsum[:], in_=gsum[:])
        S.op("dve", "tensor_tensor", reads=[Rgt, Rgsum], writes=[Rgt], out=gt[:], in0=gt[:],
             in1=gsum[:].unsqueeze(2).broadcast_to([128, 8, 16]), op=ALU.mult)

    def peer_gather(tb, p, bg):
        xr, Rxr = xrp[p]
        h3b, Rh3b = h3bp[p]
        ei, Rei = eip[p]
        gt, Rgt = gtp[p]
        gtf = gt[:].rearrange("p h k -> p (h k)")
        S.op("pool", "memset", writes=Ractv, ap=actv[:], constant=0.0)

        def stA(g4):
            uvt, Ruvt = UVt[g4 % NUV]
            for i in range(4):
                s_ = g4 * 4 + i
                S.dma("pool", reads=[Rei, Ruv], writes=[Ruvt[i]], meth="indirect_dma_start", out=uvt[:, i, :],
                      out_offset=None, in_=uv_scr, in_offset=bass.IndirectOffsetOnAxis(ap=ei[:, s_:s_ + 1], axis=0))

        def stBC(g4):
            uvt, Ruvt = UVt[g4 % NUV]
            for i in range(4):
                s_ = g4 * 4 + i
                S.op("dve", "scalar_tensor_tensor", reads=[Ruvt[i], Rh3b], writes=[Ractv[g4]], out=djunk[:],
                     in0=uvt[:, i, 0:1024], scalar=1.0, in1=h3b[:], op0=ALU.mult, op1=ALU.mult,
                     accum_out=actv[:, s_:s_ + 1])
            wg, Rwg = WG[g4 % 2]
            S.op("act", "activation", reads=[Ractv[g4]], writes=[Rwg], out=wg[:], in_=actv[:, g4 * 4:g4 * 4 + 4],
                 func=AF.Gelu)

        def stDE(g4):
            uvt, Ruvt = UVt[g4 % NUV]
            wg, Rwg = WG[g4 % 2]
            S.op("dve", "tensor_tensor", reads=[Rwg, Rgt], writes=[Rwgt[g4]], out=wgt[:, g4 * 4:g4 * 4 + 4], in0=wg[:],
                 in1=gtf[:, g4 * 4:g4 * 4 + 4], op=ALU.mult)
            for i in range(4):
                s_ = g4 * 4 + i
                sv, Rsv = SV[s_ % 4]
                S.op("act", "activation", reads=[Ruvt[i], Rwgt[g4]], writes=[Rsv], out=sv[:], in_=uvt[:, i, 1024:2048],
                     func=AF.Copy, scale=wgt[:, s_:s_ + 1])
                for half in range(2):
                    S.op("pe", "matmul", reads=[Rsv, Ridb], writes=[RPS[5 + half]], out=PS[5 + half][:, :],
                         lhsT=ident_b[:], rhs=sv[:, half * 512:(half + 1) * 512], start=(s_ == 0), stop=(s_ == 127))

        for g4 in range(NUV - 2):
            stA(g4)
        for g4 in range(33):
            if g4 + NUV - 2 < 32:
                stA(g4 + NUV - 2)
            if g4 < 32:
                stBC(g4)
            if g4 >= 1:
                stDE(g4 - 1)
            if bg is not None:
                next(bg, None)
                next(bg, None)
        if bg is not None:
            for _ in bg:
                pass
        for half in range(2):
            cs_ = slice(half * 512, (half + 1) * 512)
            S.op("dve", "tensor_tensor", reads=[RPS[5 + half], Rxr], writes=[Rxr], out=xr[:, cs_],
                 in0=xr[:, cs_], in1=PS[5 + half][:, :], op=ALU.add)
        R = Res()
        S.dma("sp", reads=[Rxr], writes=[R, Rx2d[tb]], out=out_d[tb * 128:(tb + 1) * 128, :], in_=xr[:])
        fin.append(R)

    for _ in peer_setup(0, 0):
        pass
    for tb in range(16):
        bg = peer_setup(tb + 1, (tb + 1) % 2) if tb + 1 < 16 else None
        peer_gather(tb, tb % 2, bg)
    for R in fin:
        S._wait("sp", R.w)
    S.emit()
    return nc


def _consts():
    c = np.zeros((128, NCONST), np.float32)
    p = np.arange(128)
    c[:, 0:128] = np.eye(128)
    c[:, 128:256] = (p[:, None] <= p[None, :])
    c[:, 256:384] = 1.0
    c[:, 384:512] = (p[None, :] - p[:, None])
    c[:, 512:528] = np.arange(16)[None, :]
    k = p[:, None, None]
    j = np.arange(17)[None, :, None]
    q = p[None, None, :]
    delta = (16 - j) * 128 + q - k
    m = ((delta >= 0) & (delta <= 128)).astype(np.float32)
    m += ((delta >= 0) & (delta <= 512) & (delta % 4 == 0))
    m += ((delta >= 0) & (delta <= 2048) & (delta % 16 == 0))
    return c, np.ascontiguousarray(m.reshape(128, 17 * 128).astype(np.float32))


_NC_CACHE = {}


def kernel(**inputs):
    stop_after = os.environ.get("KSTOP") or None
    x = np.asarray(inputs["x"], np.float32)
    mem = np.asarray(inputs["mem"], np.float32)
    cst, mtab = _consts()
    common = {
        "cst": cst, "mtab": mtab,
        "w_in": np.ascontiguousarray(inputs["w_in"][0], dtype=np.float32),
        "convw": np.ascontiguousarray(
            np.asarray(inputs["conv_w"][0], np.float32).T.reshape(18, 128, 4).transpose(1, 0, 2).reshape(128, 72)),
        "convb": np.ascontiguousarray(np.asarray(inputs["conv_b"][0], np.float32).reshape(18, 128).T),
        "w_out": np.ascontiguousarray(inputs["w_out"][0], dtype=np.float32),
    }
    for nm in ("mix_norm_g", "dt_bias", "a_log", "d_skip", "ssd_norm_g", "attn_q_norm_g", "attn_k_norm_g",
               "xattn_norm_g", "mem_norm_g", "mem_q_norm_g", "mem_k_norm_g", "ffn_norm_g"):
        common[nm] = np.ascontiguousarray(np.asarray(inputs[nm][0], np.float32).reshape(1, -1))
    if True:
        common["mem_w_q"] = np.ascontiguousarray(inputs["mem_w_q"][0], dtype=np.float32)
        common["mem_w_kv"] = np.ascontiguousarray(inputs["mem_w_kv"][0], dtype=np.float32)
        common["mem_w_o"] = np.ascontiguousarray(inputs["mem_w_o"][0], dtype=np.float32)
    if True:
        common["peer_w_query"] = np.ascontiguousarray(inputs["peer_w_query"][0], dtype=np.float32)
        common["keys1T"] = np.ascontiguousarray(np.asarray(inputs["peer_sub_keys1"][0], np.float32).T)
        common["keys2T"] = np.ascontiguousarray(np.asarray(inputs["peer_sub_keys2"][0], np.float32).T)
        common["peer_u"] = np.ascontiguousarray(inputs["peer_u"][0], dtype=np.float32)
        common["peer_v"] = np.ascontiguousarray(inputs["peer_v"][0], dtype=np.float32)
    in_maps = []
    for c in range(8):
        b, j = c // 2, c % 2
        xw = np.zeros((WIN, D), np.float32)
        if j == 1:
            xw[:HALF] = x[b, :HALF]
        xw[HALF:] = x[b, j * HALF:(j + 1) * HALF]
        m = dict(common)
        m["xw"] = xw
        m["flag"] = np.full((128, 1), float(j), np.float32)
        if True:
            m["mem"] = np.ascontiguousarray(mem[b])
        in_maps.append(m)
    nc = build(stop_after)
    res = run_bass_kernel_spmd(nc, in_maps, core_ids=list(range(8)))
    out = np.zeros((NB, SEQ, D), np.float32)
    for c in range(8):
        b, j = c // 2, c % 2
        out[b, j * HALF:(j + 1) * HALF] = np.asarray(res.results[c]["out"])
    return out
```

```python
import os
import math
import numpy as np
import concourse.bass as bass
import concourse.mybir as mybir
from concourse.bass_utils import run_bass_kernel_spmd

F32 = mybir.dt.float32
BF16 = mybir.dt.bfloat16
I32 = mybir.dt.int32
U32 = mybir.dt.uint32
AF = mybir.ActivationFunctionType
ALU = mybir.AluOpType
AX = mybir.AxisListType

D = 1024
NB = 4
SEQ = 4096
HALF = 2048
WIN = 4096
EPS = 1e-6
OFF_Z = 1280
OFF_XBC = OFF_Z + 2304
OFF_DT = OFF_XBC + 20
OFF_Q = OFF_DT + 768
OFF_K = OFF_Q + 768
IN_COLS = OFF_K + 768
NCONST = 128 * 4 + 16


class Res:
    __slots__ = ("w", "r")

    def __init__(self):
        self.w = None
        self.r = {}


class Sched:
    COMPUTE = ("pe", "act", "dve", "pool")
    NDMA = 12
    INORDER = ("pe",)

    def __init__(self, nc):
        self.nc = nc
        self.streams = {k: [] for k in ("pe", "act", "dve", "pool", "sp")}
        self.esem = {k: nc.alloc_semaphore("es_" + k) for k in self.COMPUTE}
        self.ecnt = {k: 0 for k in self.COMPUTE}
        self.dsem = {q: [nc.alloc_semaphore("ds_%s%d" % (q, i)) for i in range(self.NDMA)]
                     for q in ("sp", "pool")}
        self.dcnt = {q: [0] * self.NDMA for q in ("sp", "pool")}
        self.dnext = {q: 0 for q in ("sp", "pool")}
        self.known = {k: {} for k in self.streams}
        self.nwaits = 0
        self.nops = 0

    def _sem_of(self, pid):
        if pid[0] == "e":
            return self.esem[pid[1]], pid[2], ("e", pid[1])
        return self.dsem[pid[1]][pid[2]], pid[3], ("d", pid[1], pid[2])

    def _wait(self, eng, pid):
        sem, val, key = self._sem_of(pid)
        if self.known[eng].get(key, 0) >= val:
            return
        self.known[eng][key] = val
        self.nwaits += 1
        self.streams[eng].append(lambda e, sem=sem, val=val: e.wait_ge(sem, val))

    def _deps(self, eng, reads, writes):
        deps = []
        for r in reads:
            if r.w is not None:
                deps.append(r.w)
        for w in writes:
            if w.w is not None:
                deps.append(w.w)
            deps.extend(w.r.values())
        for pid in deps:
            if pid[0] == "e" and pid[1] == eng and eng in self.INORDER:
                continue
            self._wait(eng, pid)

    def _commit(self, pid, reads, writes):
        key = pid[:2] if pid[0] == "e" else pid[:3]
        for r in reads:
            r.r[key] = pid
        for w in writes:
            w.w = pid
            w.r = {}

    def op(self, eng, meth, reads=(), writes=(), **kw):
        fn = lambda e, meth=meth, kw=kw: getattr(e, meth)(**kw)
        self._deps(eng, reads, writes)
        self.ecnt[eng] += 1
        idx = self.ecnt[eng]
        sem = self.esem[eng]
        self.streams[eng].append(lambda e, fn=fn, sem=sem: fn(e).then_inc(sem, 1))
        self.nops += 1
        self._commit(("e", eng, idx), reads, writes)

    def dma(self, q, reads=(), writes=(), meth="dma_start", **kw):
        fn = lambda e, meth=meth, kw=kw: getattr(e, meth)(**kw)
        i = self.dnext[q]
        self.dnext[q] = (i + 1) % self.NDMA
        prev = self.dcnt[q][i]
        if prev > 0:
            self._wait(q, ("d", q, i, prev))
        self._deps(q, reads, writes)
        self.dcnt[q][i] = prev + 16
        sem = self.dsem[q][i]
        self.streams[q].append(lambda e, fn=fn, sem=sem: fn(e).then_inc(sem, 16))
        self.nops += 1
        self._commit(("d", q, i, prev + 16), reads, writes)

    def barrier(self):
        for eng in self.streams:
            for k in self.COMPUTE:
                if k != eng and self.ecnt[k] > 0:
                    self._wait(eng, ("e", k, self.ecnt[k]))
            for q in self.dsem:
                for i in range(self.NDMA):
                    if self.dcnt[q][i] > 0:
                        self._wait(eng, ("d", q, i, self.dcnt[q][i]))

    def emit(self):
        nc = self.nc
        with nc.Block() as block:
            @block.tensor
            def _(e):
                for it in self.streams["pe"]:
                    it(e)

            @block.scalar
            def _(e):
                for it in self.streams["act"]:
                    it(e)

            @block.vector
            def _(e):
                for it in self.streams["dve"]:
                    it(e)

            @block.gpsimd
            def _(e):
                for it in self.streams["pool"]:
                    it(e)

            @block.sync
            def _(e):
                for it in self.streams["sp"]:
                    it(e)


class Arena:
    def __init__(self, nc):
        self.nc = nc
        self.off = (nc.sbuf_base + 31) // 32 * 32
        self.top = nc.sbuf_top
        self.n = 0

    def alloc(self, shape, dt=F32, name="t"):
        sz = {F32: 4, BF16: 2, I32: 4, U32: 4}[dt]
        nb = int(np.prod(shape[1:])) * sz
        off = self.off
        self.off += (nb + 31) // 32 * 32
        assert self.off <= self.top, "SBUF overflow at %s: %d > %d" % (name, self.off, self.top)
        self.n += 1
        return self.nc.alloc_sbuf_tensor_at("%s_%d" % (name, self.n), list(shape), dt, offset=off)

    def mark(self):
        return self.off

    def release(self, m):
        self.off = m


def build(stop_after=None):
    nc = bass.Bass("TRN2", target_bir_lowering=False)
    S = Sched(nc)
    A = Arena(nc)

    def din(name, shape, dt=F32):
        return nc.dram_tensor(name, list(shape), dt, kind="ExternalInput").ap()

    xw_d = din("xw", [WIN, D])
    flag_d = din("flag", [128, 1])
    mem_d = din("mem", [256, D])
    cst_d = din("cst", [128, NCONST])
    mtab_d = din("mtab", [128, 17 * 128])
    w_in_d = din("w_in", [D, IN_COLS])
    convw_d = din("convw", [128, 72])
    convb_d = din("convb", [128, 18])
    vec_d = {}
    for nm, n in [("mix_norm_g", 1024), ("dt_bias", 20), ("a_log", 20), ("d_skip", 20), ("ssd_norm_g", 1280),
                  ("attn_q_norm_g", 64), ("attn_k_norm_g", 64), ("xattn_norm_g", 1024), ("mem_norm_g", 1024),
                  ("mem_q_norm_g", 128), ("mem_k_norm_g", 128), ("ffn_norm_g", 1024)]:
        vec_d[nm] = din(nm, [1, n])
    w_out_d = din("w_out", [2048, D])
    mem_w_q_d = din("mem_w_q", [D, 512])
    mem_w_kv_d = din("mem_w_kv", [D, 1024])
    mem_w_o_d = din("mem_w_o", [512, D])
    peer_wq_d = din("peer_w_query", [D, 2048])
    keys1T_d = din("keys1T", [128, 128])
    keys2T_d = din("keys2T", [128, 128])
    peer_u_d = din("peer_u", [16384, D])
    peer_v_d = din("peer_v", [16384, D])
    out_d = nc.dram_tensor("out", [HALF, D], F32, kind="ExternalOutput").ap()
    yT_scr = nc.dram_tensor("yT_scr", [128, 10, HALF], BF16).ap()
    oT_scr = nc.dram_tensor("oT_scr", [128, 6, HALF], BF16).ap()
    uv_scr = nc.dram_tensor("uv_scr", [16384, 2048], BF16).ap()
    Ruv = Res()
    conv_jobs = [(tbl, h) for h in range(32) for tbl in (0, 1)]

    def issue_conv(n):
        for _ in range(n):
            if not conv_jobs or stop_after is not None:
                return
            tbl, h = conv_jobs.pop(0)
            src = peer_u_d if tbl == 0 else peer_v_d
            S.dma("pool", writes=[Ruv], out=uv_scr[h * 512:(h + 1) * 512, tbl * 1024:(tbl + 1) * 1024],
                  in_=src[h * 512:(h + 1) * 512, :])

    PS = [nc.alloc_psum_tensor("ps%d" % i, [128, 512], F32) for i in range(7)]
    RPS = [Res() for _ in range(7)]
    PB = nc.alloc_psum_tensor("psb", [128, 1024], BF16)
    RPB = Res()

    def T(shape, dt=F32, name="t"):
        return A.alloc(shape, dt, name), Res()

    def v3(ap, a):
        return ap.rearrange("p (a b) -> p a b", a=a)

    cst, Rcst = T([128, NCONST], F32, "cst")
    S.dma("sp", writes=[Rcst], out=cst[:], in_=cst_d)
    ident_f = cst[:, 0:128]
    tri_f = cst[:, 128:256]
    ones_f = cst[:, 256:384]
    d0_f = cst[:, 384:512]
    iota16 = cst[:, 512:528]
    ident_b, Ridb = T([128, 128], BF16, "identb")
    S.op("dve", "tensor_copy", reads=[Rcst], writes=[Ridb], out=ident_b[:], in_=ident_f)
    flag, Rflag = T([128, 1], F32, "flag")
    S.dma("sp", writes=[Rflag], out=flag[:], in_=flag_d)
    epsT, Reps = T([128, 1], F32, "eps")
    S.op("pool", "memset", writes=[Reps], ap=epsT[:], constant=EPS)

    def bc_load(n, name):
        t, R = T([128, n], F32, name)
        S.dma("sp", writes=[R], out=t[:], in_=vec_d[name].broadcast_to([128, n]))
        return t, R

    def rstd_from(ss_ap, Rss, n, H, out_ap, Rout, lnv_ap, Rln):
        Rss_l = Rss if isinstance(Rss, list) else [Rss]
        S.op("act", "activation", reads=Rss_l + [Reps], writes=[Rln], out=lnv_ap, in_=ss_ap, func=AF.Ln,
             bias=epsT[:, 0:1], scale=1.0 / n)
        S.op("act", "activation", reads=[Rln], writes=[Rout], out=out_ap, in_=lnv_ap, func=AF.Exp, scale=-0.5)

    nrm_junk, Rnj = T([128, 1024], BF16, "nrmjunk")
    nrm_ss, Rnss = T([128, 4], F32, "nrmss")
    nrm_ln, Rnln = T([128, 4], F32, "nrmln")
    nrm_rs, Rnrs = T([128, 4], F32, "nrmrs")

    def rmsnorm(x_ap, Rx, g_bc, Rg, out_ap, Rout):
        S.op("act", "activation", reads=[Rx], writes=[Rnj, Rnss], out=nrm_junk[:], in_=x_ap, func=AF.Square,
             accum_out=nrm_ss[:, 0:1])
        rstd_from(nrm_ss[:, 0:1], Rnss, 1024, 1, nrm_rs[:, 0:1], Rnrs, nrm_ln[:, 0:1], Rnln)
        S.op("dve", "scalar_tensor_tensor", reads=[Rx, Rnrs, Rg], writes=[Rout], out=out_ap, in0=x_ap,
             scalar=nrm_rs[:, 0:1], in1=g_bc[:], op0=ALU.mult, op1=ALU.mult)

    def transposeN(src_bf, Rsrc, n, width, dst3, Rdst, eng="act"):
        for k0 in range(0, n, 8):
            m = min(8, n - k0)
            for k in range(m):
                S.op("pe", "transpose", reads=[Rsrc, Ridb], writes=[RPB], out=PB[0:width, k * 128:(k + 1) * 128],
                     in_=src_bf[:, (k0 + k) * width:(k0 + k + 1) * width], identity=ident_b[:])
            src = v3(PB[0:width, 0:m * 128], m)
            dst = dst3[:, k0:k0 + m, :]
            if eng == "act":
                S.op("act", "copy", reads=[RPB], writes=Rdst, out=dst, in_=src)
            else:
                S.op("dve", "tensor_copy", reads=[RPB], writes=Rdst, out=dst, in_=src)

    qk_sq, Rqsq = T([128, 512], F32, "qksq")
    qk_tmp, Rqtmp = T([128, 512], F32, "qktmp")

    def qknorm(ps_ap, Rps, H, dh, g_bc, Rg, out_bf, Rout):
        n = H * dh
        S.op("act", "activation", reads=[Rps], writes=[Rqsq], out=qk_sq[:, 0:n], in_=ps_ap, func=AF.Square)
        S.op("dve", "tensor_reduce", reads=[Rqsq], writes=[Rnss], out=nrm_ss[:, 0:H], in_=v3(qk_sq[:, 0:n], H),
             axis=AX.X, op=ALU.add)
        rstd_from(nrm_ss[:, 0:H], Rnss, dh, H, nrm_rs[:, 0:H], Rnrs, nrm_ln[:, 0:H], Rnln)
        S.op("dve", "tensor_tensor", reads=[Rps, Rnrs], writes=[Rqtmp], out=v3(qk_tmp[:, 0:n], H), in0=v3(ps_ap, H),
             in1=nrm_rs[:, 0:H].unsqueeze(2).broadcast_to([128, H, dh]), op=ALU.mult)
        S.op("dve", "tensor_tensor", reads=[Rqtmp, Rg], writes=[Rout], out=out_bf, in0=qk_tmp[:, 0:n], in1=g_bc,
             op=ALU.mult)

    m_base = A.mark()
    hT, _ = T([128, 8, WIN], BF16, "hT")
    RhT = [Res() for _ in range(32)]
    m_after_hT = A.mark()

    gmix, Rgmix = bc_load(1024, "mix_norm_g")
    xb = [T([128, 1024], F32, "xb") for _ in range(2)]
    hb = [T([128, 1024], BF16, "hb") for _ in range(2)]
    for blk in range(32):
        x_t, Rx = xb[blk % 2]
        h_t, Rh = hb[blk % 2]
        S.dma("sp", writes=[Rx], out=x_t[:], in_=xw_d[blk * 128:(blk + 1) * 128, :])
        rmsnorm(x_t[:], Rx, gmix, Rgmix, h_t[:], Rh)
        transposeN(h_t, Rh, 8, 128, hT[:, :, blk * 128:(blk + 1) * 128], [RhT[blk]], eng="act" if blk % 2 else "dve")
    S.barrier()
    A.release(m_after_hT)

    wxbc, _ = T([128, 8, 2304], BF16, "wxbc")
    Rwxbc = [Res() for _ in range(8)]
    wz, _ = T([128, 8, 1280], BF16, "wz")
    Rwz = [Res() for _ in range(8)]
    wdt, _ = T([128, 8, 20], BF16, "wdt")
    Rwdt = [Res() for _ in range(8)]
    for k in range(8):
        S.dma("pool", writes=[Rwxbc[k]], out=wxbc[:, k, :], in_=w_in_d[k * 128:(k + 1) * 128, OFF_Z:OFF_XBC])
        S.dma("pool", writes=[Rwz[k]], out=wz[:, k, :], in_=w_in_d[k * 128:(k + 1) * 128, 0:OFF_Z])
        S.dma("pool", writes=[Rwdt[k]], out=wdt[:, k, :], in_=w_in_d[k * 128:(k + 1) * 128, OFF_XBC:OFF_DT])
    convw, Rcw = T([128, 18, 4], F32, "convw")
    convb, Rcb = T([128, 18], F32, "convb")
    S.dma("sp", writes=[Rcw], out=convw[:].rearrange("p a b -> p (a b)"), in_=convw_d)
    S.dma("sp", writes=[Rcb], out=convb[:], in_=convb_d)
    dtb, Rdtb = bc_load(20, "dt_bias")
    alog, Ralog = bc_load(20, "a_log")
    dsk, Rdsk = bc_load(20, "d_skip")
    gssd, Rgssd = bc_load(1280, "ssd_norm_g")
    a_bc, Ra = T([128, 20], F32, "a_bc")
    S.op("act", "activation", reads=[Ralog], writes=[Ra], out=a_bc[:], in_=alog[:], func=AF.Exp)
    S.op("dve", "tensor_scalar", reads=[Ra], writes=[Ra], out=a_bc[:], in0=a_bc[:], scalar1=-1.0, scalar2=None,
         op0=ALU.mult)
    halo, _ = T([128, 18, 3], F32, "halo")
    Rhalo = [Res() for _ in range(18)]
    S.op("pool", "memset", writes=Rhalo, ap=halo[:], constant=0.0)
    state, Rst = T([128, 1280], F32, "state")
    state_bf, Rstb = T([128, 1280], BF16, "statebf")
    S.op("pool", "memset", writes=[Rst], ap=state[:], constant=0.0)
    S.op("pool", "memset", writes=[Rstb], ap=state_bf[:], constant=0.0)
    U = [T([128, 259], F32, "U") for _ in range(2)]
    acc = [T([128, 256], F32, "acc") for _ in range(2)]
    xTf = [T([128, 256], F32, "xTf") for _ in range(2)]
    x_tm, Rxtm = T([128, 2, 1280], F32, "x_tm")
    BT, _ = T([128, 4, 256], BF16, "BT")
    RBTg = [Res() for _ in range(4)]
    CTt, _ = T([128, 4, 256], BF16, "CT")
    RCTg = [Res() for _ in range(4)]
    B_tm, RBtm = T([128, 2, 512], BF16, "B_tm")
    dt_tm, Rdt = T([128, 2, 20], F32, "dt_tm")
    adt, Radt = T([128, 2, 20], F32, "adt")
    ncs, Rncs = T([128, 20], F32, "ncs")
    tmp20, Rt20 = T([128, 20], F32, "tmp20")
    dec, Rdec = T([128, 20], F32, "dec")
    cd, Rcd = T([128, 20], F32, "cd")
    w2, Rw2 = T([128, 20], F32, "w2")
    ecs, Recs = T([128, 20], F32, "ecs")
    xdd, Rxdd = T([128, 1280], BF16, "xdd")
    xdt, Rxdt = T([128, 1280], BF16, "xdt")
    CBm, RCBm = T([128, 4, 128], F32, "CBm")
    adtb2 = [T([128, 4, 128], F32, "adtb") for _ in range(2)]
    Lt = [T([128, 4, 128], F32, "Lt") for _ in range(2)]
    Wm, RWm = T([128, 20, 128], BF16, "Wm")
    yg4, _ = T([128, 4, 320], F32, "yg4")
    Ryg4 = [Res() for _ in range(4)]
    sz4, _ = T([128, 4, 320], F32, "sz4")
    Rsz4 = [Res() for _ in range(4)]
    sqj, _ = T([128, 320], BF16, "sqj")
    ss4, _ = T([128, 4], F32, "ss4")
    Rss4 = [Res() for _ in range(4)]
    ln4, Rln4 = T([128, 4], F32, "ln4")
    rs4, Rrs4 = T([128, 4], F32, "rs4")
    ytmp, Rytmp = T([128, 320], F32, "ytmp")
    ss1, Rss1 = T([128, 1], F32, "ss1")
    ln1, Rln1 = T([128, 1], F32, "ln1")
    rs1, Rrs1 = T([128, 1], F32, "rs1")
    yn, Ryn = T([128, 1280], BF16, "yn")
    yTc = [T([128, 10, 128], BF16, "yTc") for _ in range(2)]

    for G in range(16):
        own = G >= 8
        RhG = RhT[G * 2:(G + 1) * 2]
        tokG = slice(G * 256, (G + 1) * 256)
        def cc_s1(cc):
            b = cc % 2
            for k in range(8):
                S.op("pe", "matmul", reads=[Rwxbc[k]] + RhG, writes=[RPS[b]], out=PS[b][:, 0:256],
                     lhsT=wxbc[:, k, cc * 128:(cc + 1) * 128], rhs=hT[:, k, tokG], start=(k == 0), stop=(k == 7))
            u_t, Ru = U[b]
            S.op("act", "copy", reads=[RPS[b]], writes=[Ru], out=u_t[:, 3:259], in_=PS[b][:, 0:256])
            S.op("act", "copy", reads=[Rhalo[cc]], writes=[Ru], out=u_t[:, 0:3], in_=halo[:, cc, :])
            S.op("act", "copy", reads=[Ru], writes=[Rhalo[cc]], out=halo[:, cc, :], in_=u_t[:, 256:259])

        def cc_s2(cc):
            b = cc % 2
            u_t, Ru = U[b]
            a_t, Rac = acc[b]
            S.op("dve", "tensor_scalar", reads=[Ru, Rcw], writes=[Rac], out=a_t[:], in0=u_t[:, 3:259],
                 scalar1=convw[:, cc, 3:4], scalar2=None, op0=ALU.mult)
            for j in (2, 1, 0):
                S.op("dve", "scalar_tensor_tensor", reads=[Ru, Rcw, Rac], writes=[Rac], out=a_t[:],
                     in0=u_t[:, j:j + 256], scalar=convw[:, cc, j:j + 1], in1=a_t[:], op0=ALU.mult, op1=ALU.add)
            if cc < 10:
                xt, Rxt = xTf[b]
                S.op("act", "activation", reads=[Rac, Rcb], writes=[Rxt], out=xt[:], in_=a_t[:], func=AF.Silu,
                     bias=convb[:, cc:cc + 1])
            elif cc < 14:
                g = cc - 10
                S.op("act", "activation", reads=[Rac, Rcb], writes=[RBTg[g]], out=BT[:, g, :], in_=a_t[:], func=AF.Silu,
                     bias=convb[:, cc:cc + 1])
            else:
                g = cc - 14
                S.op("act", "activation", reads=[Rac, Rcb], writes=[RCTg[g]], out=CTt[:, g, :], in_=a_t[:], func=AF.Silu,
                     bias=convb[:, cc:cc + 1])

        def cc_tr(cc):
            b = cc % 2
            if cc < 10:
                xt, Rxt = xTf[b]
                pb = 2 + b
                for tb in range(2):
                    S.op("pe", "transpose", reads=[Rxt, Rcst], writes=[RPS[pb]], out=PS[pb][:, tb * 128:(tb + 1) * 128],
                         in_=xt[:, tb * 128:(tb + 1) * 128], identity=ident_f)
                S.op("dve", "tensor_copy", reads=[RPS[pb]], writes=[Rxtm], out=x_tm[:, :, cc * 128:(cc + 1) * 128],
                     in_=v3(PS[pb][:, 0:256], 2))
            elif cc < 14:
                g = cc - 10
                for tb in range(2):
                    S.op("pe", "transpose", reads=[RBTg[g], Ridb], writes=[RPB], out=PB[:, tb * 128:(tb + 1) * 128],
                         in_=BT[:, g, tb * 128:(tb + 1) * 128], identity=ident_b[:])
                S.op("dve", "tensor_copy", reads=[RPB], writes=[RBtm], out=B_tm[:, :, g * 128:(g + 1) * 128],
                     in_=v3(PB[:, 0:256], 2))

        cc_s1(0)
        for cc in range(19):
            if cc + 1 < 18:
                cc_s1(cc + 1)
            if cc < 18:
                cc_s2(cc)
            if cc >= 1:
                cc_tr(cc - 1)
        issue_conv(4)
        for tb in range(2):
            blk = G * 2 + tb
            for k in range(8):
                S.op("pe", "matmul", reads=[Rwdt[k], RhT[blk]], writes=[RPS[4]], out=PS[4][:, tb * 20:(tb + 1) * 20],
                     lhsT=hT[:, k, blk * 128:(blk + 1) * 128], rhs=wdt[:, k, :], start=(k == 0), stop=(k == 7))
        S.op("dve", "tensor_tensor", reads=[RPS[4], Rdtb], writes=[Rdt], out=dt_tm[:], in0=v3(PS[4][:, 0:40], 2),
             in1=dtb[:].unsqueeze(1).broadcast_to([128, 2, 20]), op=ALU.add)
        S.op("act", "activation", reads=[Rdt], writes=[Rdt], out=dt_tm[:], in_=dt_tm[:], func=AF.Exp)
        S.op("act", "activation", reads=[Rdt], writes=[Rdt], out=dt_tm[:], in_=dt_tm[:], func=AF.Ln, bias=1.0)
        S.op("dve", "tensor_tensor", reads=[Rdt, Ra], writes=[Radt], out=adt[:], in0=dt_tm[:],
             in1=a_bc[:].unsqueeze(1).broadcast_to([128, 2, 20]), op=ALU.mult)
        for tb in range(2):
            c = G * 2 + tb
            tk = slice(tb * 128, (tb + 1) * 128)
            S.op("pe", "matmul", reads=[Rcst, Radt], writes=[RPS[5]], out=PS[5][:, 0:20], lhsT=tri_f, rhs=adt[:, tb, :],
                 start=True, stop=True)
            S.op("pe", "matmul", reads=[Rcst, Radt], writes=[RPS[5]], out=PS[5][:, 32:52], lhsT=ones_f, rhs=adt[:, tb, :],
                 start=True, stop=True)
            if own:
                for g in range(4):
                    pz = 6 if g % 2 == 0 else 4
                    for k in range(8):
                        S.op("pe", "matmul", reads=[Rwz[k], RhT[c]], writes=[RPS[pz]], out=PS[pz][:, 0:320],
                             lhsT=hT[:, k, c * 128:(c + 1) * 128], rhs=wz[:, k, g * 320:(g + 1) * 320], start=(k == 0),
                             stop=(k == 7))
                    S.op("act", "activation", reads=[RPS[pz]], writes=[Rsz4[g]], out=sz4[:, g, :], in_=PS[pz][:, 0:320],
                         func=AF.Silu)
            S.op("dve", "tensor_scalar", reads=[RPS[5]], writes=[Rncs], out=ncs[:], in0=PS[5][:, 0:20], scalar1=-1.0,
                 scalar2=None, op0=ALU.mult)
            S.op("dve", "tensor_tensor", reads=[RPS[5], Rncs], writes=[Rt20], out=tmp20[:], in0=PS[5][:, 32:52],
                 in1=ncs[:], op=ALU.add)
            S.op("act", "activation", reads=[Rt20], writes=[Rdec], out=dec[:], in_=tmp20[:], func=AF.Exp)
            S.op("act", "activation", reads=[RPS[5]], writes=[Rcd], out=cd[:], in_=PS[5][:, 32:52], func=AF.Exp)
            S.op("dve", "tensor_tensor", reads=[Rdt, Rdec], writes=[Rw2], out=w2[:], in0=dt_tm[:, tb, :], in1=dec[:],
                 op=ALU.mult)
            S.op("dve", "tensor_tensor", reads=[Rxtm, Rw2], writes=[Rxdd], out=v3(xdd[:], 20),
                 in0=v3(x_tm[:, tb, :], 20), in1=w2[:].unsqueeze(2).broadcast_to([128, 20, 64]), op=ALU.mult)
            if own:
                blk = c
                S.op("act", "activation", reads=[Rncs], writes=[Recs], out=ecs[:], in_=ncs[:], func=AF.Exp, scale=-1.0)
                S.op("dve", "tensor_tensor", reads=[Rxtm, Rdt], writes=[Rxdt], out=v3(xdt[:], 20),
                     in0=v3(x_tm[:, tb, :], 20), in1=dt_tm[:, tb, :].unsqueeze(2).broadcast_to([128, 20, 64]),
                     op=ALU.mult)
                for g in range(4):
                    S.op("pe", "matmul", reads=[RBTg[g], RCTg[g]], writes=[RPS[2]], out=PS[2][:, g * 128:(g + 1) * 128],
                         lhsT=BT[:, g, tk], rhs=CTt[:, g, tk], start=True, stop=True)
                S.op("dve", "tensor_tensor", reads=[RPS[2], Rcst], writes=[RCBm], out=CBm[:], in0=v3(PS[2][:, :], 4),
                     in1=tri_f.unsqueeze(1).broadcast_to([128, 4, 128]), op=ALU.mult)
                def hq_copy(hq):
                    adtb, Radtb = adtb2[hq % 2]
                    S.op("dve", "tensor_copy", reads=[Radt], writes=[Radtb], out=adtb[:],
                         in_=adt[:, tb, hq * 4:hq * 4 + 4].unsqueeze(2).broadcast_to([128, 4, 128]))
                    pb = 3 + hq % 2
                    for i in range(4):
                        S.op("pe", "matmul", reads=[Radtb, Rcst], writes=[RPS[pb]], out=PS[pb][:, i * 128:(i + 1) * 128],
                             lhsT=adtb[:, i, :], rhs=tri_f, start=True, stop=True)

                def hq_rest(hq):
                    pb = 3 + hq % 2
                    lt, Rlt = Lt[hq % 2]
                    S.op("dve", "tensor_tensor", reads=[RPS[pb], Rncs], writes=[Rlt], out=lt[:], in0=v3(PS[pb][:, :], 4),
                         in1=ncs[:, hq * 4:hq * 4 + 4].unsqueeze(2).broadcast_to([128, 4, 128]), op=ALU.add)
                    S.op("act", "activation", reads=[Rlt], writes=[Rlt], out=lt[:], in_=lt[:], func=AF.Exp)
                    for i in range(4):
                        h = hq * 4 + i
                        S.op("dve", "scalar_tensor_tensor", reads=[Rlt, RCBm], writes=[RWm],
                             out=Wm[:, h, :], in0=lt[:, i, :], scalar=1.0, in1=CBm[:, h // 5, :], op0=ALU.min,
                             op1=ALU.mult)

                hq_copy(0)
                for hq in range(5):
                    if hq + 1 < 5:
                        hq_copy(hq + 1)
                    hq_rest(hq)
                for g in range(4):
                    gs = slice(g * 320, (g + 1) * 320)
                    po = 0 if g % 2 == 0 else 2
                    pd = 1 if g % 2 == 0 else 3
                    S.op("pe", "matmul", reads=[RCTg[g], Rstb], writes=[RPS[po]], out=PS[po][:, 0:320], lhsT=CTt[:, g, tk],
                         rhs=state_bf[:, gs], start=True, stop=True)
                    for hh in range(5):
                        h = g * 5 + hh
                        S.op("pe", "matmul", reads=[RWm, Rxdt], writes=[RPS[pd]], out=PS[pd][:, hh * 64:(hh + 1) * 64],
                             lhsT=Wm[:, h, :], rhs=xdt[:, h * 64:(h + 1) * 64], start=True, stop=True)
                    ygg = yg4[:, g, :]
                    S.op("dve", "tensor_tensor", reads=[RPS[po], Recs], writes=[Ryg4[g]], out=v3(ygg, 5),
                         in0=v3(PS[po][:, 0:320], 5), in1=ecs[:, g * 5:g * 5 + 5].unsqueeze(2).broadcast_to([128, 5, 64]),
                         op=ALU.mult)
                    S.op("dve", "tensor_tensor", reads=[RPS[pd], Ryg4[g]], writes=[Ryg4[g]], out=ygg, in0=ygg,
                         in1=PS[pd][:, 0:320], op=ALU.add)
                    S.op("dve", "tensor_tensor", reads=[Rxtm, Rdsk], writes=[Rytmp], out=v3(ytmp[:], 5),
                         in0=v3(x_tm[:, tb, gs], 5), in1=dsk[:, g * 5:g * 5 + 5].unsqueeze(2).broadcast_to([128, 5, 64]),
                         op=ALU.mult)
                    S.op("dve", "tensor_tensor", reads=[Rytmp, Ryg4[g]], writes=[Ryg4[g]], out=ygg, in0=ygg, in1=ytmp[:],
                         op=ALU.add)
                    S.op("dve", "tensor_tensor", reads=[Ryg4[g], Rsz4[g]], writes=[Ryg4[g]], out=ygg, in0=ygg,
                         in1=sz4[:, g, :], op=ALU.mult)
                    S.op("act", "activation", reads=[Ryg4[g]], writes=[Rss4[g]], out=sqj[:], in_=ygg, func=AF.Square,
                         accum_out=ss4[:, g:g + 1])
                rstd_from(ss4[:, 0:4], Rss4, 320, 4, rs4[:, 0:4], Rrs4, ln4[:, 0:4], Rln4)
                for g in range(4):
                    gs = slice(g * 320, (g + 1) * 320)
                    S.op("dve", "scalar_tensor_tensor", reads=[Ryg4[g], Rrs4, Rgssd], writes=[Ryn], out=yn[:, gs],
                         in0=yg4[:, g, :], scalar=rs4[:, g:g + 1], in1=gssd[:, gs], op0=ALU.mult, op1=ALU.mult)
                yt, Ryt = yTc[tb % 2]
                transposeN(yn, Ryn, 10, 128, yt[:], [Ryt], eng="act")
                ob = c - 16
                S.dma("sp", reads=[Ryt], writes=[Res()], out=yT_scr[:, :, ob * 128:(ob + 1) * 128], in_=yt[:])
            for g in range(4):
                gs = slice(g * 320, (g + 1) * 320)
                S.op("pe", "matmul", reads=[RBtm, Rxdd], writes=[RPS[6]], out=PS[6][:, 0:320],
                     lhsT=B_tm[:, tb, g * 128:(g + 1) * 128], rhs=xdd[:, gs], start=True, stop=True)
                S.op("dve", "tensor_tensor", reads=[Rst, Rcd], writes=[Rst], out=v3(state[:, gs], 5),
                     in0=v3(state[:, gs], 5), in1=cd[:, g * 5:g * 5 + 5].unsqueeze(2).broadcast_to([128, 5, 64]),
                     op=ALU.mult)
                S.op("dve", "tensor_tensor", reads=[Rst, RPS[6]], writes=[Rst], out=state[:, gs], in0=state[:, gs],
                     in1=PS[6][:, 0:320], op=ALU.add)
            if c == 15:
                S.op("dve", "tensor_scalar", reads=[Rst, Rflag], writes=[Rst], out=state[:], in0=state[:],
                     scalar1=flag[:, 0:1], scalar2=None, op0=ALU.mult)
            if c >= 15:
                S.op("act", "copy", reads=[Rst], writes=[Rstb], out=state_bf[:], in_=state[:])
    S.barrier()
    A.release(m_after_hT)

    mtab, Rmtab = T([128, 17, 128], BF16, "mtab")
    S.dma("pool", writes=[Rmtab], out=mtab[:].rearrange("p a b -> p (a b)"), in_=mtab_d)
    gq64, Rgq64 = bc_load(64, "attn_q_norm_g")
    gk64, Rgk64 = bc_load(64, "attn_k_norm_g")
    gq, Rgq = T([128, 256], F32, "gq")
    gk, Rgk = T([128, 256], F32, "gk")
    S.op("dve", "tensor_scalar", reads=[Rgq64], writes=[Rgq], out=v3(gq[:], 4),
         in0=gq64[:].unsqueeze(1).broadcast_to([128, 4, 64]), scalar1=0.125, scalar2=None, op0=ALU.mult)
    S.op("dve", "tensor_copy", reads=[Rgk64], writes=[Rgk], out=v3(gk[:], 4),
         in_=gk64[:].unsqueeze(1).broadcast_to([128, 4, 64]))
    wq, _ = T([128, 8, 256], BF16, "wq")
    Rwq = [Res() for _ in range(8)]
    wk, _ = T([128, 8, 256], BF16, "wk")
    Rwk = [Res() for _ in range(8)]
    wv, _ = T([128, 8, 256], BF16, "wv")
    Rwv = [Res() for _ in range(8)]
    KT, _ = T([64, 4, WIN], BF16, "KT")
    RKT = [Res() for _ in range(32)]
    QT, _ = T([64, 4, HALF], BF16, "QT")
    RQT = [Res() for _ in range(16)]
    Vaug, _ = T([128, 32, 4, 65], BF16, "Vaug")
    RV = [Res() for _ in range(32)]
    kn = [T([128, 256], BF16, "kn") for _ in range(2)]
    qn_ = [T([128, 256], BF16, "qn_") for _ in range(2)]
    Aexp, RAexp = T([128, 128], F32, "Aexp")
    Erev, REr = T([128, 17, 128], F32, "Erev")
    Pf = [T([128, 512], F32, "Pf") for _ in range(3)]
    Pbf = [T([128, 512], BF16, "Pbf") for _ in range(3)]
    rd, Rrd = T([128, 2], F32, "rd")
    o_all, _ = T([128, 16, 256], BF16, "o_all")
    Roall = [Res() for _ in range(16)]
    oTc = [T([128, 2, 128], BF16, "oTc") for _ in range(2)]
    for r in range(3):
        for k in range(8):
            rows = slice(k * 128, (k + 1) * 128)
            S.dma("pool", writes=[Rwq[k]], out=wq[:, k, :], in_=w_in_d[rows, OFF_DT + r * 256:OFF_DT + (r + 1) * 256])
            S.dma("pool", writes=[Rwk[k]], out=wk[:, k, :], in_=w_in_d[rows, OFF_Q + r * 256:OFF_Q + (r + 1) * 256])
            S.dma("pool", writes=[Rwv[k]], out=wv[:, k, :], in_=w_in_d[rows, OFF_K + r * 256:OFF_K + (r + 1) * 256])

        def proj_mm(blk):
            bs = slice(blk * 128, (blk + 1) * 128)
            pa = blk % 2
            for k in range(8):
                S.op("pe", "matmul", reads=[Rwk[k], RhT[blk]], writes=[RPS[pa]], out=PS[pa][:, 0:256], lhsT=hT[:, k, bs],
                     rhs=wk[:, k, :], start=(k == 0), stop=(k == 7))
            kt, Rkn = kn[blk % 2]
            qknorm(PS[pa][:, 0:256], RPS[pa], 4, 64, gk[:], Rgk, kt[:], Rkn)
            pv = 2 + blk % 2
            for k in range(8):
                S.op("pe", "matmul", reads=[Rwv[k], RhT[blk]], writes=[RPS[pv]], out=PS[pv][:, 0:256], lhsT=hT[:, k, bs],
                     rhs=wv[:, k, :], start=(k == 0), stop=(k == 7))
            if blk >= 16:
                S.op("act", "copy", reads=[RPS[pv]], writes=[RV[blk]], out=Vaug[:, blk, :, 0:64],
                     in_=v3(PS[pv][:, 0:256], 4))
                S.op("pool", "memset", writes=[RV[blk]], ap=Vaug[:, blk, :, 64:65], constant=1.0)
            else:
                S.op("dve", "tensor_scalar", reads=[RPS[pv], Rflag], writes=[RV[blk]], out=Vaug[:, blk, :, 0:64],
                     in0=v3(PS[pv][:, 0:256], 4), scalar1=flag[:, 0:1], scalar2=None, op0=ALU.mult)
                S.op("pool", "tensor_copy", reads=[Rflag], writes=[RV[blk]], out=Vaug[:, blk, :, 64:65],
                     in_=flag[:, 0:1].unsqueeze(1).broadcast_to([128, 4, 1]))
            if blk >= 16:
                pq = 4 + blk % 2
                for k in range(8):
                    S.op("pe", "matmul", reads=[Rwq[k], RhT[blk]], writes=[RPS[pq]], out=PS[pq][:, 0:256], lhsT=hT[:, k, bs],
                         rhs=wq[:, k, :], start=(k == 0), stop=(k == 7))
                qt, Rqn = qn_[blk % 2]
                qknorm(PS[pq][:, 0:256], RPS[pq], 4, 64, gq[:], Rgq, qt[:], Rqn)

        def proj_tr(blk):
            bs = slice(blk * 128, (blk + 1) * 128)
            kt, Rkn = kn[blk % 2]
            transposeN(kt, Rkn, 4, 64, KT[:, :, bs], [RKT[blk]], eng="act")
            if blk >= 16:
                qt, Rqn = qn_[blk % 2]
                transposeN(qt, Rqn, 4, 64, QT[:, :, (blk - 16) * 128:(blk - 15) * 128], [RQT[blk - 16]], eng="act")

        for blk in range(33):
            if blk < 32:
                proj_mm(blk)
            if blk >= 1:
                proj_tr(blk - 1)

        seq = []
        for hh in range(4):
            for qb in range(16, 32):
                kbs = list(range(qb - 16, qb + 1))
                for gi in range(5):
                    seq.append((hh, qb, gi, kbs[gi * 4:gi * 4 + 4]))

        def emit_S(i):
            hh, qb, gi, grp = seq[i]
            qi = qb - 16
            sbk = i % 4
            for ii, kb in enumerate(grp):
                S.op("pe", "matmul", reads=[RKT[kb], RQT[qi]], writes=[RPS[sbk]],
                     out=PS[sbk][:, ii * 128:(ii + 1) * 128], lhsT=KT[:, hh, kb * 128:(kb + 1) * 128],
                     rhs=QT[:, hh, qi * 128:(qi + 1) * 128], start=True, stop=True)

        def emit_rest(i):
            hh, qb, gi, grp = seq[i]
            qi = qb - 16
            n = len(grp)
            sbk = i % 4
            unit = i // 5
            ob = 4 + unit % 2
            pf, Rpf = Pf[i % 3]
            pbf, Rpbf = Pbf[i % 3]
            if qb == 16 and gi == 0:
                h = r * 4 + hh
                slope = 2.0 ** (-8.0 * (h + 1) / 12.0)
                S.op("act", "activation", reads=[Rcst], writes=[RAexp], out=Aexp[:], in_=d0_f, func=AF.Exp, scale=-slope)
                for j in range(17):
                    cpow = math.exp(-slope * 128.0 * (16 - j))
                    S.op("dve", "scalar_tensor_tensor", reads=[RAexp, Rmtab], writes=[REr],
                         out=Erev[:, j, :], in0=Aexp[:], scalar=float(cpow), in1=mtab[:, j, :], op0=ALU.mult,
                         op1=ALU.mult)
            S.op("act", "activation", reads=[RPS[sbk]], writes=[Rpf], out=pf[:, 0:n * 128],
                 in_=PS[sbk][:, 0:n * 128], func=AF.Exp)
            S.op("dve", "tensor_tensor", reads=[Rpf, REr], writes=[Rpbf], out=pbf[:, 0:n * 128],
                 in0=pf[:, 0:n * 128], in1=Erev[:, gi * 4:gi * 4 + n, :].rearrange("p a b -> p (a b)"),
                 op=ALU.mult)
            for ii, kb in enumerate(grp):
                S.op("pe", "matmul", reads=[Rpbf, RV[kb]], writes=[RPS[ob]], out=PS[ob][:, 0:65],
                     lhsT=pbf[:, ii * 128:(ii + 1) * 128], rhs=Vaug[:, kb, hh, :], start=(gi == 0 and ii == 0),
                     stop=(gi == 4))
            if gi == 4:
                S.op("dve", "reciprocal", reads=[RPS[ob]], writes=[Rrd], out=rd[:, 0:1], in_=PS[ob][:, 64:65])
                S.op("dve", "tensor_scalar", reads=[RPS[ob], Rrd], writes=[Roall[qi]],
                     out=o_all[:, qi, hh * 64:(hh + 1) * 64], in0=PS[ob][:, 0:64], scalar1=rd[:, 0:1], scalar2=None,
                     op0=ALU.mult)

        LA = 3
        for i in range(LA):
            emit_S(i)
        for i in range(len(seq)):
            if i + LA < len(seq):
                emit_S(i + LA)
            emit_rest(i)
        for qi in range(16):
            for i in range(2):
                S.op("pe", "transpose", reads=[Roall[qi], Ridb], writes=[RPB], out=PB[:, i * 128:(i + 1) * 128],
                     in_=o_all[:, qi, i * 128:(i + 1) * 128], identity=ident_b[:])
            ot, Rot = oTc[qi % 2]
            S.op("act", "copy", reads=[RPB], writes=[Rot], out=ot[:], in_=v3(PB[:, 0:256], 2))
            S.dma("sp", reads=[Rot], writes=[Res()], out=oT_scr[:, 2 * r:2 * r + 2, qi * 128:(qi + 1) * 128], in_=ot[:])
    S.barrier()
    A.release(m_base)

    x1, _ = T([128, 16, 1024], F32, "x1")
    Rx1 = [Res() for _ in range(16)]
    m_after_x1 = A.mark()
    yT_sb, _ = T([128, 10, HALF], BF16, "yT_sb")
    RyT = [Res() for _ in range(10)]
    for kc in range(10):
        S.dma("sp", writes=[RyT[kc]], out=yT_sb[:, kc, :], in_=yT_scr[:, kc, :])
    oT, _ = T([128, 6, HALF], BF16, "oT")
    RoTs = [Res() for _ in range(6)]
    for kc in range(6):
        S.dma("sp", writes=[RoTs[kc]], out=oT[:, kc, :], in_=oT_scr[:, kc, :])
    wout, _ = T([128, 16, 1024], BF16, "wout")
    Rwout = [Res() for _ in range(16)]
    for kc in range(16):
        S.dma("pool", writes=[Rwout[kc]], out=wout[:, kc, :], in_=w_out_d[kc * 128:(kc + 1) * 128, :])
    xo = [T([128, 1024], F32, "xo") for _ in range(2)]
    for tb in range(16):
        ts_ = slice(tb * 128, (tb + 1) * 128)
        xo_t, Rxo = xo[tb % 2]
        S.dma("sp", writes=[Rxo], out=xo_t[:], in_=xw_d[HALF + tb * 128:HALF + (tb + 1) * 128, :])
        for half in range(2):
            pb = 2 * (tb % 2) + half
            cs_ = slice(half * 512, (half + 1) * 512)
            for kc in range(16):
                lhsT = yT_sb[:, kc, ts_] if kc < 10 else oT[:, kc - 10, ts_]
                S.op("pe", "matmul", reads=[RyT[kc] if kc < 10 else RoTs[kc - 10], Rwout[kc]], writes=[RPS[pb]], out=PS[pb][:, :], lhsT=lhsT,
                     rhs=wout[:, kc, cs_], start=(kc == 0), stop=(kc == 15))
            S.op("dve", "tensor_tensor", reads=[RPS[pb], Rxo], writes=[Rx1[tb]], out=x1[:, tb, cs_], in0=PS[pb][:, :],
                 in1=xo_t[:, cs_], op=ALU.add)
    S.barrier()
    A.release(m_after_x1)

    def store_x1_and_finish():
        fin = []
        for tb in range(16):
            R = Res()
            S.dma("sp", reads=[Rx1[tb]], writes=[R], out=out_d[tb * 128:(tb + 1) * 128, :], in_=x1[:, tb, :])
            fin.append(R)
        for R in fin:
            S._wait("sp", R.w)
        S.emit()
        return nc

    if stop_after == "C":
        return store_x1_and_finish()

    wqm, _ = T([128, 8, 512], BF16, "wqm")
    Rwqm = [Res() for _ in range(8)]
    wkv, _ = T([128, 8, 1024], BF16, "wkv")
    Rwkv = [Res() for _ in range(8)]
    wo, _ = T([128, 4, 1024], BF16, "wo")
    Rwo = [Res() for _ in range(4)]
    for k in range(8):
        S.dma("pool", writes=[Rwqm[k]], out=wqm[:, k, :], in_=mem_w_q_d[k * 128:(k + 1) * 128, :])
        S.dma("pool", writes=[Rwkv[k]], out=wkv[:, k, :], in_=mem_w_kv_d[k * 128:(k + 1) * 128, :])
    for k in range(4):
        S.dma("pool", writes=[Rwo[k]], out=wo[:, k, :], in_=mem_w_o_d[k * 128:(k + 1) * 128, :])
    gx, Rgx = bc_load(1024, "xattn_norm_g")
    gm, Rgm = bc_load(1024, "mem_norm_g")
    gqm128, Rgqm128 = bc_load(128, "mem_q_norm_g")
    gkm128, Rgkm128 = bc_load(128, "mem_k_norm_g")
    gqm, Rgqm = T([128, 512], F32, "gqm")
    gkm, Rgkm = T([128, 512], F32, "gkm")
    S.op("dve", "tensor_scalar", reads=[Rgqm128], writes=[Rgqm], out=v3(gqm[:], 4),
         in0=gqm128[:].unsqueeze(1).broadcast_to([128, 4, 128]), scalar1=128.0 ** -0.5, scalar2=None, op0=ALU.mult)
    S.op("dve", "tensor_copy", reads=[Rgkm128], writes=[Rgkm], out=v3(gkm[:], 4),
         in_=gkm128[:].unsqueeze(1).broadcast_to([128, 4, 128]))
    KmT, RKmT = T([128, 4, 256], BF16, "KmT")
    Vm, RVm = T([128, 2, 4, 129], BF16, "Vm")
    S.op("pool", "memset", writes=[RVm], ap=Vm[:].rearrange("p a b c -> p (a b c)"), constant=1.0)
    memb, Rmemb = T([128, 1024], F32, "memb")
    mh, Rmh = T([128, 1024], BF16, "mh")
    mT, RmT = T([128, 8, 128], BF16, "mT")
    knm, Rknm = T([128, 512], BF16, "knm")
    for mb in range(2):
        ms = slice(mb * 128, (mb + 1) * 128)
        S.dma("sp", writes=[Rmemb], out=memb[:], in_=mem_d[ms, :])
        rmsnorm(memb[:], Rmemb, gm, Rgm, mh[:], Rmh)
        transposeN(mh, Rmh, 8, 128, mT[:], [RmT], eng="act")
        for half in range(2):
            for k in range(8):
                S.op("pe", "matmul", reads=[RmT, Rwkv[k]], writes=[RPS[half]], out=PS[half][:, :], lhsT=mT[:, k, :],
                     rhs=wkv[:, k, half * 512:(half + 1) * 512], start=(k == 0), stop=(k == 7))
        qknorm(PS[0][:, :], RPS[0], 4, 128, gkm[:], Rgkm, knm[:], Rknm)
        transposeN(knm, Rknm, 4, 128, KmT[:, :, ms], [RKmT], eng="act")
        S.op("act", "copy", reads=[RPS[1]], writes=[RVm], out=Vm[:, mb, :, 0:128], in_=v3(PS[1][:, :], 4))
    h2, Rh2 = T([128, 1024], BF16, "h2")
    h2T, Rh2T = T([128, 8, 128], BF16, "h2T")
    qn, Rqn2 = T([128, 512], BF16, "qn")
    qT, RqT = T([128, 4, 128], BF16, "qT")
    Pm = [T([128, 256], BF16, "Pm") for _ in range(2)]
    o2, Ro2 = T([128, 512], BF16, "o2")
    o2T, Ro2T = T([128, 4, 128], BF16, "o2T")
    for tb in range(16):
        rmsnorm(x1[:, tb, :], Rx1[tb], gx, Rgx, h2[:], Rh2)
        transposeN(h2, Rh2, 8, 128, h2T[:], [Rh2T], eng="act")
        for k in range(8):
            S.op("pe", "matmul", reads=[Rh2T, Rwqm[k]], writes=[RPS[2]], out=PS[2][:, :], lhsT=h2T[:, k, :], rhs=wqm[:, k, :],
                 start=(k == 0), stop=(k == 7))
        qknorm(PS[2][:, :], RPS[2], 4, 128, gqm[:], Rgqm, qn[:], Rqn2)
        transposeN(qn, Rqn2, 4, 128, qT[:], [RqT], eng="act")
        for hh in range(4):
            sb_ = 3 + hh % 2
            ob = 5 + hh % 2
            pm, Rpm = Pm[hh % 2]
            for mb in range(2):
                S.op("pe", "matmul", reads=[RKmT, RqT], writes=[RPS[sb_]], out=PS[sb_][:, mb * 128:(mb + 1) * 128],
                     lhsT=KmT[:, hh, mb * 128:(mb + 1) * 128], rhs=qT[:, hh, :], start=True, stop=True)
            S.op("act", "activation", reads=[RPS[sb_]], writes=[Rpm], out=pm[:], in_=PS[sb_][:, 0:256], func=AF.Exp)
            for mb in range(2):
                S.op("pe", "matmul", reads=[Rpm, RVm], writes=[RPS[ob]], out=PS[ob][:, 0:129],
                     lhsT=pm[:, mb * 128:(mb + 1) * 128], rhs=Vm[:, mb, hh, :], start=(mb == 0), stop=(mb == 1))
            S.op("dve", "reciprocal", reads=[RPS[ob]], writes=[Rrd], out=rd[:, 0:1], in_=PS[ob][:, 128:129])
            S.op("dve", "tensor_scalar", reads=[RPS[ob], Rrd], writes=[Ro2], out=o2[:, hh * 128:(hh + 1) * 128],
                 in0=PS[ob][:, 0:128], scalar1=rd[:, 0:1], scalar2=None, op0=ALU.mult)
        transposeN(o2, Ro2, 4, 128, o2T[:], [Ro2T], eng="act")
        for half in range(2):
            cs_ = slice(half * 512, (half + 1) * 512)
            for kc in range(4):
                S.op("pe", "matmul", reads=[Ro2T, Rwo[kc]], writes=[RPS[half]], out=PS[half][:, :], lhsT=o2T[:, kc, :],
                     rhs=wo[:, kc, cs_], start=(kc == 0), stop=(kc == 3))
            S.op("dve", "tensor_tensor", reads=[RPS[half], Rx1[tb]], writes=[Rx1[tb]], out=x1[:, tb, cs_],
                 in0=PS[half][:, :], in1=x1[:, tb, cs_], op=ALU.add)
    if stop_after == "D":
        S.barrier()
        return store_x1_and_finish()
    Rx2d = [Res() for _ in range(16)]
    for tb in range(16):
        S.dma("sp", reads=[Rx1[tb]], writes=[Rx2d[tb]], out=out_d[tb * 128:(tb + 1) * 128, :], in_=x1[:, tb, :])
    S.barrier()
    A.release(m_base)

    wpq, _ = T([128, 8, 2048], BF16, "wpq")
    Rwpq = [Res() for _ in range(8)]
    for k in range(8):
        S.dma("pool", writes=[Rwpq[k]], out=wpq[:, k, :], in_=peer_wq_d[k * 128:(k + 1) * 128, :])
    keysT, RkeysT = T([128, 2, 128], F32, "keysT")
    S.dma("sp", writes=[RkeysT], out=keysT[:, 0, :], in_=keys1T_d)
    S.dma("sp", writes=[RkeysT], out=keysT[:, 1, :], in_=keys2T_d)
    gf, Rgf = bc_load(1024, "ffn_norm_g")
    xrp = [T([128, 1024], F32, "xr") for _ in range(2)]
    h3bp = [T([128, 1024], BF16, "h3b") for _ in range(2)]
    h3T, Rh3T = T([128, 8, 128], BF16, "h3T")
    off_q = A.mark()
    qrT, RqrT = T([128, 16, 128], F32, "qrT")
    eq = nc.alloc_sbuf_tensor_at("eq_alias", [128, 8, 16, 16], F32, offset=off_q)
    Req = RqrT
    off_s = A.mark()
    sc, Rsc = T([128, 16, 128], F32, "sc")
    cand = nc.alloc_sbuf_tensor_at("cand_alias", [128, 8, 256], F32, offset=off_s)
    Rcand = Rsc
    sc2, Rsc2 = T([128, 128], F32, "sc2")
    tv, Rtv = T([128, 16, 16], F32, "tv")
    ti, Rti = T([128, 16, 16], U32, "ti")
    tif, Rtif = T([128, 16, 16], F32, "tif")
    cand2, Rcand2 = T([128, 256], F32, "cand2")
    cv, Rcv = T([128, 8, 16], F32, "cv")
    cp, Rcp = T([128, 8, 16], U32, "cp")
    ca, Rca = T([128, 8, 16], U32, "ca")
    cb_, Rcb_ = T([128, 8, 16], U32, "cb")
    caf, Rcaf = T([128, 8, 16], F32, "caf")
    cbf, Rcbf = T([128, 8, 16], F32, "cbf")
    i1f, Ri1f = T([128, 8, 16], F32, "i1f")
    i2f, Ri2f = T([128, 8, 16], F32, "i2f")
    ef, Ref = T([128, 128], F32, "ef")
    eip = [T([128, 128], I32, "ei") for _ in range(2)]
    gtp = [T([128, 8, 16], F32, "gt") for _ in range(2)]
    gsum, Rgsum = T([128, 8], F32, "gsum")
    actv, _ = T([128, 128], F32, "actv")
    Ractv = [Res() for _ in range(128)]
    WG = [T([128, 4], F32, "wg4") for _ in range(2)]
    wgt, _ = T([128, 128], F32, "wgt")
    Rwgt = [Res() for _ in range(32)]
    NUV = 6
    UVt = [(T([128, 4, 2048], BF16, "UV")[0], [Res() for _ in range(4)]) for _ in range(NUV)]
    PR = [T([128, 1024], BF16, "PR") for _ in range(8)]
    DG = [T([128, 128], BF16, "DG") for _ in range(4)]
    fin = []

    def peer_setup(tb, p):
        xr, Rxr = xrp[p]
        h3b, Rh3b = h3bp[p]
        ei, Rei = eip[p]
        gt, Rgt = gtp[p]
        S.dma("sp", reads=[Rx2d[tb]], writes=[Rxr], out=xr[:], in_=out_d[tb * 128:(tb + 1) * 128, :])
        rmsnorm(xr[:], Rxr, gf, Rgf, h3b[:], Rh3b)
        transposeN(h3b, Rh3b, 8, 128, h3T[:], [Rh3T], eng="act")
        yield
        for q4 in range(4):
            pb = q4 % 2
            for i in range(4):
                ccq = q4 * 4 + i
                for k in range(8):
                    S.op("pe", "matmul", reads=[Rwpq[k], Rh3T], writes=[RPS[pb]], out=PS[pb][:, i * 128:(i + 1) * 128],
                         lhsT=wpq[:, k, ccq * 128:(ccq + 1) * 128], rhs=h3T[:, k, :], start=(k == 0), stop=(k == 7))
            S.op("act", "copy", reads=[RPS[pb]], writes=[RqrT], out=qrT[:, q4 * 4:q4 * 4 + 4, :], in_=v3(PS[pb][:, :], 4))
            yield
        for q4 in range(4):
            pb = 2 + q4 % 2
            for i in range(4):
                j = q4 * 4 + i
                S.op("pe", "matmul", reads=[RqrT, RkeysT], writes=[RPS[pb]], out=PS[pb][:, i * 128:(i + 1) * 128],
                     lhsT=qrT[:, j, :], rhs=keysT[:, j % 2, :], start=True, stop=True)
            S.op("act", "copy", reads=[RPS[pb]], writes=[Rsc], out=sc[:, q4 * 4:q4 * 4 + 4, :], in_=v3(PS[pb][:, :], 4))
        for j in range(16):
            S.op("dve", "max", reads=[Rsc], writes=[Rtv], out=tv[:, j, 0:8], in_=sc[:, j, :])
            S.op("dve", "max_index", reads=[Rsc, Rtv], writes=[Rti], out=ti[:, j, 0:8], in_max=tv[:, j, 0:8],
                 in_values=sc[:, j, :])
            S.op("dve", "match_replace", reads=[Rsc, Rtv], writes=[Rsc2], out=sc2[:], in_to_replace=tv[:, j, 0:8],
                 in_values=sc[:, j, :], imm_value=-1e30)
            S.op("dve", "max", reads=[Rsc2], writes=[Rtv], out=tv[:, j, 8:16], in_=sc2[:])
            S.op("dve", "max_index", reads=[Rsc2, Rtv], writes=[Rti], out=ti[:, j, 8:16], in_max=tv[:, j, 8:16],
                 in_values=sc2[:])
            yield
        S.op("dve", "tensor_copy", reads=[Rti], writes=[Rtif], out=tif[:], in_=ti[:])
        tv4 = tv[:].rearrange("p (h two) k -> p h two k", two=2)
        tif4 = tif[:].rearrange("p (h two) k -> p h two k", two=2)
        S.op("dve", "tensor_tensor", reads=[Rtv], writes=[Rcand], out=cand[:].rearrange("p h (a b) -> p h a b", a=16),
             in0=tv4[:, :, 0, :].unsqueeze(3).broadcast_to([128, 8, 16, 16]),
             in1=tv4[:, :, 1, :].unsqueeze(2).broadcast_to([128, 8, 16, 16]), op=ALU.add)
        for h in range(8):
            S.op("dve", "max", reads=[Rcand], writes=[Rcv], out=cv[:, h, 0:8], in_=cand[:, h, :])
            S.op("dve", "max_index", reads=[Rcand, Rcv], writes=[Rcp], out=cp[:, h, 0:8], in_max=cv[:, h, 0:8],
                 in_values=cand[:, h, :])
            S.op("dve", "match_replace", reads=[Rcand, Rcv], writes=[Rcand2], out=cand2[:], in_to_replace=cv[:, h, 0:8],
                 in_values=cand[:, h, :], imm_value=-1e30)
            S.op("dve", "max", reads=[Rcand2], writes=[Rcv], out=cv[:, h, 8:16], in_=cand2[:])
            S.op("dve", "max_index", reads=[Rcand2, Rcv], writes=[Rcp], out=cp[:, h, 8:16], in_max=cv[:, h, 8:16],
                 in_values=cand2[:])
            yield
        S.op("dve", "tensor_single_scalar", reads=[Rcp], writes=[Rca], out=ca[:], in_=cp[:], scalar=4,
             op=ALU.logical_shift_right)
        S.op("dve", "tensor_single_scalar", reads=[Rcp], writes=[Rcb_], out=cb_[:], in_=cp[:], scalar=15,
             op=ALU.bitwise_and)
        S.op("dve", "tensor_copy", reads=[Rca], writes=[Rcaf], out=caf[:], in_=ca[:])
        S.op("dve", "tensor_copy", reads=[Rcb_], writes=[Rcbf], out=cbf[:], in_=cb_[:])
        io4 = iota16.unsqueeze(1).unsqueeze(1).broadcast_to([128, 8, 16, 16])
        for (sel, Rsel, half, dst, Rdst) in ((caf, Rcaf, 0, i1f, Ri1f), (cbf, Rcbf, 1, i2f, Ri2f)):
            S.op("dve", "tensor_tensor", reads=[Rsel, Rcst], writes=[Req], out=eq[:],
                 in0=sel[:].unsqueeze(3).broadcast_to([128, 8, 16, 16]), in1=io4, op=ALU.is_equal)
            S.op("dve", "tensor_tensor", reads=[Req, Rtif], writes=[Req], out=eq[:], in0=eq[:],
                 in1=tif4[:, :, half, :].unsqueeze(2).broadcast_to([128, 8, 16, 16]), op=ALU.mult)
            S.op("dve", "tensor_reduce", reads=[Req], writes=[Rdst], out=dst[:].rearrange("p h k -> p (h k)"),
                 in_=eq[:].rearrange("p h k a -> p (h k) a"), axis=AX.X, op=ALU.add)
        S.op("dve", "scalar_tensor_tensor", reads=[Ri1f, Ri2f], writes=[Ref], out=ef[:],
             in0=i1f[:].rearrange("p h k -> p (h k)"), scalar=128.0, in1=i2f[:].rearrange("p h k -> p (h k)"),
             op0=ALU.mult, op1=ALU.add)
        S.op("dve", "tensor_copy", reads=[Ref], writes=[Rei], out=ei[:], in_=ef[:])
        S.op("dve", "tensor_tensor", reads=[Rcv], writes=[Rgt], out=gt[:], in0=cv[:],
             in1=cv[:, :, 0:1].broadcast_to([128, 8, 16]), op=ALU.subtract)
        S.op("act", "activation", reads=[Rgt], writes=[Rgt], out=gt[:], in_=gt[:], func=AF.Exp)
        S.op("dve", "tensor_reduce", reads=[Rgt], writes=[Rgsum], out=gsum[:], in_=gt[:], axis=AX.X, op=ALU.add)
        S.op("dve", "reciprocal", reads=[Rgsum], writes=[Rgsum], out=gsum[:], in_=gsum[:])
        S.op("dve", "tensor_tensor", reads=[Rgt, Rgsum], writes=[Rgt], out=gt[:], in0=gt[:],
             in1=gsum[:].unsqueeze(2).broadcast_to([128, 8, 16]), op=ALU.mult)

    def peer_gather(tb, p, bg):
        xr, Rxr = xrp[p]
        h3b, Rh3b = h3bp[p]
        ei, Rei = eip[p]
        gt, Rgt = gtp[p]
        gtf = gt[:].rearrange("p h k -> p (h k)")
        S.op("pool", "memset", writes=Ractv, ap=actv[:], constant=0.0)

        def stA(g4):
            uvt, Ruvt = UVt[g4 % NUV]
            for i in range(4):
                s_ = g4 * 4 + i
                S.dma("pool", reads=[Rei, Ruv], writes=[Ruvt[i]], meth="indirect_dma_start", out=uvt[:, i, :],
                      out_offset=None, in_=uv_scr, in_offset=bass.IndirectOffsetOnAxis(ap=ei[:, s_:s_ + 1], axis=0))

        def stBC(g4):
            uvt, Ruvt = UVt[g4 % NUV]
            for i in range(4):
                s_ = g4 * 4 + i
                pr, Rpr = PR[s_ % 8]
                S.op("dve", "tensor_tensor", reads=[Ruvt[i], Rh3b], writes=[Rpr], out=pr[:], in0=uvt[:, i, 0:1024],
                     in1=h3b[:], op=ALU.mult)
                S.op("act", "activation", reads=[Rpr], writes=[Ractv[s_]], out=pr[:], in_=pr[:], func=AF.Copy,
                     accum_out=actv[:, s_:s_ + 1])
            wg, Rwg = WG[g4 % 2]
            S.op("act", "activation", reads=Ractv[g4 * 4:g4 * 4 + 4], writes=[Rwg], out=wg[:], in_=actv[:, g4 * 4:g4 * 4 + 4],
                 func=AF.Gelu)

        def stDE(g4):
            uvt, Ruvt = UVt[g4 % NUV]
            wg, Rwg = WG[g4 % 2]
            S.op("dve", "tensor_tensor", reads=[Rwg, Rgt], writes=[Rwgt[g4]], out=wgt[:, g4 * 4:g4 * 4 + 4], in0=wg[:],
                 in1=gtf[:, g4 * 4:g4 * 4 + 4], op=ALU.mult)
            for i in range(4):
                s_ = g4 * 4 + i
                dg, Rdg = DG[s_ % 4]
                S.op("act", "activation", reads=[Ridb, Rwgt[g4]], writes=[Rdg], out=dg[:], in_=ident_b[:],
                     func=AF.Copy, scale=wgt[:, s_:s_ + 1])
                for half in range(2):
                    S.op("pe", "matmul", reads=[Rdg, Ruvt[i]], writes=[RPS[5 + half]], out=PS[5 + half][:, :],
                         lhsT=dg[:], rhs=uvt[:, i, 1024 + half * 512:1024 + (half + 1) * 512], start=(s_ == 0),
                         stop=(s_ == 127))

        for g4 in range(NUV - 2):
            stA(g4)
        for g4 in range(33):
            if g4 + NUV - 2 < 32:
                stA(g4 + NUV - 2)
            if g4 < 32:
                stBC(g4)
            if g4 >= 1:
                stDE(g4 - 1)
            if bg is not None:
                next(bg, None)
                next(bg, None)
        if bg is not None:
            for _ in bg:
                pass
        for half in range(2):
            cs_ = slice(half * 512, (half + 1) * 512)
            S.op("dve", "tensor_tensor", reads=[RPS[5 + half], Rxr], writes=[Rxr], out=xr[:, cs_],
                 in0=xr[:, cs_], in1=PS[5 + half][:, :], op=ALU.add)
        R = Res()
        S.dma("sp", reads=[Rxr], writes=[R, Rx2d[tb]], out=out_d[tb * 128:(tb + 1) * 128, :], in_=xr[:])
        fin.append(R)

    for _ in peer_setup(0, 0):
        pass
    for tb in range(16):
        bg = peer_setup(tb + 1, (tb + 1) % 2) if tb + 1 < 16 else None
        peer_gather(tb, tb % 2, bg)
    for R in fin:
        S._wait("sp", R.w)
    S.emit()
    return nc


def _consts():
    c = np.zeros((128, NCONST), np.float32)
    p = np.arange(128)
    c[:, 0:128] = np.eye(128)
    c[:, 128:256] = (p[:, None] <= p[None, :])
    c[:, 256:384] = 1.0
    c[:, 384:512] = (p[None, :] - p[:, None])
    c[:, 512:528] = np.arange(16)[None, :]
    k = p[:, None, None]
    j = np.arange(17)[None, :, None]
    q = p[None, None, :]
    delta = (16 - j) * 128 + q - k
    m = ((delta >= 0) & (delta <= 128)).astype(np.float32)
    m += ((delta >= 0) & (delta <= 512) & (delta % 4 == 0))
    m += ((delta >= 0) & (delta <= 2048) & (delta % 16 == 0))
    return c, np.ascontiguousarray(m.reshape(128, 17 * 128).astype(np.float32))


_NC_CACHE = {}


def kernel(**inputs):
    stop_after = os.environ.get("KSTOP") or None
    x = np.asarray(inputs["x"], np.float32)
    mem = np.asarray(inputs["mem"], np.float32)
    cst, mtab = _consts()
    common = {
        "cst": cst, "mtab": mtab,
        "w_in": np.ascontiguousarray(inputs["w_in"][0], dtype=np.float32),
        "convw": np.ascontiguousarray(
            np.asarray(inputs["conv_w"][0], np.float32).T.reshape(18, 128, 4).transpose(1, 0, 2).reshape(128, 72)),
        "convb": np.ascontiguousarray(np.asarray(inputs["conv_b"][0], np.float32).reshape(18, 128).T),
        "w_out": np.ascontiguousarray(inputs["w_out"][0], dtype=np.float32),
    }
    for nm in ("mix_norm_g", "dt_bias", "a_log", "d_skip", "ssd_norm_g", "attn_q_norm_g", "attn_k_norm_g",
               "xattn_norm_g", "mem_norm_g", "mem_q_norm_g", "mem_k_norm_g", "ffn_norm_g"):
        common[nm] = np.ascontiguousarray(np.asarray(inputs[nm][0], np.float32).reshape(1, -1))
    if True:
        common["mem_w_q"] = np.ascontiguousarray(inputs["mem_w_q"][0], dtype=np.float32)
        common["mem_w_kv"] = np.ascontiguousarray(inputs["mem_w_kv"][0], dtype=np.float32)
        common["mem_w_o"] = np.ascontiguousarray(inputs["mem_w_o"][0], dtype=np.float32)
    if True:
        common["peer_w_query"] = np.ascontiguousarray(inputs["peer_w_query"][0], dtype=np.float32)
        common["keys1T"] = np.ascontiguousarray(np.asarray(inputs["peer_sub_keys1"][0], np.float32).T)
        common["keys2T"] = np.ascontiguousarray(np.asarray(inputs["peer_sub_keys2"][0], np.float32).T)
        common["peer_u"] = np.ascontiguousarray(inputs["peer_u"][0], dtype=np.float32)
        common["peer_v"] = np.ascontiguousarray(inputs["peer_v"][0], dtype=np.float32)
    in_maps = []
    for c in range(8):
        b, j = c // 2, c % 2
        xw = np.zeros((WIN, D), np.float32)
        if j == 1:
            xw[:HALF] = x[b, :HALF]
        xw[HALF:] = x[b, j * HALF:(j + 1) * HALF]
        m = dict(common)
        m["xw"] = xw
        m["flag"] = np.full((128, 1), float(j), np.float32)
        if True:
            m["mem"] = np.ascontiguousarray(mem[b])
        in_maps.append(m)
    nc = build(stop_after)
    res = run_bass_kernel_spmd(nc, in_maps, core_ids=list(range(8)))
    out = np.zeros((NB, SEQ, D), np.float32)
    for c in range(8):
        b, j = c // 2, c % 2
        out[b, j * HALF:(j + 1) * HALF] = np.asarray(res.results[c]["out"])
    return out
```

```python
import os
import math
import numpy as np
import concourse.bass as bass
import concourse.mybir as mybir
from concourse.bass_utils import run_bass_kernel_spmd

F32 = mybir.dt.float32
BF16 = mybir.dt.bfloat16
I32 = mybir.dt.int32
U32 = mybir.dt.uint32
AF = mybir.ActivationFunctionType
ALU = mybir.AluOpType
AX = mybir.AxisListType

D = 1024
NB = 4
SEQ = 4096
HALF = 2048
WIN = 4096
EPS = 1e-6
OFF_Z = 1280
OFF_XBC = OFF_Z + 2304
OFF_DT = OFF_XBC + 20
OFF_Q = OFF_DT + 768
OFF_K = OFF_Q + 768
IN_COLS = OFF_K + 768
NCONST = 128 * 4 + 16


class Res:
    __slots__ = ("w", "r")

    def __init__(self):
        self.w = None
        self.r = {}


class Sched:
    COMPUTE = ("pe", "act", "dve", "pool")
    NDMA = 12
    INORDER = ("pe",)

    def __init__(self, nc):
        self.nc = nc
        self.streams = {k: [] for k in ("pe", "act", "dve", "pool", "sp")}
        self.esem = {k: nc.alloc_semaphore("es_" + k) for k in self.COMPUTE}
        self.ecnt = {k: 0 for k in self.COMPUTE}
        self.dsem = {q: [nc.alloc_semaphore("ds_%s%d" % (q, i)) for i in range(self.NDMA)]
                     for q in ("sp", "pool")}
        self.dcnt = {q: [0] * self.NDMA for q in ("sp", "pool")}
        self.dnext = {q: 0 for q in ("sp", "pool")}
        self.known = {k: {} for k in self.streams}
        self.nwaits = 0
        self.nops = 0

    def _sem_of(self, pid):
        if pid[0] == "e":
            return self.esem[pid[1]], pid[2], ("e", pid[1])
        return self.dsem[pid[1]][pid[2]], pid[3], ("d", pid[1], pid[2])

    def _wait(self, eng, pid):
        sem, val, key = self._sem_of(pid)
        if self.known[eng].get(key, 0) >= val:
            return
        self.known[eng][key] = val
        self.nwaits += 1
        self.streams[eng].append(lambda e, sem=sem, val=val: e.wait_ge(sem, val))

    def _deps(self, eng, reads, writes):
        deps = []
        for r in reads:
            if r.w is not None:
                deps.append(r.w)
        for w in writes:
            if w.w is not None:
                deps.append(w.w)
            deps.extend(w.r.values())
        for pid in deps:
            if pid[0] == "e" and pid[1] == eng and eng in self.INORDER:
                continue
            self._wait(eng, pid)

    def _commit(self, pid, reads, writes):
        key = pid[:2] if pid[0] == "e" else pid[:3]
        for r in reads:
            r.r[key] = pid
        for w in writes:
            w.w = pid
            w.r = {}

    def op(self, eng, meth, reads=(), writes=(), **kw):
        fn = lambda e, meth=meth, kw=kw: getattr(e, meth)(**kw)
        self._deps(eng, reads, writes)
        self.ecnt[eng] += 1
        idx = self.ecnt[eng]
        sem = self.esem[eng]
        self.streams[eng].append(lambda e, fn=fn, sem=sem: fn(e).then_inc(sem, 1))
        self.nops += 1
        self._commit(("e", eng, idx), reads, writes)

    def dma(self, q, reads=(), writes=(), meth="dma_start", **kw):
        fn = lambda e, meth=meth, kw=kw: getattr(e, meth)(**kw)
        i = self.dnext[q]
        self.dnext[q] = (i + 1) % self.NDMA
        prev = self.dcnt[q][i]
        if prev > 0:
            self._wait(q, ("d", q, i, prev))
        self._deps(q, reads, writes)
        self.dcnt[q][i] = prev + 16
        sem = self.dsem[q][i]
        self.streams[q].append(lambda e, fn=fn, sem=sem: fn(e).then_inc(sem, 16))
        self.nops += 1
        self._commit(("d", q, i, prev + 16), reads, writes)

    def barrier(self):
        for eng in self.streams:
            for k in self.COMPUTE:
                if k != eng and self.ecnt[k] > 0:
                    self._wait(eng, ("e", k, self.ecnt[k]))
            for q in self.dsem:
                for i in range(self.NDMA):
                    if self.dcnt[q][i] > 0:
                        self._wait(eng, ("d", q, i, self.dcnt[q][i]))

    def emit(self):
        nc = self.nc
        with nc.Block() as block:
            @block.tensor
            def _(e):
                for it in self.streams["pe"]:
                    it(e)

            @block.scalar
            def _(e):
                for it in self.streams["act"]:
                    it(e)

            @block.vector
            def _(e):
                for it in self.streams["dve"]:
                    it(e)

            @block.gpsimd
            def _(e):
                for it in self.streams["pool"]:
                    it(e)

            @block.sync
            def _(e):
                for it in self.streams["sp"]:
                    it(e)


class Arena:
    def __init__(self, nc):
        self.nc = nc
        self.off = (nc.sbuf_base + 31) // 32 * 32
        self.top = nc.sbuf_top
        self.n = 0

    def alloc(self, shape, dt=F32, name="t"):
        sz = {F32: 4, BF16: 2, I32: 4, U32: 4}[dt]
        nb = int(np.prod(shape[1:])) * sz
        off = self.off
        self.off += (nb + 31) // 32 * 32
        assert self.off <= self.top, "SBUF overflow at %s: %d > %d" % (name, self.off, self.top)
        self.n += 1
        return self.nc.alloc_sbuf_tensor_at("%s_%d" % (name, self.n), list(shape), dt, offset=off)

    def mark(self):
        return self.off

    def release(self, m):
        self.off = m


def build(stop_after=None):
    nc = bass.Bass("TRN2", target_bir_lowering=False)
    S = Sched(nc)
    A = Arena(nc)

    def din(name, shape, dt=F32):
        return nc.dram_tensor(name, list(shape), dt, kind="ExternalInput").ap()

    xw_d = din("xw", [WIN, D])
    flag_d = din("flag", [128, 1])
    mem_d = din("mem", [256, D])
    cst_d = din("cst", [128, NCONST])
    mtab_d = din("mtab", [128, 17 * 128])
    w_in_d = din("w_in", [D, IN_COLS])
    convw_d = din("convw", [128, 72])
    convb_d = din("convb", [128, 18])
    vec_d = {}
    for nm, n in [("mix_norm_g", 1024), ("dt_bias", 20), ("a_log", 20), ("d_skip", 20), ("ssd_norm_g", 1280),
                  ("attn_q_norm_g", 64), ("attn_k_norm_g", 64), ("xattn_norm_g", 1024), ("mem_norm_g", 1024),
                  ("mem_q_norm_g", 128), ("mem_k_norm_g", 128), ("ffn_norm_g", 1024)]:
        vec_d[nm] = din(nm, [1, n])
    w_out_d = din("w_out", [2048, D])
    mem_w_q_d = din("mem_w_q", [D, 512])
    mem_w_kv_d = din("mem_w_kv", [D, 1024])
    mem_w_o_d = din("mem_w_o", [512, D])
    peer_wq_d = din("peer_w_query", [D, 2048])
    keys1T_d = din("keys1T", [128, 128])
    keys2T_d = din("keys2T", [128, 128])
    peer_u_d = din("peer_u", [16384, D])
    peer_v_d = din("peer_v", [16384, D])
    out_d = nc.dram_tensor("out", [HALF, D], F32, kind="ExternalOutput").ap()
    yT_scr = nc.dram_tensor("yT_scr", [128, 10, HALF], BF16).ap()
    oT_scr = nc.dram_tensor("oT_scr", [128, 6, HALF], BF16).ap()
    uv_scr = nc.dram_tensor("uv_scr", [16384, 2048], BF16).ap()
    Ruv = Res()
    conv_jobs = [(tbl, h) for h in range(32) for tbl in (0, 1)]

    def issue_conv(n):
        for _ in range(n):
            if not conv_jobs or stop_after is not None:
                return
            tbl, h = conv_jobs.pop(0)
            src = peer_u_d if tbl == 0 else peer_v_d
            S.dma("pool", writes=[Ruv], out=uv_scr[h * 512:(h + 1) * 512, tbl * 1024:(tbl + 1) * 1024],
                  in_=src[h * 512:(h + 1) * 512, :])

    PS = [nc.alloc_psum_tensor("ps%d" % i, [128, 512], F32) for i in range(7)]
    RPS = [Res() for _ in range(7)]
    PB = nc.alloc_psum_tensor("psb", [128, 1024], BF16)
    RPB = Res()

    def T(shape, dt=F32, name="t"):
        return A.alloc(shape, dt, name), Res()

    def v3(ap, a):
        return ap.rearrange("p (a b) -> p a b", a=a)

    cst, Rcst = T([128, NCONST], F32, "cst")
    S.dma("sp", writes=[Rcst], out=cst[:], in_=cst_d)
    ident_f = cst[:, 0:128]
    tri_f = cst[:, 128:256]
    ones_f = cst[:, 256:384]
    d0_f = cst[:, 384:512]
    iota16 = cst[:, 512:528]
    ident_b, Ridb = T([128, 128], BF16, "identb")
    S.op("dve", "tensor_copy", reads=[Rcst], writes=[Ridb], out=ident_b[:], in_=ident_f)
    flag, Rflag = T([128, 1], F32, "flag")
    S.dma("sp", writes=[Rflag], out=flag[:], in_=flag_d)
    epsT, Reps = T([128, 1], F32, "eps")
    S.op("pool", "memset", writes=[Reps], ap=epsT[:], constant=EPS)

    def bc_load(n, name):
        t, R = T([128, n], F32, name)
        S.dma("sp", writes=[R], out=t[:], in_=vec_d[name].broadcast_to([128, n]))
        return t, R

    def rstd_from(ss_ap, Rss, n, H, out_ap, Rout, lnv_ap, Rln):
        Rss_l = Rss if isinstance(Rss, list) else [Rss]
        S.op("act", "activation", reads=Rss_l + [Reps], writes=[Rln], out=lnv_ap, in_=ss_ap, func=AF.Ln,
             bias=epsT[:, 0:1], scale=1.0 / n)
        S.op("act", "activation", reads=[Rln], writes=[Rout], out=out_ap, in_=lnv_ap, func=AF.Exp, scale=-0.5)

    nrm_junk, Rnj = T([128, 1024], BF16, "nrmjunk")
    nrm_ss, Rnss = T([128, 4], F32, "nrmss")
    nrm_ln, Rnln = T([128, 4], F32, "nrmln")
    nrm_rs, Rnrs = T([128, 4], F32, "nrmrs")

    def rmsnorm(x_ap, Rx, g_bc, Rg, out_ap, Rout):
        S.op("act", "activation", reads=[Rx], writes=[Rnj, Rnss], out=nrm_junk[:], in_=x_ap, func=AF.Square,
             accum_out=nrm_ss[:, 0:1])
        rstd_from(nrm_ss[:, 0:1], Rnss, 1024, 1, nrm_rs[:, 0:1], Rnrs, nrm_ln[:, 0:1], Rnln)
        S.op("dve", "scalar_tensor_tensor", reads=[Rx, Rnrs, Rg], writes=[Rout], out=out_ap, in0=x_ap,
             scalar=nrm_rs[:, 0:1], in1=g_bc[:], op0=ALU.mult, op1=ALU.mult)

    def transposeN(src_bf, Rsrc, n, width, dst3, Rdst, eng="act"):
        for k0 in range(0, n, 8):
            m = min(8, n - k0)
            for k in range(m):
                S.op("pe", "transpose", reads=[Rsrc, Ridb], writes=[RPB], out=PB[0:width, k * 128:(k + 1) * 128],
                     in_=src_bf[:, (k0 + k) * width:(k0 + k + 1) * width], identity=ident_b[:])
            src = v3(PB[0:width, 0:m * 128], m)
            dst = dst3[:, k0:k0 + m, :]
            if eng == "act":
                S.op("act", "copy", reads=[RPB], writes=Rdst, out=dst, in_=src)
            else:
                S.op("dve", "tensor_copy", reads=[RPB], writes=Rdst, out=dst, in_=src)

    qk_sq, Rqsq = T([128, 512], F32, "qksq")
    qk_tmp, Rqtmp = T([128, 512], F32, "qktmp")

    def qknorm(ps_ap, Rps, H, dh, g_bc, Rg, out_bf, Rout):
        n = H * dh
        S.op("act", "activation", reads=[Rps], writes=[Rqsq], out=qk_sq[:, 0:n], in_=ps_ap, func=AF.Square)
        S.op("dve", "tensor_reduce", reads=[Rqsq], writes=[Rnss], out=nrm_ss[:, 0:H], in_=v3(qk_sq[:, 0:n], H),
             axis=AX.X, op=ALU.add)
        rstd_from(nrm_ss[:, 0:H], Rnss, dh, H, nrm_rs[:, 0:H], Rnrs, nrm_ln[:, 0:H], Rnln)
        S.op("dve", "tensor_tensor", reads=[Rps, Rnrs], writes=[Rqtmp], out=v3(qk_tmp[:, 0:n], H), in0=v3(ps_ap, H),
             in1=nrm_rs[:, 0:H].unsqueeze(2).broadcast_to([128, H, dh]), op=ALU.mult)
        S.op("dve", "tensor_tensor", reads=[Rqtmp, Rg], writes=[Rout], out=out_bf, in0=qk_tmp[:, 0:n], in1=g_bc,
             op=ALU.mult)

    m_base = A.mark()
    hT, _ = T([128, 8, WIN], BF16, "hT")
    RhT = [Res() for _ in range(32)]
    m_after_hT = A.mark()

    gmix, Rgmix = bc_load(1024, "mix_norm_g")
    xb = [T([128, 1024], F32, "xb") for _ in range(2)]
    hb = [T([128, 1024], BF16, "hb") for _ in range(2)]
    for blk in range(32):
        x_t, Rx = xb[blk % 2]
        h_t, Rh = hb[blk % 2]
        S.dma("sp", writes=[Rx], out=x_t[:], in_=xw_d[blk * 128:(blk + 1) * 128, :])
        rmsnorm(x_t[:], Rx, gmix, Rgmix, h_t[:], Rh)
        transposeN(h_t, Rh, 8, 128, hT[:, :, blk * 128:(blk + 1) * 128], [RhT[blk]], eng="act" if blk % 2 else "dve")
    S.barrier()
    A.release(m_after_hT)

    wxbc, _ = T([128, 8, 2304], BF16, "wxbc")
    Rwxbc = [Res() for _ in range(8)]
    wz, _ = T([128, 8, 1280], BF16, "wz")
    Rwz = [Res() for _ in range(8)]
    wdt, _ = T([128, 8, 20], BF16, "wdt")
    Rwdt = [Res() for _ in range(8)]
    for k in range(8):
        S.dma("pool", writes=[Rwxbc[k]], out=wxbc[:, k, :], in_=w_in_d[k * 128:(k + 1) * 128, OFF_Z:OFF_XBC])
        S.dma("pool", writes=[Rwz[k]], out=wz[:, k, :], in_=w_in_d[k * 128:(k + 1) * 128, 0:OFF_Z])
        S.dma("pool", writes=[Rwdt[k]], out=wdt[:, k, :], in_=w_in_d[k * 128:(k + 1) * 128, OFF_XBC:OFF_DT])
    convw, Rcw = T([128, 18, 4], F32, "convw")
    convb, Rcb = T([128, 18], F32, "convb")
    S.dma("sp", writes=[Rcw], out=convw[:].rearrange("p a b -> p (a b)"), in_=convw_d)
    S.dma("sp", writes=[Rcb], out=convb[:], in_=convb_d)
    dtb, Rdtb = bc_load(20, "dt_bias")
    alog, Ralog = bc_load(20, "a_log")
    dsk, Rdsk = bc_load(20, "d_skip")
    gssd, Rgssd = bc_load(1280, "ssd_norm_g")
    a_bc, Ra = T([128, 20], F32, "a_bc")
    S.op("act", "activation", reads=[Ralog], writes=[Ra], out=a_bc[:], in_=alog[:], func=AF.Exp)
    S.op("dve", "tensor_scalar", reads=[Ra], writes=[Ra], out=a_bc[:], in0=a_bc[:], scalar1=-1.0, scalar2=None,
         op0=ALU.mult)
    halo, _ = T([128, 18, 3], F32, "halo")
    Rhalo = [Res() for _ in range(18)]
    S.op("pool", "memset", writes=Rhalo, ap=halo[:], constant=0.0)
    state, Rst = T([128, 1280], F32, "state")
    state_bf, Rstb = T([128, 1280], BF16, "statebf")
    S.op("pool", "memset", writes=[Rst], ap=state[:], constant=0.0)
    S.op("pool", "memset", writes=[Rstb], ap=state_bf[:], constant=0.0)
    U = [T([128, 259], F32, "U") for _ in range(2)]
    acc = [T([128, 256], F32, "acc") for _ in range(2)]
    xTf = [T([128, 256], F32, "xTf") for _ in range(2)]
    x_tm, Rxtm = T([128, 2, 1280], F32, "x_tm")
    BT, _ = T([128, 4, 256], BF16, "BT")
    RBTg = [Res() for _ in range(4)]
    CTt, _ = T([128, 4, 256], BF16, "CT")
    RCTg = [Res() for _ in range(4)]
    B_tm, RBtm = T([128, 2, 512], BF16, "B_tm")
    dt_tm, Rdt = T([128, 2, 20], F32, "dt_tm")
    adt, Radt = T([128, 2, 20], F32, "adt")
    ncs, Rncs = T([128, 20], F32, "ncs")
    tmp20, Rt20 = T([128, 20], F32, "tmp20")
    dec, Rdec = T([128, 20], F32, "dec")
    cd, Rcd = T([128, 20], F32, "cd")
    w2, Rw2 = T([128, 20], F32, "w2")
    ecs, Recs = T([128, 20], F32, "ecs")
    xdd, Rxdd = T([128, 1280], BF16, "xdd")
    xdt, Rxdt = T([128, 1280], BF16, "xdt")
    CBm, RCBm = T([128, 4, 128], F32, "CBm")
    adtb2 = [T([128, 4, 128], F32, "adtb") for _ in range(2)]
    Lt = [T([128, 4, 128], F32, "Lt") for _ in range(2)]
    Wm, RWm = T([128, 20, 128], BF16, "Wm")
    yg4, _ = T([128, 4, 320], F32, "yg4")
    Ryg4 = [Res() for _ in range(4)]
    sz4, _ = T([128, 4, 320], F32, "sz4")
    Rsz4 = [Res() for _ in range(4)]
    sqj, _ = T([128, 320], BF16, "sqj")
    ss4, _ = T([128, 4], F32, "ss4")
    Rss4 = [Res() for _ in range(4)]
    ln4, Rln4 = T([128, 4], F32, "ln4")
    rs4, Rrs4 = T([128, 4], F32, "rs4")
    ytmp, Rytmp = T([128, 320], F32, "ytmp")
    ss1, Rss1 = T([128, 1], F32, "ss1")
    ln1, Rln1 = T([128, 1], F32, "ln1")
    rs1, Rrs1 = T([128, 1], F32, "rs1")
    yn, Ryn = T([128, 1280], BF16, "yn")
    yTc = [T([128, 10, 128], BF16, "yTc") for _ in range(2)]

    for G in range(16):
        own = G >= 8
        RhG = RhT[G * 2:(G + 1) * 2]
        tokG = slice(G * 256, (G + 1) * 256)
        def cc_s1(cc):
            b = cc % 2
            for k in range(8):
                S.op("pe", "matmul", reads=[Rwxbc[k]] + RhG, writes=[RPS[b]], out=PS[b][:, 0:256],
                     lhsT=wxbc[:, k, cc * 128:(cc + 1) * 128], rhs=hT[:, k, tokG], start=(k == 0), stop=(k == 7))
            u_t, Ru = U[b]
            S.op("act", "copy", reads=[RPS[b]], writes=[Ru], out=u_t[:, 3:259], in_=PS[b][:, 0:256])
            S.op("act", "copy", reads=[Rhalo[cc]], writes=[Ru], out=u_t[:, 0:3], in_=halo[:, cc, :])
            S.op("act", "copy", reads=[Ru], writes=[Rhalo[cc]], out=halo[:, cc, :], in_=u_t[:, 256:259])

        def cc_s2(cc):
            b = cc % 2
            u_t, Ru = U[b]
            a_t, Rac = acc[b]
            S.op("dve", "tensor_scalar", reads=[Ru, Rcw], writes=[Rac], out=a_t[:], in0=u_t[:, 3:259],
                 scalar1=convw[:, cc, 3:4], scalar2=None, op0=ALU.mult)
            for j in (2, 1, 0):
                S.op("dve", "scalar_tensor_tensor", reads=[Ru, Rcw, Rac], writes=[Rac], out=a_t[:],
                     in0=u_t[:, j:j + 256], scalar=convw[:, cc, j:j + 1], in1=a_t[:], op0=ALU.mult, op1=ALU.add)
            if cc < 10:
                xt, Rxt = xTf[b]
                S.op("act", "activation", reads=[Rac, Rcb], writes=[Rxt], out=xt[:], in_=a_t[:], func=AF.Silu,
                     bias=convb[:, cc:cc + 1])
            elif cc < 14:
                g = cc - 10
                S.op("act", "activation", reads=[Rac, Rcb], writes=[RBTg[g]], out=BT[:, g, :], in_=a_t[:], func=AF.Silu,
                     bias=convb[:, cc:cc + 1])
            else:
                g = cc - 14
                S.op("act", "activation", reads=[Rac, Rcb], writes=[RCTg[g]], out=CTt[:, g, :], in_=a_t[:], func=AF.Silu,
                     bias=convb[:, cc:cc + 1])

        def cc_tr(cc):
            b = cc % 2
            if cc < 10:
                xt, Rxt = xTf[b]
                pb = 2 + b
                for tb in range(2):
                    S.op("pe", "transpose", reads=[Rxt, Rcst], writes=[RPS[pb]], out=PS[pb][:, tb * 128:(tb + 1) * 128],
                         in_=xt[:, tb * 128:(tb + 1) * 128], identity=ident_f)
                S.op("dve", "tensor_copy", reads=[RPS[pb]], writes=[Rxtm], out=x_tm[:, :, cc * 128:(cc + 1) * 128],
                     in_=v3(PS[pb][:, 0:256], 2))
            elif cc < 14:
                g = cc - 10
                for tb in range(2):
                    S.op("pe", "transpose", reads=[RBTg[g], Ridb], writes=[RPB], out=PB[:, tb * 128:(tb + 1) * 128],
                         in_=BT[:, g, tb * 128:(tb + 1) * 128], identity=ident_b[:])
                S.op("dve", "tensor_copy", reads=[RPB], writes=[RBtm], out=B_tm[:, :, g * 128:(g + 1) * 128],
                     in_=v3(PB[:, 0:256], 2))

        cc_s1(0)
        for cc in range(19):
            if cc + 1 < 18:
                cc_s1(cc + 1)
            if cc < 18:
                cc_s2(cc)
            if cc >= 1:
                cc_tr(cc - 1)
        issue_conv(4)
        for tb in range(2):
            blk = G * 2 + tb
            for k in range(8):
                S.op("pe", "matmul", reads=[Rwdt[k], RhT[blk]], writes=[RPS[4]], out=PS[4][:, tb * 20:(tb + 1) * 20],
                     lhsT=hT[:, k, blk * 128:(blk + 1) * 128], rhs=wdt[:, k, :], start=(k == 0), stop=(k == 7))
        S.op("dve", "tensor_tensor", reads=[RPS[4], Rdtb], writes=[Rdt], out=dt_tm[:], in0=v3(PS[4][:, 0:40], 2),
             in1=dtb[:].unsqueeze(1).broadcast_to([128, 2, 20]), op=ALU.add)
        S.op("act", "activation", reads=[Rdt], writes=[Rdt], out=dt_tm[:], in_=dt_tm[:], func=AF.Exp)
        S.op("act", "activation", reads=[Rdt], writes=[Rdt], out=dt_tm[:], in_=dt_tm[:], func=AF.Ln, bias=1.0)
        S.op("dve", "tensor_tensor", reads=[Rdt, Ra], writes=[Radt], out=adt[:], in0=dt_tm[:],
             in1=a_bc[:].unsqueeze(1).broadcast_to([128, 2, 20]), op=ALU.mult)
        for tb in range(2):
            c = G * 2 + tb
            tk = slice(tb * 128, (tb + 1) * 128)
            S.op("pe", "matmul", reads=[Rcst, Radt], writes=[RPS[5]], out=PS[5][:, 0:20], lhsT=tri_f, rhs=adt[:, tb, :],
                 start=True, stop=True)
            S.op("pe", "matmul", reads=[Rcst, Radt], writes=[RPS[5]], out=PS[5][:, 32:52], lhsT=ones_f, rhs=adt[:, tb, :],
                 start=True, stop=True)
            if own:
                for g in range(4):
                    pz = 6 if g % 2 == 0 else 4
                    for k in range(8):
                        S.op("pe", "matmul", reads=[Rwz[k], RhT[c]], writes=[RPS[pz]], out=PS[pz][:, 0:320],
                             lhsT=hT[:, k, c * 128:(c + 1) * 128], rhs=wz[:, k, g * 320:(g + 1) * 320], start=(k == 0),
                             stop=(k == 7))
                    S.op("act", "activation", reads=[RPS[pz]], writes=[Rsz4[g]], out=sz4[:, g, :], in_=PS[pz][:, 0:320],
                         func=AF.Silu)
            S.op("dve", "tensor_scalar", reads=[RPS[5]], writes=[Rncs], out=ncs[:], in0=PS[5][:, 0:20], scalar1=-1.0,
                 scalar2=None, op0=ALU.mult)
            S.op("dve", "tensor_tensor", reads=[RPS[5], Rncs], writes=[Rt20], out=tmp20[:], in0=PS[5][:, 32:52],
                 in1=ncs[:], op=ALU.add)
            S.op("act", "activation", reads=[Rt20], writes=[Rdec], out=dec[:], in_=tmp20[:], func=AF.Exp)
            S.op("act", "activation", reads=[RPS[5]], writes=[Rcd], out=cd[:], in_=PS[5][:, 32:52], func=AF.Exp)
            S.op("dve", "tensor_tensor", reads=[Rdt, Rdec], writes=[Rw2], out=w2[:], in0=dt_tm[:, tb, :], in1=dec[:],
                 op=ALU.mult)
            S.op("dve", "tensor_tensor", reads=[Rxtm, Rw2], writes=[Rxdd], out=v3(xdd[:], 20),
                 in0=v3(x_tm[:, tb, :], 20), in1=w2[:].unsqueeze(2).broadcast_to([128, 20, 64]), op=ALU.mult)
            if own:
                blk = c
                S.op("act", "activation", reads=[Rncs], writes=[Recs], out=ecs[:], in_=ncs[:], func=AF.Exp, scale=-1.0)
                S.op("dve", "tensor_tensor", reads=[Rxtm, Rdt], writes=[Rxdt], out=v3(xdt[:], 20),
                     in0=v3(x_tm[:, tb, :], 20), in1=dt_tm[:, tb, :].unsqueeze(2).broadcast_to([128, 20, 64]),
                     op=ALU.mult)
                for g in range(4):
                    S.op("pe", "matmul", reads=[RBTg[g], RCTg[g]], writes=[RPS[2]], out=PS[2][:, g * 128:(g + 1) * 128],
                         lhsT=BT[:, g, tk], rhs=CTt[:, g, tk], start=True, stop=True)
                S.op("dve", "tensor_tensor", reads=[RPS[2], Rcst], writes=[RCBm], out=CBm[:], in0=v3(PS[2][:, :], 4),
                     in1=tri_f.unsqueeze(1).broadcast_to([128, 4, 128]), op=ALU.mult)
                def hq_copy(hq):
                    adtb, Radtb = adtb2[hq % 2]
                    S.op("dve", "tensor_copy", reads=[Radt], writes=[Radtb], out=adtb[:],
                         in_=adt[:, tb, hq * 4:hq * 4 + 4].unsqueeze(2).broadcast_to([128, 4, 128]))
                    pb = 3 + hq % 2
                    for i in range(4):
                        S.op("pe", "matmul", reads=[Radtb, Rcst], writes=[RPS[pb]], out=PS[pb][:, i * 128:(i + 1) * 128],
                             lhsT=adtb[:, i, :], rhs=tri_f, start=True, stop=True)

                def hq_rest(hq):
                    pb = 3 + hq % 2
                    lt, Rlt = Lt[hq % 2]
                    S.op("dve", "tensor_tensor", reads=[RPS[pb], Rncs], writes=[Rlt], out=lt[:], in0=v3(PS[pb][:, :], 4),
                         in1=ncs[:, hq * 4:hq * 4 + 4].unsqueeze(2).broadcast_to([128, 4, 128]), op=ALU.add)
                    S.op("act", "activation", reads=[Rlt], writes=[Rlt], out=lt[:], in_=lt[:], func=AF.Exp)
                    for i in range(4):
                        h = hq * 4 + i
                        S.op("dve", "scalar_tensor_tensor", reads=[Rlt, RCBm], writes=[RWm],
                             out=Wm[:, h, :], in0=lt[:, i, :], scalar=1.0, in1=CBm[:, h // 5, :], op0=ALU.min,
                             op1=ALU.mult)

                hq_copy(0)
                for hq in range(5):
                    if hq + 1 < 5:
                        hq_copy(hq + 1)
                    hq_rest(hq)
                for g in range(4):
                    gs = slice(g * 320, (g + 1) * 320)
                    po = 0 if g % 2 == 0 else 2
                    pd = 1 if g % 2 == 0 else 3
                    S.op("pe", "matmul", reads=[RCTg[g], Rstb], writes=[RPS[po]], out=PS[po][:, 0:320], lhsT=CTt[:, g, tk],
                         rhs=state_bf[:, gs], start=True, stop=True)
                    for hh in range(5):
                        h = g * 5 + hh
                        S.op("pe", "matmul", reads=[RWm, Rxdt], writes=[RPS[pd]], out=PS[pd][:, hh * 64:(hh + 1) * 64],
                             lhsT=Wm[:, h, :], rhs=xdt[:, h * 64:(h + 1) * 64], start=True, stop=True)
                    ygg = yg4[:, g, :]
                    S.op("dve", "tensor_tensor", reads=[RPS[po], Recs], writes=[Ryg4[g]], out=v3(ygg, 5),
                         in0=v3(PS[po][:, 0:320], 5), in1=ecs[:, g * 5:g * 5 + 5].unsqueeze(2).broadcast_to([128, 5, 64]),
                         op=ALU.mult)
                    S.op("dve", "tensor_tensor", reads=[RPS[pd], Ryg4[g]], writes=[Ryg4[g]], out=ygg, in0=ygg,
                         in1=PS[pd][:, 0:320], op=ALU.add)
                    S.op("dve", "tensor_tensor", reads=[Rxtm, Rdsk], writes=[Rytmp], out=v3(ytmp[:], 5),
                         in0=v3(x_tm[:, tb, gs], 5), in1=dsk[:, g * 5:g * 5 + 5].unsqueeze(2).broadcast_to([128, 5, 64]),
                         op=ALU.mult)
                    S.op("dve", "tensor_tensor", reads=[Rytmp, Ryg4[g]], writes=[Ryg4[g]], out=ygg, in0=ygg, in1=ytmp[:],
                         op=ALU.add)
                    S.op("dve", "tensor_tensor", reads=[Ryg4[g], Rsz4[g]], writes=[Ryg4[g]], out=ygg, in0=ygg,
                         in1=sz4[:, g, :], op=ALU.mult)
                    S.op("act", "activation", reads=[Ryg4[g]], writes=[Rss4[g]], out=sqj[:], in_=ygg, func=AF.Square,
                         accum_out=ss4[:, g:g + 1])
                rstd_from(ss4[:, 0:4], Rss4, 320, 4, rs4[:, 0:4], Rrs4, ln4[:, 0:4], Rln4)
                for g in range(4):
                    gs = slice(g * 320, (g + 1) * 320)
                    S.op("dve", "scalar_tensor_tensor", reads=[Ryg4[g], Rrs4, Rgssd], writes=[Ryn], out=yn[:, gs],
                         in0=yg4[:, g, :], scalar=rs4[:, g:g + 1], in1=gssd[:, gs], op0=ALU.mult, op1=ALU.mult)
                yt, Ryt = yTc[tb % 2]
                transposeN(yn, Ryn, 10, 128, yt[:], [Ryt], eng="act")
                ob = c - 16
                S.dma("sp", reads=[Ryt], writes=[Res()], out=yT_scr[:, :, ob * 128:(ob + 1) * 128], in_=yt[:])
            for g in range(4):
                gs = slice(g * 320, (g + 1) * 320)
                S.op("pe", "matmul", reads=[RBtm, Rxdd], writes=[RPS[6]], out=PS[6][:, 0:320],
                     lhsT=B_tm[:, tb, g * 128:(g + 1) * 128], rhs=xdd[:, gs], start=True, stop=True)
                S.op("dve", "tensor_tensor", reads=[Rst, Rcd], writes=[Rst], out=v3(state[:, gs], 5),
                     in0=v3(state[:, gs], 5), in1=cd[:, g * 5:g * 5 + 5].unsqueeze(2).broadcast_to([128, 5, 64]),
                     op=ALU.mult)
                S.op("dve", "tensor_tensor", reads=[Rst, RPS[6]], writes=[Rst], out=state[:, gs], in0=state[:, gs],
                     in1=PS[6][:, 0:320], op=ALU.add)
            if c == 15:
                S.op("dve", "tensor_scalar", reads=[Rst, Rflag], writes=[Rst], out=state[:], in0=state[:],
                     scalar1=flag[:, 0:1], scalar2=None, op0=ALU.mult)
            if c >= 15:
                S.op("act", "copy", reads=[Rst], writes=[Rstb], out=state_bf[:], in_=state[:])
    S.barrier()
    A.release(m_after_hT)

    mtab, Rmtab = T([128, 17, 128], BF16, "mtab")
    S.dma("pool", writes=[Rmtab], out=mtab[:].rearrange("p a b -> p (a b)"), in_=mtab_d)
    gq64, Rgq64 = bc_load(64, "attn_q_norm_g")
    gk64, Rgk64 = bc_load(64, "attn_k_norm_g")
    gq, Rgq = T([128, 256], F32, "gq")
    gk, Rgk = T([128, 256], F32, "gk")
    S.op("dve", "tensor_scalar", reads=[Rgq64], writes=[Rgq], out=v3(gq[:], 4),
         in0=gq64[:].unsqueeze(1).broadcast_to([128, 4, 64]), scalar1=0.125, scalar2=None, op0=ALU.mult)
    S.op("dve", "tensor_copy", reads=[Rgk64], writes=[Rgk], out=v3(gk[:], 4),
         in_=gk64[:].unsqueeze(1).broadcast_to([128, 4, 64]))
    wq, _ = T([128, 8, 256], BF16, "wq")
    Rwq = [Res() for _ in range(8)]
    wk, _ = T([128, 8, 256], BF16, "wk")
    Rwk = [Res() for _ in range(8)]
    wv, _ = T([128, 8, 256], BF16, "wv")
    Rwv = [Res() for _ in range(8)]
    KT, _ = T([64, 4, WIN], BF16, "KT")
    RKT = [Res() for _ in range(32)]
    QT, _ = T([64, 4, HALF], BF16, "QT")
    RQT = [Res() for _ in range(16)]
    Vaug, _ = T([128, 32, 4, 65], BF16, "Vaug")
    RV = [Res() for _ in range(32)]
    kn = [T([128, 256], BF16, "kn") for _ in range(2)]
    qn_ = [T([128, 256], BF16, "qn_") for _ in range(2)]
    Aexp, RAexp = T([128, 128], F32, "Aexp")
    Erev, REr = T([128, 17, 128], F32, "Erev")
    Pf = [T([128, 512], F32, "Pf") for _ in range(3)]
    Pbf = [T([128, 512], BF16, "Pbf") for _ in range(3)]
    rd, Rrd = T([128, 2], F32, "rd")
    o_all, _ = T([128, 16, 256], BF16, "o_all")
    Roall = [Res() for _ in range(16)]
    oTc = [T([128, 2, 128], BF16, "oTc") for _ in range(2)]
    for r in range(3):
        for k in range(8):
            rows = slice(k * 128, (k + 1) * 128)
            S.dma("pool", writes=[Rwq[k]], out=wq[:, k, :], in_=w_in_d[rows, OFF_DT + r * 256:OFF_DT + (r + 1) * 256])
            S.dma("pool", writes=[Rwk[k]], out=wk[:, k, :], in_=w_in_d[rows, OFF_Q + r * 256:OFF_Q + (r + 1) * 256])
            S.dma("pool", writes=[Rwv[k]], out=wv[:, k, :], in_=w_in_d[rows, OFF_K + r * 256:OFF_K + (r + 1) * 256])

        def proj_mm(blk):
            bs = slice(blk * 128, (blk + 1) * 128)
            pa = blk % 2
            for k in range(8):
                S.op("pe", "matmul", reads=[Rwk[k], RhT[blk]], writes=[RPS[pa]], out=PS[pa][:, 0:256], lhsT=hT[:, k, bs],
                     rhs=wk[:, k, :], start=(k == 0), stop=(k == 7))
            kt, Rkn = kn[blk % 2]
            qknorm(PS[pa][:, 0:256], RPS[pa], 4, 64, gk[:], Rgk, kt[:], Rkn)
            pv = 2 + blk % 2
            for k in range(8):
                S.op("pe", "matmul", reads=[Rwv[k], RhT[blk]], writes=[RPS[pv]], out=PS[pv][:, 0:256], lhsT=hT[:, k, bs],
                     rhs=wv[:, k, :], start=(k == 0), stop=(k == 7))
            if blk >= 16:
                S.op("act", "copy", reads=[RPS[pv]], writes=[RV[blk]], out=Vaug[:, blk, :, 0:64],
                     in_=v3(PS[pv][:, 0:256], 4))
                S.op("pool", "memset", writes=[RV[blk]], ap=Vaug[:, blk, :, 64:65], constant=1.0)
            else:
                S.op("dve", "tensor_scalar", reads=[RPS[pv], Rflag], writes=[RV[blk]], out=Vaug[:, blk, :, 0:64],
                     in0=v3(PS[pv][:, 0:256], 4), scalar1=flag[:, 0:1], scalar2=None, op0=ALU.mult)
                S.op("pool", "tensor_copy", reads=[Rflag], writes=[RV[blk]], out=Vaug[:, blk, :, 64:65],
                     in_=flag[:, 0:1].unsqueeze(1).broadcast_to([128, 4, 1]))
            if blk >= 16:
                pq = 4 + blk % 2
                for k in range(8):
                    S.op("pe", "matmul", reads=[Rwq[k], RhT[blk]], writes=[RPS[pq]], out=PS[pq][:, 0:256], lhsT=hT[:, k, bs],
                         rhs=wq[:, k, :], start=(k == 0), stop=(k == 7))
                qt, Rqn = qn_[blk % 2]
                qknorm(PS[pq][:, 0:256], RPS[pq], 4, 64, gq[:], Rgq, qt[:], Rqn)

        def proj_tr(blk):
            bs = slice(blk * 128, (blk + 1) * 128)
            kt, Rkn = kn[blk % 2]
            transposeN(kt, Rkn, 4, 64, KT[:, :, bs], [RKT[blk]], eng="act")
            if blk >= 16:
                qt, Rqn = qn_[blk % 2]
                transposeN(qt, Rqn, 4, 64, QT[:, :, (blk - 16) * 128:(blk - 15) * 128], [RQT[blk - 16]], eng="act")

        for blk in range(33):
            if blk < 32:
                proj_mm(blk)
            if blk >= 1:
                proj_tr(blk - 1)

        seq = []
        for hh in range(4):
            for qb in range(16, 32):
                kbs = list(range(qb - 16, qb + 1))
                for gi in range(5):
                    seq.append((hh, qb, gi, kbs[gi * 4:gi * 4 + 4]))

        def emit_S(i):
            hh, qb, gi, grp = seq[i]
            qi = qb - 16
            sbk = i % 4
            for ii, kb in enumerate(grp):
                S.op("pe", "matmul", reads=[RKT[kb], RQT[qi]], writes=[RPS[sbk]],
                     out=PS[sbk][:, ii * 128:(ii + 1) * 128], lhsT=KT[:, hh, kb * 128:(kb + 1) * 128],
                     rhs=QT[:, hh, qi * 128:(qi + 1) * 128], start=True, stop=True)

        def emit_rest(i):
            hh, qb, gi, grp = seq[i]
            qi = qb - 16
            n = len(grp)
            sbk = i % 4
            unit = i // 5
            ob = 4 + unit % 2
            pf, Rpf = Pf[i % 3]
            pbf, Rpbf = Pbf[i % 3]
            if qb == 16 and gi == 0:
                h = r * 4 + hh
                slope = 2.0 ** (-8.0 * (h + 1) / 12.0)
                S.op("act", "activation", reads=[Rcst], writes=[RAexp], out=Aexp[:], in_=d0_f, func=AF.Exp, scale=-slope)
                for j in range(17):
                    cpow = math.exp(-slope * 128.0 * (16 - j))
                    S.op("dve", "scalar_tensor_tensor", reads=[RAexp, Rmtab], writes=[REr],
                         out=Erev[:, j, :], in0=Aexp[:], scalar=float(cpow), in1=mtab[:, j, :], op0=ALU.mult,
                         op1=ALU.mult)
            S.op("act", "activation", reads=[RPS[sbk]], writes=[Rpf], out=pf[:, 0:n * 128],
                 in_=PS[sbk][:, 0:n * 128], func=AF.Exp)
            S.op("dve", "tensor_tensor", reads=[Rpf, REr], writes=[Rpbf], out=pbf[:, 0:n * 128],
                 in0=pf[:, 0:n * 128], in1=Erev[:, gi * 4:gi * 4 + n, :].rearrange("p a b -> p (a b)"),
                 op=ALU.mult)
            for ii, kb in enumerate(grp):
                S.op("pe", "matmul", reads=[Rpbf, RV[kb]], writes=[RPS[ob]], out=PS[ob][:, 0:65],
                     lhsT=pbf[:, ii * 128:(ii + 1) * 128], rhs=Vaug[:, kb, hh, :], start=(gi == 0 and ii == 0),
                     stop=(gi == 4))
            if gi == 4:
                S.op("dve", "reciprocal", reads=[RPS[ob]], writes=[Rrd], out=rd[:, 0:1], in_=PS[ob][:, 64:65])
                S.op("dve", "tensor_scalar", reads=[RPS[ob], Rrd], writes=[Roall[qi]],
                     out=o_all[:, qi, hh * 64:(hh + 1) * 64], in0=PS[ob][:, 0:64], scalar1=rd[:, 0:1], scalar2=None,
                     op0=ALU.mult)

        LA = 3
        for i in range(LA):
            emit_S(i)
        for i in range(len(seq)):
            if i + LA < len(seq):
                emit_S(i + LA)
            emit_rest(i)
        for qi in range(16):
            for i in range(2):
                S.op("pe", "transpose", reads=[Roall[qi], Ridb], writes=[RPB], out=PB[:, i * 128:(i + 1) * 128],
                     in_=o_all[:, qi, i * 128:(i + 1) * 128], identity=ident_b[:])
            ot, Rot = oTc[qi % 2]
            S.op("act", "copy", reads=[RPB], writes=[Rot], out=ot[:], in_=v3(PB[:, 0:256], 2))
            S.dma("sp", reads=[Rot], writes=[Res()], out=oT_scr[:, 2 * r:2 * r + 2, qi * 128:(qi + 1) * 128], in_=ot[:])
    S.barrier()
    A.release(m_base)

    x1, _ = T([128, 16, 1024], F32, "x1")
    Rx1 = [Res() for _ in range(16)]
    m_after_x1 = A.mark()
    yT_sb, _ = T([128, 10, HALF], BF16, "yT_sb")
    RyT = [Res() for _ in range(10)]
    for kc in range(10):
        S.dma("sp", writes=[RyT[kc]], out=yT_sb[:, kc, :], in_=yT_scr[:, kc, :])
    oT, _ = T([128, 6, HALF], BF16, "oT")
    RoTs = [Res() for _ in range(6)]
    for kc in range(6):
        S.dma("sp", writes=[RoTs[kc]], out=oT[:, kc, :], in_=oT_scr[:, kc, :])
    wout, _ = T([128, 16, 1024], BF16, "wout")
    Rwout = [Res() for _ in range(16)]
    for kc in range(16):
        S.dma("pool", writes=[Rwout[kc]], out=wout[:, kc, :], in_=w_out_d[kc * 128:(kc + 1) * 128, :])
    xo = [T([128, 1024], F32, "xo") for _ in range(2)]
    for tb in range(16):
        ts_ = slice(tb * 128, (tb + 1) * 128)
        xo_t, Rxo = xo[tb % 2]
        S.dma("sp", writes=[Rxo], out=xo_t[:], in_=xw_d[HALF + tb * 128:HALF + (tb + 1) * 128, :])
        for half in range(2):
            pb = 2 * (tb % 2) + half
            cs_ = slice(half * 512, (half + 1) * 512)
            for kc in range(16):
                lhsT = yT_sb[:, kc, ts_] if kc < 10 else oT[:, kc - 10, ts_]
                S.op("pe", "matmul", reads=[RyT[kc] if kc < 10 else RoTs[kc - 10], Rwout[kc]], writes=[RPS[pb]], out=PS[pb][:, :], lhsT=lhsT,
                     rhs=wout[:, kc, cs_], start=(kc == 0), stop=(kc == 15))
            S.op("dve", "tensor_tensor", reads=[RPS[pb], Rxo], writes=[Rx1[tb]], out=x1[:, tb, cs_], in0=PS[pb][:, :],
                 in1=xo_t[:, cs_], op=ALU.add)
    S.barrier()
    A.release(m_after_x1)

    def store_x1_and_finish():
        fin = []
        for tb in range(16):
            R = Res()
            S.dma("sp", reads=[Rx1[tb]], writes=[R], out=out_d[tb * 128:(tb + 1) * 128, :], in_=x1[:, tb, :])
            fin.append(R)
        for R in fin:
            S._wait("sp", R.w)
        S.emit()
        return nc

    if stop_after == "C":
        return store_x1_and_finish()

    wqm, _ = T([128, 8, 512], BF16, "wqm")
    Rwqm = [Res() for _ in range(8)]
    wkv, _ = T([128, 8, 1024], BF16, "wkv")
    Rwkv = [Res() for _ in range(8)]
    wo, _ = T([128, 4, 1024], BF16, "wo")
    Rwo = [Res() for _ in range(4)]
    for k in range(8):
        S.dma("pool", writes=[Rwqm[k]], out=wqm[:, k, :], in_=mem_w_q_d[k * 128:(k + 1) * 128, :])
        S.dma("pool", writes=[Rwkv[k]], out=wkv[:, k, :], in_=mem_w_kv_d[k * 128:(k + 1) * 128, :])
    for k in range(4):
        S.dma("pool", writes=[Rwo[k]], out=wo[:, k, :], in_=mem_w_o_d[k * 128:(k + 1) * 128, :])
    gx, Rgx = bc_load(1024, "xattn_norm_g")
    gm, Rgm = bc_load(1024, "mem_norm_g")
    gqm128, Rgqm128 = bc_load(128, "mem_q_norm_g")
    gkm128, Rgkm128 = bc_load(128, "mem_k_norm_g")
    gqm, Rgqm = T([128, 512], F32, "gqm")
    gkm, Rgkm = T([128, 512], F32, "gkm")
    S.op("dve", "tensor_scalar", reads=[Rgqm128], writes=[Rgqm], out=v3(gqm[:], 4),
         in0=gqm128[:].unsqueeze(1).broadcast_to([128, 4, 128]), scalar1=128.0 ** -0.5, scalar2=None, op0=ALU.mult)
    S.op("dve", "tensor_copy", reads=[Rgkm128], writes=[Rgkm], out=v3(gkm[:], 4),
         in_=gkm128[:].unsqueeze(1).broadcast_to([128, 4, 128]))
    KmT, RKmT = T([128, 4, 256], BF16, "KmT")
    Vm, RVm = T([128, 2, 4, 129], BF16, "Vm")
    S.op("pool", "memset", writes=[RVm], ap=Vm[:].rearrange("p a b c -> p (a b c)"), constant=1.0)
    memb, Rmemb = T([128, 1024], F32, "memb")
    mh, Rmh = T([128, 1024], BF16, "mh")
    mT, RmT = T([128, 8, 128], BF16, "mT")
    knm, Rknm = T([128, 512], BF16, "knm")
    for mb in range(2):
        ms = slice(mb * 128, (mb + 1) * 128)
        S.dma("sp", writes=[Rmemb], out=memb[:], in_=mem_d[ms, :])
        rmsnorm(memb[:], Rmemb, gm, Rgm, mh[:], Rmh)
        transposeN(mh, Rmh, 8, 128, mT[:], [RmT], eng="act")
        for half in range(2):
            for k in range(8):
                S.op("pe", "matmul", reads=[RmT, Rwkv[k]], writes=[RPS[half]], out=PS[half][:, :], lhsT=mT[:, k, :],
                     rhs=wkv[:, k, half * 512:(half + 1) * 512], start=(k == 0), stop=(k == 7))
        qknorm(PS[0][:, :], RPS[0], 4, 128, gkm[:], Rgkm, knm[:], Rknm)
        transposeN(knm, Rknm, 4, 128, KmT[:, :, ms], [RKmT], eng="act")
        S.op("act", "copy", reads=[RPS[1]], writes=[RVm], out=Vm[:, mb, :, 0:128], in_=v3(PS[1][:, :], 4))
    h2, Rh2 = T([128, 1024], BF16, "h2")
    h2T, Rh2T = T([128, 8, 128], BF16, "h2T")
    qn, Rqn2 = T([128, 512], BF16, "qn")
    qT, RqT = T([128, 4, 128], BF16, "qT")
    Pm = [T([128, 256], BF16, "Pm") for _ in range(2)]
    o2, Ro2 = T([128, 512], BF16, "o2")
    o2T, Ro2T = T([128, 4, 128], BF16, "o2T")
    for tb in range(16):
        rmsnorm(x1[:, tb, :], Rx1[tb], gx, Rgx, h2[:], Rh2)
        transposeN(h2, Rh2, 8, 128, h2T[:], [Rh2T], eng="act")
        for k in range(8):
            S.op("pe", "matmul", reads=[Rh2T, Rwqm[k]], writes=[RPS[2]], out=PS[2][:, :], lhsT=h2T[:, k, :], rhs=wqm[:, k, :],
                 start=(k == 0), stop=(k == 7))
        qknorm(PS[2][:, :], RPS[2], 4, 128, gqm[:], Rgqm, qn[:], Rqn2)
        transposeN(qn, Rqn2, 4, 128, qT[:], [RqT], eng="act")
        for hh in range(4):
            sb_ = 3 + hh % 2
            ob = 5 + hh % 2
            pm, Rpm = Pm[hh % 2]
            for mb in range(2):
                S.op("pe", "matmul", reads=[RKmT, RqT], writes=[RPS[sb_]], out=PS[sb_][:, mb * 128:(mb + 1) * 128],
                     lhsT=KmT[:, hh, mb * 128:(mb + 1) * 128], rhs=qT[:, hh, :], start=True, stop=True)
            S.op("act", "activation", reads=[RPS[sb_]], writes=[Rpm], out=pm[:], in_=PS[sb_][:, 0:256], func=AF.Exp)
            for mb in range(2):
                S.op("pe", "matmul", reads=[Rpm, RVm], writes=[RPS[ob]], out=PS[ob][:, 0:129],
                     lhsT=pm[:, mb * 128:(mb + 1) * 128], rhs=Vm[:, mb, hh, :], start=(mb == 0), stop=(mb == 1))
            S.op("dve", "reciprocal", reads=[RPS[ob]], writes=[Rrd], out=rd[:, 0:1], in_=PS[ob][:, 128:129])
            S.op("dve", "tensor_scalar", reads=[RPS[ob], Rrd], writes=[Ro2], out=o2[:, hh * 128:(hh + 1) * 128],
                 in0=PS[ob][:, 0:128], scalar1=rd[:, 0:1], scalar2=None, op0=ALU.mult)
        transposeN(o2, Ro2, 4, 128, o2T[:], [Ro2T], eng="act")
        for half in range(2):
            cs_ = slice(half * 512, (half + 1) * 512)
            for kc in range(4):
                S.op("pe", "matmul", reads=[Ro2T, Rwo[kc]], writes=[RPS[half]], out=PS[half][:, :], lhsT=o2T[:, kc, :],
                     rhs=wo[:, kc, cs_], start=(kc == 0), stop=(kc == 3))
            S.op("dve", "tensor_tensor", reads=[RPS[half], Rx1[tb]], writes=[Rx1[tb]], out=x1[:, tb, cs_],
                 in0=PS[half][:, :], in1=x1[:, tb, cs_], op=ALU.add)
    if stop_after == "D":
        S.barrier()
        return store_x1_and_finish()
    Rx2d = [Res() for _ in range(16)]
    for tb in range(16):
        S.dma("sp", reads=[Rx1[tb]], writes=[Rx2d[tb]], out=out_d[tb * 128:(tb + 1) * 128, :], in_=x1[:, tb, :])
    S.barrier()
    A.release(m_base)

    wpq, _ = T([128, 8, 2048], BF16, "wpq")
    Rwpq = [Res() for _ in range(8)]
    for k in range(8):
        S.dma("pool", writes=[Rwpq[k]], out=wpq[:, k, :], in_=peer_wq_d[k * 128:(k + 1) * 128, :])
    keysT, RkeysT = T([128, 2, 128], F32, "keysT")
    S.dma("sp", writes=[RkeysT], out=keysT[:, 0, :], in_=keys1T_d)
    S.dma("sp", writes=[RkeysT], out=keysT[:, 1, :], in_=keys2T_d)
    gf, Rgf = bc_load(1024, "ffn_norm_g")
    xrp = [T([128, 1024], F32, "xr") for _ in range(2)]
    h3bp = [T([128, 1024], BF16, "h3b") for _ in range(2)]
    h3T, Rh3T = T([128, 8, 128], BF16, "h3T")
    off_q = A.mark()
    qrT, RqrT = T([128, 16, 128], F32, "qrT")
    eq = nc.alloc_sbuf_tensor_at("eq_alias", [128, 8, 16, 16], F32, offset=off_q)
    Req = RqrT
    off_s = A.mark()
    sc, Rsc = T([128, 16, 128], F32, "sc")
    cand = nc.alloc_sbuf_tensor_at("cand_alias", [128, 8, 256], F32, offset=off_s)
    Rcand = Rsc
    sc2, Rsc2 = T([128, 128], F32, "sc2")
    tv, Rtv = T([128, 16, 16], F32, "tv")
    ti, Rti = T([128, 16, 16], U32, "ti")
    tif, Rtif = T([128, 16, 16], F32, "tif")
    cand2, Rcand2 = T([128, 256], F32, "cand2")
    cv, Rcv = T([128, 8, 16], F32, "cv")
    cp, Rcp = T([128, 8, 16], U32, "cp")
    ca, Rca = T([128, 8, 16], U32, "ca")
    cb_, Rcb_ = T([128, 8, 16], U32, "cb")
    caf, Rcaf = T([128, 8, 16], F32, "caf")
    cbf, Rcbf = T([128, 8, 16], F32, "cbf")
    i1f, Ri1f = T([128, 8, 16], F32, "i1f")
    i2f, Ri2f = T([128, 8, 16], F32, "i2f")
    ef, Ref = T([128, 128], F32, "ef")
    eip = [T([128, 128], I32, "ei") for _ in range(2)]
    gtp = [T([128, 8, 16], F32, "gt") for _ in range(2)]
    gsum, Rgsum = T([128, 8], F32, "gsum")
    actv, _ = T([128, 128], F32, "actv")
    Ractv = [Res() for _ in range(128)]
    WG = [T([128, 4], F32, "wg4") for _ in range(2)]
    wgt, _ = T([128, 128], F32, "wgt")
    Rwgt = [Res() for _ in range(32)]
    NUV = 6
    UVt = [(T([128, 4, 2048], BF16, "UV")[0], [Res() for _ in range(4)]) for _ in range(NUV)]
    PR = [T([128, 1024], BF16, "PR") for _ in range(8)]
    DG = [T([128, 128], BF16, "DG") for _ in range(4)]
    fin = []

    def peer_setup(tb, p):
        xr, Rxr = xrp[p]
        h3b, Rh3b = h3bp[p]
        ei, Rei = eip[p]
        gt, Rgt = gtp[p]
        S.dma("sp", reads=[Rx2d[tb]], writes=[Rxr], out=xr[:], in_=out_d[tb * 128:(tb + 1) * 128, :])
        rmsnorm(xr[:], Rxr, gf, Rgf, h3b[:], Rh3b)
        transposeN(h3b, Rh3b, 8, 128, h3T[:], [Rh3T], eng="act")
        yield
        for q4 in range(4):
            pb = q4 % 2
            for i in range(4):
                ccq = q4 * 4 + i
                for k in range(8):
                    S.op("pe", "matmul", reads=[Rwpq[k], Rh3T], writes=[RPS[pb]], out=PS[pb][:, i * 128:(i + 1) * 128],
                         lhsT=wpq[:, k, ccq * 128:(ccq + 1) * 128], rhs=h3T[:, k, :], start=(k == 0), stop=(k == 7))
            S.op("act", "copy", reads=[RPS[pb]], writes=[RqrT], out=qrT[:, q4 * 4:q4 * 4 + 4, :], in_=v3(PS[pb][:, :], 4))
            yield
        for q4 in range(4):
            pb = 2 + q4 % 2
            for i in range(4):
                j = q4 * 4 + i
                S.op("pe", "matmul", reads=[RqrT, RkeysT], writes=[RPS[pb]], out=PS[pb][:, i * 128:(i + 1) * 128],
                     lhsT=qrT[:, j, :], rhs=keysT[:, j % 2, :], start=True, stop=True)
            S.op("act", "copy", reads=[RPS[pb]], writes=[Rsc], out=sc[:, q4 * 4:q4 * 4 + 4, :], in_=v3(PS[pb][:, :], 4))
        for j in range(16):
            S.op("dve", "max", reads=[Rsc], writes=[Rtv], out=tv[:, j, 0:8], in_=sc[:, j, :])
            S.op("dve", "max_index", reads=[Rsc, Rtv], writes=[Rti], out=ti[:, j, 0:8], in_max=tv[:, j, 0:8],
                 in_values=sc[:, j, :])
            S.op("dve", "match_replace", reads=[Rsc, Rtv], writes=[Rsc2], out=sc2[:], in_to_replace=tv[:, j, 0:8],
                 in_values=sc[:, j, :], imm_value=-1e30)
            S.op("dve", "max", reads=[Rsc2], writes=[Rtv], out=tv[:, j, 8:16], in_=sc2[:])
            S.op("dve", "max_index", reads=[Rsc2, Rtv], writes=[Rti], out=ti[:, j, 8:16], in_max=tv[:, j, 8:16],
                 in_values=sc2[:])
            yield
        S.op("dve", "tensor_copy", reads=[Rti], writes=[Rtif], out=tif[:], in_=ti[:])
        tv4 = tv[:].rearrange("p (h two) k -> p h two k", two=2)
        tif4 = tif[:].rearrange("p (h two) k -> p h two k", two=2)
        S.op("dve", "tensor_tensor", reads=[Rtv], writes=[Rcand], out=cand[:].rearrange("p h (a b) -> p h a b", a=16),
             in0=tv4[:, :, 0, :].unsqueeze(3).broadcast_to([128, 8, 16, 16]),
             in1=tv4[:, :, 1, :].unsqueeze(2).broadcast_to([128, 8, 16, 16]), op=ALU.add)
        for h in range(8):
            S.op("dve", "max", reads=[Rcand], writes=[Rcv], out=cv[:, h, 0:8], in_=cand[:, h, :])
            S.op("dve", "max_index", reads=[Rcand, Rcv], writes=[Rcp], out=cp[:, h, 0:8], in_max=cv[:, h, 0:8],
                 in_values=cand[:, h, :])
            S.op("dve", "match_replace", reads=[Rcand, Rcv], writes=[Rcand2], out=cand2[:], in_to_replace=cv[:, h, 0:8],
                 in_values=cand[:, h, :], imm_value=-1e30)
            S.op("dve", "max", reads=[Rcand2], writes=[Rcv], out=cv[:, h, 8:16], in_=cand2[:])
            S.op("dve", "max_index", reads=[Rcand2, Rcv], writes=[Rcp], out=cp[:, h, 8:16], in_max=cv[:, h, 8:16],
                 in_values=cand2[:])
            yield
        S.op("dve", "tensor_single_scalar", reads=[Rcp], writes=[Rca], out=ca[:], in_=cp[:], scalar=4,
             op=ALU.logical_shift_right)
        S.op("dve", "tensor_single_scalar", reads=[Rcp], writes=[Rcb_], out=cb_[:], in_=cp[:], scalar=15,
             op=ALU.bitwise_and)
        S.op("dve", "tensor_copy", reads=[Rca], writes=[Rcaf], out=caf[:], in_=ca[:])
        S.op("dve", "tensor_copy", reads=[Rcb_], writes=[Rcbf], out=cbf[:], in_=cb_[:])
        io4 = iota16.unsqueeze(1).unsqueeze(1).broadcast_to([128, 8, 16, 16])
        for (sel, Rsel, half, dst, Rdst) in ((caf, Rcaf, 0, i1f, Ri1f), (cbf, Rcbf, 1, i2f, Ri2f)):
            S.op("dve", "tensor_tensor", reads=[Rsel, Rcst], writes=[Req], out=eq[:],
                 in0=sel[:].unsqueeze(3).broadcast_to([128, 8, 16, 16]), in1=io4, op=ALU.is_equal)
            S.op("dve", "tensor_tensor", reads=[Req, Rtif], writes=[Req], out=eq[:], in0=eq[:],
                 in1=tif4[:, :, half, :].unsqueeze(2).broadcast_to([128, 8, 16, 16]), op=ALU.mult)
            S.op("dve", "tensor_reduce", reads=[Req], writes=[Rdst], out=dst[:].rearrange("p h k -> p (h k)"),
                 in_=eq[:].rearrange("p h k a -> p (h k) a"), axis=AX.X, op=ALU.add)
        S.op("dve", "scalar_tensor_tensor", reads=[Ri1f, Ri2f], writes=[Ref], out=ef[:],
             in0=i1f[:].rearrange("p h k -> p (h k)"), scalar=128.0, in1=i2f[:].rearrange("p h k -> p (h k)"),
             op0=ALU.mult, op1=ALU.add)
        S.op("dve", "tensor_copy", reads=[Ref], writes=[Rei], out=ei[:], in_=ef[:])
        S.op("dve", "tensor_tensor", reads=[Rcv], writes=[Rgt], out=gt[:], in0=cv[:],
             in1=cv[:, :, 0:1].broadcast_to([128, 8, 16]), op=ALU.subtract)
        S.op("act", "activation", reads=[Rgt], writes=[Rgt], out=gt[:], in_=gt[:], func=AF.Exp)
        S.op("dve", "tensor_reduce", reads=[Rgt], writes=[Rgsum], out=gsum[:], in_=gt[:], axis=AX.X, op=ALU.add)
        S.op("dve", "reciprocal", reads=[Rgsum], writes=[Rgsum], out=gsum[:], in_=gsum[:])
        S.op("dve", "tensor_tensor", reads=[Rgt, Rgsum], writes=[Rgt], out=gt[:], in0=gt[:],
             in1=gsum[:].unsqueeze(2).broadcast_to([128, 8, 16]), op=ALU.mult)

    def peer_gather(tb, p, bg):
        xr, Rxr = xrp[p]
        h3b, Rh3b = h3bp[p]
        ei, Rei = eip[p]
        gt, Rgt = gtp[p]
        gtf = gt[:].rearrange("p h k -> p (h k)")
        S.op("pool", "memset", writes=Ractv, ap=actv[:], constant=0.0)

        def stA(g4):
            uvt, Ruvt = UVt[g4 % NUV]
            for i in range(4):
                s_ = g4 * 4 + i
                S.dma("pool", reads=[Rei, Ruv], writes=[Ruvt[i]], meth="indirect_dma_start", out=uvt[:, i, :],
                      out_offset=None, in_=uv_scr, in_offset=bass.IndirectOffsetOnAxis(ap=ei[:, s_:s_ + 1], axis=0))

        def stBC(g4):
            uvt, Ruvt = UVt[g4 % NUV]
            for i in range(4):
                s_ = g4 * 4 + i
                pr, Rpr = PR[s_ % 8]
                S.op("dve", "tensor_tensor", reads=[Ruvt[i], Rh3b], writes=[Rpr], out=pr[:], in0=uvt[:, i, 0:1024],
                     in1=h3b[:], op=ALU.mult)
                S.op("act", "activation", reads=[Rpr], writes=[Ractv[s_]], out=pr[:], in_=pr[:], func=AF.Copy,
                     accum_out=actv[:, s_:s_ + 1])
            wg, Rwg = WG[g4 % 2]
            S.op("act", "activation", reads=Ractv[g4 * 4:g4 * 4 + 4], writes=[Rwg], out=wg[:], in_=actv[:, g4 * 4:g4 * 4 + 4],
                 func=AF.Gelu)

        def stDE(g4):
            uvt, Ruvt = UVt[g4 % NUV]
            wg, Rwg = WG[g4 % 2]
            S.op("dve", "tensor_tensor", reads=[Rwg, Rgt], writes=[Rwgt[g4]], out=wgt[:, g4 * 4:g4 * 4 + 4], in0=wg[:],
                 in1=gtf[:, g4 * 4:g4 * 4 + 4], op=ALU.mult)
            for i in range(4):
                s_ = g4 * 4 + i
                dg, Rdg = DG[s_ % 4]
                S.op("dve", "tensor_scalar", reads=[Ridb, Rwgt[g4]], writes=[Rdg], out=dg[:], in0=ident_b[:],
                     scalar1=wgt[:, s_:s_ + 1], scalar2=None, op0=ALU.mult)
                for half in range(2):
                    S.op("pe", "matmul", reads=[Rdg, Ruvt[i]], writes=[RPS[5 + half]], out=PS[5 + half][:, :],
                         lhsT=dg[:], rhs=uvt[:, i, 1024 + half * 512:1024 + (half + 1) * 512], start=(s_ == 0),
                         stop=(s_ == 127))

        for g4 in range(NUV - 2):
            stA(g4)
        for g4 in range(33):
            if g4 + NUV - 2 < 32:
                stA(g4 + NUV - 2)
            if g4 < 32:
                stBC(g4)
            if g4 >= 1:
                stDE(g4 - 1)
            if bg is not None:
                next(bg, None)
                next(bg, None)
        if bg is not None:
            for _ in bg:
                pass
        for half in range(2):
            cs_ = slice(half * 512, (half + 1) * 512)
            S.op("dve", "tensor_tensor", reads=[RPS[5 + half], Rxr], writes=[Rxr], out=xr[:, cs_],
                 in0=xr[:, cs_], in1=PS[5 + half][:, :], op=ALU.add)
        R = Res()
        S.dma("sp", reads=[Rxr], writes=[R, Rx2d[tb]], out=out_d[tb * 128:(tb + 1) * 128, :], in_=xr[:])
        fin.append(R)

    for _ in peer_setup(0, 0):
        pass
    for tb in range(16):
        bg = peer_setup(tb + 1, (tb + 1) % 2) if tb + 1 < 16 else None
        peer_gather(tb, tb % 2, bg)
    for R in fin:
        S._wait("sp", R.w)
    S.emit()
    return nc


def _consts():
    c = np.zeros((128, NCONST), np.float32)
    p = np.arange(128)
    c[:, 0:128] = np.eye(128)
    c[:, 128:256] = (p[:, None] <= p[None, :])
    c[:, 256:384] = 1.0
    c[:, 384:512] = (p[None, :] - p[:, None])
    c[:, 512:528] = np.arange(16)[None, :]
    k = p[:, None, None]
    j = np.arange(17)[None, :, None]
    q = p[None, None, :]
    delta = (16 - j) * 128 + q - k
    m = ((delta >= 0) & (delta <= 128)).astype(np.float32)
    m += ((delta >= 0) & (delta <= 512) & (delta % 4 == 0))
    m += ((delta >= 0) & (delta <= 2048) & (delta % 16 == 0))
    return c, np.ascontiguousarray(m.reshape(128, 17 * 128).astype(np.float32))


_NC_CACHE = {}


def kernel(**inputs):
    stop_after = os.environ.get("KSTOP") or None
    x = np.asarray(inputs["x"], np.float32)
    mem = np.asarray(inputs["mem"], np.float32)
    cst, mtab = _consts()
    common = {
        "cst": cst, "mtab": mtab,
        "w_in": np.ascontiguousarray(inputs["w_in"][0], dtype=np.float32),
        "convw": np.ascontiguousarray(
            np.asarray(inputs["conv_w"][0], np.float32).T.reshape(18, 128, 4).transpose(1, 0, 2).reshape(128, 72)),
        "convb": np.ascontiguousarray(np.asarray(inputs["conv_b"][0], np.float32).reshape(18, 128).T),
        "w_out": np.ascontiguousarray(inputs["w_out"][0], dtype=np.float32),
    }
    for nm in ("mix_norm_g", "dt_bias", "a_log", "d_skip", "ssd_norm_g", "attn_q_norm_g", "attn_k_norm_g",
               "xattn_norm_g", "mem_norm_g", "mem_q_norm_g", "mem_k_norm_g", "ffn_norm_g"):
        common[nm] = np.ascontiguousarray(np.asarray(inputs[nm][0], np.float32).reshape(1, -1))
    if True:
        common["mem_w_q"] = np.ascontiguousarray(inputs["mem_w_q"][0], dtype=np.float32)
        common["mem_w_kv"] = np.ascontiguousarray(inputs["mem_w_kv"][0], dtype=np.float32)
        common["mem_w_o"] = np.ascontiguousarray(inputs["mem_w_o"][0], dtype=np.float32)
    if True:
        common["peer_w_query"] = np.ascontiguousarray(inputs["peer_w_query"][0], dtype=np.float32)
        common["keys1T"] = np.ascontiguousarray(np.asarray(inputs["peer_sub_keys1"][0], np.float32).T)
        common["keys2T"] = np.ascontiguousarray(np.asarray(inputs["peer_sub_keys2"][0], np.float32).T)
        common["peer_u"] = np.ascontiguousarray(inputs["peer_u"][0], dtype=np.float32)
        common["peer_v"] = np.ascontiguousarray(inputs["peer_v"][0], dtype=np.float32)
    in_maps = []
    for c in range(8):
        b, j = c // 2, c % 2
        xw = np.zeros((WIN, D), np.float32)
        if j == 1:
            xw[:HALF] = x[b, :HALF]
        xw[HALF:] = x[b, j * HALF:(j + 1) * HALF]
        m = dict(common)
        m["xw"] = xw
        m["flag"] = np.full((128, 1), float(j), np.float32)
        if True:
            m["mem"] = np.ascontiguousarray(mem[b])
        in_maps.append(m)
    nc = build(stop_after)
    res = run_bass_kernel_spmd(nc, in_maps, core_ids=list(range(8)))
    out = np.zeros((NB, SEQ, D), np.float32)
    for c in range(8):
        b, j = c // 2, c % 2
        out[b, j * HALF:(j + 1) * HALF] = np.asarray(res.results[c]["out"])
    return out
```

```python
import os
import math
import numpy as np
import concourse.bass as bass
import concourse.mybir as mybir
from concourse.bass_utils import run_bass_kernel_spmd

F32 = mybir.dt.float32
BF16 = mybir.dt.bfloat16
I32 = mybir.dt.int32
U32 = mybir.dt.uint32
AF = mybir.ActivationFunctionType
ALU = mybir.AluOpType
AX = mybir.AxisListType

D = 1024
NB = 4
SEQ = 4096
HALF = 2048
WIN = 4096
EPS = 1e-6
OFF_Z = 1280
OFF_XBC = OFF_Z + 2304
OFF_DT = OFF_XBC + 20
OFF_Q = OFF_DT + 768
OFF_K = OFF_Q + 768
IN_COLS = OFF_K + 768
NCONST = 128 * 4 + 16


class Res:
    __slots__ = ("w", "r")

    def __init__(self):
        self.w = None
        self.r = {}


class Sched:
    COMPUTE = ("pe", "act", "dve", "pool")
    NDMA = 12
    INORDER = ("pe",)

    def __init__(self, nc):
        self.nc = nc
        self.streams = {k: [] for k in ("pe", "act", "dve", "pool", "sp")}
        self.esem = {k: nc.alloc_semaphore("es_" + k) for k in self.COMPUTE}
        self.ecnt = {k: 0 for k in self.COMPUTE}
        self.dsem = {q: [nc.alloc_semaphore("ds_%s%d" % (q, i)) for i in range(self.NDMA)]
                     for q in ("sp", "pool")}
        self.dcnt = {q: [0] * self.NDMA for q in ("sp", "pool")}
        self.dnext = {q: 0 for q in ("sp", "pool")}
        self.known = {k: {} for k in self.streams}
        self.nwaits = 0
        self.nops = 0

    def _sem_of(self, pid):
        if pid[0] == "e":
            return self.esem[pid[1]], pid[2], ("e", pid[1])
        return self.dsem[pid[1]][pid[2]], pid[3], ("d", pid[1], pid[2])

    def _wait(self, eng, pid):
        sem, val, key = self._sem_of(pid)
        if self.known[eng].get(key, 0) >= val:
            return
        self.known[eng][key] = val
        self.nwaits += 1
        self.streams[eng].append(lambda e, sem=sem, val=val: e.wait_ge(sem, val))

    def _deps(self, eng, reads, writes):
        deps = []
        for r in reads:
            if r.w is not None:
                deps.append(r.w)
        for w in writes:
            if w.w is not None:
                deps.append(w.w)
            deps.extend(w.r.values())
        for pid in deps:
            if pid[0] == "e" and pid[1] == eng and eng in self.INORDER:
                continue
            self._wait(eng, pid)

    def _commit(self, pid, reads, writes):
        key = pid[:2] if pid[0] == "e" else pid[:3]
        for r in reads:
            r.r[key] = pid
        for w in writes:
            w.w = pid
            w.r = {}

    def op(self, eng, meth, reads=(), writes=(), **kw):
        fn = lambda e, meth=meth, kw=kw: getattr(e, meth)(**kw)
        self._deps(eng, reads, writes)
        self.ecnt[eng] += 1
        idx = self.ecnt[eng]
        sem = self.esem[eng]
        self.streams[eng].append(lambda e, fn=fn, sem=sem: fn(e).then_inc(sem, 1))
        self.nops += 1
        self._commit(("e", eng, idx), reads, writes)

    def dma(self, q, reads=(), writes=(), meth="dma_start", **kw):
        fn = lambda e, meth=meth, kw=kw: getattr(e, meth)(**kw)
        i = self.dnext[q]
        self.dnext[q] = (i + 1) % self.NDMA
        prev = self.dcnt[q][i]
        if prev > 0:
            self._wait(q, ("d", q, i, prev))
        self._deps(q, reads, writes)
        self.dcnt[q][i] = prev + 16
        sem = self.dsem[q][i]
        self.streams[q].append(lambda e, fn=fn, sem=sem: fn(e).then_inc(sem, 16))
        self.nops += 1
        self._commit(("d", q, i, prev + 16), reads, writes)

    def barrier(self):
        for eng in self.streams:
            for k in self.COMPUTE:
                if k != eng and self.ecnt[k] > 0:
                    self._wait(eng, ("e", k, self.ecnt[k]))
            for q in self.dsem:
                for i in range(self.NDMA):
                    if self.dcnt[q][i] > 0:
                        self._wait(eng, ("d", q, i, self.dcnt[q][i]))

    def emit(self):
        nc = self.nc
        with nc.Block() as block:
            @block.tensor
            def _(e):
                for it in self.streams["pe"]:
                    it(e)

            @block.scalar
            def _(e):
                for it in self.streams["act"]:
                    it(e)

            @block.vector
            def _(e):
                for it in self.streams["dve"]:
                    it(e)

            @block.gpsimd
            def _(e):
                for it in self.streams["pool"]:
                    it(e)

            @block.sync
            def _(e):
                for it in self.streams["sp"]:
                    it(e)


class Arena:
    def __init__(self, nc):
        self.nc = nc
        self.off = (nc.sbuf_base + 31) // 32 * 32
        self.top = nc.sbuf_top
        self.n = 0

    def alloc(self, shape, dt=F32, name="t"):
        sz = {F32: 4, BF16: 2, I32: 4, U32: 4}[dt]
        nb = int(np.prod(shape[1:])) * sz
        off = self.off
        self.off += (nb + 31) // 32 * 32
        assert self.off <= self.top, "SBUF overflow at %s: %d > %d" % (name, self.off, self.top)
        self.n += 1
        return self.nc.alloc_sbuf_tensor_at("%s_%d" % (name, self.n), list(shape), dt, offset=off)

    def mark(self):
        return self.off

    def release(self, m):
        self.off = m


def build(stop_after=None):
    nc = bass.Bass("TRN2", target_bir_lowering=False)
    S = Sched(nc)
    A = Arena(nc)

    def din(name, shape, dt=F32):
        return nc.dram_tensor(name, list(shape), dt, kind="ExternalInput").ap()

    xw_d = din("xw", [WIN, D])
    flag_d = din("flag", [128, 1])
    mem_d = din("mem", [256, D])
    cst_d = din("cst", [128, NCONST])
    mtab_d = din("mtab", [128, 17 * 128])
    w_in_d = din("w_in", [D, IN_COLS])
    convw_d = din("convw", [128, 72])
    convb_d = din("convb", [128, 18])
    vec_d = {}
    for nm, n in [("mix_norm_g", 1024), ("dt_bias", 20), ("a_log", 20), ("d_skip", 20), ("ssd_norm_g", 1280),
                  ("attn_q_norm_g", 64), ("attn_k_norm_g", 64), ("xattn_norm_g", 1024), ("mem_norm_g", 1024),
                  ("mem_q_norm_g", 128), ("mem_k_norm_g", 128), ("ffn_norm_g", 1024)]:
        vec_d[nm] = din(nm, [1, n])
    w_out_d = din("w_out", [2048, D])
    mem_w_q_d = din("mem_w_q", [D, 512])
    mem_w_kv_d = din("mem_w_kv", [D, 1024])
    mem_w_o_d = din("mem_w_o", [512, D])
    peer_wq_d = din("peer_w_query", [D, 2048])
    keys1T_d = din("keys1T", [128, 128])
    keys2T_d = din("keys2T", [128, 128])
    peer_u_d = din("peer_u", [16384, D])
    peer_v_d = din("peer_v", [16384, D])
    out_d = nc.dram_tensor("out", [HALF, D], F32, kind="ExternalOutput").ap()
    yT_scr = nc.dram_tensor("yT_scr", [128, 10, HALF], BF16).ap()
    oT_scr = nc.dram_tensor("oT_scr", [128, 6, HALF], BF16).ap()
    uv_scr = nc.dram_tensor("uv_scr", [16384, 2048], BF16).ap()
    Ruv = Res()
    conv_jobs = [(tbl, h) for h in range(32) for tbl in (0, 1)]

    def issue_conv(n):
        for _ in range(n):
            if not conv_jobs or stop_after is not None:
                return
            tbl, h = conv_jobs.pop(0)
            src = peer_u_d if tbl == 0 else peer_v_d
            S.dma("pool", writes=[Ruv], out=uv_scr[h * 512:(h + 1) * 512, tbl * 1024:(tbl + 1) * 1024],
                  in_=src[h * 512:(h + 1) * 512, :])

    PS = [nc.alloc_psum_tensor("ps%d" % i, [128, 512], F32) for i in range(7)]
    RPS = [Res() for _ in range(7)]
    PB = nc.alloc_psum_tensor("psb", [128, 1024], BF16)
    RPB = Res()

    def T(shape, dt=F32, name="t"):
        return A.alloc(shape, dt, name), Res()

    def v3(ap, a):
        return ap.rearrange("p (a b) -> p a b", a=a)

    cst, Rcst = T([128, NCONST], F32, "cst")
    S.dma("sp", writes=[Rcst], out=cst[:], in_=cst_d)
    ident_f = cst[:, 0:128]
    tri_f = cst[:, 128:256]
    ones_f = cst[:, 256:384]
    d0_f = cst[:, 384:512]
    iota16 = cst[:, 512:528]
    ident_b, Ridb = T([128, 128], BF16, "identb")
    S.op("dve", "tensor_copy", reads=[Rcst], writes=[Ridb], out=ident_b[:], in_=ident_f)
    flag, Rflag = T([128, 1], F32, "flag")
    S.dma("sp", writes=[Rflag], out=flag[:], in_=flag_d)
    epsT, Reps = T([128, 1], F32, "eps")
    S.op("pool", "memset", writes=[Reps], ap=epsT[:], constant=EPS)

    def bc_load(n, name):
        t, R = T([128, n], F32, name)
        S.dma("sp", writes=[R], out=t[:], in_=vec_d[name].broadcast_to([128, n]))
        return t, R

    def rstd_from(ss_ap, Rss, n, H, out_ap, Rout, lnv_ap, Rln):
        Rss_l = Rss if isinstance(Rss, list) else [Rss]
        S.op("act", "activation", reads=Rss_l + [Reps], writes=[Rln], out=lnv_ap, in_=ss_ap, func=AF.Ln,
             bias=epsT[:, 0:1], scale=1.0 / n)
        S.op("act", "activation", reads=[Rln], writes=[Rout], out=out_ap, in_=lnv_ap, func=AF.Exp, scale=-0.5)

    nrm_junk, Rnj = T([128, 1024], BF16, "nrmjunk")
    nrm_ss, Rnss = T([128, 4], F32, "nrmss")
    nrm_ln, Rnln = T([128, 4], F32, "nrmln")
    nrm_rs, Rnrs = T([128, 4], F32, "nrmrs")

    def rmsnorm(x_ap, Rx, g_bc, Rg, out_ap, Rout):
        S.op("act", "activation", reads=[Rx], writes=[Rnj, Rnss], out=nrm_junk[:], in_=x_ap, func=AF.Square,
             accum_out=nrm_ss[:, 0:1])
        rstd_from(nrm_ss[:, 0:1], Rnss, 1024, 1, nrm_rs[:, 0:1], Rnrs, nrm_ln[:, 0:1], Rnln)
        S.op("dve", "scalar_tensor_tensor", reads=[Rx, Rnrs, Rg], writes=[Rout], out=out_ap, in0=x_ap,
             scalar=nrm_rs[:, 0:1], in1=g_bc[:], op0=ALU.mult, op1=ALU.mult)

    def transposeN(src_bf, Rsrc, n, width, dst3, Rdst, eng="act"):
        for k0 in range(0, n, 8):
            m = min(8, n - k0)
            for k in range(m):
                S.op("pe", "transpose", reads=[Rsrc, Ridb], writes=[RPB], out=PB[0:width, k * 128:(k + 1) * 128],
                     in_=src_bf[:, (k0 + k) * width:(k0 + k + 1) * width], identity=ident_b[:])
            src = v3(PB[0:width, 0:m * 128], m)
            dst = dst3[:, k0:k0 + m, :]
            if eng == "act":
                S.op("act", "copy", reads=[RPB], writes=Rdst, out=dst, in_=src)
            else:
                S.op("dve", "tensor_copy", reads=[RPB], writes=Rdst, out=dst, in_=src)

    qk_sq, Rqsq = T([128, 512], F32, "qksq")
    qk_tmp, Rqtmp = T([128, 512], F32, "qktmp")

    def qknorm(ps_ap, Rps, H, dh, g_bc, Rg, out_bf, Rout):
        n = H * dh
        S.op("act", "activation", reads=[Rps], writes=[Rqsq], out=qk_sq[:, 0:n], in_=ps_ap, func=AF.Square)
        S.op("dve", "tensor_reduce", reads=[Rqsq], writes=[Rnss], out=nrm_ss[:, 0:H], in_=v3(qk_sq[:, 0:n], H),
             axis=AX.X, op=ALU.add)
        rstd_from(nrm_ss[:, 0:H], Rnss, dh, H, nrm_rs[:, 0:H], Rnrs, nrm_ln[:, 0:H], Rnln)
        S.op("dve", "tensor_tensor", reads=[Rps, Rnrs], writes=[Rqtmp], out=v3(qk_tmp[:, 0:n], H), in0=v3(ps_ap, H),
             in1=nrm_rs[:, 0:H].unsqueeze(2).broadcast_to([128, H, dh]), op=ALU.mult)
        S.op("dve", "tensor_tensor", reads=[Rqtmp, Rg], writes=[Rout], out=out_bf, in0=qk_tmp[:, 0:n], in1=g_bc,
             op=ALU.mult)

    m_base = A.mark()
    hT, _ = T([128, 8, WIN], BF16, "hT")
    RhT = [Res() for _ in range(32)]
    m_after_hT = A.mark()

    gmix, Rgmix = bc_load(1024, "mix_norm_g")
    xb = [T([128, 1024], F32, "xb") for _ in range(2)]
    hb = [T([128, 1024], BF16, "hb") for _ in range(2)]
    for blk in range(32):
        x_t, Rx = xb[blk % 2]
        h_t, Rh = hb[blk % 2]
        S.dma("sp", writes=[Rx], out=x_t[:], in_=xw_d[blk * 128:(blk + 1) * 128, :])
        rmsnorm(x_t[:], Rx, gmix, Rgmix, h_t[:], Rh)
        transposeN(h_t, Rh, 8, 128, hT[:, :, blk * 128:(blk + 1) * 128], [RhT[blk]], eng="act" if blk % 2 else "dve")
    S.barrier()
    A.release(m_after_hT)

    wxbc, _ = T([128, 8, 2304], BF16, "wxbc")
    Rwxbc = [Res() for _ in range(8)]
    wz, _ = T([128, 8, 1280], BF16, "wz")
    Rwz = [Res() for _ in range(8)]
    wdt, _ = T([128, 8, 20], BF16, "wdt")
    Rwdt = [Res() for _ in range(8)]
    for k in range(8):
        S.dma("pool", writes=[Rwxbc[k]], out=wxbc[:, k, :], in_=w_in_d[k * 128:(k + 1) * 128, OFF_Z:OFF_XBC])
        S.dma("pool", writes=[Rwz[k]], out=wz[:, k, :], in_=w_in_d[k * 128:(k + 1) * 128, 0:OFF_Z])
        S.dma("pool", writes=[Rwdt[k]], out=wdt[:, k, :], in_=w_in_d[k * 128:(k + 1) * 128, OFF_XBC:OFF_DT])
    convw, Rcw = T([128, 18, 4], F32, "convw")
    convb, Rcb = T([128, 18], F32, "convb")
    S.dma("sp", writes=[Rcw], out=convw[:].rearrange("p a b -> p (a b)"), in_=convw_d)
    S.dma("sp", writes=[Rcb], out=convb[:], in_=convb_d)
    dtb, Rdtb = bc_load(20, "dt_bias")
    alog, Ralog = bc_load(20, "a_log")
    dsk, Rdsk = bc_load(20, "d_skip")
    gssd, Rgssd = bc_load(1280, "ssd_norm_g")
    a_bc, Ra = T([128, 20], F32, "a_bc")
    S.op("act", "activation", reads=[Ralog], writes=[Ra], out=a_bc[:], in_=alog[:], func=AF.Exp)
    S.op("dve", "tensor_scalar", reads=[Ra], writes=[Ra], out=a_bc[:], in0=a_bc[:], scalar1=-1.0, scalar2=None,
         op0=ALU.mult)
    halo, _ = T([128, 18, 3], F32, "halo")
    Rhalo = [Res() for _ in range(18)]
    S.op("pool", "memset", writes=Rhalo, ap=halo[:], constant=0.0)
    state, Rst = T([128, 1280], F32, "state")
    state_bf, Rstb = T([128, 1280], BF16, "statebf")
    S.op("pool", "memset", writes=[Rst], ap=state[:], constant=0.0)
    S.op("pool", "memset", writes=[Rstb], ap=state_bf[:], constant=0.0)
    U = [T([128, 259], F32, "U") for _ in range(2)]
    acc = [T([128, 256], F32, "acc") for _ in range(2)]
    xTf = [T([128, 256], F32, "xTf") for _ in range(2)]
    x_tm, Rxtm = T([128, 2, 1280], F32, "x_tm")
    BT, _ = T([128, 4, 256], BF16, "BT")
    RBTg = [Res() for _ in range(4)]
    CTt, _ = T([128, 4, 256], BF16, "CT")
    RCTg = [Res() for _ in range(4)]
    B_tm, RBtm = T([128, 2, 512], BF16, "B_tm")
    dt_tm, Rdt = T([128, 2, 20], F32, "dt_tm")
    adt, Radt = T([128, 2, 20], F32, "adt")
    ncs, Rncs = T([128, 20], F32, "ncs")
    tmp20, Rt20 = T([128, 20], F32, "tmp20")
    dec, Rdec = T([128, 20], F32, "dec")
    cd, Rcd = T([128, 20], F32, "cd")
    w2, Rw2 = T([128, 20], F32, "w2")
    ecs, Recs = T([128, 20], F32, "ecs")
    xdd, Rxdd = T([128, 1280], BF16, "xdd")
    xdt, Rxdt = T([128, 1280], BF16, "xdt")
    CBm, RCBm = T([128, 4, 128], F32, "CBm")
    adtb2 = [T([128, 4, 128], F32, "adtb") for _ in range(2)]
    Lt = [T([128, 4, 128], F32, "Lt") for _ in range(2)]
    Wm, RWm = T([128, 20, 128], BF16, "Wm")
    yg4, _ = T([128, 4, 320], F32, "yg4")
    Ryg4 = [Res() for _ in range(4)]
    sz4, _ = T([128, 4, 320], F32, "sz4")
    Rsz4 = [Res() for _ in range(4)]
    sqj, _ = T([128, 320], BF16, "sqj")
    ss4, _ = T([128, 4], F32, "ss4")
    Rss4 = [Res() for _ in range(4)]
    ln4, Rln4 = T([128, 4], F32, "ln4")
    rs4, Rrs4 = T([128, 4], F32, "rs4")
    ytmp, Rytmp = T([128, 320], F32, "ytmp")
    ss1, Rss1 = T([128, 1], F32, "ss1")
    ln1, Rln1 = T([128, 1], F32, "ln1")
    rs1, Rrs1 = T([128, 1], F32, "rs1")
    yn, Ryn = T([128, 1280], BF16, "yn")
    yTc = [T([128, 10, 128], BF16, "yTc") for _ in range(2)]

    for G in range(16):
        own = G >= 8
        RhG = RhT[G * 2:(G + 1) * 2]
        tokG = slice(G * 256, (G + 1) * 256)
        def cc_s1(cc):
            b = cc % 2
            for k in range(8):
                S.op("pe", "matmul", reads=[Rwxbc[k]] + RhG, writes=[RPS[b]], out=PS[b][:, 0:256],
                     lhsT=wxbc[:, k, cc * 128:(cc + 1) * 128], rhs=hT[:, k, tokG], start=(k == 0), stop=(k == 7))
            u_t, Ru = U[b]
            S.op("act", "copy", reads=[RPS[b]], writes=[Ru], out=u_t[:, 3:259], in_=PS[b][:, 0:256])
            S.op("act", "copy", reads=[Rhalo[cc]], writes=[Ru], out=u_t[:, 0:3], in_=halo[:, cc, :])
            S.op("act", "copy", reads=[Ru], writes=[Rhalo[cc]], out=halo[:, cc, :], in_=u_t[:, 256:259])

        def cc_s2(cc):
            b = cc % 2
            u_t, Ru = U[b]
            a_t, Rac = acc[b]
            S.op("dve", "tensor_scalar", reads=[Ru, Rcw], writes=[Rac], out=a_t[:], in0=u_t[:, 3:259],
                 scalar1=convw[:, cc, 3:4], scalar2=None, op0=ALU.mult)
            for j in (2, 1, 0):
                S.op("dve", "scalar_tensor_tensor", reads=[Ru, Rcw, Rac], writes=[Rac], out=a_t[:],
                     in0=u_t[:, j:j + 256], scalar=convw[:, cc, j:j + 1], in1=a_t[:], op0=ALU.mult, op1=ALU.add)
            if cc < 10:
                xt, Rxt = xTf[b]
                S.op("act", "activation", reads=[Rac, Rcb], writes=[Rxt], out=xt[:], in_=a_t[:], func=AF.Silu,
                     bias=convb[:, cc:cc + 1])
            elif cc < 14:
                g = cc - 10
                S.op("act", "activation", reads=[Rac, Rcb], writes=[RBTg[g]], out=BT[:, g, :], in_=a_t[:], func=AF.Silu,
                     bias=convb[:, cc:cc + 1])
            else:
                g = cc - 14
                S.op("act", "activation", reads=[Rac, Rcb], writes=[RCTg[g]], out=CTt[:, g, :], in_=a_t[:], func=AF.Silu,
                     bias=convb[:, cc:cc + 1])

        def cc_tr(cc):
            b = cc % 2
            if cc < 10:
                xt, Rxt = xTf[b]
                pb = 2 + b
                for tb in range(2):
                    S.op("pe", "transpose", reads=[Rxt, Rcst], writes=[RPS[pb]], out=PS[pb][:, tb * 128:(tb + 1) * 128],
                         in_=xt[:, tb * 128:(tb + 1) * 128], identity=ident_f)
                S.op("dve", "tensor_copy", reads=[RPS[pb]], writes=[Rxtm], out=x_tm[:, :, cc * 128:(cc + 1) * 128],
                     in_=v3(PS[pb][:, 0:256], 2))
            elif cc < 14:
                g = cc - 10
                for tb in range(2):
                    S.op("pe", "transpose", reads=[RBTg[g], Ridb], writes=[RPB], out=PB[:, tb * 128:(tb + 1) * 128],
                         in_=BT[:, g, tb * 128:(tb + 1) * 128], identity=ident_b[:])
                S.op("dve", "tensor_copy", reads=[RPB], writes=[RBtm], out=B_tm[:, :, g * 128:(g + 1) * 128],
                     in_=v3(PB[:, 0:256], 2))

        cc_s1(0)
        for cc in range(19):
            if cc + 1 < 18:
                cc_s1(cc + 1)
            if cc < 18:
                cc_s2(cc)
            if cc >= 1:
                cc_tr(cc - 1)
        issue_conv(4)
        for tb in range(2):
            blk = G * 2 + tb
            for k in range(8):
                S.op("pe", "matmul", reads=[Rwdt[k], RhT[blk]], writes=[RPS[4]], out=PS[4][:, tb * 20:(tb + 1) * 20],
                     lhsT=hT[:, k, blk * 128:(blk + 1) * 128], rhs=wdt[:, k, :], start=(k == 0), stop=(k == 7))
        S.op("dve", "tensor_tensor", reads=[RPS[4], Rdtb], writes=[Rdt], out=dt_tm[:], in0=v3(PS[4][:, 0:40], 2),
             in1=dtb[:].unsqueeze(1).broadcast_to([128, 2, 20]), op=ALU.add)
        S.op("act", "activation", reads=[Rdt], writes=[Rdt], out=dt_tm[:], in_=dt_tm[:], func=AF.Exp)
        S.op("act", "activation", reads=[Rdt], writes=[Rdt], out=dt_tm[:], in_=dt_tm[:], func=AF.Ln, bias=1.0)
        S.op("dve", "tensor_tensor", reads=[Rdt, Ra], writes=[Radt], out=adt[:], in0=dt_tm[:],
             in1=a_bc[:].unsqueeze(1).broadcast_to([128, 2, 20]), op=ALU.mult)
        for tb in range(2):
            c = G * 2 + tb
            tk = slice(tb * 128, (tb + 1) * 128)
            S.op("pe", "matmul", reads=[Rcst, Radt], writes=[RPS[5]], out=PS[5][:, 0:20], lhsT=tri_f, rhs=adt[:, tb, :],
                 start=True, stop=True)
            S.op("pe", "matmul", reads=[Rcst, Radt], writes=[RPS[5]], out=PS[5][:, 32:52], lhsT=ones_f, rhs=adt[:, tb, :],
                 start=True, stop=True)
            if own:
                for g in range(4):
                    pz = 6 if g % 2 == 0 else 4
                    for k in range(8):
                        S.op("pe", "matmul", reads=[Rwz[k], RhT[c]], writes=[RPS[pz]], out=PS[pz][:, 0:320],
                             lhsT=hT[:, k, c * 128:(c + 1) * 128], rhs=wz[:, k, g * 320:(g + 1) * 320], start=(k == 0),
                             stop=(k == 7))
                    S.op("act", "activation", reads=[RPS[pz]], writes=[Rsz4[g]], out=sz4[:, g, :], in_=PS[pz][:, 0:320],
                         func=AF.Silu)
            S.op("dve", "tensor_scalar", reads=[RPS[5]], writes=[Rncs], out=ncs[:], in0=PS[5][:, 0:20], scalar1=-1.0,
                 scalar2=None, op0=ALU.mult)
            S.op("dve", "tensor_tensor", reads=[RPS[5], Rncs], writes=[Rt20], out=tmp20[:], in0=PS[5][:, 32:52],
                 in1=ncs[:], op=ALU.add)
            S.op("act", "activation", reads=[Rt20], writes=[Rdec], out=dec[:], in_=tmp20[:], func=AF.Exp)
            S.op("act", "activation", reads=[RPS[5]], writes=[Rcd], out=cd[:], in_=PS[5][:, 32:52], func=AF.Exp)
            S.op("dve", "tensor_tensor", reads=[Rdt, Rdec], writes=[Rw2], out=w2[:], in0=dt_tm[:, tb, :], in1=dec[:],
                 op=ALU.mult)
            S.op("dve", "tensor_tensor", reads=[Rxtm, Rw2], writes=[Rxdd], out=v3(xdd[:], 20),
                 in0=v3(x_tm[:, tb, :], 20), in1=w2[:].unsqueeze(2).broadcast_to([128, 20, 64]), op=ALU.mult)
            if own:
                blk = c
                S.op("act", "activation", reads=[Rncs], writes=[Recs], out=ecs[:], in_=ncs[:], func=AF.Exp, scale=-1.0)
                S.op("dve", "tensor_tensor", reads=[Rxtm, Rdt], writes=[Rxdt], out=v3(xdt[:], 20),
                     in0=v3(x_tm[:, tb, :], 20), in1=dt_tm[:, tb, :].unsqueeze(2).broadcast_to([128, 20, 64]),
                     op=ALU.mult)
                for g in range(4):
                    S.op("pe", "matmul", reads=[RBTg[g], RCTg[g]], writes=[RPS[2]], out=PS[2][:, g * 128:(g + 1) * 128],
                         lhsT=BT[:, g, tk], rhs=CTt[:, g, tk], start=True, stop=True)
                S.op("dve", "tensor_tensor", reads=[RPS[2], Rcst], writes=[RCBm], out=CBm[:], in0=v3(PS[2][:, :], 4),
                     in1=tri_f.unsqueeze(1).broadcast_to([128, 4, 128]), op=ALU.mult)
                def hq_copy(hq):
                    adtb, Radtb = adtb2[hq % 2]
                    S.op("dve", "tensor_copy", reads=[Radt], writes=[Radtb], out=adtb[:],
                         in_=adt[:, tb, hq * 4:hq * 4 + 4].unsqueeze(2).broadcast_to([128, 4, 128]))
                    pb = 3 + hq % 2
                    for i in range(4):
                        S.op("pe", "matmul", reads=[Radtb, Rcst], writes=[RPS[pb]], out=PS[pb][:, i * 128:(i + 1) * 128],
                             lhsT=adtb[:, i, :], rhs=tri_f, start=True, stop=True)

                def hq_rest(hq):
                    pb = 3 + hq % 2
                    lt, Rlt = Lt[hq % 2]
                    S.op("dve", "tensor_tensor", reads=[RPS[pb], Rncs], writes=[Rlt], out=lt[:], in0=v3(PS[pb][:, :], 4),
                         in1=ncs[:, hq * 4:hq * 4 + 4].unsqueeze(2).broadcast_to([128, 4, 128]), op=ALU.add)
                    S.op("act", "activation", reads=[Rlt], writes=[Rlt], out=lt[:], in_=lt[:], func=AF.Exp)
                    for i in range(4):
                        h = hq * 4 + i
                        S.op("dve", "scalar_tensor_tensor", reads=[Rlt, RCBm], writes=[RWm],
                             out=Wm[:, h, :], in0=lt[:, i, :], scalar=1.0, in1=CBm[:, h // 5, :], op0=ALU.min,
                             op1=ALU.mult)

                hq_copy(0)
                for hq in range(5):
                    if hq + 1 < 5:
                        hq_copy(hq + 1)
                    hq_rest(hq)
                for g in range(4):
                    gs = slice(g * 320, (g + 1) * 320)
                    po = 0 if g % 2 == 0 else 2
                    pd = 1 if g % 2 == 0 else 3
                    S.op("pe", "matmul", reads=[RCTg[g], Rstb], writes=[RPS[po]], out=PS[po][:, 0:320], lhsT=CTt[:, g, tk],
                         rhs=state_bf[:, gs], start=True, stop=True)
                    for hh in range(5):
                        h = g * 5 + hh
                        S.op("pe", "matmul", reads=[RWm, Rxdt], writes=[RPS[pd]], out=PS[pd][:, hh * 64:(hh + 1) * 64],
                             lhsT=Wm[:, h, :], rhs=xdt[:, h * 64:(h + 1) * 64], start=True, stop=True)
                    ygg = yg4[:, g, :]
                    S.op("dve", "tensor_tensor", reads=[RPS[po], Recs], writes=[Ryg4[g]], out=v3(ygg, 5),
                         in0=v3(PS[po][:, 0:320], 5), in1=ecs[:, g * 5:g * 5 + 5].unsqueeze(2).broadcast_to([128, 5, 64]),
                         op=ALU.mult)
                    S.op("dve", "tensor_tensor", reads=[RPS[pd], Ryg4[g]], writes=[Ryg4[g]], out=ygg, in0=ygg,
                         in1=PS[pd][:, 0:320], op=ALU.add)
                    S.op("dve", "tensor_tensor", reads=[Rxtm, Rdsk], writes=[Rytmp], out=v3(ytmp[:], 5),
                         in0=v3(x_tm[:, tb, gs], 5), in1=dsk[:, g * 5:g * 5 + 5].unsqueeze(2).broadcast_to([128, 5, 64]),
                         op=ALU.mult)
                    S.op("dve", "tensor_tensor", reads=[Rytmp, Ryg4[g]], writes=[Ryg4[g]], out=ygg, in0=ygg, in1=ytmp[:],
                         op=ALU.add)
                    S.op("dve", "tensor_tensor", reads=[Ryg4[g], Rsz4[g]], writes=[Ryg4[g]], out=ygg, in0=ygg,
                         in1=sz4[:, g, :], op=ALU.mult)
                    S.op("act", "activation", reads=[Ryg4[g]], writes=[Rss4[g]], out=sqj[:], in_=ygg, func=AF.Square,
                         accum_out=ss4[:, g:g + 1])
                rstd_from(ss4[:, 0:4], Rss4, 320, 4, rs4[:, 0:4], Rrs4, ln4[:, 0:4], Rln4)
                for g in range(4):
                    gs = slice(g * 320, (g + 1) * 320)
                    S.op("dve", "scalar_tensor_tensor", reads=[Ryg4[g], Rrs4, Rgssd], writes=[Ryn], out=yn[:, gs],
                         in0=yg4[:, g, :], scalar=rs4[:, g:g + 1], in1=gssd[:, gs], op0=ALU.mult, op1=ALU.mult)
                yt, Ryt = yTc[tb % 2]
                transposeN(yn, Ryn, 10, 128, yt[:], [Ryt], eng="act")
                ob = c - 16
                S.dma("sp", reads=[Ryt], writes=[Res()], out=yT_scr[:, :, ob * 128:(ob + 1) * 128], in_=yt[:])
            for g in range(4):
                gs = slice(g * 320, (g + 1) * 320)
                S.op("pe", "matmul", reads=[RBtm, Rxdd], writes=[RPS[6]], out=PS[6][:, 0:320],
                     lhsT=B_tm[:, tb, g * 128:(g + 1) * 128], rhs=xdd[:, gs], start=True, stop=True)
                S.op("dve", "tensor_tensor", reads=[Rst, Rcd], writes=[Rst], out=v3(state[:, gs], 5),
                     in0=v3(state[:, gs], 5), in1=cd[:, g * 5:g * 5 + 5].unsqueeze(2).broadcast_to([128, 5, 64]),
                     op=ALU.mult)
                S.op("dve", "tensor_tensor", reads=[Rst, RPS[6]], writes=[Rst], out=state[:, gs], in0=state[:, gs],
                     in1=PS[6][:, 0:320], op=ALU.add)
            if c == 15:
                S.op("dve", "tensor_scalar", reads=[Rst, Rflag], writes=[Rst], out=state[:], in0=state[:],
                     scalar1=flag[:, 0:1], scalar2=None, op0=ALU.mult)
            if c >= 15:
                S.op("act", "copy", reads=[Rst], writes=[Rstb], out=state_bf[:], in_=state[:])
    S.barrier()
    A.release(m_after_hT)

    mtab, Rmtab = T([128, 17, 128], BF16, "mtab")
    S.dma("pool", writes=[Rmtab], out=mtab[:].rearrange("p a b -> p (a b)"), in_=mtab_d)
    gq64, Rgq64 = bc_load(64, "attn_q_norm_g")
    gk64, Rgk64 = bc_load(64, "attn_k_norm_g")
    gq, Rgq = T([128, 256], F32, "gq")
    gk, Rgk = T([128, 256], F32, "gk")
    S.op("dve", "tensor_scalar", reads=[Rgq64], writes=[Rgq], out=v3(gq[:], 4),
         in0=gq64[:].unsqueeze(1).broadcast_to([128, 4, 64]), scalar1=0.125, scalar2=None, op0=ALU.mult)
    S.op("dve", "tensor_copy", reads=[Rgk64], writes=[Rgk], out=v3(gk[:], 4),
         in_=gk64[:].unsqueeze(1).broadcast_to([128, 4, 64]))
    wq, _ = T([128, 8, 256], BF16, "wq")
    Rwq = [Res() for _ in range(8)]
    wk, _ = T([128, 8, 256], BF16, "wk")
    Rwk = [Res() for _ in range(8)]
    wv, _ = T([128, 8, 256], BF16, "wv")
    Rwv = [Res() for _ in range(8)]
    KT, _ = T([64, 4, WIN], BF16, "KT")
    RKT = [Res() for _ in range(32)]
    QT, _ = T([64, 4, HALF], BF16, "QT")
    RQT = [Res() for _ in range(16)]
    Vaug, _ = T([128, 32, 4, 65], BF16, "Vaug")
    RV = [Res() for _ in range(32)]
    kn = [T([128, 256], BF16, "kn") for _ in range(2)]
    qn_ = [T([128, 256], BF16, "qn_") for _ in range(2)]
    Aexp, RAexp = T([128, 128], F32, "Aexp")
    Erev, REr = T([128, 17, 128], BF16, "Erev")
    Pf = [T([128, 512], BF16, "Pf") for _ in range(3)]
    Pbf = [T([128, 512], BF16, "Pbf") for _ in range(3)]
    rd, Rrd = T([128, 2], F32, "rd")
    o_all, _ = T([128, 16, 256], BF16, "o_all")
    Roall = [Res() for _ in range(16)]
    oTc = [T([128, 2, 128], BF16, "oTc") for _ in range(2)]
    for r in range(3):
        for k in range(8):
            rows = slice(k * 128, (k + 1) * 128)
            S.dma("pool", writes=[Rwq[k]], out=wq[:, k, :], in_=w_in_d[rows, OFF_DT + r * 256:OFF_DT + (r + 1) * 256])
            S.dma("pool", writes=[Rwk[k]], out=wk[:, k, :], in_=w_in_d[rows, OFF_Q + r * 256:OFF_Q + (r + 1) * 256])
            S.dma("pool", writes=[Rwv[k]], out=wv[:, k, :], in_=w_in_d[rows, OFF_K + r * 256:OFF_K + (r + 1) * 256])

        def proj_mm(blk):
            bs = slice(blk * 128, (blk + 1) * 128)
            pa = blk % 2
            for k in range(8):
                S.op("pe", "matmul", reads=[Rwk[k], RhT[blk]], writes=[RPS[pa]], out=PS[pa][:, 0:256], lhsT=hT[:, k, bs],
                     rhs=wk[:, k, :], start=(k == 0), stop=(k == 7))
            kt, Rkn = kn[blk % 2]
            qknorm(PS[pa][:, 0:256], RPS[pa], 4, 64, gk[:], Rgk, kt[:], Rkn)
            pv = 2 + blk % 2
            for k in range(8):
                S.op("pe", "matmul", reads=[Rwv[k], RhT[blk]], writes=[RPS[pv]], out=PS[pv][:, 0:256], lhsT=hT[:, k, bs],
                     rhs=wv[:, k, :], start=(k == 0), stop=(k == 7))
            if blk >= 16:
                S.op("act", "copy", reads=[RPS[pv]], writes=[RV[blk]], out=Vaug[:, blk, :, 0:64],
                     in_=v3(PS[pv][:, 0:256], 4))
                S.op("pool", "memset", writes=[RV[blk]], ap=Vaug[:, blk, :, 64:65], constant=1.0)
            else:
                S.op("dve", "tensor_scalar", reads=[RPS[pv], Rflag], writes=[RV[blk]], out=Vaug[:, blk, :, 0:64],
                     in0=v3(PS[pv][:, 0:256], 4), scalar1=flag[:, 0:1], scalar2=None, op0=ALU.mult)
                S.op("pool", "tensor_copy", reads=[Rflag], writes=[RV[blk]], out=Vaug[:, blk, :, 64:65],
                     in_=flag[:, 0:1].unsqueeze(1).broadcast_to([128, 4, 1]))
            if blk >= 16:
                pq = 4 + blk % 2
                for k in range(8):
                    S.op("pe", "matmul", reads=[Rwq[k], RhT[blk]], writes=[RPS[pq]], out=PS[pq][:, 0:256], lhsT=hT[:, k, bs],
                         rhs=wq[:, k, :], start=(k == 0), stop=(k == 7))
                qt, Rqn = qn_[blk % 2]
                qknorm(PS[pq][:, 0:256], RPS[pq], 4, 64, gq[:], Rgq, qt[:], Rqn)

        def proj_tr(blk):
            bs = slice(blk * 128, (blk + 1) * 128)
            kt, Rkn = kn[blk % 2]
            transposeN(kt, Rkn, 4, 64, KT[:, :, bs], [RKT[blk]], eng="act")
            if blk >= 16:
                qt, Rqn = qn_[blk % 2]
                transposeN(qt, Rqn, 4, 64, QT[:, :, (blk - 16) * 128:(blk - 15) * 128], [RQT[blk - 16]], eng="act")

        for blk in range(33):
            if blk < 32:
                proj_mm(blk)
            if blk >= 1:
                proj_tr(blk - 1)

        seq = []
        for hh in range(4):
            for qb in range(16, 32):
                kbs = list(range(qb - 16, qb + 1))
                for gi in range(5):
                    seq.append((hh, qb, gi, kbs[gi * 4:gi * 4 + 4]))

        def emit_S(i):
            hh, qb, gi, grp = seq[i]
            qi = qb - 16
            sbk = i % 4
            for ii, kb in enumerate(grp):
                S.op("pe", "matmul", reads=[RKT[kb], RQT[qi]], writes=[RPS[sbk]],
                     out=PS[sbk][:, ii * 128:(ii + 1) * 128], lhsT=KT[:, hh, kb * 128:(kb + 1) * 128],
                     rhs=QT[:, hh, qi * 128:(qi + 1) * 128], start=True, stop=True)

        def emit_rest(i):
            hh, qb, gi, grp = seq[i]
            qi = qb - 16
            n = len(grp)
            sbk = i % 4
            unit = i // 5
            ob = 4 + unit % 2
            pf, Rpf = Pf[i % 3]
            pbf, Rpbf = Pbf[i % 3]
            if qb == 16 and gi == 0:
                h = r * 4 + hh
                slope = 2.0 ** (-8.0 * (h + 1) / 12.0)
                S.op("act", "activation", reads=[Rcst], writes=[RAexp], out=Aexp[:], in_=d0_f, func=AF.Exp, scale=-slope)
                for j in range(17):
                    cpow = math.exp(-slope * 128.0 * (16 - j))
                    S.op("dve", "scalar_tensor_tensor", reads=[RAexp, Rmtab], writes=[REr],
                         out=Erev[:, j, :], in0=Aexp[:], scalar=float(cpow), in1=mtab[:, j, :], op0=ALU.mult,
                         op1=ALU.mult)
            S.op("act", "activation", reads=[RPS[sbk]], writes=[Rpf], out=pf[:, 0:n * 128],
                 in_=PS[sbk][:, 0:n * 128], func=AF.Exp)
            S.op("dve", "tensor_tensor", reads=[Rpf, REr], writes=[Rpbf], out=pbf[:, 0:n * 128],
                 in0=pf[:, 0:n * 128], in1=Erev[:, gi * 4:gi * 4 + n, :].rearrange("p a b -> p (a b)"),
                 op=ALU.mult)
            for ii, kb in enumerate(grp):
                S.op("pe", "matmul", reads=[Rpbf, RV[kb]], writes=[RPS[ob]], out=PS[ob][:, 0:65],
                     lhsT=pbf[:, ii * 128:(ii + 1) * 128], rhs=Vaug[:, kb, hh, :], start=(gi == 0 and ii == 0),
                     stop=(gi == 4))
            if gi == 4:
                S.op("dve", "reciprocal", reads=[RPS[ob]], writes=[Rrd], out=rd[:, 0:1], in_=PS[ob][:, 64:65])
                S.op("dve", "tensor_scalar", reads=[RPS[ob], Rrd], writes=[Roall[qi]],
                     out=o_all[:, qi, hh * 64:(hh + 1) * 64], in0=PS[ob][:, 0:64], scalar1=rd[:, 0:1], scalar2=None,
                     op0=ALU.mult)

        LA = 3
        for i in range(LA):
            emit_S(i)
        for i in range(len(seq)):
            if i + LA < len(seq):
                emit_S(i + LA)
            emit_rest(i)
        for qi in range(16):
            for i in range(2):
                S.op("pe", "transpose", reads=[Roall[qi], Ridb], writes=[RPB], out=PB[:, i * 128:(i + 1) * 128],
                     in_=o_all[:, qi, i * 128:(i + 1) * 128], identity=ident_b[:])
            ot, Rot = oTc[qi % 2]
            S.op("act", "copy", reads=[RPB], writes=[Rot], out=ot[:], in_=v3(PB[:, 0:256], 2))
            S.dma("sp", reads=[Rot], writes=[Res()], out=oT_scr[:, 2 * r:2 * r + 2, qi * 128:(qi + 1) * 128], in_=ot[:])
    S.barrier()
    A.release(m_base)

    x1, _ = T([128, 16, 1024], F32, "x1")
    Rx1 = [Res() for _ in range(16)]
    m_after_x1 = A.mark()
    yT_sb, _ = T([128, 10, HALF], BF16, "yT_sb")
    RyT = [Res() for _ in range(10)]
    for kc in range(10):
        S.dma("sp", writes=[RyT[kc]], out=yT_sb[:, kc, :], in_=yT_scr[:, kc, :])
    oT, _ = T([128, 6, HALF], BF16, "oT")
    RoTs = [Res() for _ in range(6)]
    for kc in range(6):
        S.dma("sp", writes=[RoTs[kc]], out=oT[:, kc, :], in_=oT_scr[:, kc, :])
    wout, _ = T([128, 16, 1024], BF16, "wout")
    Rwout = [Res() for _ in range(16)]
    for kc in range(16):
        S.dma("pool", writes=[Rwout[kc]], out=wout[:, kc, :], in_=w_out_d[kc * 128:(kc + 1) * 128, :])
    xo = [T([128, 1024], F32, "xo") for _ in range(2)]
    for tb in range(16):
        ts_ = slice(tb * 128, (tb + 1) * 128)
        xo_t, Rxo = xo[tb % 2]
        S.dma("sp", writes=[Rxo], out=xo_t[:], in_=xw_d[HALF + tb * 128:HALF + (tb + 1) * 128, :])
        for half in range(2):
            pb = 2 * (tb % 2) + half
            cs_ = slice(half * 512, (half + 1) * 512)
            for kc in range(16):
                lhsT = yT_sb[:, kc, ts_] if kc < 10 else oT[:, kc - 10, ts_]
                S.op("pe", "matmul", reads=[RyT[kc] if kc < 10 else RoTs[kc - 10], Rwout[kc]], writes=[RPS[pb]], out=PS[pb][:, :], lhsT=lhsT,
                     rhs=wout[:, kc, cs_], start=(kc == 0), stop=(kc == 15))
            S.op("dve", "tensor_tensor", reads=[RPS[pb], Rxo], writes=[Rx1[tb]], out=x1[:, tb, cs_], in0=PS[pb][:, :],
                 in1=xo_t[:, cs_], op=ALU.add)
    S.barrier()
    A.release(m_after_x1)

    def store_x1_and_finish():
        fin = []
        for tb in range(16):
            R = Res()
            S.dma("sp", reads=[Rx1[tb]], writes=[R], out=out_d[tb * 128:(tb + 1) * 128, :], in_=x1[:, tb, :])
            fin.append(R)
        for R in fin:
            S._wait("sp", R.w)
        S.emit()
        return nc

    if stop_after == "C":
        return store_x1_and_finish()

    wqm, _ = T([128, 8, 512], BF16, "wqm")
    Rwqm = [Res() for _ in range(8)]
    wkv, _ = T([128, 8, 1024], BF16, "wkv")
    Rwkv = [Res() for _ in range(8)]
    wo, _ = T([128, 4, 1024], BF16, "wo")
    Rwo = [Res() for _ in range(4)]
    for k in range(8):
        S.dma("pool", writes=[Rwqm[k]], out=wqm[:, k, :], in_=mem_w_q_d[k * 128:(k + 1) * 128, :])
        S.dma("pool", writes=[Rwkv[k]], out=wkv[:, k, :], in_=mem_w_kv_d[k * 128:(k + 1) * 128, :])
    for k in range(4):
        S.dma("pool", writes=[Rwo[k]], out=wo[:, k, :], in_=mem_w_o_d[k * 128:(k + 1) * 128, :])
    gx, Rgx = bc_load(1024, "xattn_norm_g")
    gm, Rgm = bc_load(1024, "mem_norm_g")
    gqm128, Rgqm128 = bc_load(128, "mem_q_norm_g")
    gkm128, Rgkm128 = bc_load(128, "mem_k_norm_g")
    gqm, Rgqm = T([128, 512], F32, "gqm")
    gkm, Rgkm = T([128, 512], F32, "gkm")
    S.op("dve", "tensor_scalar", reads=[Rgqm128], writes=[Rgqm], out=v3(gqm[:], 4),
         in0=gqm128[:].unsqueeze(1).broadcast_to([128, 4, 128]), scalar1=128.0 ** -0.5, scalar2=None, op0=ALU.mult)
    S.op("dve", "tensor_copy", reads=[Rgkm128], writes=[Rgkm], out=v3(gkm[:], 4),
         in_=gkm128[:].unsqueeze(1).broadcast_to([128, 4, 128]))
    KmT, RKmT = T([128, 4, 256], BF16, "KmT")
    Vm, RVm = T([128, 2, 4, 129], BF16, "Vm")
    S.op("pool", "memset", writes=[RVm], ap=Vm[:].rearrange("p a b c -> p (a b c)"), constant=1.0)
    memb, Rmemb = T([128, 1024], F32, "memb")
    mh, Rmh = T([128, 1024], BF16, "mh")
    mT, RmT = T([128, 8, 128], BF16, "mT")
    knm, Rknm = T([128, 512], BF16, "knm")
    for mb in range(2):
        ms = slice(mb * 128, (mb + 1) * 128)
        S.dma("sp", writes=[Rmemb], out=memb[:], in_=mem_d[ms, :])
        rmsnorm(memb[:], Rmemb, gm, Rgm, mh[:], Rmh)
        transposeN(mh, Rmh, 8, 128, mT[:], [RmT], eng="act")
        for half in range(2):
            for k in range(8):
                S.op("pe", "matmul", reads=[RmT, Rwkv[k]], writes=[RPS[half]], out=PS[half][:, :], lhsT=mT[:, k, :],
                     rhs=wkv[:, k, half * 512:(half + 1) * 512], start=(k == 0), stop=(k == 7))
        qknorm(PS[0][:, :], RPS[0], 4, 128, gkm[:], Rgkm, knm[:], Rknm)
        transposeN(knm, Rknm, 4, 128, KmT[:, :, ms], [RKmT], eng="act")
        S.op("act", "copy", reads=[RPS[1]], writes=[RVm], out=Vm[:, mb, :, 0:128], in_=v3(PS[1][:, :], 4))
    h2p = [T([128, 1024], BF16, "h2") for _ in range(2)]
    h2Tp = [T([128, 8, 128], BF16, "h2T") for _ in range(2)]
    qnp = [T([128, 512], BF16, "qn") for _ in range(2)]
    qTp = [T([128, 4, 128], BF16, "qT") for _ in range(2)]
    Pm = [T([128, 256], BF16, "Pm") for _ in range(2)]
    o2, Ro2 = T([128, 512], BF16, "o2")
    o2T, Ro2T = T([128, 4, 128], BF16, "o2T")
    rd4, _ = T([128, 4], F32, "rd4")
    Rrd4 = [Res() for _ in range(4)]

    def d_front(tb):
        p = tb % 2
        h2, Rh2 = h2p[p]
        h2T, Rh2T = h2Tp[p]
        qn, Rqn2 = qnp[p]
        qT, RqT = qTp[p]
        rmsnorm(x1[:, tb, :], Rx1[tb], gx, Rgx, h2[:], Rh2)
        transposeN(h2, Rh2, 8, 128, h2T[:], [Rh2T], eng="act")
        for k in range(8):
            S.op("pe", "matmul", reads=[Rh2T, Rwqm[k]], writes=[RPS[2]], out=PS[2][:, :], lhsT=h2T[:, k, :], rhs=wqm[:, k, :],
                 start=(k == 0), stop=(k == 7))
        qknorm(PS[2][:, :], RPS[2], 4, 128, gqm[:], Rgqm, qn[:], Rqn2)
        transposeN(qn, Rqn2, 4, 128, qT[:], [RqT], eng="act")

    def d_back(tb):
        p = tb % 2
        qT, RqT = qTp[p]
        for hh in range(4):
            sb_ = 3 + hh % 2
            ob = 5 + hh % 2
            pm, Rpm = Pm[hh % 2]
            for mb in range(2):
                S.op("pe", "matmul", reads=[RKmT, RqT], writes=[RPS[sb_]], out=PS[sb_][:, mb * 128:(mb + 1) * 128],
                     lhsT=KmT[:, hh, mb * 128:(mb + 1) * 128], rhs=qT[:, hh, :], start=True, stop=True)
            S.op("act", "activation", reads=[RPS[sb_]], writes=[Rpm], out=pm[:], in_=PS[sb_][:, 0:256], func=AF.Exp)
            for mb in range(2):
                S.op("pe", "matmul", reads=[Rpm, RVm], writes=[RPS[ob]], out=PS[ob][:, 0:129],
                     lhsT=pm[:, mb * 128:(mb + 1) * 128], rhs=Vm[:, mb, hh, :], start=(mb == 0), stop=(mb == 1))
            S.op("dve", "reciprocal", reads=[RPS[ob]], writes=[Rrd4[hh]], out=rd4[:, hh:hh + 1], in_=PS[ob][:, 128:129])
            S.op("dve", "tensor_scalar", reads=[RPS[ob], Rrd4[hh]], writes=[Ro2], out=o2[:, hh * 128:(hh + 1) * 128],
                 in0=PS[ob][:, 0:128], scalar1=rd4[:, hh:hh + 1], scalar2=None, op0=ALU.mult)
        transposeN(o2, Ro2, 4, 128, o2T[:], [Ro2T], eng="act")
        for half in range(2):
            cs_ = slice(half * 512, (half + 1) * 512)
            for kc in range(4):
                S.op("pe", "matmul", reads=[Ro2T, Rwo[kc]], writes=[RPS[half]], out=PS[half][:, :], lhsT=o2T[:, kc, :],
                     rhs=wo[:, kc, cs_], start=(kc == 0), stop=(kc == 3))
            S.op("dve", "tensor_tensor", reads=[RPS[half], Rx1[tb]], writes=[Rx1[tb]], out=x1[:, tb, cs_],
                 in0=PS[half][:, :], in1=x1[:, tb, cs_], op=ALU.add)

    d_front(0)
    for tb in range(16):
        if tb + 1 < 16:
            d_front(tb + 1)
        d_back(tb)
    if stop_after == "D":
        S.barrier()
        return store_x1_and_finish()
    Rx2d = [Res() for _ in range(16)]
    for tb in range(16):
        S.dma("sp", reads=[Rx1[tb]], writes=[Rx2d[tb]], out=out_d[tb * 128:(tb + 1) * 128, :], in_=x1[:, tb, :])
    S.barrier()
    A.release(m_base)

    wpq, _ = T([128, 8, 2048], BF16, "wpq")
    Rwpq = [Res() for _ in range(8)]
    for k in range(8):
        S.dma("pool", writes=[Rwpq[k]], out=wpq[:, k, :], in_=peer_wq_d[k * 128:(k + 1) * 128, :])
    keysT, RkeysT = T([128, 2, 128], F32, "keysT")
    S.dma("sp", writes=[RkeysT], out=keysT[:, 0, :], in_=keys1T_d)
    S.dma("sp", writes=[RkeysT], out=keysT[:, 1, :], in_=keys2T_d)
    gf, Rgf = bc_load(1024, "ffn_norm_g")
    xrp = [T([128, 1024], F32, "xr") for _ in range(2)]
    h3bp = [T([128, 1024], BF16, "h3b") for _ in range(2)]
    h3T, Rh3T = T([128, 8, 128], BF16, "h3T")
    off_q = A.mark()
    qrT, RqrT = T([128, 16, 128], F32, "qrT")
    eq = nc.alloc_sbuf_tensor_at("eq_alias", [128, 8, 16, 16], F32, offset=off_q)
    Req = RqrT
    off_s = A.mark()
    sc, Rsc = T([128, 16, 128], F32, "sc")
    cand = nc.alloc_sbuf_tensor_at("cand_alias", [128, 8, 256], F32, offset=off_s)
    Rcand = Rsc
    sc2, Rsc2 = T([128, 128], F32, "sc2")
    tv, Rtv = T([128, 16, 16], F32, "tv")
    ti, Rti = T([128, 16, 16], U32, "ti")
    tif, Rtif = T([128, 16, 16], F32, "tif")
    cand2, Rcand2 = T([128, 256], F32, "cand2")
    cv, Rcv = T([128, 8, 16], F32, "cv")
    cp, Rcp = T([128, 8, 16], U32, "cp")
    ca, Rca = T([128, 8, 16], U32, "ca")
    cb_, Rcb_ = T([128, 8, 16], U32, "cb")
    caf, Rcaf = T([128, 8, 16], F32, "caf")
    cbf, Rcbf = T([128, 8, 16], F32, "cbf")
    i1f, Ri1f = T([128, 8, 16], F32, "i1f")
    i2f, Ri2f = T([128, 8, 16], F32, "i2f")
    ef, Ref = T([128, 128], F32, "ef")
    eip = [T([128, 128], I32, "ei") for _ in range(2)]
    gtp = [T([128, 8, 16], F32, "gt") for _ in range(2)]
    gsum, Rgsum = T([128, 8], F32, "gsum")
    actv, _ = T([128, 128], F32, "actv")
    Ractv = [Res() for _ in range(128)]
    WG = [T([128, 4], F32, "wg4") for _ in range(2)]
    wgt, _ = T([128, 128], F32, "wgt")
    Rwgt = [Res() for _ in range(32)]
    NUV = 6
    UVt = [(T([128, 4, 2048], BF16, "UV")[0], [Res() for _ in range(4)]) for _ in range(NUV)]
    PR = [T([128, 1024], BF16, "PR") for _ in range(8)]
    DG = [T([128, 128], BF16, "DG") for _ in range(4)]
    fin = []

    def peer_setup(tb, p):
        xr, Rxr = xrp[p]
        h3b, Rh3b = h3bp[p]
        ei, Rei = eip[p]
        gt, Rgt = gtp[p]
        S.dma("sp", reads=[Rx2d[tb]], writes=[Rxr], out=xr[:], in_=out_d[tb * 128:(tb + 1) * 128, :])
        rmsnorm(xr[:], Rxr, gf, Rgf, h3b[:], Rh3b)
        transposeN(h3b, Rh3b, 8, 128, h3T[:], [Rh3T], eng="act")
        yield
        for q4 in range(4):
            pb = q4 % 2
            for i in range(4):
                ccq = q4 * 4 + i
                for k in range(8):
                    S.op("pe", "matmul", reads=[Rwpq[k], Rh3T], writes=[RPS[pb]], out=PS[pb][:, i * 128:(i + 1) * 128],
                         lhsT=wpq[:, k, ccq * 128:(ccq + 1) * 128], rhs=h3T[:, k, :], start=(k == 0), stop=(k == 7))
            S.op("act", "copy", reads=[RPS[pb]], writes=[RqrT], out=qrT[:, q4 * 4:q4 * 4 + 4, :], in_=v3(PS[pb][:, :], 4))
            yield
        for q4 in range(4):
            pb = 2 + q4 % 2
            for i in range(4):
                j = q4 * 4 + i
                S.op("pe", "matmul", reads=[RqrT, RkeysT], writes=[RPS[pb]], out=PS[pb][:, i * 128:(i + 1) * 128],
                     lhsT=qrT[:, j, :], rhs=keysT[:, j % 2, :], start=True, stop=True)
            S.op("act", "copy", reads=[RPS[pb]], writes=[Rsc], out=sc[:, q4 * 4:q4 * 4 + 4, :], in_=v3(PS[pb][:, :], 4))
        for j in range(16):
            S.op("dve", "max", reads=[Rsc], writes=[Rtv], out=tv[:, j, 0:8], in_=sc[:, j, :])
            S.op("dve", "max_index", reads=[Rsc, Rtv], writes=[Rti], out=ti[:, j, 0:8], in_max=tv[:, j, 0:8],
                 in_values=sc[:, j, :])
            S.op("dve", "match_replace", reads=[Rsc, Rtv], writes=[Rsc2], out=sc2[:], in_to_replace=tv[:, j, 0:8],
                 in_values=sc[:, j, :], imm_value=-1e30)
            S.op("dve", "max", reads=[Rsc2], writes=[Rtv], out=tv[:, j, 8:16], in_=sc2[:])
            S.op("dve", "max_index", reads=[Rsc2, Rtv], writes=[Rti], out=ti[:, j, 8:16], in_max=tv[:, j, 8:16],
                 in_values=sc2[:])
            yield
        S.op("dve", "tensor_copy", reads=[Rti], writes=[Rtif], out=tif[:], in_=ti[:])
        tv4 = tv[:].rearrange("p (h two) k -> p h two k", two=2)
        tif4 = tif[:].rearrange("p (h two) k -> p h two k", two=2)
        S.op("dve", "tensor_tensor", reads=[Rtv], writes=[Rcand], out=cand[:].rearrange("p h (a b) -> p h a b", a=16),
             in0=tv4[:, :, 0, :].unsqueeze(3).broadcast_to([128, 8, 16, 16]),
             in1=tv4[:, :, 1, :].unsqueeze(2).broadcast_to([128, 8, 16, 16]), op=ALU.add)
        for h in range(8):
            S.op("dve", "max", reads=[Rcand], writes=[Rcv], out=cv[:, h, 0:8], in_=cand[:, h, :])
            S.op("dve", "max_index", reads=[Rcand, Rcv], writes=[Rcp], out=cp[:, h, 0:8], in_max=cv[:, h, 0:8],
                 in_values=cand[:, h, :])
            S.op("dve", "match_replace", reads=[Rcand, Rcv], writes=[Rcand2], out=cand2[:], in_to_replace=cv[:, h, 0:8],
                 in_values=cand[:, h, :], imm_value=-1e30)
            S.op("dve", "max", reads=[Rcand2], writes=[Rcv], out=cv[:, h, 8:16], in_=cand2[:])
            S.op("dve", "max_index", reads=[Rcand2, Rcv], writes=[Rcp], out=cp[:, h, 8:16], in_max=cv[:, h, 8:16],
                 in_values=cand2[:])
            yield
        S.op("dve", "tensor_single_scalar", reads=[Rcp], writes=[Rca], out=ca[:], in_=cp[:], scalar=4,
             op=ALU.logical_shift_right)
        S.op("dve", "tensor_single_scalar", reads=[Rcp], writes=[Rcb_], out=cb_[:], in_=cp[:], scalar=15,
             op=ALU.bitwise_and)
        S.op("dve", "tensor_copy", reads=[Rca], writes=[Rcaf], out=caf[:], in_=ca[:])
        S.op("dve", "tensor_copy", reads=[Rcb_], writes=[Rcbf], out=cbf[:], in_=cb_[:])
        io4 = iota16.unsqueeze(1).unsqueeze(1).broadcast_to([128, 8, 16, 16])
        for (sel, Rsel, half, dst, Rdst) in ((caf, Rcaf, 0, i1f, Ri1f), (cbf, Rcbf, 1, i2f, Ri2f)):
            S.op("dve", "tensor_tensor", reads=[Rsel, Rcst], writes=[Req], out=eq[:],
                 in0=sel[:].unsqueeze(3).broadcast_to([128, 8, 16, 16]), in1=io4, op=ALU.is_equal)
            S.op("dve", "tensor_tensor", reads=[Req, Rtif], writes=[Req], out=eq[:], in0=eq[:],
                 in1=tif4[:, :, half, :].unsqueeze(2).broadcast_to([128, 8, 16, 16]), op=ALU.mult)
            S.op("dve", "tensor_reduce", reads=[Req], writes=[Rdst], out=dst[:].rearrange("p h k -> p (h k)"),
                 in_=eq[:].rearrange("p h k a -> p (h k) a"), axis=AX.X, op=ALU.add)
        S.op("dve", "scalar_tensor_tensor", reads=[Ri1f, Ri2f], writes=[Ref], out=ef[:],
             in0=i1f[:].rearrange("p h k -> p (h k)"), scalar=128.0, in1=i2f[:].rearrange("p h k -> p (h k)"),
             op0=ALU.mult, op1=ALU.add)
        S.op("dve", "tensor_copy", reads=[Ref], writes=[Rei], out=ei[:], in_=ef[:])
        S.op("dve", "tensor_tensor", reads=[Rcv], writes=[Rgt], out=gt[:], in0=cv[:],
             in1=cv[:, :, 0:1].broadcast_to([128, 8, 16]), op=ALU.subtract)
        S.op("act", "activation", reads=[Rgt], writes=[Rgt], out=gt[:], in_=gt[:], func=AF.Exp)
        S.op("dve", "tensor_reduce", reads=[Rgt], writes=[Rgsum], out=gsum[:], in_=gt[:], axis=AX.X, op=ALU.add)
        S.op("dve", "reciprocal", reads=[Rgsum], writes=[Rgsum], out=gsum[:], in_=gsum[:])
        S.op("dve", "tensor_tensor", reads=[Rgt, Rgsum], writes=[Rgt], out=gt[:], in0=gt[:],
             in1=gsum[:].unsqueeze(2).broadcast_to([128, 8, 16]), op=ALU.mult)

    def peer_gather(tb, p, bg):
        xr, Rxr = xrp[p]
        h3b, Rh3b = h3bp[p]
        ei, Rei = eip[p]
        gt, Rgt = gtp[p]
        gtf = gt[:].rearrange("p h k -> p (h k)")
        S.op("pool", "memset", writes=Ractv, ap=actv[:], constant=0.0)

        def stA(g4):
            uvt, Ruvt = UVt[g4 % NUV]
            for i in range(4):
                s_ = g4 * 4 + i
                S.dma("pool", reads=[Rei, Ruv], writes=[Ruvt[i]], meth="indirect_dma_start", out=uvt[:, i, :],
                      out_offset=None, in_=uv_scr, in_offset=bass.IndirectOffsetOnAxis(ap=ei[:, s_:s_ + 1], axis=0))

        def stBC(g4):
            uvt, Ruvt = UVt[g4 % NUV]
            for i in range(4):
                s_ = g4 * 4 + i
                pr, Rpr = PR[s_ % 8]
                S.op("dve", "tensor_tensor", reads=[Ruvt[i], Rh3b], writes=[Rpr], out=pr[:], in0=uvt[:, i, 0:1024],
                     in1=h3b[:], op=ALU.mult)
                S.op("act", "activation", reads=[Rpr], writes=[Ractv[s_]], out=pr[:], in_=pr[:], func=AF.Copy,
                     accum_out=actv[:, s_:s_ + 1])
            wg, Rwg = WG[g4 % 2]
            S.op("act", "activation", reads=Ractv[g4 * 4:g4 * 4 + 4], writes=[Rwg], out=wg[:], in_=actv[:, g4 * 4:g4 * 4 + 4],
                 func=AF.Gelu)

        def stDE(g4):
            uvt, Ruvt = UVt[g4 % NUV]
            wg, Rwg = WG[g4 % 2]
            S.op("dve", "tensor_tensor", reads=[Rwg, Rgt], writes=[Rwgt[g4]], out=wgt[:, g4 * 4:g4 * 4 + 4], in0=wg[:],
                 in1=gtf[:, g4 * 4:g4 * 4 + 4], op=ALU.mult)
            for i in range(4):
                s_ = g4 * 4 + i
                dg, Rdg = DG[s_ % 4]
                S.op("dve", "tensor_scalar", reads=[Ridb, Rwgt[g4]], writes=[Rdg], out=dg[:], in0=ident_b[:],
                     scalar1=wgt[:, s_:s_ + 1], scalar2=None, op0=ALU.mult)
                for half in range(2):
                    S.op("pe", "matmul", reads=[Rdg, Ruvt[i]], writes=[RPS[5 + half]], out=PS[5 + half][:, :],
                         lhsT=dg[:], rhs=uvt[:, i, 1024 + half * 512:1024 + (half + 1) * 512], start=(s_ == 0),
                         stop=(s_ == 127))

        for g4 in range(NUV - 2):
            stA(g4)
        for g4 in range(33):
            if g4 + NUV - 2 < 32:
                stA(g4 + NUV - 2)
            if g4 < 32:
                stBC(g4)
            if g4 >= 1:
                stDE(g4 - 1)
            if bg is not None:
                next(bg, None)
                next(bg, None)
        if bg is not None:
            for _ in bg:
                pass
        for half in range(2):
            cs_ = slice(half * 512, (half + 1) * 512)
            S.op("dve", "tensor_tensor", reads=[RPS[5 + half], Rxr], writes=[Rxr], out=xr[:, cs_],
                 in0=xr[:, cs_], in1=PS[5 + half][:, :], op=ALU.add)
        R = Res()
        S.dma("sp", reads=[Rxr], writes=[R, Rx2d[tb]], out=out_d[tb * 128:(tb + 1) * 128, :], in_=xr[:])
        fin.append(R)

    for _ in peer_setup(0, 0):
        pass
    for tb in range(16):
        bg = peer_setup(tb + 1, (tb + 1) % 2) if tb + 1 < 16 else None
        peer_gather(tb, tb % 2, bg)
    for R in fin:
        S._wait("sp", R.w)
    S.emit()
    return nc


def _consts():
    c = np.zeros((128, NCONST), np.float32)
    p = np.arange(128)
    c[:, 0:128] = np.eye(128)
    c[:, 128:256] = (p[:, None] <= p[None, :])
    c[:, 256:384] = 1.0
    c[:, 384:512] = (p[None, :] - p[:, None])
    c[:, 512:528] = np.arange(16)[None, :]
    k = p[:, None, None]
    j = np.arange(17)[None, :, None]
    q = p[None, None, :]
    delta = (16 - j) * 128 + q - k
    m = ((delta >= 0) & (delta <= 128)).astype(np.float32)
    m += ((delta >= 0) & (delta <= 512) & (delta % 4 == 0))
    m += ((delta >= 0) & (delta <= 2048) & (delta % 16 == 0))
    return c, np.ascontiguousarray(m.reshape(128, 17 * 128).astype(np.float32))


_NC_CACHE = {}


def kernel(**inputs):
    stop_after = os.environ.get("KSTOP") or None
    x = np.asarray(inputs["x"], np.float32)
    mem = np.asarray(inputs["mem"], np.float32)
    cst, mtab = _consts()
    common = {
        "cst": cst, "mtab": mtab,
        "w_in": np.ascontiguousarray(inputs["w_in"][0], dtype=np.float32),
        "convw": np.ascontiguousarray(
            np.asarray(inputs["conv_w"][0], np.float32).T.reshape(18, 128, 4).transpose(1, 0, 2).reshape(128, 72)),
        "convb": np.ascontiguousarray(np.asarray(inputs["conv_b"][0], np.float32).reshape(18, 128).T),
        "w_out": np.ascontiguousarray(inputs["w_out"][0], dtype=np.float32),
    }
    for nm in ("mix_norm_g", "dt_bias", "a_log", "d_skip", "ssd_norm_g", "attn_q_norm_g", "attn_k_norm_g",
               "xattn_norm_g", "mem_norm_g", "mem_q_norm_g", "mem_k_norm_g", "ffn_norm_g"):
        common[nm] = np.ascontiguousarray(np.asarray(inputs[nm][0], np.float32).reshape(1, -1))
    if True:
        common["mem_w_q"] = np.ascontiguousarray(inputs["mem_w_q"][0], dtype=np.float32)
        common["mem_w_kv"] = np.ascontiguousarray(inputs["mem_w_kv"][0], dtype=np.float32)
        common["mem_w_o"] = np.ascontiguousarray(inputs["mem_w_o"][0], dtype=np.float32)
    if True:
        common["peer_w_query"] = np.ascontiguousarray(inputs["peer_w_query"][0], dtype=np.float32)
        common["keys1T"] = np.ascontiguousarray(np.asarray(inputs["peer_sub_keys1"][0], np.float32).T)
        common["keys2T"] = np.ascontiguousarray(np.asarray(inputs["peer_sub_keys2"][0], np.float32).T)
        common["peer_u"] = np.ascontiguousarray(inputs["peer_u"][0], dtype=np.float32)
        common["peer_v"] = np.ascontiguousarray(inputs["peer_v"][0], dtype=np.float32)
    in_maps = []
    for c in range(8):
        b, j = c // 2, c % 2
        xw = np.zeros((WIN, D), np.float32)
        if j == 1:
            xw[:HALF] = x[b, :HALF]
        xw[HALF:] = x[b, j * HALF:(j + 1) * HALF]
        m = dict(common)
        m["xw"] = xw
        m["flag"] = np.full((128, 1), float(j), np.float32)
        if True:
            m["mem"] = np.ascontiguousarray(mem[b])
        in_maps.append(m)
    nc = build(stop_after)
    res = run_bass_kernel_spmd(nc, in_maps, core_ids=list(range(8)))
    out = np.zeros((NB, SEQ, D), np.float32)
    for c in range(8):
        b, j = c // 2, c % 2
        out[b, j * HALF:(j + 1) * HALF] = np.asarray(res.results[c]["out"])
    return out
```

```python
import os
import math
import numpy as np
import concourse.bass as bass
import concourse.mybir as mybir
from concourse.bass_utils import run_bass_kernel_spmd

F32 = mybir.dt.float32
BF16 = mybir.dt.bfloat16
I32 = mybir.dt.int32
U32 = mybir.dt.uint32
AF = mybir.ActivationFunctionType
ALU = mybir.AluOpType
AX = mybir.AxisListType

D = 1024
NB = 4
SEQ = 4096
HALF = 2048
WIN = 4096
EPS = 1e-6
OFF_Z = 1280
OFF_XBC = OFF_Z + 2304
OFF_DT = OFF_XBC + 20
OFF_Q = OFF_DT + 768
OFF_K = OFF_Q + 768
IN_COLS = OFF_K + 768
NCONST = 128 * 4 + 16


class Res:
    __slots__ = ("w", "r")

    def __init__(self):
        self.w = None
        self.r = {}


class Sched:
    COMPUTE = ("pe", "act", "dve", "pool")
    NDMA = 12
    INORDER = ("pe",)

    def __init__(self, nc):
        self.nc = nc
        self.streams = {k: [] for k in ("pe", "act", "dve", "pool", "sp")}
        self.esem = {k: nc.alloc_semaphore("es_" + k) for k in self.COMPUTE}
        self.ecnt = {k: 0 for k in self.COMPUTE}
        self.dsem = {q: [nc.alloc_semaphore("ds_%s%d" % (q, i)) for i in range(self.NDMA)]
                     for q in ("sp", "pool")}
        self.dcnt = {q: [0] * self.NDMA for q in ("sp", "pool")}
        self.dnext = {q: 0 for q in ("sp", "pool")}
        self.known = {k: {} for k in self.streams}
        self.nwaits = 0
        self.nops = 0

    def _sem_of(self, pid):
        if pid[0] == "e":
            return self.esem[pid[1]], pid[2], ("e", pid[1])
        return self.dsem[pid[1]][pid[2]], pid[3], ("d", pid[1], pid[2])

    def _wait(self, eng, pid):
        sem, val, key = self._sem_of(pid)
        if self.known[eng].get(key, 0) >= val:
            return
        self.known[eng][key] = val
        self.nwaits += 1
        self.streams[eng].append(lambda e, sem=sem, val=val: e.wait_ge(sem, val))

    def _deps(self, eng, reads, writes):
        deps = []
        for r in reads:
            if r.w is not None:
                deps.append(r.w)
        for w in writes:
            if w.w is not None:
                deps.append(w.w)
            deps.extend(w.r.values())
        for pid in deps:
            if pid[0] == "e" and pid[1] == eng and eng in self.INORDER:
                continue
            self._wait(eng, pid)

    def _commit(self, pid, reads, writes):
        key = pid[:2] if pid[0] == "e" else pid[:3]
        for r in reads:
            r.r[key] = pid
        for w in writes:
            w.w = pid
            w.r = {}

    def op(self, eng, meth, reads=(), writes=(), **kw):
        fn = lambda e, meth=meth, kw=kw: getattr(e, meth)(**kw)
        self._deps(eng, reads, writes)
        self.ecnt[eng] += 1
        idx = self.ecnt[eng]
        sem = self.esem[eng]
        self.streams[eng].append(lambda e, fn=fn, sem=sem: fn(e).then_inc(sem, 1))
        self.nops += 1
        self._commit(("e", eng, idx), reads, writes)

    def dma(self, q, reads=(), writes=(), meth="dma_start", **kw):
        fn = lambda e, meth=meth, kw=kw: getattr(e, meth)(**kw)
        i = self.dnext[q]
        self.dnext[q] = (i + 1) % self.NDMA
        prev = self.dcnt[q][i]
        if prev > 0:
            self._wait(q, ("d", q, i, prev))
        self._deps(q, reads, writes)
        self.dcnt[q][i] = prev + 16
        sem = self.dsem[q][i]
        self.streams[q].append(lambda e, fn=fn, sem=sem: fn(e).then_inc(sem, 16))
        self.nops += 1
        self._commit(("d", q, i, prev + 16), reads, writes)

    def barrier(self):
        for eng in self.streams:
            for k in self.COMPUTE:
                if k != eng and self.ecnt[k] > 0:
                    self._wait(eng, ("e", k, self.ecnt[k]))
            for q in self.dsem:
                for i in range(self.NDMA):
                    if self.dcnt[q][i] > 0:
                        self._wait(eng, ("d", q, i, self.dcnt[q][i]))

    def emit(self):
        nc = self.nc
        with nc.Block() as block:
            @block.tensor
            def _(e):
                for it in self.streams["pe"]:
                    it(e)

            @block.scalar
            def _(e):
                for it in self.streams["act"]:
                    it(e)

            @block.vector
            def _(e):
                for it in self.streams["dve"]:
                    it(e)

            @block.gpsimd
            def _(e):
                for it in self.streams["pool"]:
                    it(e)

            @block.sync
            def _(e):
                for it in self.streams["sp"]:
                    it(e)


class Arena:
    def __init__(self, nc):
        self.nc = nc
        self.off = (nc.sbuf_base + 31) // 32 * 32
        self.top = nc.sbuf_top
        self.n = 0

    def alloc(self, shape, dt=F32, name="t"):
        sz = {F32: 4, BF16: 2, I32: 4, U32: 4}[dt]
        nb = int(np.prod(shape[1:])) * sz
        off = self.off
        self.off += (nb + 31) // 32 * 32
        assert self.off <= self.top, "SBUF overflow at %s: %d > %d" % (name, self.off, self.top)
        self.n += 1
        return self.nc.alloc_sbuf_tensor_at("%s_%d" % (name, self.n), list(shape), dt, offset=off)

    def mark(self):
        return self.off

    def release(self, m):
        self.off = m


def build(stop_after=None):
    nc = bass.Bass("TRN2", target_bir_lowering=False)
    S = Sched(nc)
    A = Arena(nc)

    def din(name, shape, dt=F32):
        return nc.dram_tensor(name, list(shape), dt, kind="ExternalInput").ap()

    xw_d = din("xw", [WIN, D])
    flag_d = din("flag", [128, 1])
    mem_d = din("mem", [256, D])
    cst_d = din("cst", [128, NCONST])
    mtab_d = din("mtab", [128, 17 * 128])
    w_in_d = din("w_in", [D, IN_COLS])
    convw_d = din("convw", [128, 72])
    convb_d = din("convb", [128, 18])
    vec_d = {}
    for nm, n in [("mix_norm_g", 1024), ("dt_bias", 20), ("a_log", 20), ("d_skip", 20), ("ssd_norm_g", 1280),
                  ("attn_q_norm_g", 64), ("attn_k_norm_g", 64), ("xattn_norm_g", 1024), ("mem_norm_g", 1024),
                  ("mem_q_norm_g", 128), ("mem_k_norm_g", 128), ("ffn_norm_g", 1024)]:
        vec_d[nm] = din(nm, [1, n])
    w_out_d = din("w_out", [2048, D])
    mem_w_q_d = din("mem_w_q", [D, 512])
    mem_w_kv_d = din("mem_w_kv", [D, 1024])
    mem_w_o_d = din("mem_w_o", [512, D])
    peer_wq_d = din("peer_w_query", [D, 2048])
    keys1T_d = din("keys1T", [128, 128])
    keys2T_d = din("keys2T", [128, 128])
    peer_u_d = din("peer_u", [16384, D])
    peer_v_d = din("peer_v", [16384, D])
    out_d = nc.dram_tensor("out", [HALF, D], F32, kind="ExternalOutput").ap()
    yT_scr = nc.dram_tensor("yT_scr", [128, 10, HALF], BF16).ap()
    oT_scr = nc.dram_tensor("oT_scr", [128, 6, HALF], BF16).ap()
    uv_scr = nc.dram_tensor("uv_scr", [16384, 2048], BF16).ap()
    Ruv = Res()
    conv_jobs = [(tbl, h) for h in range(32) for tbl in (0, 1)]

    def issue_conv(n):
        for _ in range(n):
            if not conv_jobs or stop_after is not None:
                return
            tbl, h = conv_jobs.pop(0)
            src = peer_u_d if tbl == 0 else peer_v_d
            S.dma("pool", writes=[Ruv], out=uv_scr[h * 512:(h + 1) * 512, tbl * 1024:(tbl + 1) * 1024],
                  in_=src[h * 512:(h + 1) * 512, :])

    PS = [nc.alloc_psum_tensor("ps%d" % i, [128, 512], F32) for i in range(7)]
    RPS = [Res() for _ in range(7)]
    PB = nc.alloc_psum_tensor("psb", [128, 1024], BF16)
    RPB = Res()

    def T(shape, dt=F32, name="t"):
        return A.alloc(shape, dt, name), Res()

    def v3(ap, a):
        return ap.rearrange("p (a b) -> p a b", a=a)

    cst, Rcst = T([128, NCONST], F32, "cst")
    S.dma("sp", writes=[Rcst], out=cst[:], in_=cst_d)
    ident_f = cst[:, 0:128]
    tri_f = cst[:, 128:256]
    ones_f = cst[:, 256:384]
    d0_f = cst[:, 384:512]
    iota16 = cst[:, 512:528]
    ident_b, Ridb = T([128, 128], BF16, "identb")
    S.op("dve", "tensor_copy", reads=[Rcst], writes=[Ridb], out=ident_b[:], in_=ident_f)
    flag, Rflag = T([128, 1], F32, "flag")
    S.dma("sp", writes=[Rflag], out=flag[:], in_=flag_d)
    epsT, Reps = T([128, 1], F32, "eps")
    S.op("pool", "memset", writes=[Reps], ap=epsT[:], constant=EPS)

    def bc_load(n, name):
        t, R = T([128, n], F32, name)
        S.dma("sp", writes=[R], out=t[:], in_=vec_d[name].broadcast_to([128, n]))
        return t, R

    def rstd_from(ss_ap, Rss, n, H, out_ap, Rout, lnv_ap, Rln):
        Rss_l = Rss if isinstance(Rss, list) else [Rss]
        S.op("act", "activation", reads=Rss_l + [Reps], writes=[Rln], out=lnv_ap, in_=ss_ap, func=AF.Ln,
             bias=epsT[:, 0:1], scale=1.0 / n)
        S.op("act", "activation", reads=[Rln], writes=[Rout], out=out_ap, in_=lnv_ap, func=AF.Exp, scale=-0.5)

    nrm_junk, Rnj = T([128, 1024], BF16, "nrmjunk")
    nrm_ss, Rnss = T([128, 4], F32, "nrmss")
    nrm_ln, Rnln = T([128, 4], F32, "nrmln")
    nrm_rs, Rnrs = T([128, 4], F32, "nrmrs")

    def rmsnorm(x_ap, Rx, g_bc, Rg, out_ap, Rout):
        S.op("act", "activation", reads=[Rx], writes=[Rnj, Rnss], out=nrm_junk[:], in_=x_ap, func=AF.Square,
             accum_out=nrm_ss[:, 0:1])
        rstd_from(nrm_ss[:, 0:1], Rnss, 1024, 1, nrm_rs[:, 0:1], Rnrs, nrm_ln[:, 0:1], Rnln)
        S.op("dve", "scalar_tensor_tensor", reads=[Rx, Rnrs, Rg], writes=[Rout], out=out_ap, in0=x_ap,
             scalar=nrm_rs[:, 0:1], in1=g_bc[:], op0=ALU.mult, op1=ALU.mult)

    def transposeN(src_bf, Rsrc, n, width, dst3, Rdst, eng="act"):
        for k0 in range(0, n, 8):
            m = min(8, n - k0)
            for k in range(m):
                S.op("pe", "transpose", reads=[Rsrc, Ridb], writes=[RPB], out=PB[0:width, k * 128:(k + 1) * 128],
                     in_=src_bf[:, (k0 + k) * width:(k0 + k + 1) * width], identity=ident_b[:])
            src = v3(PB[0:width, 0:m * 128], m)
            dst = dst3[:, k0:k0 + m, :]
            if eng == "act":
                S.op("act", "copy", reads=[RPB], writes=Rdst, out=dst, in_=src)
            else:
                S.op("dve", "tensor_copy", reads=[RPB], writes=Rdst, out=dst, in_=src)

    qk_sq, Rqsq = T([128, 512], F32, "qksq")
    qk_tmp, Rqtmp = T([128, 512], F32, "qktmp")

    def qknorm(ps_ap, Rps, H, dh, g_bc, Rg, out_bf, Rout):
        n = H * dh
        S.op("act", "activation", reads=[Rps], writes=[Rqsq], out=qk_sq[:, 0:n], in_=ps_ap, func=AF.Square)
        S.op("dve", "tensor_reduce", reads=[Rqsq], writes=[Rnss], out=nrm_ss[:, 0:H], in_=v3(qk_sq[:, 0:n], H),
             axis=AX.X, op=ALU.add)
        rstd_from(nrm_ss[:, 0:H], Rnss, dh, H, nrm_rs[:, 0:H], Rnrs, nrm_ln[:, 0:H], Rnln)
        S.op("dve", "tensor_tensor", reads=[Rps, Rnrs], writes=[Rqtmp], out=v3(qk_tmp[:, 0:n], H), in0=v3(ps_ap, H),
             in1=nrm_rs[:, 0:H].unsqueeze(2).broadcast_to([128, H, dh]), op=ALU.mult)
        S.op("dve", "tensor_tensor", reads=[Rqtmp, Rg], writes=[Rout], out=out_bf, in0=qk_tmp[:, 0:n], in1=g_bc,
             op=ALU.mult)

    m_base = A.mark()
    hT, _ = T([128, 8, WIN], BF16, "hT")
    RhT = [Res() for _ in range(32)]
    m_after_hT = A.mark()

    gmix, Rgmix = bc_load(1024, "mix_norm_g")
    xb = [T([128, 1024], F32, "xb") for _ in range(2)]
    hb = [T([128, 1024], BF16, "hb") for _ in range(2)]
    for blk in range(32):
        x_t, Rx = xb[blk % 2]
        h_t, Rh = hb[blk % 2]
        S.dma("sp", writes=[Rx], out=x_t[:], in_=xw_d[blk * 128:(blk + 1) * 128, :])
        rmsnorm(x_t[:], Rx, gmix, Rgmix, h_t[:], Rh)
        transposeN(h_t, Rh, 8, 128, hT[:, :, blk * 128:(blk + 1) * 128], [RhT[blk]], eng="act" if blk % 2 else "dve")
    S.barrier()
    A.release(m_after_hT)

    wxbc, _ = T([128, 8, 2304], BF16, "wxbc")
    Rwxbc = [Res() for _ in range(8)]
    wz, _ = T([128, 8, 1280], BF16, "wz")
    Rwz = [Res() for _ in range(8)]
    wdt, _ = T([128, 8, 20], BF16, "wdt")
    Rwdt = [Res() for _ in range(8)]
    for k in range(8):
        S.dma("pool", writes=[Rwxbc[k]], out=wxbc[:, k, :], in_=w_in_d[k * 128:(k + 1) * 128, OFF_Z:OFF_XBC])
        S.dma("pool", writes=[Rwz[k]], out=wz[:, k, :], in_=w_in_d[k * 128:(k + 1) * 128, 0:OFF_Z])
        S.dma("pool", writes=[Rwdt[k]], out=wdt[:, k, :], in_=w_in_d[k * 128:(k + 1) * 128, OFF_XBC:OFF_DT])
    convw, Rcw = T([128, 18, 4], F32, "convw")
    convb, Rcb = T([128, 18], F32, "convb")
    S.dma("sp", writes=[Rcw], out=convw[:].rearrange("p a b -> p (a b)"), in_=convw_d)
    S.dma("sp", writes=[Rcb], out=convb[:], in_=convb_d)
    dtb, Rdtb = bc_load(20, "dt_bias")
    alog, Ralog = bc_load(20, "a_log")
    dsk, Rdsk = bc_load(20, "d_skip")
    gssd, Rgssd = bc_load(1280, "ssd_norm_g")
    a_bc, Ra = T([128, 20], F32, "a_bc")
    S.op("act", "activation", reads=[Ralog], writes=[Ra], out=a_bc[:], in_=alog[:], func=AF.Exp)
    S.op("dve", "tensor_scalar", reads=[Ra], writes=[Ra], out=a_bc[:], in0=a_bc[:], scalar1=-1.0, scalar2=None,
         op0=ALU.mult)
    halo, _ = T([128, 18, 3], BF16, "halo")
    Rhalo = [Res() for _ in range(18)]
    S.op("pool", "memset", writes=Rhalo, ap=halo[:], constant=0.0)
    state, Rst = T([128, 1280], F32, "state")
    state_bf, Rstb = T([128, 1280], BF16, "statebf")
    S.op("pool", "memset", writes=[Rst], ap=state[:], constant=0.0)
    S.op("pool", "memset", writes=[Rstb], ap=state_bf[:], constant=0.0)
    U = [T([128, 259], BF16, "U") for _ in range(2)]
    DGc = [(T([128, 4, 128], BF16, "DGc")[0], [Res() for _ in range(4)]) for _ in range(2)]
    xTf = [T([128, 256], F32, "xTf") for _ in range(2)]
    x_tm, Rxtm = T([128, 2, 1280], F32, "x_tm")
    BT, _ = T([128, 4, 256], BF16, "BT")
    RBTg = [Res() for _ in range(4)]
    CTt, _ = T([128, 4, 256], BF16, "CT")
    RCTg = [Res() for _ in range(4)]
    B_tm, RBtm = T([128, 2, 512], BF16, "B_tm")
    dt_tm, Rdt = T([128, 2, 20], F32, "dt_tm")
    adt, Radt = T([128, 2, 20], F32, "adt")
    ncs, Rncs = T([128, 20], F32, "ncs")
    tmp20, Rt20 = T([128, 20], F32, "tmp20")
    dec, Rdec = T([128, 20], F32, "dec")
    cd, Rcd = T([128, 20], F32, "cd")
    w2, Rw2 = T([128, 20], F32, "w2")
    ecs, Recs = T([128, 20], F32, "ecs")
    xdd, Rxdd = T([128, 1280], BF16, "xdd")
    xdt, Rxdt = T([128, 1280], BF16, "xdt")
    CBm, RCBm = T([128, 4, 128], F32, "CBm")
    adtb2 = [T([128, 4, 128], F32, "adtb") for _ in range(2)]
    Lt = [T([128, 4, 128], F32, "Lt") for _ in range(2)]
    Wm, RWm = T([128, 20, 128], BF16, "Wm")
    yg4, _ = T([128, 4, 320], F32, "yg4")
    Ryg4 = [Res() for _ in range(4)]
    sz4, _ = T([128, 4, 320], F32, "sz4")
    Rsz4 = [Res() for _ in range(4)]
    sqj, _ = T([128, 320], BF16, "sqj")
    ss4, _ = T([128, 4], F32, "ss4")
    Rss4 = [Res() for _ in range(4)]
    ln4, Rln4 = T([128, 4], F32, "ln4")
    rs4, Rrs4 = T([128, 4], F32, "rs4")
    ytmp, Rytmp = T([128, 320], F32, "ytmp")
    ss1, Rss1 = T([128, 1], F32, "ss1")
    ln1, Rln1 = T([128, 1], F32, "ln1")
    rs1, Rrs1 = T([128, 1], F32, "rs1")
    yn, Ryn = T([128, 1280], BF16, "yn")
    yTc = [T([128, 10, 128], BF16, "yTc") for _ in range(2)]

    for G in range(16):
        own = G >= 8
        RhG = RhT[G * 2:(G + 1) * 2]
        tokG = slice(G * 256, (G + 1) * 256)
        def cc_s1(cc):
            b = cc % 2
            for k in range(8):
                S.op("pe", "matmul", reads=[Rwxbc[k]] + RhG, writes=[RPS[b]], out=PS[b][:, 0:256],
                     lhsT=wxbc[:, k, cc * 128:(cc + 1) * 128], rhs=hT[:, k, tokG], start=(k == 0), stop=(k == 7))
            u_t, Ru = U[b]
            S.op("act", "copy", reads=[RPS[b]], writes=[Ru], out=u_t[:, 3:259], in_=PS[b][:, 0:256])
            S.op("act", "copy", reads=[Rhalo[cc]], writes=[Ru], out=u_t[:, 0:3], in_=halo[:, cc, :])
            S.op("act", "copy", reads=[Ru], writes=[Rhalo[cc]], out=halo[:, cc, :], in_=u_t[:, 256:259])
            dg, Rdgc = DGc[b]
            for j in range(4):
                S.op("dve", "tensor_scalar", reads=[Ridb, Rcw], writes=[Rdgc[j]], out=dg[:, j, :], in0=ident_b[:],
                     scalar1=convw[:, cc, j:j + 1], scalar2=None, op0=ALU.mult)

        def cc_s2(cc):
            b = cc % 2
            u_t, Ru = U[b]
            dg, Rdgc = DGc[b]
            pc = 5 + b
            for j in range(4):
                S.op("pe", "matmul", reads=[Rdgc[j], Ru], writes=[RPS[pc]], out=PS[pc][:, 0:256], lhsT=dg[:, j, :],
                     rhs=u_t[:, j:j + 256], start=(j == 0), stop=(j == 3))
            if cc < 10:
                xt, Rxt = xTf[b]
                S.op("act", "activation", reads=[RPS[pc], Rcb], writes=[Rxt], out=xt[:], in_=PS[pc][:, 0:256], func=AF.Silu,
                     bias=convb[:, cc:cc + 1])
            elif cc < 14:
                g = cc - 10
                S.op("act", "activation", reads=[RPS[pc], Rcb], writes=[RBTg[g]], out=BT[:, g, :], in_=PS[pc][:, 0:256],
                     func=AF.Silu, bias=convb[:, cc:cc + 1])
            else:
                g = cc - 14
                S.op("act", "activation", reads=[RPS[pc], Rcb], writes=[RCTg[g]], out=CTt[:, g, :], in_=PS[pc][:, 0:256],
                     func=AF.Silu, bias=convb[:, cc:cc + 1])

        def cc_tr(cc):
            b = cc % 2
            if cc < 10:
                xt, Rxt = xTf[b]
                pb = 2 + b
                for tb in range(2):
                    S.op("pe", "transpose", reads=[Rxt, Rcst], writes=[RPS[pb]], out=PS[pb][:, tb * 128:(tb + 1) * 128],
                         in_=xt[:, tb * 128:(tb + 1) * 128], identity=ident_f)
                S.op("dve", "tensor_copy", reads=[RPS[pb]], writes=[Rxtm], out=x_tm[:, :, cc * 128:(cc + 1) * 128],
                     in_=v3(PS[pb][:, 0:256], 2))
            elif cc < 14:
                g = cc - 10
                for tb in range(2):
                    S.op("pe", "transpose", reads=[RBTg[g], Ridb], writes=[RPB], out=PB[:, tb * 128:(tb + 1) * 128],
                         in_=BT[:, g, tb * 128:(tb + 1) * 128], identity=ident_b[:])
                S.op("dve", "tensor_copy", reads=[RPB], writes=[RBtm], out=B_tm[:, :, g * 128:(g + 1) * 128],
                     in_=v3(PB[:, 0:256], 2))

        cc_s1(0)
        for cc in range(19):
            if cc + 1 < 18:
                cc_s1(cc + 1)
            if cc < 18:
                cc_s2(cc)
            if cc >= 1:
                cc_tr(cc - 1)
        issue_conv(4)
        for tb in range(2):
            blk = G * 2 + tb
            for k in range(8):
                S.op("pe", "matmul", reads=[Rwdt[k], RhT[blk]], writes=[RPS[4]], out=PS[4][:, tb * 20:(tb + 1) * 20],
                     lhsT=hT[:, k, blk * 128:(blk + 1) * 128], rhs=wdt[:, k, :], start=(k == 0), stop=(k == 7))
        S.op("dve", "tensor_tensor", reads=[RPS[4], Rdtb], writes=[Rdt], out=dt_tm[:], in0=v3(PS[4][:, 0:40], 2),
             in1=dtb[:].unsqueeze(1).broadcast_to([128, 2, 20]), op=ALU.add)
        S.op("act", "activation", reads=[Rdt], writes=[Rdt], out=dt_tm[:], in_=dt_tm[:], func=AF.Exp)
        S.op("act", "activation", reads=[Rdt], writes=[Rdt], out=dt_tm[:], in_=dt_tm[:], func=AF.Ln, bias=1.0)
        S.op("dve", "tensor_tensor", reads=[Rdt, Ra], writes=[Radt], out=adt[:], in0=dt_tm[:],
             in1=a_bc[:].unsqueeze(1).broadcast_to([128, 2, 20]), op=ALU.mult)
        for tb in range(2):
            c = G * 2 + tb
            tk = slice(tb * 128, (tb + 1) * 128)
            S.op("pe", "matmul", reads=[Rcst, Radt], writes=[RPS[5]], out=PS[5][:, 0:20], lhsT=tri_f, rhs=adt[:, tb, :],
                 start=True, stop=True)
            S.op("pe", "matmul", reads=[Rcst, Radt], writes=[RPS[5]], out=PS[5][:, 32:52], lhsT=ones_f, rhs=adt[:, tb, :],
                 start=True, stop=True)
            if own:
                for g in range(4):
                    pz = 6 if g % 2 == 0 else 4
                    for k in range(8):
                        S.op("pe", "matmul", reads=[Rwz[k], RhT[c]], writes=[RPS[pz]], out=PS[pz][:, 0:320],
                             lhsT=hT[:, k, c * 128:(c + 1) * 128], rhs=wz[:, k, g * 320:(g + 1) * 320], start=(k == 0),
                             stop=(k == 7))
                    S.op("act", "activation", reads=[RPS[pz]], writes=[Rsz4[g]], out=sz4[:, g, :], in_=PS[pz][:, 0:320],
                         func=AF.Silu)
            S.op("dve", "tensor_scalar", reads=[RPS[5]], writes=[Rncs], out=ncs[:], in0=PS[5][:, 0:20], scalar1=-1.0,
                 scalar2=None, op0=ALU.mult)
            S.op("dve", "tensor_tensor", reads=[RPS[5], Rncs], writes=[Rt20], out=tmp20[:], in0=PS[5][:, 32:52],
                 in1=ncs[:], op=ALU.add)
            S.op("act", "activation", reads=[Rt20], writes=[Rdec], out=dec[:], in_=tmp20[:], func=AF.Exp)
            S.op("act", "activation", reads=[RPS[5]], writes=[Rcd], out=cd[:], in_=PS[5][:, 32:52], func=AF.Exp)
            S.op("dve", "tensor_tensor", reads=[Rdt, Rdec], writes=[Rw2], out=w2[:], in0=dt_tm[:, tb, :], in1=dec[:],
                 op=ALU.mult)
            S.op("dve", "tensor_tensor", reads=[Rxtm, Rw2], writes=[Rxdd], out=v3(xdd[:], 20),
                 in0=v3(x_tm[:, tb, :], 20), in1=w2[:].unsqueeze(2).broadcast_to([128, 20, 64]), op=ALU.mult)
            if own:
                blk = c
                S.op("act", "activation", reads=[Rncs], writes=[Recs], out=ecs[:], in_=ncs[:], func=AF.Exp, scale=-1.0)
                S.op("dve", "tensor_tensor", reads=[Rxtm, Rdt], writes=[Rxdt], out=v3(xdt[:], 20),
                     in0=v3(x_tm[:, tb, :], 20), in1=dt_tm[:, tb, :].unsqueeze(2).broadcast_to([128, 20, 64]),
                     op=ALU.mult)
                for g in range(4):
                    S.op("pe", "matmul", reads=[RBTg[g], RCTg[g]], writes=[RPS[2]], out=PS[2][:, g * 128:(g + 1) * 128],
                         lhsT=BT[:, g, tk], rhs=CTt[:, g, tk], start=True, stop=True)
                S.op("dve", "tensor_tensor", reads=[RPS[2], Rcst], writes=[RCBm], out=CBm[:], in0=v3(PS[2][:, :], 4),
                     in1=tri_f.unsqueeze(1).broadcast_to([128, 4, 128]), op=ALU.mult)
                def hq_copy(hq):
                    adtb, Radtb = adtb2[hq % 2]
                    S.op("dve", "tensor_copy", reads=[Radt], writes=[Radtb], out=adtb[:],
                         in_=adt[:, tb, hq * 4:hq * 4 + 4].unsqueeze(2).broadcast_to([128, 4, 128]))
                    pb = 3 + hq % 2
                    for i in range(4):
                        S.op("pe", "matmul", reads=[Radtb, Rcst], writes=[RPS[pb]], out=PS[pb][:, i * 128:(i + 1) * 128],
                             lhsT=adtb[:, i, :], rhs=tri_f, start=True, stop=True)

                def hq_rest(hq):
                    pb = 3 + hq % 2
                    lt, Rlt = Lt[hq % 2]
                    S.op("dve", "tensor_tensor", reads=[RPS[pb], Rncs], writes=[Rlt], out=lt[:], in0=v3(PS[pb][:, :], 4),
                         in1=ncs[:, hq * 4:hq * 4 + 4].unsqueeze(2).broadcast_to([128, 4, 128]), op=ALU.add)
                    S.op("act", "activation", reads=[Rlt], writes=[Rlt], out=lt[:], in_=lt[:], func=AF.Exp)
                    for i in range(4):
                        h = hq * 4 + i
                        S.op("dve", "scalar_tensor_tensor", reads=[Rlt, RCBm], writes=[RWm],
                             out=Wm[:, h, :], in0=lt[:, i, :], scalar=1.0, in1=CBm[:, h // 5, :], op0=ALU.min,
                             op1=ALU.mult)

                hq_copy(0)
                for hq in range(5):
                    if hq + 1 < 5:
                        hq_copy(hq + 1)
                    hq_rest(hq)
                for g in range(4):
                    gs = slice(g * 320, (g + 1) * 320)
                    po = 0 if g % 2 == 0 else 2
                    pd = 1 if g % 2 == 0 else 3
                    S.op("pe", "matmul", reads=[RCTg[g], Rstb], writes=[RPS[po]], out=PS[po][:, 0:320], lhsT=CTt[:, g, tk],
                         rhs=state_bf[:, gs], start=True, stop=True)
                    for hh in range(5):
                        h = g * 5 + hh
                        S.op("pe", "matmul", reads=[RWm, Rxdt], writes=[RPS[pd]], out=PS[pd][:, hh * 64:(hh + 1) * 64],
                             lhsT=Wm[:, h, :], rhs=xdt[:, h * 64:(h + 1) * 64], start=True, stop=True)
                    ygg = yg4[:, g, :]
                    S.op("dve", "tensor_tensor", reads=[RPS[po], Recs], writes=[Ryg4[g]], out=v3(ygg, 5),
                         in0=v3(PS[po][:, 0:320], 5), in1=ecs[:, g * 5:g * 5 + 5].unsqueeze(2).broadcast_to([128, 5, 64]),
                         op=ALU.mult)
                    S.op("dve", "tensor_tensor", reads=[RPS[pd], Ryg4[g]], writes=[Ryg4[g]], out=ygg, in0=ygg,
                         in1=PS[pd][:, 0:320], op=ALU.add)
                    S.op("dve", "tensor_tensor", reads=[Rxtm, Rdsk], writes=[Rytmp], out=v3(ytmp[:], 5),
                         in0=v3(x_tm[:, tb, gs], 5), in1=dsk[:, g * 5:g * 5 + 5].unsqueeze(2).broadcast_to([128, 5, 64]),
                         op=ALU.mult)
                    S.op("dve", "tensor_tensor", reads=[Rytmp, Ryg4[g]], writes=[Ryg4[g]], out=ygg, in0=ygg, in1=ytmp[:],
                         op=ALU.add)
                    S.op("dve", "tensor_tensor", reads=[Ryg4[g], Rsz4[g]], writes=[Ryg4[g]], out=ygg, in0=ygg,
                         in1=sz4[:, g, :], op=ALU.mult)
                    S.op("act", "activation", reads=[Ryg4[g]], writes=[Rss4[g]], out=sqj[:], in_=ygg, func=AF.Square,
                         accum_out=ss4[:, g:g + 1])
                rstd_from(ss4[:, 0:4], Rss4, 320, 4, rs4[:, 0:4], Rrs4, ln4[:, 0:4], Rln4)
                for g in range(4):
                    gs = slice(g * 320, (g + 1) * 320)
                    S.op("dve", "scalar_tensor_tensor", reads=[Ryg4[g], Rrs4, Rgssd], writes=[Ryn], out=yn[:, gs],
                         in0=yg4[:, g, :], scalar=rs4[:, g:g + 1], in1=gssd[:, gs], op0=ALU.mult, op1=ALU.mult)
                yt, Ryt = yTc[tb % 2]
                transposeN(yn, Ryn, 10, 128, yt[:], [Ryt], eng="act")
                ob = c - 16
                S.dma("sp", reads=[Ryt], writes=[Res()], out=yT_scr[:, :, ob * 128:(ob + 1) * 128], in_=yt[:])
            for g in range(4):
                gs = slice(g * 320, (g + 1) * 320)
                S.op("pe", "matmul", reads=[RBtm, Rxdd], writes=[RPS[6]], out=PS[6][:, 0:320],
                     lhsT=B_tm[:, tb, g * 128:(g + 1) * 128], rhs=xdd[:, gs], start=True, stop=True)
                S.op("dve", "tensor_tensor", reads=[Rst, Rcd], writes=[Rst], out=v3(state[:, gs], 5),
                     in0=v3(state[:, gs], 5), in1=cd[:, g * 5:g * 5 + 5].unsqueeze(2).broadcast_to([128, 5, 64]),
                     op=ALU.mult)
                S.op("dve", "tensor_tensor", reads=[Rst, RPS[6]], writes=[Rst], out=state[:, gs], in0=state[:, gs],
                     in1=PS[6][:, 0:320], op=ALU.add)
            if c == 15:
                S.op("dve", "tensor_scalar", reads=[Rst, Rflag], writes=[Rst], out=state[:], in0=state[:],
                     scalar1=flag[:, 0:1], scalar2=None, op0=ALU.mult)
            if c >= 15:
                S.op("act", "copy", reads=[Rst], writes=[Rstb], out=state_bf[:], in_=state[:])
    S.barrier()
    A.release(m_after_hT)

    mtab, Rmtab = T([128, 17, 128], BF16, "mtab")
    S.dma("pool", writes=[Rmtab], out=mtab[:].rearrange("p a b -> p (a b)"), in_=mtab_d)
    gq64, Rgq64 = bc_load(64, "attn_q_norm_g")
    gk64, Rgk64 = bc_load(64, "attn_k_norm_g")
    gq, Rgq = T([128, 256], F32, "gq")
    gk, Rgk = T([128, 256], F32, "gk")
    S.op("dve", "tensor_scalar", reads=[Rgq64], writes=[Rgq], out=v3(gq[:], 4),
         in0=gq64[:].unsqueeze(1).broadcast_to([128, 4, 64]), scalar1=0.125, scalar2=None, op0=ALU.mult)
    S.op("dve", "tensor_copy", reads=[Rgk64], writes=[Rgk], out=v3(gk[:], 4),
         in_=gk64[:].unsqueeze(1).broadcast_to([128, 4, 64]))
    wq, _ = T([128, 8, 256], BF16, "wq")
    Rwq = [Res() for _ in range(8)]
    wk, _ = T([128, 8, 256], BF16, "wk")
    Rwk = [Res() for _ in range(8)]
    wv, _ = T([128, 8, 256], BF16, "wv")
    Rwv = [Res() for _ in range(8)]
    KT, _ = T([64, 4, WIN], BF16, "KT")
    RKT = [Res() for _ in range(32)]
    QT, _ = T([64, 4, HALF], BF16, "QT")
    RQT = [Res() for _ in range(16)]
    Vaug, _ = T([128, 32, 4, 65], BF16, "Vaug")
    RV = [Res() for _ in range(32)]
    kn = [T([128, 256], BF16, "kn") for _ in range(2)]
    qn_ = [T([128, 256], BF16, "qn_") for _ in range(2)]
    Aexp, RAexp = T([128, 128], F32, "Aexp")
    Erev, REr = T([128, 17, 128], BF16, "Erev")
    Pf = [T([128, 512], BF16, "Pf") for _ in range(3)]
    Pbf = [T([128, 512], BF16, "Pbf") for _ in range(3)]
    rd, Rrd = T([128, 2], F32, "rd")
    o_all, _ = T([128, 16, 256], BF16, "o_all")
    Roall = [Res() for _ in range(16)]
    oTc = [T([128, 2, 128], BF16, "oTc") for _ in range(2)]
    for r in range(3):
        for k in range(8):
            rows = slice(k * 128, (k + 1) * 128)
            S.dma("pool", writes=[Rwq[k]], out=wq[:, k, :], in_=w_in_d[rows, OFF_DT + r * 256:OFF_DT + (r + 1) * 256])
            S.dma("pool", writes=[Rwk[k]], out=wk[:, k, :], in_=w_in_d[rows, OFF_Q + r * 256:OFF_Q + (r + 1) * 256])
            S.dma("pool", writes=[Rwv[k]], out=wv[:, k, :], in_=w_in_d[rows, OFF_K + r * 256:OFF_K + (r + 1) * 256])

        def proj_mm(blk):
            bs = slice(blk * 128, (blk + 1) * 128)
            pa = blk % 2
            for k in range(8):
                S.op("pe", "matmul", reads=[Rwk[k], RhT[blk]], writes=[RPS[pa]], out=PS[pa][:, 0:256], lhsT=hT[:, k, bs],
                     rhs=wk[:, k, :], start=(k == 0), stop=(k == 7))
            kt, Rkn = kn[blk % 2]
            qknorm(PS[pa][:, 0:256], RPS[pa], 4, 64, gk[:], Rgk, kt[:], Rkn)
            pv = 2 + blk % 2
            for k in range(8):
                S.op("pe", "matmul", reads=[Rwv[k], RhT[blk]], writes=[RPS[pv]], out=PS[pv][:, 0:256], lhsT=hT[:, k, bs],
                     rhs=wv[:, k, :], start=(k == 0), stop=(k == 7))
            if blk >= 16:
                S.op("act", "copy", reads=[RPS[pv]], writes=[RV[blk]], out=Vaug[:, blk, :, 0:64],
                     in_=v3(PS[pv][:, 0:256], 4))
                S.op("pool", "memset", writes=[RV[blk]], ap=Vaug[:, blk, :, 64:65], constant=1.0)
            else:
                S.op("dve", "tensor_scalar", reads=[RPS[pv], Rflag], writes=[RV[blk]], out=Vaug[:, blk, :, 0:64],
                     in0=v3(PS[pv][:, 0:256], 4), scalar1=flag[:, 0:1], scalar2=None, op0=ALU.mult)
                S.op("pool", "tensor_copy", reads=[Rflag], writes=[RV[blk]], out=Vaug[:, blk, :, 64:65],
                     in_=flag[:, 0:1].unsqueeze(1).broadcast_to([128, 4, 1]))
            if blk >= 16:
                pq = 4 + blk % 2
                for k in range(8):
                    S.op("pe", "matmul", reads=[Rwq[k], RhT[blk]], writes=[RPS[pq]], out=PS[pq][:, 0:256], lhsT=hT[:, k, bs],
                         rhs=wq[:, k, :], start=(k == 0), stop=(k == 7))
                qt, Rqn = qn_[blk % 2]
                qknorm(PS[pq][:, 0:256], RPS[pq], 4, 64, gq[:], Rgq, qt[:], Rqn)

        def proj_tr(blk):
            bs = slice(blk * 128, (blk + 1) * 128)
            kt, Rkn = kn[blk % 2]
            transposeN(kt, Rkn, 4, 64, KT[:, :, bs], [RKT[blk]], eng="act")
            if blk >= 16:
                qt, Rqn = qn_[blk % 2]
                transposeN(qt, Rqn, 4, 64, QT[:, :, (blk - 16) * 128:(blk - 15) * 128], [RQT[blk - 16]], eng="act")

        for blk in range(33):
            if blk < 32:
                proj_mm(blk)
            if blk >= 1:
                proj_tr(blk - 1)

        seq = []
        for hh in range(4):
            for qb in range(16, 32):
                kbs = list(range(qb - 16, qb + 1))
                for gi in range(5):
                    seq.append((hh, qb, gi, kbs[gi * 4:gi * 4 + 4]))

        def emit_S(i):
            hh, qb, gi, grp = seq[i]
            qi = qb - 16
            sbk = i % 4
            for ii, kb in enumerate(grp):
                S.op("pe", "matmul", reads=[RKT[kb], RQT[qi]], writes=[RPS[sbk]],
                     out=PS[sbk][:, ii * 128:(ii + 1) * 128], lhsT=KT[:, hh, kb * 128:(kb + 1) * 128],
                     rhs=QT[:, hh, qi * 128:(qi + 1) * 128], start=True, stop=True)

        def emit_rest(i):
            hh, qb, gi, grp = seq[i]
            qi = qb - 16
            n = len(grp)
            sbk = i % 4
            unit = i // 5
            ob = 4 + unit % 2
            pf, Rpf = Pf[i % 3]
            pbf, Rpbf = Pbf[i % 3]
            if qb == 16 and gi == 0:
                h = r * 4 + hh
                slope = 2.0 ** (-8.0 * (h + 1) / 12.0)
                S.op("act", "activation", reads=[Rcst], writes=[RAexp], out=Aexp[:], in_=d0_f, func=AF.Exp, scale=-slope)
                for j in range(17):
                    cpow = math.exp(-slope * 128.0 * (16 - j))
                    S.op("dve", "scalar_tensor_tensor", reads=[RAexp, Rmtab], writes=[REr],
                         out=Erev[:, j, :], in0=Aexp[:], scalar=float(cpow), in1=mtab[:, j, :], op0=ALU.mult,
                         op1=ALU.mult)
            S.op("act", "activation", reads=[RPS[sbk]], writes=[Rpf], out=pf[:, 0:n * 128],
                 in_=PS[sbk][:, 0:n * 128], func=AF.Exp)
            S.op("dve", "tensor_tensor", reads=[Rpf, REr], writes=[Rpbf], out=pbf[:, 0:n * 128],
                 in0=pf[:, 0:n * 128], in1=Erev[:, gi * 4:gi * 4 + n, :].rearrange("p a b -> p (a b)"),
                 op=ALU.mult)
            for ii, kb in enumerate(grp):
                S.op("pe", "matmul", reads=[Rpbf, RV[kb]], writes=[RPS[ob]], out=PS[ob][:, 0:65],
                     lhsT=pbf[:, ii * 128:(ii + 1) * 128], rhs=Vaug[:, kb, hh, :], start=(gi == 0 and ii == 0),
                     stop=(gi == 4))
            if gi == 4:
                S.op("dve", "reciprocal", reads=[RPS[ob]], writes=[Rrd], out=rd[:, 0:1], in_=PS[ob][:, 64:65])
                S.op("dve", "tensor_scalar", reads=[RPS[ob], Rrd], writes=[Roall[qi]],
                     out=o_all[:, qi, hh * 64:(hh + 1) * 64], in0=PS[ob][:, 0:64], scalar1=rd[:, 0:1], scalar2=None,
                     op0=ALU.mult)

        LA = 3
        for i in range(LA):
            emit_S(i)
        for i in range(len(seq)):
            if i + LA < len(seq):
                emit_S(i + LA)
            emit_rest(i)
        for qi in range(16):
            for i in range(2):
                S.op("pe", "transpose", reads=[Roall[qi], Ridb], writes=[RPB], out=PB[:, i * 128:(i + 1) * 128],
                     in_=o_all[:, qi, i * 128:(i + 1) * 128], identity=ident_b[:])
            ot, Rot = oTc[qi % 2]
            S.op("act", "copy", reads=[RPB], writes=[Rot], out=ot[:], in_=v3(PB[:, 0:256], 2))
            S.dma("sp", reads=[Rot], writes=[Res()], out=oT_scr[:, 2 * r:2 * r + 2, qi * 128:(qi + 1) * 128], in_=ot[:])
    S.barrier()
    A.release(m_base)

    x1, _ = T([128, 16, 1024], F32, "x1")
    Rx1 = [Res() for _ in range(16)]
    m_after_x1 = A.mark()
    yT_sb, _ = T([128, 10, HALF], BF16, "yT_sb")
    RyT = [Res() for _ in range(10)]
    for kc in range(10):
        S.dma("sp", writes=[RyT[kc]], out=yT_sb[:, kc, :], in_=yT_scr[:, kc, :])
    oT, _ = T([128, 6, HALF], BF16, "oT")
    RoTs = [Res() for _ in range(6)]
    for kc in range(6):
        S.dma("sp", writes=[RoTs[kc]], out=oT[:, kc, :], in_=oT_scr[:, kc, :])
    wout, _ = T([128, 16, 1024], BF16, "wout")
    Rwout = [Res() for _ in range(16)]
    for kc in range(16):
        S.dma("pool", writes=[Rwout[kc]], out=wout[:, kc, :], in_=w_out_d[kc * 128:(kc + 1) * 128, :])
    xo = [T([128, 1024], F32, "xo") for _ in range(2)]
    for tb in range(16):
        ts_ = slice(tb * 128, (tb + 1) * 128)
        xo_t, Rxo = xo[tb % 2]
        S.dma("sp", writes=[Rxo], out=xo_t[:], in_=xw_d[HALF + tb * 128:HALF + (tb + 1) * 128, :])
        for half in range(2):
            pb = 2 * (tb % 2) + half
            cs_ = slice(half * 512, (half + 1) * 512)
            for kc in range(16):
                lhsT = yT_sb[:, kc, ts_] if kc < 10 else oT[:, kc - 10, ts_]
                S.op("pe", "matmul", reads=[RyT[kc] if kc < 10 else RoTs[kc - 10], Rwout[kc]], writes=[RPS[pb]], out=PS[pb][:, :], lhsT=lhsT,
                     rhs=wout[:, kc, cs_], start=(kc == 0), stop=(kc == 15))
            S.op("dve", "tensor_tensor", reads=[RPS[pb], Rxo], writes=[Rx1[tb]], out=x1[:, tb, cs_], in0=PS[pb][:, :],
                 in1=xo_t[:, cs_], op=ALU.add)
    S.barrier()
    A.release(m_after_x1)

    def store_x1_and_finish():
        fin = []
        for tb in range(16):
            R = Res()
            S.dma("sp", reads=[Rx1[tb]], writes=[R], out=out_d[tb * 128:(tb + 1) * 128, :], in_=x1[:, tb, :])
            fin.append(R)
        for R in fin:
            S._wait("sp", R.w)
        S.emit()
        return nc

    if stop_after == "C":
        return store_x1_and_finish()

    wqm, _ = T([128, 8, 512], BF16, "wqm")
    Rwqm = [Res() for _ in range(8)]
    wkv, _ = T([128, 8, 1024], BF16, "wkv")
    Rwkv = [Res() for _ in range(8)]
    wo, _ = T([128, 4, 1024], BF16, "wo")
    Rwo = [Res() for _ in range(4)]
    for k in range(8):
        S.dma("pool", writes=[Rwqm[k]], out=wqm[:, k, :], in_=mem_w_q_d[k * 128:(k + 1) * 128, :])
        S.dma("pool", writes=[Rwkv[k]], out=wkv[:, k, :], in_=mem_w_kv_d[k * 128:(k + 1) * 128, :])
    for k in range(4):
        S.dma("pool", writes=[Rwo[k]], out=wo[:, k, :], in_=mem_w_o_d[k * 128:(k + 1) * 128, :])
    gx, Rgx = bc_load(1024, "xattn_norm_g")
    gm, Rgm = bc_load(1024, "mem_norm_g")
    gqm128, Rgqm128 = bc_load(128, "mem_q_norm_g")
    gkm128, Rgkm128 = bc_load(128, "mem_k_norm_g")
    gqm, Rgqm = T([128, 512], F32, "gqm")
    gkm, Rgkm = T([128, 512], F32, "gkm")
    S.op("dve", "tensor_scalar", reads=[Rgqm128], writes=[Rgqm], out=v3(gqm[:], 4),
         in0=gqm128[:].unsqueeze(1).broadcast_to([128, 4, 128]), scalar1=128.0 ** -0.5, scalar2=None, op0=ALU.mult)
    S.op("dve", "tensor_copy", reads=[Rgkm128], writes=[Rgkm], out=v3(gkm[:], 4),
         in_=gkm128[:].unsqueeze(1).broadcast_to([128, 4, 128]))
    KmT, RKmT = T([128, 4, 256], BF16, "KmT")
    Vm, RVm = T([128, 2, 4, 129], BF16, "Vm")
    S.op("pool", "memset", writes=[RVm], ap=Vm[:].rearrange("p a b c -> p (a b c)"), constant=1.0)
    memb, Rmemb = T([128, 1024], F32, "memb")
    mh, Rmh = T([128, 1024], BF16, "mh")
    mT, RmT = T([128, 8, 128], BF16, "mT")
    knm, Rknm = T([128, 512], BF16, "knm")
    for mb in range(2):
        ms = slice(mb * 128, (mb + 1) * 128)
        S.dma("sp", writes=[Rmemb], out=memb[:], in_=mem_d[ms, :])
        rmsnorm(memb[:], Rmemb, gm, Rgm, mh[:], Rmh)
        transposeN(mh, Rmh, 8, 128, mT[:], [RmT], eng="act")
        for half in range(2):
            for k in range(8):
                S.op("pe", "matmul", reads=[RmT, Rwkv[k]], writes=[RPS[half]], out=PS[half][:, :], lhsT=mT[:, k, :],
                     rhs=wkv[:, k, half * 512:(half + 1) * 512], start=(k == 0), stop=(k == 7))
        qknorm(PS[0][:, :], RPS[0], 4, 128, gkm[:], Rgkm, knm[:], Rknm)
        transposeN(knm, Rknm, 4, 128, KmT[:, :, ms], [RKmT], eng="act")
        S.op("act", "copy", reads=[RPS[1]], writes=[RVm], out=Vm[:, mb, :, 0:128], in_=v3(PS[1][:, :], 4))
    h2p = [T([128, 1024], BF16, "h2") for _ in range(2)]
    h2Tp = [T([128, 8, 128], BF16, "h2T") for _ in range(2)]
    qnp = [T([128, 512], BF16, "qn") for _ in range(2)]
    qTp = [T([128, 4, 128], BF16, "qT") for _ in range(2)]
    Pm = [T([128, 256], BF16, "Pm") for _ in range(2)]
    o2, Ro2 = T([128, 512], BF16, "o2")
    o2T, Ro2T = T([128, 4, 128], BF16, "o2T")
    rd4, _ = T([128, 4], F32, "rd4")
    Rrd4 = [Res() for _ in range(4)]

    def d_front(tb):
        p = tb % 2
        h2, Rh2 = h2p[p]
        h2T, Rh2T = h2Tp[p]
        qn, Rqn2 = qnp[p]
        qT, RqT = qTp[p]
        rmsnorm(x1[:, tb, :], Rx1[tb], gx, Rgx, h2[:], Rh2)
        transposeN(h2, Rh2, 8, 128, h2T[:], [Rh2T], eng="act")
        for k in range(8):
            S.op("pe", "matmul", reads=[Rh2T, Rwqm[k]], writes=[RPS[2]], out=PS[2][:, :], lhsT=h2T[:, k, :], rhs=wqm[:, k, :],
                 start=(k == 0), stop=(k == 7))
        qknorm(PS[2][:, :], RPS[2], 4, 128, gqm[:], Rgqm, qn[:], Rqn2)
        transposeN(qn, Rqn2, 4, 128, qT[:], [RqT], eng="act")

    def d_back(tb):
        p = tb % 2
        qT, RqT = qTp[p]
        for hh in range(4):
            sb_ = 3 + hh % 2
            ob = 5 + hh % 2
            pm, Rpm = Pm[hh % 2]
            for mb in range(2):
                S.op("pe", "matmul", reads=[RKmT, RqT], writes=[RPS[sb_]], out=PS[sb_][:, mb * 128:(mb + 1) * 128],
                     lhsT=KmT[:, hh, mb * 128:(mb + 1) * 128], rhs=qT[:, hh, :], start=True, stop=True)
            S.op("act", "activation", reads=[RPS[sb_]], writes=[Rpm], out=pm[:], in_=PS[sb_][:, 0:256], func=AF.Exp)
            for mb in range(2):
                S.op("pe", "matmul", reads=[Rpm, RVm], writes=[RPS[ob]], out=PS[ob][:, 0:129],
                     lhsT=pm[:, mb * 128:(mb + 1) * 128], rhs=Vm[:, mb, hh, :], start=(mb == 0), stop=(mb == 1))
            S.op("dve", "reciprocal", reads=[RPS[ob]], writes=[Rrd4[hh]], out=rd4[:, hh:hh + 1], in_=PS[ob][:, 128:129])
            S.op("dve", "tensor_scalar", reads=[RPS[ob], Rrd4[hh]], writes=[Ro2], out=o2[:, hh * 128:(hh + 1) * 128],
                 in0=PS[ob][:, 0:128], scalar1=rd4[:, hh:hh + 1], scalar2=None, op0=ALU.mult)
        transposeN(o2, Ro2, 4, 128, o2T[:], [Ro2T], eng="act")
        for half in range(2):
            cs_ = slice(half * 512, (half + 1) * 512)
            for kc in range(4):
                S.op("pe", "matmul", reads=[Ro2T, Rwo[kc]], writes=[RPS[half]], out=PS[half][:, :], lhsT=o2T[:, kc, :],
                     rhs=wo[:, kc, cs_], start=(kc == 0), stop=(kc == 3))
            S.op("dve", "tensor_tensor", reads=[RPS[half], Rx1[tb]], writes=[Rx1[tb]], out=x1[:, tb, cs_],
                 in0=PS[half][:, :], in1=x1[:, tb, cs_], op=ALU.add)

    d_front(0)
    for tb in range(16):
        if tb + 1 < 16:
            d_front(tb + 1)
        d_back(tb)
    if stop_after == "D":
        S.barrier()
        return store_x1_and_finish()
    Rx2d = [Res() for _ in range(16)]
    for tb in range(16):
        S.dma("sp", reads=[Rx1[tb]], writes=[Rx2d[tb]], out=out_d[tb * 128:(tb + 1) * 128, :], in_=x1[:, tb, :])
    S.barrier()
    A.release(m_base)

    wpq, _ = T([128, 8, 2048], BF16, "wpq")
    Rwpq = [Res() for _ in range(8)]
    for k in range(8):
        S.dma("pool", writes=[Rwpq[k]], out=wpq[:, k, :], in_=peer_wq_d[k * 128:(k + 1) * 128, :])
    keysT, RkeysT = T([128, 2, 128], F32, "keysT")
    S.dma("sp", writes=[RkeysT], out=keysT[:, 0, :], in_=keys1T_d)
    S.dma("sp", writes=[RkeysT], out=keysT[:, 1, :], in_=keys2T_d)
    gf, Rgf = bc_load(1024, "ffn_norm_g")
    xrp = [T([128, 1024], F32, "xr") for _ in range(2)]
    h3bp = [T([128, 1024], BF16, "h3b") for _ in range(2)]
    h3T, Rh3T = T([128, 8, 128], BF16, "h3T")
    off_q = A.mark()
    qrT, RqrT = T([128, 16, 128], F32, "qrT")
    eq = nc.alloc_sbuf_tensor_at("eq_alias", [128, 8, 16, 16], F32, offset=off_q)
    Req = RqrT
    off_s = A.mark()
    sc, Rsc = T([128, 16, 128], F32, "sc")
    cand = nc.alloc_sbuf_tensor_at("cand_alias", [128, 8, 256], F32, offset=off_s)
    Rcand = Rsc
    sc2, Rsc2 = T([128, 128], F32, "sc2")
    tv, Rtv = T([128, 16, 16], F32, "tv")
    ti, Rti = T([128, 16, 16], U32, "ti")
    tif, Rtif = T([128, 16, 16], F32, "tif")
    cand2, Rcand2 = T([128, 256], F32, "cand2")
    cv, Rcv = T([128, 8, 16], F32, "cv")
    cp, Rcp = T([128, 8, 16], U32, "cp")
    ca, Rca = T([128, 8, 16], U32, "ca")
    cb_, Rcb_ = T([128, 8, 16], U32, "cb")
    caf, Rcaf = T([128, 8, 16], F32, "caf")
    cbf, Rcbf = T([128, 8, 16], F32, "cbf")
    i1f, Ri1f = T([128, 8, 16], F32, "i1f")
    i2f, Ri2f = T([128, 8, 16], F32, "i2f")
    ef, Ref = T([128, 128], F32, "ef")
    eip = [T([128, 128], I32, "ei") for _ in range(2)]
    gtp = [T([128, 8, 16], F32, "gt") for _ in range(2)]
    gsum, Rgsum = T([128, 8], F32, "gsum")
    actv, _ = T([128, 128], F32, "actv")
    Ractv = [Res() for _ in range(128)]
    WG = [T([128, 4], F32, "wg4") for _ in range(2)]
    wgt, _ = T([128, 128], F32, "wgt")
    Rwgt = [Res() for _ in range(32)]
    NUV = 6
    UVt = [(T([128, 4, 2048], BF16, "UV")[0], [Res() for _ in range(4)]) for _ in range(NUV)]
    PR = [T([128, 1024], BF16, "PR") for _ in range(8)]
    DG = [T([128, 128], BF16, "DG") for _ in range(4)]
    fin = []

    def peer_setup(tb, p):
        xr, Rxr = xrp[p]
        h3b, Rh3b = h3bp[p]
        ei, Rei = eip[p]
        gt, Rgt = gtp[p]
        S.dma("sp", reads=[Rx2d[tb]], writes=[Rxr], out=xr[:], in_=out_d[tb * 128:(tb + 1) * 128, :])
        rmsnorm(xr[:], Rxr, gf, Rgf, h3b[:], Rh3b)
        transposeN(h3b, Rh3b, 8, 128, h3T[:], [Rh3T], eng="act")
        yield
        for q4 in range(4):
            pb = q4 % 2
            for i in range(4):
                ccq = q4 * 4 + i
                for k in range(8):
                    S.op("pe", "matmul", reads=[Rwpq[k], Rh3T], writes=[RPS[pb]], out=PS[pb][:, i * 128:(i + 1) * 128],
                         lhsT=wpq[:, k, ccq * 128:(ccq + 1) * 128], rhs=h3T[:, k, :], start=(k == 0), stop=(k == 7))
            S.op("act", "copy", reads=[RPS[pb]], writes=[RqrT], out=qrT[:, q4 * 4:q4 * 4 + 4, :], in_=v3(PS[pb][:, :], 4))
            yield
        for q4 in range(4):
            pb = 2 + q4 % 2
            for i in range(4):
                j = q4 * 4 + i
                S.op("pe", "matmul", reads=[RqrT, RkeysT], writes=[RPS[pb]], out=PS[pb][:, i * 128:(i + 1) * 128],
                     lhsT=qrT[:, j, :], rhs=keysT[:, j % 2, :], start=True, stop=True)
            S.op("act", "copy", reads=[RPS[pb]], writes=[Rsc], out=sc[:, q4 * 4:q4 * 4 + 4, :], in_=v3(PS[pb][:, :], 4))
        for j in range(16):
            S.op("dve", "max", reads=[Rsc], writes=[Rtv], out=tv[:, j, 0:8], in_=sc[:, j, :])
            S.op("dve", "max_index", reads=[Rsc, Rtv], writes=[Rti], out=ti[:, j, 0:8], in_max=tv[:, j, 0:8],
                 in_values=sc[:, j, :])
            S.op("dve", "match_replace", reads=[Rsc, Rtv], writes=[Rsc2], out=sc2[:], in_to_replace=tv[:, j, 0:8],
                 in_values=sc[:, j, :], imm_value=-1e30)
            S.op("dve", "max", reads=[Rsc2], writes=[Rtv], out=tv[:, j, 8:16], in_=sc2[:])
            S.op("dve", "max_index", reads=[Rsc2, Rtv], writes=[Rti], out=ti[:, j, 8:16], in_max=tv[:, j, 8:16],
                 in_values=sc2[:])
            yield
        S.op("dve", "tensor_copy", reads=[Rti], writes=[Rtif], out=tif[:], in_=ti[:])
        tv4 = tv[:].rearrange("p (h two) k -> p h two k", two=2)
        tif4 = tif[:].rearrange("p (h two) k -> p h two k", two=2)
        S.op("dve", "tensor_tensor", reads=[Rtv], writes=[Rcand], out=cand[:].rearrange("p h (a b) -> p h a b", a=16),
             in0=tv4[:, :, 0, :].unsqueeze(3).broadcast_to([128, 8, 16, 16]),
             in1=tv4[:, :, 1, :].unsqueeze(2).broadcast_to([128, 8, 16, 16]), op=ALU.add)
        for h in range(8):
            S.op("dve", "max", reads=[Rcand], writes=[Rcv], out=cv[:, h, 0:8], in_=cand[:, h, :])
            S.op("dve", "max_index", reads=[Rcand, Rcv], writes=[Rcp], out=cp[:, h, 0:8], in_max=cv[:, h, 0:8],
                 in_values=cand[:, h, :])
            S.op("dve", "match_replace", reads=[Rcand, Rcv], writes=[Rcand2], out=cand2[:], in_to_replace=cv[:, h, 0:8],
                 in_values=cand[:, h, :], imm_value=-1e30)
            S.op("dve", "max", reads=[Rcand2], writes=[Rcv], out=cv[:, h, 8:16], in_=cand2[:])
            S.op("dve", "max_index", reads=[Rcand2, Rcv], writes=[Rcp], out=cp[:, h, 8:16], in_max=cv[:, h, 8:16],
                 in_values=cand2[:])
            yield
        S.op("dve", "tensor_single_scalar", reads=[Rcp], writes=[Rca], out=ca[:], in_=cp[:], scalar=4,
             op=ALU.logical_shift_right)
        S.op("dve", "tensor_single_scalar", reads=[Rcp], writes=[Rcb_], out=cb_[:], in_=cp[:], scalar=15,
             op=ALU.bitwise_and)
        S.op("dve", "tensor_copy", reads=[Rca], writes=[Rcaf], out=caf[:], in_=ca[:])
        S.op("dve", "tensor_copy", reads=[Rcb_], writes=[Rcbf], out=cbf[:], in_=cb_[:])
        io4 = iota16.unsqueeze(1).unsqueeze(1).broadcast_to([128, 8, 16, 16])
        for (sel, Rsel, half, dst, Rdst) in ((caf, Rcaf, 0, i1f, Ri1f), (cbf, Rcbf, 1, i2f, Ri2f)):
            S.op("dve", "tensor_tensor", reads=[Rsel, Rcst], writes=[Req], out=eq[:],
                 in0=sel[:].unsqueeze(3).broadcast_to([128, 8, 16, 16]), in1=io4, op=ALU.is_equal)
            S.op("dve", "tensor_tensor", reads=[Req, Rtif], writes=[Req], out=eq[:], in0=eq[:],
                 in1=tif4[:, :, half, :].unsqueeze(2).broadcast_to([128, 8, 16, 16]), op=ALU.mult)
            S.op("dve", "tensor_reduce", reads=[Req], writes=[Rdst], out=dst[:].rearrange("p h k -> p (h k)"),
                 in_=eq[:].rearrange("p h k a -> p (h k) a"), axis=AX.X, op=ALU.add)
        S.op("dve", "scalar_tensor_tensor", reads=[Ri1f, Ri2f], writes=[Ref], out=ef[:],
             in0=i1f[:].rearrange("p h k -> p (h k)"), scalar=128.0, in1=i2f[:].rearrange("p h k -> p (h k)"),
             op0=ALU.mult, op1=ALU.add)
        S.op("dve", "tensor_copy", reads=[Ref], writes=[Rei], out=ei[:], in_=ef[:])
        S.op("dve", "tensor_tensor", reads=[Rcv], writes=[Rgt], out=gt[:], in0=cv[:],
             in1=cv[:, :, 0:1].broadcast_to([128, 8, 16]), op=ALU.subtract)
        S.op("act", "activation", reads=[Rgt], writes=[Rgt], out=gt[:], in_=gt[:], func=AF.Exp)
        S.op("dve", "tensor_reduce", reads=[Rgt], writes=[Rgsum], out=gsum[:], in_=gt[:], axis=AX.X, op=ALU.add)
        S.op("dve", "reciprocal", reads=[Rgsum], writes=[Rgsum], out=gsum[:], in_=gsum[:])
        S.op("dve", "tensor_tensor", reads=[Rgt, Rgsum], writes=[Rgt], out=gt[:], in0=gt[:],
             in1=gsum[:].unsqueeze(2).broadcast_to([128, 8, 16]), op=ALU.mult)

    def peer_gather(tb, p, bg):
        xr, Rxr = xrp[p]
        h3b, Rh3b = h3bp[p]
        ei, Rei = eip[p]
        gt, Rgt = gtp[p]
        gtf = gt[:].rearrange("p h k -> p (h k)")
        S.op("pool", "memset", writes=Ractv, ap=actv[:], constant=0.0)

        def stA(g4):
            uvt, Ruvt = UVt[g4 % NUV]
            for i in range(4):
                s_ = g4 * 4 + i
                S.dma("pool", reads=[Rei, Ruv], writes=[Ruvt[i]], meth="indirect_dma_start", out=uvt[:, i, :],
                      out_offset=None, in_=uv_scr, in_offset=bass.IndirectOffsetOnAxis(ap=ei[:, s_:s_ + 1], axis=0))

        def stBC(g4):
            uvt, Ruvt = UVt[g4 % NUV]
            for i in range(4):
                s_ = g4 * 4 + i
                pr, Rpr = PR[s_ % 8]
                S.op("dve", "tensor_tensor", reads=[Ruvt[i], Rh3b], writes=[Rpr], out=pr[:], in0=uvt[:, i, 0:1024],
                     in1=h3b[:], op=ALU.mult)
                S.op("act", "activation", reads=[Rpr], writes=[Ractv[s_]], out=pr[:], in_=pr[:], func=AF.Copy,
                     accum_out=actv[:, s_:s_ + 1])
            wg, Rwg = WG[g4 % 2]
            S.op("act", "activation", reads=Ractv[g4 * 4:g4 * 4 + 4], writes=[Rwg], out=wg[:], in_=actv[:, g4 * 4:g4 * 4 + 4],
                 func=AF.Gelu)

        def stDE(g4):
            uvt, Ruvt = UVt[g4 % NUV]
            wg, Rwg = WG[g4 % 2]
            S.op("dve", "tensor_tensor", reads=[Rwg, Rgt], writes=[Rwgt[g4]], out=wgt[:, g4 * 4:g4 * 4 + 4], in0=wg[:],
                 in1=gtf[:, g4 * 4:g4 * 4 + 4], op=ALU.mult)
            for i in range(4):
                s_ = g4 * 4 + i
                dg, Rdg = DG[s_ % 4]
                S.op("dve", "tensor_scalar", reads=[Ridb, Rwgt[g4]], writes=[Rdg], out=dg[:], in0=ident_b[:],
                     scalar1=wgt[:, s_:s_ + 1], scalar2=None, op0=ALU.mult)
                for half in range(2):
                    S.op("pe", "matmul", reads=[Rdg, Ruvt[i]], writes=[RPS[5 + half]], out=PS[5 + half][:, :],
                         lhsT=dg[:], rhs=uvt[:, i, 1024 + half * 512:1024 + (half + 1) * 512], start=(s_ == 0),
                         stop=(s_ == 127))

        for g4 in range(NUV - 2):
            stA(g4)
        for g4 in range(33):
            if g4 + NUV - 2 < 32:
                stA(g4 + NUV - 2)
            if g4 < 32:
                stBC(g4)
            if g4 >= 1:
                stDE(g4 - 1)
            if bg is not None:
                next(bg, None)
                next(bg, None)
        if bg is not None:
            for _ in bg:
                pass
        for half in range(2):
            cs_ = slice(half * 512, (half + 1) * 512)
            S.op("dve", "tensor_tensor", reads=[RPS[5 + half], Rxr], writes=[Rxr], out=xr[:, cs_],
                 in0=xr[:, cs_], in1=PS[5 + half][:, :], op=ALU.add)
        R = Res()
        S.dma("sp", reads=[Rxr], writes=[R, Rx2d[tb]], out=out_d[tb * 128:(tb + 1) * 128, :], in_=xr[:])
        fin.append(R)

    for _ in peer_setup(0, 0):
        pass
    for tb in range(16):
        bg = peer_setup(tb + 1, (tb + 1) % 2) if tb + 1 < 16 else None
        peer_gather(tb, tb % 2, bg)
    for R in fin:
        S._wait("sp", R.w)
    S.emit()
    return nc


def _consts():
    c = np.zeros((128, NCONST), np.float32)
    p = np.arange(128)
    c[:, 0:128] = np.eye(128)
    c[:, 128:256] = (p[:, None] <= p[None, :])
    c[:, 256:384] = 1.0
    c[:, 384:512] = (p[None, :] - p[:, None])
    c[:, 512:528] = np.arange(16)[None, :]
    k = p[:, None, None]
    j = np.arange(17)[None, :, None]
    q = p[None, None, :]
    delta = (16 - j) * 128 + q - k
    m = ((delta >= 0) & (delta <= 128)).astype(np.float32)
    m += ((delta >= 0) & (delta <= 512) & (delta % 4 == 0))
    m += ((delta >= 0) & (delta <= 2048) & (delta % 16 == 0))
    return c, np.ascontiguousarray(m.reshape(128, 17 * 128).astype(np.float32))


_NC_CACHE = {}


def kernel(**inputs):
    stop_after = os.environ.get("KSTOP") or None
    x = np.asarray(inputs["x"], np.float32)
    mem = np.asarray(inputs["mem"], np.float32)
    cst, mtab = _consts()
    common = {
        "cst": cst, "mtab": mtab,
        "w_in": np.ascontiguousarray(inputs["w_in"][0], dtype=np.float32),
        "convw": np.ascontiguousarray(
            np.asarray(inputs["conv_w"][0], np.float32).T.reshape(18, 128, 4).transpose(1, 0, 2).reshape(128, 72)),
        "convb": np.ascontiguousarray(np.asarray(inputs["conv_b"][0], np.float32).reshape(18, 128).T),
        "w_out": np.ascontiguousarray(inputs["w_out"][0], dtype=np.float32),
    }
    for nm in ("mix_norm_g", "dt_bias", "a_log", "d_skip", "ssd_norm_g", "attn_q_norm_g", "attn_k_norm_g",
               "xattn_norm_g", "mem_norm_g", "mem_q_norm_g", "mem_k_norm_g", "ffn_norm_g"):
        common[nm] = np.ascontiguousarray(np.asarray(inputs[nm][0], np.float32).reshape(1, -1))
    if True:
        common["mem_w_q"] = np.ascontiguousarray(inputs["mem_w_q"][0], dtype=np.float32)
        common["mem_w_kv"] = np.ascontiguousarray(inputs["mem_w_kv"][0], dtype=np.float32)
        common["mem_w_o"] = np.ascontiguousarray(inputs["mem_w_o"][0], dtype=np.float32)
    if True:
        common["peer_w_query"] = np.ascontiguousarray(inputs["peer_w_query"][0], dtype=np.float32)
        common["keys1T"] = np.ascontiguousarray(np.asarray(inputs["peer_sub_keys1"][0], np.float32).T)
        common["keys2T"] = np.ascontiguousarray(np.asarray(inputs["peer_sub_keys2"][0], np.float32).T)
        common["peer_u"] = np.ascontiguousarray(inputs["peer_u"][0], dtype=np.float32)
        common["peer_v"] = np.ascontiguousarray(inputs["peer_v"][0], dtype=np.float32)
    in_maps = []
    for c in range(8):
        b, j = c // 2, c % 2
        xw = np.zeros((WIN, D), np.float32)
        if j == 1:
            xw[:HALF] = x[b, :HALF]
        xw[HALF:] = x[b, j * HALF:(j + 1) * HALF]
        m = dict(common)
        m["xw"] = xw
        m["flag"] = np.full((128, 1), float(j), np.float32)
        if True:
            m["mem"] = np.ascontiguousarray(mem[b])
        in_maps.append(m)
    nc = build(stop_after)
    res = run_bass_kernel_spmd(nc, in_maps, core_ids=list(range(8)))
    out = np.zeros((NB, SEQ, D), np.float32)
    for c in range(8):
        b, j = c // 2, c % 2
        out[b, j * HALF:(j + 1) * HALF] = np.asarray(res.results[c]["out"])
    return out
```

```python
import os
import math
import numpy as np
import concourse.bass as bass
import concourse.mybir as mybir
from concourse.bass_utils import run_bass_kernel_spmd

F32 = mybir.dt.float32
BF16 = mybir.dt.bfloat16
I32 = mybir.dt.int32
U32 = mybir.dt.uint32
AF = mybir.ActivationFunctionType
ALU = mybir.AluOpType
AX = mybir.AxisListType

D = 1024
NB = 4
SEQ = 4096
HALF = 2048
WIN = 4096
EPS = 1e-6
OFF_Z = 1280
OFF_XBC = OFF_Z + 2304
OFF_DT = OFF_XBC + 20
OFF_Q = OFF_DT + 768
OFF_K = OFF_Q + 768
IN_COLS = OFF_K + 768
NCONST = 128 * 4 + 16


class Res:
    __slots__ = ("w", "r")

    def __init__(self):
        self.w = None
        self.r = {}


class Sched:
    COMPUTE = ("pe", "act", "dve", "pool")
    NDMA = 12
    INORDER = ("pe",)

    def __init__(self, nc):
        self.nc = nc
        self.streams = {k: [] for k in ("pe", "act", "dve", "pool", "sp")}
        self.esem = {k: nc.alloc_semaphore("es_" + k) for k in self.COMPUTE}
        self.ecnt = {k: 0 for k in self.COMPUTE}
        self.dsem = {q: [nc.alloc_semaphore("ds_%s%d" % (q, i)) for i in range(self.NDMA)]
                     for q in ("sp", "pool")}
        self.dcnt = {q: [0] * self.NDMA for q in ("sp", "pool")}
        self.dnext = {q: 0 for q in ("sp", "pool")}
        self.known = {k: {} for k in self.streams}
        self.nwaits = 0
        self.nops = 0

    def _sem_of(self, pid):
        if pid[0] == "e":
            return self.esem[pid[1]], pid[2], ("e", pid[1])
        return self.dsem[pid[1]][pid[2]], pid[3], ("d", pid[1], pid[2])

    def _wait(self, eng, pid):
        sem, val, key = self._sem_of(pid)
        if self.known[eng].get(key, 0) >= val:
            return
        self.known[eng][key] = val
        self.nwaits += 1
        self.streams[eng].append(lambda e, sem=sem, val=val: e.wait_ge(sem, val))

    def _deps(self, eng, reads, writes):
        deps = []
        for r in reads:
            if r.w is not None:
                deps.append(r.w)
        for w in writes:
            if w.w is not None:
                deps.append(w.w)
            deps.extend(w.r.values())
        for pid in deps:
            if pid[0] == "e" and pid[1] == eng and eng in self.INORDER:
                continue
            self._wait(eng, pid)

    def _commit(self, pid, reads, writes):
        key = pid[:2] if pid[0] == "e" else pid[:3]
        for r in reads:
            r.r[key] = pid
        for w in writes:
            w.w = pid
            w.r = {}

    def op(self, eng, meth, reads=(), writes=(), **kw):
        fn = lambda e, meth=meth, kw=kw: getattr(e, meth)(**kw)
        self._deps(eng, reads, writes)
        self.ecnt[eng] += 1
        idx = self.ecnt[eng]
        sem = self.esem[eng]
        self.streams[eng].append(lambda e, fn=fn, sem=sem: fn(e).then_inc(sem, 1))
        self.nops += 1
        self._commit(("e", eng, idx), reads, writes)

    def dma(self, q, reads=(), writes=(), meth="dma_start", **kw):
        fn = lambda e, meth=meth, kw=kw: getattr(e, meth)(**kw)
        i = self.dnext[q]
        self.dnext[q] = (i + 1) % self.NDMA
        prev = self.dcnt[q][i]
        if prev > 0:
            self._wait(q, ("d", q, i, prev))
        self._deps(q, reads, writes)
        self.dcnt[q][i] = prev + 16
        sem = self.dsem[q][i]
        self.streams[q].append(lambda e, fn=fn, sem=sem: fn(e).then_inc(sem, 16))
        self.nops += 1
        self._commit(("d", q, i, prev + 16), reads, writes)

    def barrier(self):
        for eng in self.streams:
            for k in self.COMPUTE:
                if k != eng and self.ecnt[k] > 0:
                    self._wait(eng, ("e", k, self.ecnt[k]))
            for q in self.dsem:
                for i in range(self.NDMA):
                    if self.dcnt[q][i] > 0:
                        self._wait(eng, ("d", q, i, self.dcnt[q][i]))

    def emit(self):
        nc = self.nc
        with nc.Block() as block:
            @block.tensor
            def _(e):
                for it in self.streams["pe"]:
                    it(e)

            @block.scalar
            def _(e):
                for it in self.streams["act"]:
                    it(e)

            @block.vector
            def _(e):
                for it in self.streams["dve"]:
                    it(e)

            @block.gpsimd
            def _(e):
                for it in self.streams["pool"]:
                    it(e)

            @block.sync
            def _(e):
                for it in self.streams["sp"]:
                    it(e)


class Arena:
    def __init__(self, nc):
        self.nc = nc
        self.off = (nc.sbuf_base + 31) // 32 * 32
        self.top = nc.sbuf_top
        self.n = 0

    def alloc(self, shape, dt=F32, name="t"):
        sz = {F32: 4, BF16: 2, I32: 4, U32: 4}[dt]
        nb = int(np.prod(shape[1:])) * sz
        off = self.off
        self.off += (nb + 31) // 32 * 32
        assert self.off <= self.top, "SBUF overflow at %s: %d > %d" % (name, self.off, self.top)
        self.n += 1
        return self.nc.alloc_sbuf_tensor_at("%s_%d" % (name, self.n), list(shape), dt, offset=off)

    def mark(self):
        return self.off

    def release(self, m):
        self.off = m


def build(stop_after=None):
    nc = bass.Bass("TRN2", target_bir_lowering=False)
    S = Sched(nc)
    A = Arena(nc)

    def din(name, shape, dt=F32):
        return nc.dram_tensor(name, list(shape), dt, kind="ExternalInput").ap()

    xw_d = din("xw", [WIN, D])
    flag_d = din("flag", [128, 1])
    mem_d = din("mem", [256, D])
    cst_d = din("cst", [128, NCONST])
    mtab_d = din("mtab", [128, 17 * 128])
    w_in_d = din("w_in", [D, IN_COLS])
    convw_d = din("convw", [128, 72])
    convb_d = din("convb", [128, 18])
    vec_d = {}
    for nm, n in [("mix_norm_g", 1024), ("dt_bias", 20), ("a_log", 20), ("d_skip", 20), ("ssd_norm_g", 1280),
                  ("attn_q_norm_g", 64), ("attn_k_norm_g", 64), ("xattn_norm_g", 1024), ("mem_norm_g", 1024),
                  ("mem_q_norm_g", 128), ("mem_k_norm_g", 128), ("ffn_norm_g", 1024)]:
        vec_d[nm] = din(nm, [1, n])
    w_out_d = din("w_out", [2048, D])
    mem_w_q_d = din("mem_w_q", [D, 512])
    mem_w_kv_d = din("mem_w_kv", [D, 1024])
    mem_w_o_d = din("mem_w_o", [512, D])
    peer_wq_d = din("peer_w_query", [D, 2048])
    keys1T_d = din("keys1T", [128, 128])
    keys2T_d = din("keys2T", [128, 128])
    peer_u_d = din("peer_u", [16384, D])
    peer_v_d = din("peer_v", [16384, D])
    out_d = nc.dram_tensor("out", [HALF, D], F32, kind="ExternalOutput").ap()
    yT_scr = nc.dram_tensor("yT_scr", [128, 10, HALF], BF16).ap()
    oT_scr = nc.dram_tensor("oT_scr", [128, 6, HALF], BF16).ap()
    uv_scr = nc.dram_tensor("uv_scr", [16384, 2048], BF16).ap()
    Ruv = Res()
    conv_jobs = [(tbl, h) for h in range(32) for tbl in (0, 1)]

    def issue_conv(n):
        for _ in range(n):
            if not conv_jobs or stop_after is not None:
                return
            tbl, h = conv_jobs.pop(0)
            src = peer_u_d if tbl == 0 else peer_v_d
            S.dma("pool", writes=[Ruv], out=uv_scr[h * 512:(h + 1) * 512, tbl * 1024:(tbl + 1) * 1024],
                  in_=src[h * 512:(h + 1) * 512, :])

    PS = [nc.alloc_psum_tensor("ps%d" % i, [128, 512], F32) for i in range(7)]
    RPS = [Res() for _ in range(7)]
    PB = nc.alloc_psum_tensor("psb", [128, 1024], BF16)
    RPB = Res()

    def T(shape, dt=F32, name="t"):
        return A.alloc(shape, dt, name), Res()

    def v3(ap, a):
        return ap.rearrange("p (a b) -> p a b", a=a)

    cst, Rcst = T([128, NCONST], F32, "cst")
    S.dma("sp", writes=[Rcst], out=cst[:], in_=cst_d)
    ident_f = cst[:, 0:128]
    tri_f = cst[:, 128:256]
    ones_f = cst[:, 256:384]
    d0_f = cst[:, 384:512]
    iota16 = cst[:, 512:528]
    ident_b, Ridb = T([128, 128], BF16, "identb")
    S.op("dve", "tensor_copy", reads=[Rcst], writes=[Ridb], out=ident_b[:], in_=ident_f)
    flag, Rflag = T([128, 1], F32, "flag")
    S.dma("sp", writes=[Rflag], out=flag[:], in_=flag_d)
    epsT, Reps = T([128, 1], F32, "eps")
    S.op("pool", "memset", writes=[Reps], ap=epsT[:], constant=EPS)

    def bc_load(n, name):
        t, R = T([128, n], F32, name)
        S.dma("sp", writes=[R], out=t[:], in_=vec_d[name].broadcast_to([128, n]))
        return t, R

    def rstd_from(ss_ap, Rss, n, H, out_ap, Rout, lnv_ap, Rln):
        Rss_l = Rss if isinstance(Rss, list) else [Rss]
        S.op("act", "activation", reads=Rss_l + [Reps], writes=[Rln], out=lnv_ap, in_=ss_ap, func=AF.Ln,
             bias=epsT[:, 0:1], scale=1.0 / n)
        S.op("act", "activation", reads=[Rln], writes=[Rout], out=out_ap, in_=lnv_ap, func=AF.Exp, scale=-0.5)

    nrm_junk, Rnj = T([128, 1024], BF16, "nrmjunk")
    nrm_ss, Rnss = T([128, 4], F32, "nrmss")
    nrm_ln, Rnln = T([128, 4], F32, "nrmln")
    nrm_rs, Rnrs = T([128, 4], F32, "nrmrs")

    def rmsnorm(x_ap, Rx, g_bc, Rg, out_ap, Rout):
        S.op("act", "activation", reads=[Rx], writes=[Rnj, Rnss], out=nrm_junk[:], in_=x_ap, func=AF.Square,
             accum_out=nrm_ss[:, 0:1])
        rstd_from(nrm_ss[:, 0:1], Rnss, 1024, 1, nrm_rs[:, 0:1], Rnrs, nrm_ln[:, 0:1], Rnln)
        S.op("dve", "scalar_tensor_tensor", reads=[Rx, Rnrs, Rg], writes=[Rout], out=out_ap, in0=x_ap,
             scalar=nrm_rs[:, 0:1], in1=g_bc[:], op0=ALU.mult, op1=ALU.mult)

    def transposeN(src_bf, Rsrc, n, width, dst3, Rdst, eng="act"):
        for k0 in range(0, n, 8):
            m = min(8, n - k0)
            for k in range(m):
                S.op("pe", "transpose", reads=[Rsrc, Ridb], writes=[RPB], out=PB[0:width, k * 128:(k + 1) * 128],
                     in_=src_bf[:, (k0 + k) * width:(k0 + k + 1) * width], identity=ident_b[:])
            src = v3(PB[0:width, 0:m * 128], m)
            dst = dst3[:, k0:k0 + m, :]
            if eng == "act":
                S.op("act", "copy", reads=[RPB], writes=Rdst, out=dst, in_=src)
            else:
                S.op("dve", "tensor_copy", reads=[RPB], writes=Rdst, out=dst, in_=src)

    qk_sq, Rqsq = T([128, 512], F32, "qksq")
    qk_tmp, Rqtmp = T([128, 512], F32, "qktmp")

    def qknorm(ps_ap, Rps, H, dh, g_bc, Rg, out_bf, Rout):
        n = H * dh
        S.op("act", "activation", reads=[Rps], writes=[Rqsq], out=qk_sq[:, 0:n], in_=ps_ap, func=AF.Square)
        S.op("dve", "tensor_reduce", reads=[Rqsq], writes=[Rnss], out=nrm_ss[:, 0:H], in_=v3(qk_sq[:, 0:n], H),
             axis=AX.X, op=ALU.add)
        rstd_from(nrm_ss[:, 0:H], Rnss, dh, H, nrm_rs[:, 0:H], Rnrs, nrm_ln[:, 0:H], Rnln)
        S.op("dve", "tensor_tensor", reads=[Rps, Rnrs], writes=[Rqtmp], out=v3(qk_tmp[:, 0:n], H), in0=v3(ps_ap, H),
             in1=nrm_rs[:, 0:H].unsqueeze(2).broadcast_to([128, H, dh]), op=ALU.mult)
        S.op("dve", "tensor_tensor", reads=[Rqtmp, Rg], writes=[Rout], out=out_bf, in0=qk_tmp[:, 0:n], in1=g_bc,
             op=ALU.mult)

    m_base = A.mark()
    hT, _ = T([128, 8, WIN], BF16, "hT")
    RhT = [Res() for _ in range(32)]
    m_after_hT = A.mark()

    gmix, Rgmix = bc_load(1024, "mix_norm_g")
    xb = [T([128, 1024], F32, "xb") for _ in range(2)]
    hb = [T([128, 1024], BF16, "hb") for _ in range(2)]
    for blk in range(32):
        x_t, Rx = xb[blk % 2]
        h_t, Rh = hb[blk % 2]
        S.dma("sp", writes=[Rx], out=x_t[:], in_=xw_d[blk * 128:(blk + 1) * 128, :])
        rmsnorm(x_t[:], Rx, gmix, Rgmix, h_t[:], Rh)
        transposeN(h_t, Rh, 8, 128, hT[:, :, blk * 128:(blk + 1) * 128], [RhT[blk]], eng="act" if blk % 2 else "dve")
    S.barrier()
    A.release(m_after_hT)

    wxbc, _ = T([128, 8, 2304], BF16, "wxbc")
    Rwxbc = [Res() for _ in range(8)]
    wz, _ = T([128, 8, 1280], BF16, "wz")
    Rwz = [Res() for _ in range(8)]
    wdt, _ = T([128, 8, 20], BF16, "wdt")
    Rwdt = [Res() for _ in range(8)]
    for k in range(8):
        S.dma("pool", writes=[Rwxbc[k]], out=wxbc[:, k, :], in_=w_in_d[k * 128:(k + 1) * 128, OFF_Z:OFF_XBC])
        S.dma("pool", writes=[Rwz[k]], out=wz[:, k, :], in_=w_in_d[k * 128:(k + 1) * 128, 0:OFF_Z])
        S.dma("pool", writes=[Rwdt[k]], out=wdt[:, k, :], in_=w_in_d[k * 128:(k + 1) * 128, OFF_XBC:OFF_DT])
    convw, Rcw = T([128, 18, 4], F32, "convw")
    convb, Rcb = T([128, 18], F32, "convb")
    S.dma("sp", writes=[Rcw], out=convw[:].rearrange("p a b -> p (a b)"), in_=convw_d)
    S.dma("sp", writes=[Rcb], out=convb[:], in_=convb_d)
    dtb, Rdtb = bc_load(20, "dt_bias")
    alog, Ralog = bc_load(20, "a_log")
    dsk, Rdsk = bc_load(20, "d_skip")
    gssd, Rgssd = bc_load(1280, "ssd_norm_g")
    a_bc, Ra = T([128, 20], F32, "a_bc")
    S.op("act", "activation", reads=[Ralog], writes=[Ra], out=a_bc[:], in_=alog[:], func=AF.Exp)
    S.op("dve", "tensor_scalar", reads=[Ra], writes=[Ra], out=a_bc[:], in0=a_bc[:], scalar1=-1.0, scalar2=None,
         op0=ALU.mult)
    halo, _ = T([128, 18, 3], BF16, "halo")
    Rhalo = [Res() for _ in range(18)]
    S.op("pool", "memset", writes=Rhalo, ap=halo[:], constant=0.0)
    state, Rst = T([128, 1280], F32, "state")
    state_bf, Rstb = T([128, 1280], BF16, "statebf")
    S.op("pool", "memset", writes=[Rst], ap=state[:], constant=0.0)
    S.op("pool", "memset", writes=[Rstb], ap=state_bf[:], constant=0.0)
    U = [T([128, 259], BF16, "U") for _ in range(2)]
    DGc = [(T([128, 4, 128], BF16, "DGc")[0], [Res() for _ in range(4)]) for _ in range(2)]
    xTf = [T([128, 256], F32, "xTf") for _ in range(2)]
    x_tm, Rxtm = T([128, 2, 1280], F32, "x_tm")
    BT, _ = T([128, 4, 256], BF16, "BT")
    RBTg = [Res() for _ in range(4)]
    CTt, _ = T([128, 4, 256], BF16, "CT")
    RCTg = [Res() for _ in range(4)]
    B_tm, RBtm = T([128, 2, 512], BF16, "B_tm")
    dt_tm, Rdt = T([128, 2, 20], F32, "dt_tm")
    adt, Radt = T([128, 2, 20], F32, "adt")
    ncs, Rncs = T([128, 20], F32, "ncs")
    tmp20, Rt20 = T([128, 20], F32, "tmp20")
    dec, Rdec = T([128, 20], F32, "dec")
    cd, Rcd = T([128, 20], F32, "cd")
    w2, Rw2 = T([128, 20], F32, "w2")
    ecs, Recs = T([128, 20], F32, "ecs")
    xdd, Rxdd = T([128, 1280], BF16, "xdd")
    xdt, Rxdt = T([128, 1280], BF16, "xdt")
    CBm, RCBm = T([128, 4, 128], F32, "CBm")
    adtb2 = [T([128, 4, 128], F32, "adtb") for _ in range(2)]
    Lt = [T([128, 4, 128], F32, "Lt") for _ in range(2)]
    Wm, RWm = T([128, 20, 128], BF16, "Wm")
    yg4, _ = T([128, 4, 320], F32, "yg4")
    Ryg4 = [Res() for _ in range(4)]
    sz4, _ = T([128, 4, 320], F32, "sz4")
    Rsz4 = [Res() for _ in range(4)]
    sqj, _ = T([128, 320], BF16, "sqj")
    ss4, _ = T([128, 4], F32, "ss4")
    Rss4 = [Res() for _ in range(4)]
    ln4, Rln4 = T([128, 4], F32, "ln4")
    rs4, Rrs4 = T([128, 4], F32, "rs4")
    ytmp, Rytmp = T([128, 320], F32, "ytmp")
    ss1, Rss1 = T([128, 1], F32, "ss1")
    ln1, Rln1 = T([128, 1], F32, "ln1")
    rs1, Rrs1 = T([128, 1], F32, "rs1")
    yn, Ryn = T([128, 1280], BF16, "yn")
    yTc = [T([128, 10, 128], BF16, "yTc") for _ in range(2)]

    for G in range(16):
        own = G >= 8
        RhG = RhT[G * 2:(G + 1) * 2]
        tokG = slice(G * 256, (G + 1) * 256)
        def cc_s1(cc):
            b = cc % 2
            for k in range(8):
                S.op("pe", "matmul", reads=[Rwxbc[k]] + RhG, writes=[RPS[b]], out=PS[b][:, 0:256],
                     lhsT=wxbc[:, k, cc * 128:(cc + 1) * 128], rhs=hT[:, k, tokG], start=(k == 0), stop=(k == 7))
            u_t, Ru = U[b]
            S.op("act", "copy", reads=[RPS[b]], writes=[Ru], out=u_t[:, 3:259], in_=PS[b][:, 0:256])
            S.op("act", "copy", reads=[Rhalo[cc]], writes=[Ru], out=u_t[:, 0:3], in_=halo[:, cc, :])
            S.op("act", "copy", reads=[Ru], writes=[Rhalo[cc]], out=halo[:, cc, :], in_=u_t[:, 256:259])
            dg, Rdgc = DGc[b]
            for j in range(4):
                S.op("dve", "tensor_scalar", reads=[Ridb, Rcw], writes=[Rdgc[j]], out=dg[:, j, :], in0=ident_b[:],
                     scalar1=convw[:, cc, j:j + 1], scalar2=None, op0=ALU.mult)

        def cc_s2(cc):
            b = cc % 2
            u_t, Ru = U[b]
            dg, Rdgc = DGc[b]
            pc = 5 + b
            for j in range(4):
                S.op("pe", "matmul", reads=[Rdgc[j], Ru], writes=[RPS[pc]], out=PS[pc][:, 0:256], lhsT=dg[:, j, :],
                     rhs=u_t[:, j:j + 256], start=(j == 0), stop=(j == 3))
            if cc < 10:
                xt, Rxt = xTf[b]
                S.op("act", "activation", reads=[RPS[pc], Rcb], writes=[Rxt], out=xt[:], in_=PS[pc][:, 0:256], func=AF.Silu,
                     bias=convb[:, cc:cc + 1])
            elif cc < 14:
                g = cc - 10
                S.op("act", "activation", reads=[RPS[pc], Rcb], writes=[RBTg[g]], out=BT[:, g, :], in_=PS[pc][:, 0:256],
                     func=AF.Silu, bias=convb[:, cc:cc + 1])
            else:
                g = cc - 14
                S.op("act", "activation", reads=[RPS[pc], Rcb], writes=[RCTg[g]], out=CTt[:, g, :], in_=PS[pc][:, 0:256],
                     func=AF.Silu, bias=convb[:, cc:cc + 1])

        def cc_tr(cc):
            b = cc % 2
            if cc < 10:
                xt, Rxt = xTf[b]
                pb = 2 + b
                for tb in range(2):
                    S.op("pe", "transpose", reads=[Rxt, Rcst], writes=[RPS[pb]], out=PS[pb][:, tb * 128:(tb + 1) * 128],
                         in_=xt[:, tb * 128:(tb + 1) * 128], identity=ident_f)
                S.op("dve", "tensor_copy", reads=[RPS[pb]], writes=[Rxtm], out=x_tm[:, :, cc * 128:(cc + 1) * 128],
                     in_=v3(PS[pb][:, 0:256], 2))
            elif cc < 14:
                g = cc - 10
                for tb in range(2):
                    S.op("pe", "transpose", reads=[RBTg[g], Ridb], writes=[RPB], out=PB[:, tb * 128:(tb + 1) * 128],
                         in_=BT[:, g, tb * 128:(tb + 1) * 128], identity=ident_b[:])
                S.op("dve", "tensor_copy", reads=[RPB], writes=[RBtm], out=B_tm[:, :, g * 128:(g + 1) * 128],
                     in_=v3(PB[:, 0:256], 2))

        cc_s1(0)
        for cc in range(19):
            if cc + 1 < 18:
                cc_s1(cc + 1)
            if cc < 18:
                cc_s2(cc)
            if cc >= 1:
                cc_tr(cc - 1)
        issue_conv(4)
        for tb in range(2):
            blk = G * 2 + tb
            for k in range(8):
                S.op("pe", "matmul", reads=[Rwdt[k], RhT[blk]], writes=[RPS[4]], out=PS[4][:, tb * 20:(tb + 1) * 20],
                     lhsT=hT[:, k, blk * 128:(blk + 1) * 128], rhs=wdt[:, k, :], start=(k == 0), stop=(k == 7))
        S.op("dve", "tensor_tensor", reads=[RPS[4], Rdtb], writes=[Rdt], out=dt_tm[:], in0=v3(PS[4][:, 0:40], 2),
             in1=dtb[:].unsqueeze(1).broadcast_to([128, 2, 20]), op=ALU.add)
        S.op("act", "activation", reads=[Rdt], writes=[Rdt], out=dt_tm[:], in_=dt_tm[:], func=AF.Exp)
        S.op("act", "activation", reads=[Rdt], writes=[Rdt], out=dt_tm[:], in_=dt_tm[:], func=AF.Ln, bias=1.0)
        S.op("dve", "tensor_tensor", reads=[Rdt, Ra], writes=[Radt], out=adt[:], in0=dt_tm[:],
             in1=a_bc[:].unsqueeze(1).broadcast_to([128, 2, 20]), op=ALU.mult)
        for tb in range(2):
            c = G * 2 + tb
            tk = slice(tb * 128, (tb + 1) * 128)
            S.op("pe", "matmul", reads=[Rcst, Radt], writes=[RPS[5]], out=PS[5][:, 0:20], lhsT=tri_f, rhs=adt[:, tb, :],
                 start=True, stop=True)
            S.op("pe", "matmul", reads=[Rcst, Radt], writes=[RPS[5]], out=PS[5][:, 32:52], lhsT=ones_f, rhs=adt[:, tb, :],
                 start=True, stop=True)
            if own:
                for g in range(4):
                    pz = 6 if g % 2 == 0 else 4
                    for k in range(8):
                        S.op("pe", "matmul", reads=[Rwz[k], RhT[c]], writes=[RPS[pz]], out=PS[pz][:, 0:320],
                             lhsT=hT[:, k, c * 128:(c + 1) * 128], rhs=wz[:, k, g * 320:(g + 1) * 320], start=(k == 0),
                             stop=(k == 7))
                    S.op("act", "activation", reads=[RPS[pz]], writes=[Rsz4[g]], out=sz4[:, g, :], in_=PS[pz][:, 0:320],
                         func=AF.Silu)
            S.op("dve", "tensor_scalar", reads=[RPS[5]], writes=[Rncs], out=ncs[:], in0=PS[5][:, 0:20], scalar1=-1.0,
                 scalar2=None, op0=ALU.mult)
            S.op("dve", "tensor_tensor", reads=[RPS[5], Rncs], writes=[Rt20], out=tmp20[:], in0=PS[5][:, 32:52],
                 in1=ncs[:], op=ALU.add)
            S.op("act", "activation", reads=[Rt20], writes=[Rdec], out=dec[:], in_=tmp20[:], func=AF.Exp)
            S.op("act", "activation", reads=[RPS[5]], writes=[Rcd], out=cd[:], in_=PS[5][:, 32:52], func=AF.Exp)
            S.op("dve", "tensor_tensor", reads=[Rdt, Rdec], writes=[Rw2], out=w2[:], in0=dt_tm[:, tb, :], in1=dec[:],
                 op=ALU.mult)
            S.op("dve", "tensor_tensor", reads=[Rxtm, Rw2], writes=[Rxdd], out=v3(xdd[:], 20),
                 in0=v3(x_tm[:, tb, :], 20), in1=w2[:].unsqueeze(2).broadcast_to([128, 20, 64]), op=ALU.mult)
            if own:
                blk = c
                S.op("act", "activation", reads=[Rncs], writes=[Recs], out=ecs[:], in_=ncs[:], func=AF.Exp, scale=-1.0)
                S.op("dve", "tensor_tensor", reads=[Rxtm, Rdt], writes=[Rxdt], out=v3(xdt[:], 20),
                     in0=v3(x_tm[:, tb, :], 20), in1=dt_tm[:, tb, :].unsqueeze(2).broadcast_to([128, 20, 64]),
                     op=ALU.mult)
                for g in range(4):
                    S.op("pe", "matmul", reads=[RBTg[g], RCTg[g]], writes=[RPS[2]], out=PS[2][:, g * 128:(g + 1) * 128],
                         lhsT=BT[:, g, tk], rhs=CTt[:, g, tk], start=True, stop=True)
                S.op("dve", "tensor_tensor", reads=[RPS[2], Rcst], writes=[RCBm], out=CBm[:], in0=v3(PS[2][:, :], 4),
                     in1=tri_f.unsqueeze(1).broadcast_to([128, 4, 128]), op=ALU.mult)
                def hq_copy(hq):
                    adtb, Radtb = adtb2[hq % 2]
                    S.op("dve", "tensor_copy", reads=[Radt], writes=[Radtb], out=adtb[:],
                         in_=adt[:, tb, hq * 4:hq * 4 + 4].unsqueeze(2).broadcast_to([128, 4, 128]))
                    pb = 3 + hq % 2
                    for i in range(4):
                        S.op("pe", "matmul", reads=[Radtb, Rcst], writes=[RPS[pb]], out=PS[pb][:, i * 128:(i + 1) * 128],
                             lhsT=adtb[:, i, :], rhs=tri_f, start=True, stop=True)

                def hq_rest(hq):
                    pb = 3 + hq % 2
                    lt, Rlt = Lt[hq % 2]
                    S.op("dve", "tensor_tensor", reads=[RPS[pb], Rncs], writes=[Rlt], out=lt[:], in0=v3(PS[pb][:, :], 4),
                         in1=ncs[:, hq * 4:hq * 4 + 4].unsqueeze(2).broadcast_to([128, 4, 128]), op=ALU.add)
                    S.op("act", "activation", reads=[Rlt], writes=[Rlt], out=lt[:], in_=lt[:], func=AF.Exp)
                    for i in range(4):
                        h = hq * 4 + i
                        S.op("dve", "scalar_tensor_tensor", reads=[Rlt, RCBm], writes=[RWm],
                             out=Wm[:, h, :], in0=lt[:, i, :], scalar=1.0, in1=CBm[:, h // 5, :], op0=ALU.min,
                             op1=ALU.mult)

                hq_copy(0)
                for hq in range(5):
                    if hq + 1 < 5:
                        hq_copy(hq + 1)
                    hq_rest(hq)
                for g in range(4):
                    gs = slice(g * 320, (g + 1) * 320)
                    po = 0 if g % 2 == 0 else 2
                    pd = 1 if g % 2 == 0 else 3
                    S.op("pe", "matmul", reads=[RCTg[g], Rstb], writes=[RPS[po]], out=PS[po][:, 0:320], lhsT=CTt[:, g, tk],
                         rhs=state_bf[:, gs], start=True, stop=True)
                    for hh in range(5):
                        h = g * 5 + hh
                        S.op("pe", "matmul", reads=[RWm, Rxdt], writes=[RPS[pd]], out=PS[pd][:, hh * 64:(hh + 1) * 64],
                             lhsT=Wm[:, h, :], rhs=xdt[:, h * 64:(h + 1) * 64], start=True, stop=True)
                    ygg = yg4[:, g, :]
                    S.op("dve", "tensor_tensor", reads=[RPS[po], Recs], writes=[Ryg4[g]], out=v3(ygg, 5),
                         in0=v3(PS[po][:, 0:320], 5), in1=ecs[:, g * 5:g * 5 + 5].unsqueeze(2).broadcast_to([128, 5, 64]),
                         op=ALU.mult)
                    S.op("dve", "tensor_tensor", reads=[RPS[pd], Ryg4[g]], writes=[Ryg4[g]], out=ygg, in0=ygg,
                         in1=PS[pd][:, 0:320], op=ALU.add)
                    S.op("dve", "tensor_tensor", reads=[Rxtm, Rdsk], writes=[Rytmp], out=v3(ytmp[:], 5),
                         in0=v3(x_tm[:, tb, gs], 5), in1=dsk[:, g * 5:g * 5 + 5].unsqueeze(2).broadcast_to([128, 5, 64]),
                         op=ALU.mult)
                    S.op("dve", "tensor_tensor", reads=[Rytmp, Ryg4[g]], writes=[Ryg4[g]], out=ygg, in0=ygg, in1=ytmp[:],
                         op=ALU.add)
                    S.op("dve", "tensor_tensor", reads=[Ryg4[g], Rsz4[g]], writes=[Ryg4[g]], out=ygg, in0=ygg,
                         in1=sz4[:, g, :], op=ALU.mult)
                    S.op("act", "activation", reads=[Ryg4[g]], writes=[Rss4[g]], out=sqj[:], in_=ygg, func=AF.Square,
                         accum_out=ss4[:, g:g + 1])
                rstd_from(ss4[:, 0:4], Rss4, 320, 4, rs4[:, 0:4], Rrs4, ln4[:, 0:4], Rln4)
                for g in range(4):
                    gs = slice(g * 320, (g + 1) * 320)
                    S.op("dve", "scalar_tensor_tensor", reads=[Ryg4[g], Rrs4, Rgssd], writes=[Ryn], out=yn[:, gs],
                         in0=yg4[:, g, :], scalar=rs4[:, g:g + 1], in1=gssd[:, gs], op0=ALU.mult, op1=ALU.mult)
                yt, Ryt = yTc[tb % 2]
                transposeN(yn, Ryn, 10, 128, yt[:], [Ryt], eng="act")
                ob = c - 16
                S.dma("sp", reads=[Ryt], writes=[Res()], out=yT_scr[:, :, ob * 128:(ob + 1) * 128], in_=yt[:])
            for g in range(4):
                gs = slice(g * 320, (g + 1) * 320)
                S.op("pe", "matmul", reads=[RBtm, Rxdd], writes=[RPS[6]], out=PS[6][:, 0:320],
                     lhsT=B_tm[:, tb, g * 128:(g + 1) * 128], rhs=xdd[:, gs], start=True, stop=True)
                S.op("dve", "tensor_tensor", reads=[Rst, Rcd], writes=[Rst], out=v3(state[:, gs], 5),
                     in0=v3(state[:, gs], 5), in1=cd[:, g * 5:g * 5 + 5].unsqueeze(2).broadcast_to([128, 5, 64]),
                     op=ALU.mult)
                S.op("dve", "tensor_tensor", reads=[Rst, RPS[6]], writes=[Rst], out=state[:, gs], in0=state[:, gs],
                     in1=PS[6][:, 0:320], op=ALU.add)
            if c == 15:
                S.op("dve", "tensor_scalar", reads=[Rst, Rflag], writes=[Rst], out=state[:], in0=state[:],
                     scalar1=flag[:, 0:1], scalar2=None, op0=ALU.mult)
            if c >= 15:
                S.op("act", "copy", reads=[Rst], writes=[Rstb], out=state_bf[:], in_=state[:])
    S.barrier()
    A.release(m_after_hT)

    mtab, Rmtab = T([128, 17, 128], BF16, "mtab")
    S.dma("pool", writes=[Rmtab], out=mtab[:].rearrange("p a b -> p (a b)"), in_=mtab_d)
    gq64, Rgq64 = bc_load(64, "attn_q_norm_g")
    gk64, Rgk64 = bc_load(64, "attn_k_norm_g")
    gq, Rgq = T([128, 256], F32, "gq")
    gk, Rgk = T([128, 256], F32, "gk")
    S.op("dve", "tensor_scalar", reads=[Rgq64], writes=[Rgq], out=v3(gq[:], 4),
         in0=gq64[:].unsqueeze(1).broadcast_to([128, 4, 64]), scalar1=0.125, scalar2=None, op0=ALU.mult)
    S.op("dve", "tensor_copy", reads=[Rgk64], writes=[Rgk], out=v3(gk[:], 4),
         in_=gk64[:].unsqueeze(1).broadcast_to([128, 4, 64]))
    wq, _ = T([128, 8, 256], BF16, "wq")
    Rwq = [Res() for _ in range(8)]
    wkv_, _ = T([128, 8, 512], BF16, "wkv_")
    Rwkv_ = [Res() for _ in range(8)]
    KT, _ = T([64, 4, WIN], BF16, "KT")
    RKT = [Res() for _ in range(32)]
    QT, _ = T([64, 4, HALF], BF16, "QT")
    RQT = [Res() for _ in range(16)]
    Vaug, _ = T([128, 32, 4, 65], BF16, "Vaug")
    RV = [Res() for _ in range(32)]
    kn = [T([128, 256], BF16, "kn") for _ in range(2)]
    qn_ = [T([128, 256], BF16, "qn_") for _ in range(2)]
    Aexp, RAexp = T([128, 128], F32, "Aexp")
    Erev, REr = T([128, 17, 128], BF16, "Erev")
    Pf = [T([128, 512], BF16, "Pf") for _ in range(3)]
    Pbf = [T([128, 512], BF16, "Pbf") for _ in range(3)]
    rd, Rrd = T([128, 2], F32, "rd")
    o_all, _ = T([128, 16, 256], BF16, "o_all")
    Roall = [Res() for _ in range(16)]
    oTc = [T([128, 2, 128], BF16, "oTc") for _ in range(2)]
    for r in range(3):
        for k in range(8):
            rows = slice(k * 128, (k + 1) * 128)
            S.dma("pool", writes=[Rwq[k]], out=wq[:, k, :], in_=w_in_d[rows, OFF_DT + r * 256:OFF_DT + (r + 1) * 256])
            S.dma("pool", writes=[Rwkv_[k]], out=wkv_[:, k, 0:256], in_=w_in_d[rows, OFF_Q + r * 256:OFF_Q + (r + 1) * 256])
            S.dma("pool", writes=[Rwkv_[k]], out=wkv_[:, k, 256:512], in_=w_in_d[rows, OFF_K + r * 256:OFF_K + (r + 1) * 256])

        def proj_mm(blk):
            bs = slice(blk * 128, (blk + 1) * 128)
            pa = blk % 2
            for k in range(8):
                S.op("pe", "matmul", reads=[Rwkv_[k], RhT[blk]], writes=[RPS[pa]], out=PS[pa][:, :], lhsT=hT[:, k, bs],
                     rhs=wkv_[:, k, :], start=(k == 0), stop=(k == 7))
            kt, Rkn = kn[blk % 2]
            qknorm(PS[pa][:, 0:256], RPS[pa], 4, 64, gk[:], Rgk, kt[:], Rkn)
            if blk >= 16:
                S.op("act", "copy", reads=[RPS[pa]], writes=[RV[blk]], out=Vaug[:, blk, :, 0:64],
                     in_=v3(PS[pa][:, 256:512], 4))
                S.op("pool", "memset", writes=[RV[blk]], ap=Vaug[:, blk, :, 64:65], constant=1.0)
            else:
                S.op("dve", "tensor_scalar", reads=[RPS[pa], Rflag], writes=[RV[blk]], out=Vaug[:, blk, :, 0:64],
                     in0=v3(PS[pa][:, 256:512], 4), scalar1=flag[:, 0:1], scalar2=None, op0=ALU.mult)
                S.op("pool", "tensor_copy", reads=[Rflag], writes=[RV[blk]], out=Vaug[:, blk, :, 64:65],
                     in_=flag[:, 0:1].unsqueeze(1).broadcast_to([128, 4, 1]))
            if blk >= 16:
                pq = 4 + blk % 2
                for k in range(8):
                    S.op("pe", "matmul", reads=[Rwq[k], RhT[blk]], writes=[RPS[pq]], out=PS[pq][:, 0:256], lhsT=hT[:, k, bs],
                         rhs=wq[:, k, :], start=(k == 0), stop=(k == 7))
                qt, Rqn = qn_[blk % 2]
                qknorm(PS[pq][:, 0:256], RPS[pq], 4, 64, gq[:], Rgq, qt[:], Rqn)

        def proj_tr(blk):
            bs = slice(blk * 128, (blk + 1) * 128)
            kt, Rkn = kn[blk % 2]
            transposeN(kt, Rkn, 4, 64, KT[:, :, bs], [RKT[blk]], eng="act")
            if blk >= 16:
                qt, Rqn = qn_[blk % 2]
                transposeN(qt, Rqn, 4, 64, QT[:, :, (blk - 16) * 128:(blk - 15) * 128], [RQT[blk - 16]], eng="act")

        for blk in range(33):
            if blk < 32:
                proj_mm(blk)
            if blk >= 1:
                proj_tr(blk - 1)

        seq = []
        for hh in range(4):
            for qb in range(16, 32):
                kbs = list(range(qb - 16, qb + 1))
                for gi in range(5):
                    seq.append((hh, qb, gi, kbs[gi * 4:gi * 4 + 4]))

        def emit_S(i):
            hh, qb, gi, grp = seq[i]
            qi = qb - 16
            sbk = i % 4
            for ii, kb in enumerate(grp):
                S.op("pe", "matmul", reads=[RKT[kb], RQT[qi]], writes=[RPS[sbk]],
                     out=PS[sbk][:, ii * 128:(ii + 1) * 128], lhsT=KT[:, hh, kb * 128:(kb + 1) * 128],
                     rhs=QT[:, hh, qi * 128:(qi + 1) * 128], start=True, stop=True)

        def emit_rest(i):
            hh, qb, gi, grp = seq[i]
            qi = qb - 16
            n = len(grp)
            sbk = i % 4
            unit = i // 5
            ob = 4 + unit % 2
            pf, Rpf = Pf[i % 3]
            pbf, Rpbf = Pbf[i % 3]
            if qb == 16 and gi == 0:
                h = r * 4 + hh
                slope = 2.0 ** (-8.0 * (h + 1) / 12.0)
                S.op("act", "activation", reads=[Rcst], writes=[RAexp], out=Aexp[:], in_=d0_f, func=AF.Exp, scale=-slope)
                for j in range(17):
                    cpow = math.exp(-slope * 128.0 * (16 - j))
                    S.op("dve", "scalar_tensor_tensor", reads=[RAexp, Rmtab], writes=[REr],
                         out=Erev[:, j, :], in0=Aexp[:], scalar=float(cpow), in1=mtab[:, j, :], op0=ALU.mult,
                         op1=ALU.mult)
            S.op("act", "activation", reads=[RPS[sbk]], writes=[Rpf], out=pf[:, 0:n * 128],
                 in_=PS[sbk][:, 0:n * 128], func=AF.Exp)
            S.op("dve", "tensor_tensor", reads=[Rpf, REr], writes=[Rpbf], out=pbf[:, 0:n * 128],
                 in0=pf[:, 0:n * 128], in1=Erev[:, gi * 4:gi * 4 + n, :].rearrange("p a b -> p (a b)"),
                 op=ALU.mult)
            for ii, kb in enumerate(grp):
                S.op("pe", "matmul", reads=[Rpbf, RV[kb]], writes=[RPS[ob]], out=PS[ob][:, 0:65],
                     lhsT=pbf[:, ii * 128:(ii + 1) * 128], rhs=Vaug[:, kb, hh, :], start=(gi == 0 and ii == 0),
                     stop=(gi == 4))
            if gi == 4:
                S.op("dve", "reciprocal", reads=[RPS[ob]], writes=[Rrd], out=rd[:, 0:1], in_=PS[ob][:, 64:65])
                S.op("dve", "tensor_scalar", reads=[RPS[ob], Rrd], writes=[Roall[qi]],
                     out=o_all[:, qi, hh * 64:(hh + 1) * 64], in0=PS[ob][:, 0:64], scalar1=rd[:, 0:1], scalar2=None,
                     op0=ALU.mult)

        LA = 3
        for i in range(LA):
            emit_S(i)
        for i in range(len(seq)):
            if i + LA < len(seq):
                emit_S(i + LA)
            emit_rest(i)
        for qi in range(16):
            for i in range(2):
                S.op("pe", "transpose", reads=[Roall[qi], Ridb], writes=[RPB], out=PB[:, i * 128:(i + 1) * 128],
                     in_=o_all[:, qi, i * 128:(i + 1) * 128], identity=ident_b[:])
            ot, Rot = oTc[qi % 2]
            S.op("act", "copy", reads=[RPB], writes=[Rot], out=ot[:], in_=v3(PB[:, 0:256], 2))
            S.dma("sp", reads=[Rot], writes=[Res()], out=oT_scr[:, 2 * r:2 * r + 2, qi * 128:(qi + 1) * 128], in_=ot[:])
    S.barrier()
    A.release(m_base)

    x1, _ = T([128, 16, 1024], F32, "x1")
    Rx1 = [Res() for _ in range(16)]
    m_after_x1 = A.mark()
    yT_sb, _ = T([128, 10, HALF], BF16, "yT_sb")
    RyT = [Res() for _ in range(10)]
    for kc in range(10):
        S.dma("sp", writes=[RyT[kc]], out=yT_sb[:, kc, :], in_=yT_scr[:, kc, :])
    oT, _ = T([128, 6, HALF], BF16, "oT")
    RoTs = [Res() for _ in range(6)]
    for kc in range(6):
        S.dma("sp", writes=[RoTs[kc]], out=oT[:, kc, :], in_=oT_scr[:, kc, :])
    wout, _ = T([128, 16, 1024], BF16, "wout")
    Rwout = [Res() for _ in range(16)]
    for kc in range(16):
        S.dma("pool", writes=[Rwout[kc]], out=wout[:, kc, :], in_=w_out_d[kc * 128:(kc + 1) * 128, :])
    xo = [T([128, 1024], F32, "xo") for _ in range(2)]
    for tb in range(16):
        ts_ = slice(tb * 128, (tb + 1) * 128)
        xo_t, Rxo = xo[tb % 2]
        S.dma("sp", writes=[Rxo], out=xo_t[:], in_=xw_d[HALF + tb * 128:HALF + (tb + 1) * 128, :])
        for half in range(2):
            pb = 2 * (tb % 2) + half
            cs_ = slice(half * 512, (half + 1) * 512)
            for kc in range(16):
                lhsT = yT_sb[:, kc, ts_] if kc < 10 else oT[:, kc - 10, ts_]
                S.op("pe", "matmul", reads=[RyT[kc] if kc < 10 else RoTs[kc - 10], Rwout[kc]], writes=[RPS[pb]], out=PS[pb][:, :], lhsT=lhsT,
                     rhs=wout[:, kc, cs_], start=(kc == 0), stop=(kc == 15))
            S.op("dve", "tensor_tensor", reads=[RPS[pb], Rxo], writes=[Rx1[tb]], out=x1[:, tb, cs_], in0=PS[pb][:, :],
                 in1=xo_t[:, cs_], op=ALU.add)
    S.barrier()
    A.release(m_after_x1)

    def store_x1_and_finish():
        fin = []
        for tb in range(16):
            R = Res()
            S.dma("sp", reads=[Rx1[tb]], writes=[R], out=out_d[tb * 128:(tb + 1) * 128, :], in_=x1[:, tb, :])
            fin.append(R)
        for R in fin:
            S._wait("sp", R.w)
        S.emit()
        return nc

    if stop_after == "C":
        return store_x1_and_finish()

    wqm, _ = T([128, 8, 512], BF16, "wqm")
    Rwqm = [Res() for _ in range(8)]
    wkv, _ = T([128, 8, 1024], BF16, "wkv")
    Rwkv = [Res() for _ in range(8)]
    wo, _ = T([128, 4, 1024], BF16, "wo")
    Rwo = [Res() for _ in range(4)]
    for k in range(8):
        S.dma("pool", writes=[Rwqm[k]], out=wqm[:, k, :], in_=mem_w_q_d[k * 128:(k + 1) * 128, :])
        S.dma("pool", writes=[Rwkv[k]], out=wkv[:, k, :], in_=mem_w_kv_d[k * 128:(k + 1) * 128, :])
    for k in range(4):
        S.dma("pool", writes=[Rwo[k]], out=wo[:, k, :], in_=mem_w_o_d[k * 128:(k + 1) * 128, :])
    gx, Rgx = bc_load(1024, "xattn_norm_g")
    gm, Rgm = bc_load(1024, "mem_norm_g")
    gqm128, Rgqm128 = bc_load(128, "mem_q_norm_g")
    gkm128, Rgkm128 = bc_load(128, "mem_k_norm_g")
    gqm, Rgqm = T([128, 512], F32, "gqm")
    gkm, Rgkm = T([128, 512], F32, "gkm")
    S.op("dve", "tensor_scalar", reads=[Rgqm128], writes=[Rgqm], out=v3(gqm[:], 4),
         in0=gqm128[:].unsqueeze(1).broadcast_to([128, 4, 128]), scalar1=128.0 ** -0.5, scalar2=None, op0=ALU.mult)
    S.op("dve", "tensor_copy", reads=[Rgkm128], writes=[Rgkm], out=v3(gkm[:], 4),
         in_=gkm128[:].unsqueeze(1).broadcast_to([128, 4, 128]))
    KmT, RKmT = T([128, 4, 256], BF16, "KmT")
    Vm, RVm = T([128, 2, 4, 129], BF16, "Vm")
    S.op("pool", "memset", writes=[RVm], ap=Vm[:].rearrange("p a b c -> p (a b c)"), constant=1.0)
    memb, Rmemb = T([128, 1024], F32, "memb")
    mh, Rmh = T([128, 1024], BF16, "mh")
    mT, RmT = T([128, 8, 128], BF16, "mT")
    knm, Rknm = T([128, 512], BF16, "knm")
    for mb in range(2):
        ms = slice(mb * 128, (mb + 1) * 128)
        S.dma("sp", writes=[Rmemb], out=memb[:], in_=mem_d[ms, :])
        rmsnorm(memb[:], Rmemb, gm, Rgm, mh[:], Rmh)
        transposeN(mh, Rmh, 8, 128, mT[:], [RmT], eng="act")
        for half in range(2):
            for k in range(8):
                S.op("pe", "matmul", reads=[RmT, Rwkv[k]], writes=[RPS[half]], out=PS[half][:, :], lhsT=mT[:, k, :],
                     rhs=wkv[:, k, half * 512:(half + 1) * 512], start=(k == 0), stop=(k == 7))
        qknorm(PS[0][:, :], RPS[0], 4, 128, gkm[:], Rgkm, knm[:], Rknm)
        transposeN(knm, Rknm, 4, 128, KmT[:, :, ms], [RKmT], eng="act")
        S.op("act", "copy", reads=[RPS[1]], writes=[RVm], out=Vm[:, mb, :, 0:128], in_=v3(PS[1][:, :], 4))
    h2p = [T([128, 1024], BF16, "h2") for _ in range(2)]
    h2Tp = [T([128, 8, 128], BF16, "h2T") for _ in range(2)]
    qnp = [T([128, 512], BF16, "qn") for _ in range(2)]
    qTp = [T([128, 4, 128], BF16, "qT") for _ in range(2)]
    Pm = [T([128, 256], BF16, "Pm") for _ in range(2)]
    o2, Ro2 = T([128, 512], BF16, "o2")
    o2T, Ro2T = T([128, 4, 128], BF16, "o2T")
    rd4, _ = T([128, 4], F32, "rd4")
    Rrd4 = [Res() for _ in range(4)]

    def d_front(tb):
        p = tb % 2
        h2, Rh2 = h2p[p]
        h2T, Rh2T = h2Tp[p]
        qn, Rqn2 = qnp[p]
        qT, RqT = qTp[p]
        rmsnorm(x1[:, tb, :], Rx1[tb], gx, Rgx, h2[:], Rh2)
        yield
        transposeN(h2, Rh2, 8, 128, h2T[:], [Rh2T], eng="act")
        yield
        for k in range(8):
            S.op("pe", "matmul", reads=[Rh2T, Rwqm[k]], writes=[RPS[2]], out=PS[2][:, :], lhsT=h2T[:, k, :], rhs=wqm[:, k, :],
                 start=(k == 0), stop=(k == 7))
        yield
        qknorm(PS[2][:, :], RPS[2], 4, 128, gqm[:], Rgqm, qn[:], Rqn2)
        yield
        transposeN(qn, Rqn2, 4, 128, qT[:], [RqT], eng="act")
        yield

    def d_back(tb):
        p = tb % 2
        qT, RqT = qTp[p]
        for hh in range(4):
            sb_ = 3 + hh % 2
            ob = 5 + hh % 2
            pm, Rpm = Pm[hh % 2]
            for mb in range(2):
                S.op("pe", "matmul", reads=[RKmT, RqT], writes=[RPS[sb_]], out=PS[sb_][:, mb * 128:(mb + 1) * 128],
                     lhsT=KmT[:, hh, mb * 128:(mb + 1) * 128], rhs=qT[:, hh, :], start=True, stop=True)
            S.op("act", "activation", reads=[RPS[sb_]], writes=[Rpm], out=pm[:], in_=PS[sb_][:, 0:256], func=AF.Exp)
            for mb in range(2):
                S.op("pe", "matmul", reads=[Rpm, RVm], writes=[RPS[ob]], out=PS[ob][:, 0:129],
                     lhsT=pm[:, mb * 128:(mb + 1) * 128], rhs=Vm[:, mb, hh, :], start=(mb == 0), stop=(mb == 1))
            S.op("dve", "reciprocal", reads=[RPS[ob]], writes=[Rrd4[hh]], out=rd4[:, hh:hh + 1], in_=PS[ob][:, 128:129])
            S.op("dve", "tensor_scalar", reads=[RPS[ob], Rrd4[hh]], writes=[Ro2], out=o2[:, hh * 128:(hh + 1) * 128],
                 in0=PS[ob][:, 0:128], scalar1=rd4[:, hh:hh + 1], scalar2=None, op0=ALU.mult)
            yield
        transposeN(o2, Ro2, 4, 128, o2T[:], [Ro2T], eng="act")
        yield
        for half in range(2):
            cs_ = slice(half * 512, (half + 1) * 512)
            for kc in range(4):
                S.op("pe", "matmul", reads=[Ro2T, Rwo[kc]], writes=[RPS[half]], out=PS[half][:, :], lhsT=o2T[:, kc, :],
                     rhs=wo[:, kc, cs_], start=(kc == 0), stop=(kc == 3))
            S.op("dve", "tensor_tensor", reads=[RPS[half], Rx1[tb]], writes=[Rx1[tb]], out=x1[:, tb, cs_],
                 in0=PS[half][:, :], in1=x1[:, tb, cs_], op=ALU.add)
            yield

    for _ in d_front(0):
        pass
    for tb in range(16):
        gens = [d_back(tb)]
        if tb + 1 < 16:
            gens.append(d_front(tb + 1))
        while gens:
            for gen in list(gens):
                try:
                    next(gen)
                except StopIteration:
                    gens.remove(gen)
    if stop_after == "D":
        S.barrier()
        return store_x1_and_finish()
    Rx2d = [Res() for _ in range(16)]
    for tb in range(16):
        S.dma("sp", reads=[Rx1[tb]], writes=[Rx2d[tb]], out=out_d[tb * 128:(tb + 1) * 128, :], in_=x1[:, tb, :])
    S.barrier()
    A.release(m_base)

    wpq, _ = T([128, 8, 2048], BF16, "wpq")
    Rwpq = [Res() for _ in range(8)]
    for k in range(8):
        S.dma("pool", writes=[Rwpq[k]], out=wpq[:, k, :], in_=peer_wq_d[k * 128:(k + 1) * 128, :])
    keysT, RkeysT = T([128, 2, 128], F32, "keysT")
    S.dma("sp", writes=[RkeysT], out=keysT[:, 0, :], in_=keys1T_d)
    S.dma("sp", writes=[RkeysT], out=keysT[:, 1, :], in_=keys2T_d)
    gf, Rgf = bc_load(1024, "ffn_norm_g")
    xrp = [T([128, 1024], F32, "xr") for _ in range(2)]
    h3bp = [T([128, 1024], BF16, "h3b") for _ in range(2)]
    h3T, Rh3T = T([128, 8, 128], BF16, "h3T")
    off_q = A.mark()
    qrT, RqrT = T([128, 16, 128], F32, "qrT")
    eq = nc.alloc_sbuf_tensor_at("eq_alias", [128, 8, 16, 16], F32, offset=off_q)
    Req = RqrT
    off_s = A.mark()
    sc, Rsc = T([128, 16, 128], F32, "sc")
    cand = nc.alloc_sbuf_tensor_at("cand_alias", [128, 8, 256], F32, offset=off_s)
    Rcand = Rsc
    sc2, Rsc2 = T([128, 128], F32, "sc2")
    tv, Rtv = T([128, 16, 16], F32, "tv")
    ti, Rti = T([128, 16, 16], U32, "ti")
    tif, Rtif = T([128, 16, 16], F32, "tif")
    cand2, Rcand2 = T([128, 256], F32, "cand2")
    cv, Rcv = T([128, 8, 16], F32, "cv")
    cp, Rcp = T([128, 8, 16], U32, "cp")
    ca, Rca = T([128, 8, 16], U32, "ca")
    cb_, Rcb_ = T([128, 8, 16], U32, "cb")
    caf, Rcaf = T([128, 8, 16], F32, "caf")
    cbf, Rcbf = T([128, 8, 16], F32, "cbf")
    i1f, Ri1f = T([128, 8, 16], F32, "i1f")
    i2f, Ri2f = T([128, 8, 16], F32, "i2f")
    ef, Ref = T([128, 128], F32, "ef")
    eip = [T([128, 128], I32, "ei") for _ in range(2)]
    gtp = [T([128, 8, 16], F32, "gt") for _ in range(2)]
    gsum, Rgsum = T([128, 8], F32, "gsum")
    actv, _ = T([128, 128], F32, "actv")
    Ractv = [Res() for _ in range(128)]
    WG = [T([128, 4], F32, "wg4") for _ in range(2)]
    wgt, _ = T([128, 128], F32, "wgt")
    Rwgt = [Res() for _ in range(32)]
    NUV = 6
    UVt = [(T([128, 4, 2048], BF16, "UV")[0], [Res() for _ in range(4)]) for _ in range(NUV)]
    PR = [T([128, 1024], BF16, "PR") for _ in range(8)]
    DG = [T([128, 128], BF16, "DG") for _ in range(4)]
    fin = []

    def peer_setup(tb, p):
        xr, Rxr = xrp[p]
        h3b, Rh3b = h3bp[p]
        ei, Rei = eip[p]
        gt, Rgt = gtp[p]
        S.dma("sp", reads=[Rx2d[tb]], writes=[Rxr], out=xr[:], in_=out_d[tb * 128:(tb + 1) * 128, :])
        rmsnorm(xr[:], Rxr, gf, Rgf, h3b[:], Rh3b)
        transposeN(h3b, Rh3b, 8, 128, h3T[:], [Rh3T], eng="act")
        yield
        for q4 in range(4):
            pb = q4 % 2
            for i in range(4):
                ccq = q4 * 4 + i
                for k in range(8):
                    S.op("pe", "matmul", reads=[Rwpq[k], Rh3T], writes=[RPS[pb]], out=PS[pb][:, i * 128:(i + 1) * 128],
                         lhsT=wpq[:, k, ccq * 128:(ccq + 1) * 128], rhs=h3T[:, k, :], start=(k == 0), stop=(k == 7))
            S.op("act", "copy", reads=[RPS[pb]], writes=[RqrT], out=qrT[:, q4 * 4:q4 * 4 + 4, :], in_=v3(PS[pb][:, :], 4))
            yield
        for q4 in range(4):
            pb = 2 + q4 % 2
            for i in range(4):
                j = q4 * 4 + i
                S.op("pe", "matmul", reads=[RqrT, RkeysT], writes=[RPS[pb]], out=PS[pb][:, i * 128:(i + 1) * 128],
                     lhsT=qrT[:, j, :], rhs=keysT[:, j % 2, :], start=True, stop=True)
            S.op("act", "copy", reads=[RPS[pb]], writes=[Rsc], out=sc[:, q4 * 4:q4 * 4 + 4, :], in_=v3(PS[pb][:, :], 4))
        for j in range(16):
            S.op("dve", "max", reads=[Rsc], writes=[Rtv], out=tv[:, j, 0:8], in_=sc[:, j, :])
            S.op("dve", "max_index", reads=[Rsc, Rtv], writes=[Rti], out=ti[:, j, 0:8], in_max=tv[:, j, 0:8],
                 in_values=sc[:, j, :])
            S.op("dve", "match_replace", reads=[Rsc, Rtv], writes=[Rsc2], out=sc2[:], in_to_replace=tv[:, j, 0:8],
                 in_values=sc[:, j, :], imm_value=-1e30)
            S.op("dve", "max", reads=[Rsc2], writes=[Rtv], out=tv[:, j, 8:16], in_=sc2[:])
            S.op("dve", "max_index", reads=[Rsc2, Rtv], writes=[Rti], out=ti[:, j, 8:16], in_max=tv[:, j, 8:16],
                 in_values=sc2[:])
            yield
        S.op("dve", "tensor_copy", reads=[Rti], writes=[Rtif], out=tif[:], in_=ti[:])
        tv4 = tv[:].rearrange("p (h two) k -> p h two k", two=2)
        tif4 = tif[:].rearrange("p (h two) k -> p h two k", two=2)
        S.op("dve", "tensor_tensor", reads=[Rtv], writes=[Rcand], out=cand[:].rearrange("p h (a b) -> p h a b", a=16),
             in0=tv4[:, :, 0, :].unsqueeze(3).broadcast_to([128, 8, 16, 16]),
             in1=tv4[:, :, 1, :].unsqueeze(2).broadcast_to([128, 8, 16, 16]), op=ALU.add)
        for h in range(8):
            S.op("dve", "max", reads=[Rcand], writes=[Rcv], out=cv[:, h, 0:8], in_=cand[:, h, :])
            S.op("dve", "max_index", reads=[Rcand, Rcv], writes=[Rcp], out=cp[:, h, 0:8], in_max=cv[:, h, 0:8],
                 in_values=cand[:, h, :])
            S.op("dve", "match_replace", reads=[Rcand, Rcv], writes=[Rcand2], out=cand2[:], in_to_replace=cv[:, h, 0:8],
                 in_values=cand[:, h, :], imm_value=-1e30)
            S.op("dve", "max", reads=[Rcand2], writes=[Rcv], out=cv[:, h, 8:16], in_=cand2[:])
            S.op("dve", "max_index", reads=[Rcand2, Rcv], writes=[Rcp], out=cp[:, h, 8:16], in_max=cv[:, h, 8:16],
                 in_values=cand2[:])
            yield
        S.op("dve", "tensor_single_scalar", reads=[Rcp], writes=[Rca], out=ca[:], in_=cp[:], scalar=4,
             op=ALU.logical_shift_right)
        S.op("dve", "tensor_single_scalar", reads=[Rcp], writes=[Rcb_], out=cb_[:], in_=cp[:], scalar=15,
             op=ALU.bitwise_and)
        S.op("dve", "tensor_copy", reads=[Rca], writes=[Rcaf], out=caf[:], in_=ca[:])
        S.op("dve", "tensor_copy", reads=[Rcb_], writes=[Rcbf], out=cbf[:], in_=cb_[:])
        io4 = iota16.unsqueeze(1).unsqueeze(1).broadcast_to([128, 8, 16, 16])
        for (sel, Rsel, half, dst, Rdst) in ((caf, Rcaf, 0, i1f, Ri1f), (cbf, Rcbf, 1, i2f, Ri2f)):
            S.op("dve", "tensor_tensor", reads=[Rsel, Rcst], writes=[Req], out=eq[:],
                 in0=sel[:].unsqueeze(3).broadcast_to([128, 8, 16, 16]), in1=io4, op=ALU.is_equal)
            S.op("dve", "tensor_tensor", reads=[Req, Rtif], writes=[Req], out=eq[:], in0=eq[:],
                 in1=tif4[:, :, half, :].unsqueeze(2).broadcast_to([128, 8, 16, 16]), op=ALU.mult)
            S.op("dve", "tensor_reduce", reads=[Req], writes=[Rdst], out=dst[:].rearrange("p h k -> p (h k)"),
                 in_=eq[:].rearrange("p h k a -> p (h k) a"), axis=AX.X, op=ALU.add)
        S.op("dve", "scalar_tensor_tensor", reads=[Ri1f, Ri2f], writes=[Ref], out=ef[:],
             in0=i1f[:].rearrange("p h k -> p (h k)"), scalar=128.0, in1=i2f[:].rearrange("p h k -> p (h k)"),
             op0=ALU.mult, op1=ALU.add)
        S.op("dve", "tensor_copy", reads=[Ref], writes=[Rei], out=ei[:], in_=ef[:])
        S.op("dve", "tensor_tensor", reads=[Rcv], writes=[Rgt], out=gt[:], in0=cv[:],
             in1=cv[:, :, 0:1].broadcast_to([128, 8, 16]), op=ALU.subtract)
        S.op("act", "activation", reads=[Rgt], writes=[Rgt], out=gt[:], in_=gt[:], func=AF.Exp)
        S.op("dve", "tensor_reduce", reads=[Rgt], writes=[Rgsum], out=gsum[:], in_=gt[:], axis=AX.X, op=ALU.add)
        S.op("dve", "reciprocal", reads=[Rgsum], writes=[Rgsum], out=gsum[:], in_=gsum[:])
        S.op("dve", "tensor_tensor", reads=[Rgt, Rgsum], writes=[Rgt], out=gt[:], in0=gt[:],
             in1=gsum[:].unsqueeze(2).broadcast_to([128, 8, 16]), op=ALU.mult)

    def peer_gather(tb, p, bg):
        xr, Rxr = xrp[p]
        h3b, Rh3b = h3bp[p]
        ei, Rei = eip[p]
        gt, Rgt = gtp[p]
        gtf = gt[:].rearrange("p h k -> p (h k)")
        S.op("pool", "memset", writes=Ractv, ap=actv[:], constant=0.0)

        def stA(g4):
            uvt, Ruvt = UVt[g4 % NUV]
            for i in range(4):
                s_ = g4 * 4 + i
                S.dma("pool", reads=[Rei, Ruv], writes=[Ruvt[i]], meth="indirect_dma_start", out=uvt[:, i, :],
                      out_offset=None, in_=uv_scr, in_offset=bass.IndirectOffsetOnAxis(ap=ei[:, s_:s_ + 1], axis=0))

        def stBC(g4):
            uvt, Ruvt = UVt[g4 % NUV]
            for i in range(4):
                s_ = g4 * 4 + i
                pr, Rpr = PR[s_ % 8]
                S.op("dve", "tensor_tensor", reads=[Ruvt[i], Rh3b], writes=[Rpr], out=pr[:], in0=uvt[:, i, 0:1024],
                     in1=h3b[:], op=ALU.mult)
                S.op("act", "activation", reads=[Rpr], writes=[Ractv[s_]], out=pr[:], in_=pr[:], func=AF.Copy,
                     accum_out=actv[:, s_:s_ + 1])
            wg, Rwg = WG[g4 % 2]
            S.op("act", "activation", reads=Ractv[g4 * 4:g4 * 4 + 4], writes=[Rwg], out=wg[:], in_=actv[:, g4 * 4:g4 * 4 + 4],
                 func=AF.Gelu)

        def stDE(g4):
            uvt, Ruvt = UVt[g4 % NUV]
            wg, Rwg = WG[g4 % 2]
            S.op("dve", "tensor_tensor", reads=[Rwg, Rgt], writes=[Rwgt[g4]], out=wgt[:, g4 * 4:g4 * 4 + 4], in0=wg[:],
                 in1=gtf[:, g4 * 4:g4 * 4 + 4], op=ALU.mult)
            for i in range(4):
                s_ = g4 * 4 + i
                dg, Rdg = DG[s_ % 4]
                S.op("dve", "tensor_scalar", reads=[Ridb, Rwgt[g4]], writes=[Rdg], out=dg[:], in0=ident_b[:],
                     scalar1=wgt[:, s_:s_ + 1], scalar2=None, op0=ALU.mult)
                for half in range(2):
                    S.op("pe", "matmul", reads=[Rdg, Ruvt[i]], writes=[RPS[5 + half]], out=PS[5 + half][:, :],
                         lhsT=dg[:], rhs=uvt[:, i, 1024 + half * 512:1024 + (half + 1) * 512], start=(s_ == 0),
                         stop=(s_ == 127))

        for g4 in range(NUV - 2):
            stA(g4)
        for g4 in range(33):
            if g4 + NUV - 2 < 32:
                stA(g4 + NUV - 2)
            if g4 < 32:
                stBC(g4)
            if g4 >= 1:
                stDE(g4 - 1)
            if bg is not None:
                next(bg, None)
                next(bg, None)
        if bg is not None:
            for _ in bg:
                pass
        for half in range(2):
            cs_ = slice(half * 512, (half + 1) * 512)
            S.op("dve", "tensor_tensor", reads=[RPS[5 + half], Rxr], writes=[Rxr], out=xr[:, cs_],
                 in0=xr[:, cs_], in1=PS[5 + half][:, :], op=ALU.add)
        R = Res()
        S.dma("sp", reads=[Rxr], writes=[R, Rx2d[tb]], out=out_d[tb * 128:(tb + 1) * 128, :], in_=xr[:])
        fin.append(R)

    for _ in peer_setup(0, 0):
        pass
    for tb in range(16):
        bg = peer_setup(tb + 1, (tb + 1) % 2) if tb + 1 < 16 else None
        peer_gather(tb, tb % 2, bg)
    for R in fin:
        S._wait("sp", R.w)
    S.emit()
    return nc


def _consts():
    c = np.zeros((128, NCONST), np.float32)
    p = np.arange(128)
    c[:, 0:128] = np.eye(128)
    c[:, 128:256] = (p[:, None] <= p[None, :])
    c[:, 256:384] = 1.0
    c[:, 384:512] = (p[None, :] - p[:, None])
    c[:, 512:528] = np.arange(16)[None, :]
    k = p[:, None, None]
    j = np.arange(17)[None, :, None]
    q = p[None, None, :]
    delta = (16 - j) * 128 + q - k
    m = ((delta >= 0) & (delta <= 128)).astype(np.float32)
    m += ((delta >= 0) & (delta <= 512) & (delta % 4 == 0))
    m += ((delta >= 0) & (delta <= 2048) & (delta % 16 == 0))
    return c, np.ascontiguousarray(m.reshape(128, 17 * 128).astype(np.float32))


_NC_CACHE = {}


def kernel(**inputs):
    stop_after = os.environ.get("KSTOP") or None
    x = np.asarray(inputs["x"], np.float32)
    mem = np.asarray(inputs["mem"], np.float32)
    cst, mtab = _consts()
    common = {
        "cst": cst, "mtab": mtab,
        "w_in": np.ascontiguousarray(inputs["w_in"][0], dtype=np.float32),
        "convw": np.ascontiguousarray(
            np.asarray(inputs["conv_w"][0], np.float32).T.reshape(18, 128, 4).transpose(1, 0, 2).reshape(128, 72)),
        "convb": np.ascontiguousarray(np.asarray(inputs["conv_b"][0], np.float32).reshape(18, 128).T),
        "w_out": np.ascontiguousarray(inputs["w_out"][0], dtype=np.float32),
    }
    for nm in ("mix_norm_g", "dt_bias", "a_log", "d_skip", "ssd_norm_g", "attn_q_norm_g", "attn_k_norm_g",
               "xattn_norm_g", "mem_norm_g", "mem_q_norm_g", "mem_k_norm_g", "ffn_norm_g"):
        common[nm] = np.ascontiguousarray(np.asarray(inputs[nm][0], np.float32).reshape(1, -1))
    if True:
        common["mem_w_q"] = np.ascontiguousarray(inputs["mem_w_q"][0], dtype=np.float32)
        common["mem_w_kv"] = np.ascontiguousarray(inputs["mem_w_kv"][0], dtype=np.float32)
        common["mem_w_o"] = np.ascontiguousarray(inputs["mem_w_o"][0], dtype=np.float32)
    if True:
        common["peer_w_query"] = np.ascontiguousarray(inputs["peer_w_query"][0], dtype=np.float32)
        common["keys1T"] = np.ascontiguousarray(np.asarray(inputs["peer_sub_keys1"][0], np.float32).T)
        common["keys2T"] = np.ascontiguousarray(np.asarray(inputs["peer_sub_keys2"][0], np.float32).T)
        common["peer_u"] = np.ascontiguousarray(inputs["peer_u"][0], dtype=np.float32)
        common["peer_v"] = np.ascontiguousarray(inputs["peer_v"][0], dtype=np.float32)
    in_maps = []
    for c in range(8):
        b, j = c // 2, c % 2
        xw = np.zeros((WIN, D), np.float32)
        if j == 1:
            xw[:HALF] = x[b, :HALF]
        xw[HALF:] = x[b, j * HALF:(j + 1) * HALF]
        m = dict(common)
        m["xw"] = xw
        m["flag"] = np.full((128, 1), float(j), np.float32)
        if True:
            m["mem"] = np.ascontiguousarray(mem[b])
        in_maps.append(m)
    nc = build(stop_after)
    res = run_bass_kernel_spmd(nc, in_maps, core_ids=list(range(8)))
    out = np.zeros((NB, SEQ, D), np.float32)
    for c in range(8):
        b, j = c // 2, c % 2
        out[b, j * HALF:(j + 1) * HALF] = np.asarray(res.results[c]["out"])
    return out
```
